# Optimizing a Trainium2 kernel written in Bass

```python
import jax, jax.numpy as jnp
from jax import lax
import numpy as np

D_MODEL = 1024
BATCH = 4
SEQ = 4096
DEPTH = 2

D_PLE = 256
D_LRU = 3 * D_MODEL // 4
D_RET = 3 * D_MODEL // 4
D_SGU = D_MODEL // 2
D_MIX = D_LRU + D_RET + D_SGU
LRU_BLOCKS = 8
LRU_BLOCK_DIM = D_LRU // LRU_BLOCKS
CONV_WIDTH = 4
LRU_C = 8.0
RET_HEADS = 6
RET_HEAD_DIM = D_RET // RET_HEADS
RET_CHUNK = 128
ROPE_BASE = 10000.0
SGU_GROUPS = 4
SGU_GROUP_DIM = D_SGU // SGU_GROUPS
SGU_CHUNK = 128
D_IN = 2 * D_LRU + 4 * D_RET + 3 * D_SGU
EPS = 1e-6

kernel_name = "hymba_lru_retention_sgu_trunk"


def rms_norm(x, g):
    xf = x.astype(jnp.float32)
    y = xf * lax.rsqrt(jnp.mean(xf * xf, axis=-1, keepdims=True) + EPS)
    return (y * g.astype(jnp.float32)).astype(x.dtype)


def causal_depthwise_conv(x, w, b):
    S = x.shape[1]
    xp = jnp.pad(x, ((0, 0), (CONV_WIDTH - 1, 0), (0, 0)))
    y = b + xp[:, 0:S] * w[0]
    for k in range(1, CONV_WIDTH):
        y = y + xp[:, k:k + S] * w[k]
    return y


def rg_lru(x, w_a, b_a, w_x, b_x, lam):
    B, S, _ = x.shape
    f32 = jnp.float32
    xf = x.astype(f32)
    xb = xf.reshape(B, S, LRU_BLOCKS, LRU_BLOCK_DIM)
    r = jax.nn.sigmoid(jnp.einsum('bsnc,ncd->bsnd', xb, w_a.astype(f32)).reshape(B, S, D_LRU) + b_a.astype(f32))
    i = jax.nn.sigmoid(jnp.einsum('bsnc,ncd->bsnd', xb, w_x.astype(f32)).reshape(B, S, D_LRU) + b_x.astype(f32))
    log_a = -LRU_C * r * jax.nn.softplus(-lam.astype(f32))
    a = jnp.exp(log_a)
    u = jnp.sqrt(-jnp.expm1(2.0 * log_a)) * (i * xf)

    def combine(c1, c2):
        a1, b1 = c1
        a2, b2 = c2
        return a1 * a2, a2 * b1 + b2

    _, h = lax.associative_scan(combine, (a, u), axis=1)
    return h.astype(x.dtype)


def rope(x, positions):
    half = RET_HEAD_DIM // 2
    inv = ROPE_BASE ** (-jnp.arange(half, dtype=jnp.float32) / half)
    ang = positions.astype(jnp.float32)[..., None] * inv
    cos = jnp.cos(ang)[:, :, None, :]
    sin = jnp.sin(ang)[:, :, None, :]
    x1, x2 = x[..., :half], x[..., half:]
    return jnp.concatenate([x1 * cos - x2 * sin, x2 * cos + x1 * sin], axis=-1)


def retention(q, k, v, positions, g_norm):
    B, S, _ = q.shape
    f32 = jnp.float32
    H, Dh, C = RET_HEADS, RET_HEAD_DIM, RET_CHUNK
    N = S // C
    qh = rope(q.reshape(B, S, H, Dh).astype(f32), positions)
    kh = rope(k.reshape(B, S, H, Dh).astype(f32), positions) * (Dh ** -0.5)
    vh = v.reshape(B, S, H, Dh).astype(f32)
    log_g = jnp.log1p(-jnp.exp2(-5.0 - jnp.arange(H, dtype=f32)))
    idx = jnp.arange(C, dtype=f32)
    diff = idx[:, None] - idx[None, :]
    causal = diff >= 0
    decay = jnp.where(causal[None], jnp.exp(jnp.where(causal, diff, 0.0)[None] * log_g[:, None, None]), 0.0)
    qc = qh.reshape(B, N, C, H, Dh)
    kc = kh.reshape(B, N, C, H, Dh)
    vc = vh.reshape(B, N, C, H, Dh)
    scores = jnp.einsum('bnihd,bnjhd->bnhij', qc, kc) * decay
    intra = jnp.einsum('bnhij,bnjhe->bnihe', scores, vc)
    k_w = jnp.exp((C - 1.0 - idx)[:, None] * log_g[None, :])
    kv = jnp.einsum('bnjhd,jh,bnjhe->nbhde', kc, k_w, vc)
    chunk_decay = jnp.exp(C * log_g)[None, :, None, None]

    def step(state, kv_n):
        return state * chunk_decay + kv_n, state

    _, s_prev = lax.scan(step, jnp.zeros((B, H, Dh, Dh), f32), kv)
    q_w = jnp.exp((idx + 1.0)[:, None] * log_g[None, :])
    cross = jnp.einsum('bnihd,ih,nbhde->bnihe', qc, q_w, s_prev)
    o = (intra + cross).reshape(B, S, H, Dh)
    o = o * lax.rsqrt(jnp.mean(o * o, axis=-1, keepdims=True) + EPS)
    o = o.reshape(B, S, D_RET) * g_norm.astype(f32)
    return o.astype(q.dtype)


def spatial_gating(u, v, ln_g, ln_b, w_s, b_s):
    B, S, _ = u.shape
    f32 = jnp.float32
    N = S // SGU_CHUNK
    vf = v.astype(f32)
    mu = jnp.mean(vf, axis=-1, keepdims=True)
    var = jnp.mean(jnp.square(vf - mu), axis=-1, keepdims=True)
    vn = (vf - mu) * lax.rsqrt(var + EPS) * ln_g.astype(f32) + ln_b.astype(f32)
    vc = vn.reshape(B, N, SGU_CHUNK, SGU_GROUPS, SGU_GROUP_DIM)
    mask = jnp.tril(jnp.ones((SGU_CHUNK, SGU_CHUNK), dtype=bool))
    ws = jnp.where(mask[None], w_s.astype(f32), 0.0)
    mixed = jnp.einsum('gij,bnjgc->bnigc', ws, vc) + b_s.astype(f32).T[None, None, :, :, None]
    return (u.astype(f32) * mixed.reshape(B, S, D_SGU)).astype(u.dtype)


def setup_inputs(seed: int = 0) -> dict:
    key = jax.random.key(seed)
    ks = jax.random.split(key, 24)
    f32 = jnp.float32

    def nrm(k, shape, scale):
        return jax.random.normal(k, shape, f32) * scale

    x = jax.random.normal(ks[0], (BATCH, SEQ, D_MODEL), f32)
    p = jax.random.normal(ks[1], (DEPTH, BATCH, SEQ, D_PLE), f32)
    positions = jnp.broadcast_to(jnp.arange(SEQ, dtype=jnp.int32), (BATCH, SEQ))
    a_init = jax.random.uniform(ks[9], (DEPTH, D_LRU), f32, 0.9, 0.999)
    return {
        "x": x,
        "p": p,
        "positions": positions,
        "mix_norm": 1.0 + nrm(ks[2], (DEPTH, D_MODEL), 0.02),
        "w_in": nrm(ks[3], (DEPTH, D_MODEL, D_IN), D_MODEL ** -0.5),
        "lru_conv_w": nrm(ks[4], (DEPTH, CONV_WIDTH, D_LRU), CONV_WIDTH ** -0.5),
        "lru_conv_b": nrm(ks[5], (DEPTH, D_LRU), 0.01),
        "lru_w_a": nrm(ks[6], (DEPTH, LRU_BLOCKS, LRU_BLOCK_DIM, LRU_BLOCK_DIM), LRU_BLOCK_DIM ** -0.5),
        "lru_b_a": nrm(ks[7], (DEPTH, D_LRU), 0.01),
        "lru_w_x": nrm(ks[8], (DEPTH, LRU_BLOCKS, LRU_BLOCK_DIM, LRU_BLOCK_DIM), LRU_BLOCK_DIM ** -0.5),
        "lru_b_x": nrm(ks[10], (DEPTH, D_LRU), 0.01),
        "lru_lambda": jnp.log(a_init) - jnp.log1p(-a_init),
        "ret_norm": 1.0 + nrm(ks[11], (DEPTH, D_RET), 0.02),
        "sgu_ln_g": 1.0 + nrm(ks[12], (DEPTH, D_SGU), 0.02),
        "sgu_ln_b": nrm(ks[13], (DEPTH, D_SGU), 0.01),
        "sgu_w_s": nrm(ks[14], (DEPTH, SGU_GROUPS, SGU_CHUNK, SGU_CHUNK), 0.5 * SGU_CHUNK ** -0.5),
        "sgu_b_s": 1.0 + nrm(ks[15], (DEPTH, SGU_GROUPS, SGU_CHUNK), 0.02),
        "w_out": nrm(ks[16], (DEPTH, D_MIX, D_MODEL), D_MIX ** -0.5),
        "ple_norm": 1.0 + nrm(ks[17], (DEPTH, D_MODEL), 0.02),
        "w_ple_gate": nrm(ks[18], (DEPTH, D_MODEL, D_MODEL), D_MODEL ** -0.5),
        "w_ple_proj": nrm(ks[19], (DEPTH, D_PLE, D_MODEL), D_PLE ** -0.5),
        "final_norm": 1.0 + nrm(ks[20], (D_MODEL,), 0.02),
    }


def reference(x, p, positions, mix_norm, w_in, lru_conv_w, lru_conv_b, lru_w_a, lru_b_a, lru_w_x, lru_b_x,
              lru_lambda, ret_norm, sgu_ln_g, sgu_ln_b, sgu_w_s, sgu_b_s, w_out, ple_norm, w_ple_gate,
              w_ple_proj, final_norm):
    sizes = (D_LRU, D_LRU, D_RET, D_RET, D_RET, D_RET, D_SGU, D_SGU, D_SGU)
    split_points = []
    acc = 0
    for s in sizes[:-1]:
        acc += s
        split_points.append(acc)
    h = x
    for i in range(DEPTH):
        y = rms_norm(h, mix_norm[i])
        proj = y @ w_in[i]
        x_lru, g_lru, q, k, v, g_ret, u_sgu, v_sgu, g_sgu = jnp.split(proj, split_points, axis=-1)
        lru_out = rg_lru(causal_depthwise_conv(x_lru, lru_conv_w[i], lru_conv_b[i]),
                         lru_w_a[i], lru_b_a[i], lru_w_x[i], lru_b_x[i], lru_lambda[i])
        ret_out = retention(q, k, v, positions, ret_norm[i])
        sgu_out = spatial_gating(u_sgu, v_sgu, sgu_ln_g[i], sgu_ln_b[i], sgu_w_s[i], sgu_b_s[i])
        mixed = jnp.concatenate([lru_out * jax.nn.silu(g_lru),
                                 ret_out * jax.nn.silu(g_ret),
                                 sgu_out * jax.nn.silu(g_sgu)], axis=-1)
        h = h + mixed @ w_out[i]
        z = rms_norm(h, ple_norm[i])
        h = h + jax.nn.sigmoid(z @ w_ple_gate[i]) * (p[i] @ w_ple_proj[i])
    return rms_norm(h, final_norm)
```

```python
import os
import numpy as np
import concourse.bass as bass
import concourse.mybir as mybir
from concourse.bass_utils import run_bass_kernel_spmd

F32 = mybir.dt.float32
BF16 = mybir.dt.bfloat16
I32 = mybir.dt.int32
ALU = mybir.AluOpType
AF = mybir.ActivationFunctionType

D = 1024
S = 4096
TT = 512
NT = S // TT
DEPTH = 2
D_IN = 6144
EPS = 1e-6
NPK = 192
PI = float(np.pi)

DEBUG = os.environ.get("KDEBUG", "") != ""
NT_RUN = int(os.environ.get("KNT", NT))
KSTAGE = int(os.environ.get("KSTAGE", 99))
KRET = int(os.environ.get("KRET", 99))
KQ = int(os.environ.get("KQ", 99))
KSKIPW = os.environ.get("KSKIPW", "") != ""


class Op:
    __slots__ = ("eng", "fn", "reads", "writes", "dma_sem", "dma_val", "deps", "signal", "val", "idx")


class Prog:
    ENGS = ("pe", "act", "dve", "pool", "sp")

    def __init__(self, nc):
        self.nc = nc
        self.ops = []
        self.last_w = {}
        self.readers = {}
        self.dma_counts = {}
        self.shared = set()

    def op(self, eng, fn, reads=(), writes=(), dma_sem=None):
        o = Op()
        o.eng, o.fn, o.reads, o.writes = eng, fn, tuple(reads), tuple(writes)
        o.dma_sem = dma_sem
        o.signal = False
        o.val = 0
        o.idx = len(self.ops)
        if dma_sem is not None:
            self.dma_counts[dma_sem] = self.dma_counts.get(dma_sem, 0) + 16
            o.dma_val = self.dma_counts[dma_sem]
        else:
            o.dma_val = 0
        deps = set()
        is_dma = dma_sem is not None
        def need(p):
            return p.dma_sem is not None or is_dma or not (p.eng == eng == "pe")
        for k in o.reads:
            w = self.last_w.get(k)
            if w is not None and need(w):
                deps.add(w.idx)
            if isinstance(k, tuple) and k and k[0] == "ps":
                for r in self.readers.get(k, ()):
                    if r.eng != eng:
                        deps.add(r.idx)
        for k in o.writes:
            w = self.last_w.get(k)
            if w is not None and need(w):
                deps.add(w.idx)
            for r in self.readers.get(k, ()):
                if need(r):
                    deps.add(r.idx)
        deps.discard(o.idx)
        o.deps = deps
        wset = set(o.writes)
        for k in o.writes:
            self.last_w[k] = o
            self.readers[k] = []
        for k in o.reads:
            if k not in wset:
                self.readers.setdefault(k, []).append(o)
        self.ops.append(o)
        return o

    def emit(self, block, eng_sems, final_waits):
        ops = self.ops
        waits = {}
        for en in self.ENGS:
            widx = {}
            wdma = {}
            for o in ops:
                if o.eng != en:
                    continue
                best = {}
                dm = {}
                for d in o.deps:
                    p = ops[d]
                    if p.dma_sem is not None:
                        v = self.dma_counts[p.dma_sem] if p.dma_sem in self.shared else p.dma_val
                        if v > dm.get(p.dma_sem, 0):
                            dm[p.dma_sem] = v
                    else:
                        if d > best.get(p.eng, -1):
                            best[p.eng] = d
                lst = []
                for pe_, d in best.items():
                    if d > widx.get(pe_, -1):
                        widx[pe_] = d
                        ops[d].signal = True
                        lst.append(("e", d))
                for sm, v in dm.items():
                    if v > wdma.get(sm, 0):
                        wdma[sm] = v
                        lst.append(("d", sm, v))
                waits[o.idx] = lst
        cnt = {e: 0 for e in self.ENGS}
        for o in ops:
            if o.dma_sem is None and o.signal:
                cnt[o.eng] += 1
                o.val = cnt[o.eng]

        def run(eng_name, eng):
            for o in ops:
                if o.eng != eng_name:
                    continue
                for w in waits[o.idx]:
                    if w[0] == "e":
                        p = ops[w[1]]
                        eng.wait_ge(eng_sems[p.eng], p.val)
                    else:
                        eng.wait_ge(w[1], w[2])
                ins = o.fn(eng)
                if o.dma_sem is not None:
                    ins.then_inc(o.dma_sem, 16)
                elif o.signal:
                    ins.then_inc(eng_sems[o.eng], 1)
            if eng_name == "sp":
                for s, v in final_waits():
                    eng.wait_ge(s, v)

        @block.tensor
        def _(e):
            run("pe", e)

        @block.scalar
        def _(e):
            run("act", e)

        @block.vector
        def _(e):
            run("dve", e)

        @block.gpsimd
        def _(e):
            run("pool", e)

        @block.sync
        def _(e):
            run("sp", e)


class Pool_:
    def __init__(self, items):
        self.free = list(items)

    def get(self):
        return self.free.pop(0)

    def put(self, x):
        self.free.append(x)


def build_nc():
    nc = bass.Bass("TRN2", target_bir_lowering=False)
    dt_in = lambda name, shape, dt=F32: nc.dram_tensor(name, shape, dt, kind="ExternalInput").ap()
    xT = dt_in("xT", [D, S])
    pT = dt_in("pT", [DEPTH, 256, S])
    pos = dt_in("pos", [1, S], I32)
    w_in = dt_in("w_in", [DEPTH, D, D_IN])
    w_out = dt_in("w_out", [DEPTH, 2048, D])
    w_gate = dt_in("w_gate", [DEPTH, D, D])
    w_pp = dt_in("w_pp", [DEPTH, 256, D])
    w_a = dt_in("w_a", [DEPTH, 8, 96, 96])
    w_x = dt_in("w_x", [DEPTH, 8, 96, 96])
    w_sT = dt_in("w_sT", [DEPTH, 128, 512])
    b_s = dt_in("b_s", [1, DEPTH * 512])
    ln_g = dt_in("ln_g", [DEPTH, 512])
    ln_b = dt_in("ln_b", [DEPTH, 512])
    pk_d = dt_in("pk", [128, NPK])
    cst_d = dt_in("cst", [128, 4 * 768 + 128])
    outT = nc.dram_tensor("outT", [D, S], F32, kind="ExternalOutput").ap()
    dbg_outs = {}

    wb_in = wb_out = wb_gate = wb_pp = pTb = None
    scr = nc.dram_tensor("scr", [DEPTH * 16, 128, 6144], BF16, kind="Internal").ap()

    from contextlib import ExitStack
    es = ExitStack()
    sb = lambda name, shape, dt=F32: es.enter_context(nc.sbuf_tensor("sb_" + name, shape, dt))
    sem = lambda name: es.enter_context(nc.semaphore(name))

    with es:
        hT = sb("hT", [128, 8, TT])
        yT = sb("yT", [128, 8, TT], BF16)
        NSLOT = 4
        ring = [sb(f"ring{i}", [128, 6144], BF16) for i in range(NSLOT)]
        mixL = sb("mixL", [128, 8, TT], BF16)
        mixR = sb("mixR", [128, 6, TT], BF16)
        mixS = sb("mixS", [128, 4, TT], BF16)
        NF, NB = 12, 8
        funits = [sb(f"fu{i}", [128, TT]) for i in range(NF)]
        bunits = [sb(f"bu{i}", [128, TT], BF16) for i in range(NB)]
        xls = [sb(f"xl{i}", [128, TT + 4]) for i in range(2)]
        qt = sb("qt", [128, 3, TT], BF16)
        kt = sb("kt", [128, 3, TT], BF16)
        qg = sb("qg", [128, 3, TT], BF16)
        qlo = sb("qlo", [128, 3, TT], BF16)
        klo = sb("klo", [128, 3, TT], BF16)
        vbf = sb("vbf", [128, 4, 384], BF16)
        vw = sb("vw", [128, 4, 384], BF16)
        ktm = sb("ktm", [128, 4, 384], BF16)
        Sst = sb("Sst", [128, DEPTH, 768])
        Sb = sb("Sb", [128, DEPTH, 768], BF16)
        ropeT = sb("ropeT", [128, 4, TT])
        cstT = sb("cstT", [128, 3 * 768])
        maskT = sb("maskT", [128, 128])
        cbT = sb("cbT", [128, 3 * 128], BF16)
        pk = sb("pk", [128, NPK])
        dv = sb("dv", [128, 64])
        vn = sb("vn", [128, 4, TT], BF16)
        wsT = sb("wsT", [128, DEPTH, 512], BF16)
        lng = sb("lng", [128, DEPTH, 512])
        lnb = sb("lnb", [128, DEPTH, 512])
        bsb = sb("bsb", [1, DEPTH * 512], BF16)
        pTt = sb("pTt", [128, 2, TT], BF16)
        wab = sb("wab", [96, DEPTH * 8 * 96], BF16)
        wxb = sb("wxb", [96, DEPTH * 8 * 96], BF16)
        hst = sb("hst", [128, DEPTH * 8])
        hist = sb("hist", [128, DEPTH * 8 * 4])
        posi = sb("posi", [128, TT], I32)
        ki = sb("ki", [128, TT], I32)
        small = sb("small", [128, 64])
        banks = [es.enter_context(nc.psum_tensor(f"ps{i}", [128, TT], F32)) for i in range(8)]

        eng_sems = {e: sem(f"s_{e}") for e in Prog.ENGS}
        s_cast = [sem(f"s_cast{i}") for i in range(4)]
        s_wb = [sem(f"s_wb{i}") for i in range(NSLOT)]
        s_rl = [sem(f"s_rl{i}") for i in range(NSLOT)]
        s_ring = [sem(f"s_ring{i}") for i in range(NSLOT)]
        s_ld = sem("s_ld")
        s_x = sem("s_x")
        s_pos = sem("s_pos")
        s_p = sem("s_p")
        s_out = [sem(f"s_out{i}") for i in range(8)]
        s_dbg = sem("s_dbg")

        pg = Prog(nc)
        pg.shared = set([s_ld] + list(s_cast))
        fpool = Pool_(range(NF))
        bpool = Pool_(range(NB))
        pspool = Pool_(range(8))

        ident = cbT[:, 0:128]
        perm = cbT[:, 128:256]
        ones = cbT[:, 256:384]
        decayT = cstT[:, 0:768]
        kwc = cstT[:, 768:1536]
        qwc = cstT[:, 1536:2304]
        epsc = pk[:, 183:184]
        onec = pk[:, 184:185]

        def dbg(name, ap, shape, dt, reads):
            if not DEBUG:
                return
            o = nc.dram_tensor("dbg_" + name, list(shape), dt, kind="ExternalOutput").ap()
            dbg_outs[name] = o
            pg.op("sp", lambda e, o=o, ap=ap: e.dma_start(out=o, in_=ap), reads=reads, writes=["dbgsem"], dma_sem=s_dbg)

        cast_i = [0]

        def cast_dma(out_ap, in_ap, wkey):
            sm = s_cast[cast_i[0] % len(s_cast)]
            cast_i[0] += 1
            pg.op("pool", lambda e: e.dma_start(out=out_ap, in_=in_ap), reads=[], writes=[wkey], dma_sem=sm)

        def ld(out_ap, in_ap, wkey):
            pg.op("sp", lambda e: e.dma_start(out=out_ap, in_=in_ap), reads=[], writes=[wkey], dma_sem=s_ld)

        ld(pk[:], pk_d, "pk")
        ld(cstT[:], cst_d[:, 0:2304], "cst")
        ld(maskT[:], cst_d[:, 2304 + 768:2304 + 768 + 128], "mask")
        for l in range(DEPTH):
            ld(lng[:, l, :], ln_g[l:l + 1, :].broadcast_to([128, 512]), ("lng", l))
            ld(lnb[:, l, :], ln_b[l:l + 1, :].broadcast_to([128, 512]), ("lnb", l))
        cast_dma(cbT[:], cst_d[:, 2304:2304 + 384], "cb")
        cast_dma(bsb[:], b_s, "bsb")
        cast_dma(wab[:].rearrange("c (l n d) -> c l n d", l=DEPTH, n=8), w_a.rearrange("l n c d -> c l n d"), "wab")
        cast_dma(wxb[:].rearrange("c (l n d) -> c l n d", l=DEPTH, n=8), w_x.rearrange("l n c d -> c l n d"), "wxb")
        for l in range(0):
            for g in range(8):
                cast_dma(wb_in[l, :, g * 768:(g + 1) * 768], w_in[l, :, g * 768:(g + 1) * 768], ("wb_in", l, g))
            for g in range(2):
                cast_dma(wb_out[l, g * 1024:(g + 1) * 1024, :], w_out[l, g * 1024:(g + 1) * 1024, :], ("wb_out", l))
            cast_dma(wb_gate[l], w_gate[l], ("wb_gate", l))
            cast_dma(wb_pp[l], w_pp[l], ("wb_pp", l))
            cast_dma(pTb[l], pT[l], ("pTb", l))

        pg.op("dve", lambda e: e.memset(Sst[:], 0.0), writes=["Sst"])
        pg.op("dve", lambda e: e.memset(Sb[:], 0.0), writes=["Sb"])
        pg.op("dve", lambda e: e.memset(hst[:], 0.0), writes=["hst"])
        pg.op("dve", lambda e: e.memset(hist[:], 0.0), writes=[("hist", i) for i in range(16)])
        for i in range(2):
            pg.op("dve", lambda e, i=i: e.memset(xls[i][:], 0.0), writes=[("xl", i)])
        for l in range(DEPTH):
            u = fpool.get()
            ld(funits[u][:], w_sT[l], ("f", u))
            pg.op("dve", lambda e, l=l, u=u: e.tensor_tensor(
                out=wsT[:, l, :].rearrange("p (g i) -> p g i", g=4),
                in0=funits[u][:].rearrange("p (g i) -> p g i", g=4),
                in1=maskT[:].unsqueeze(1).broadcast_to([128, 4, 128]), op=ALU.mult),
                reads=[("f", u), "mask"], writes=[("wsT", l)])
            fpool.put(u)
        pg.op("dve", lambda e: e.tensor_scalar(out=dv[:, 0:32], in0=pk[:, 132:164], scalar1=0.5, scalar2=None, op0=ALU.mult),
              reads=["pk"], writes=["dv0"])
        pg.op("act", lambda e: e.activation(out=small[:, 0:16], in_=pk[:, 164:180], func=AF.Exp, scale=-1.0),
              reads=["pk"], writes=["sm0"])
        pg.op("act", lambda e: e.activation(out=small[:, 16:32], in_=small[:, 0:16], func=AF.Ln, bias=onec, scale=1.0),
              reads=["sm0", "pk"], writes=["sm1"])
        pg.op("dve", lambda e: e.tensor_scalar(out=dv[:, 32:48], in0=small[:, 16:32], scalar1=-8.0, scalar2=None, op0=ALU.mult),
              reads=["sm1"], writes=["dv1"])
        pg.op("dve", lambda e: e.tensor_scalar(out=dv[:, 48:64], in0=small[:, 16:32], scalar1=-4.0, scalar2=None, op0=ALU.mult),
              reads=["sm1"], writes=["dv2"])
        DVK = ["dv0", "dv1", "dv2"]

        ring_i = [0]
        cur_t = [0]
        job_i = [0]

        def wload(dmas, rkeys):
            sl = ring_i[0] % NSLOT
            ring_i[0] += 1
            j = job_i[0]
            job_i[0] += 1
            for di, (vf, src) in enumerate(dmas):
                if cur_t[0] == 0:
                    pg.op("pool", lambda e, vf=vf, src=src, sl=sl: e.dma_start(out=vf(ring[sl]), in_=src),
                          reads=[], writes=[("ring", sl), "poolq"], dma_sem=s_ring[sl])
                    pg.op("sp", lambda e, vf=vf, sl=sl, j=j: e.dma_start(out=vf(scr[j]), in_=vf(ring[sl])),
                          reads=[("ring", sl)], writes=[("scr", j, di)], dma_sem=s_wb[sl])
                else:
                    pg.op("sp", lambda e, vf=vf, sl=sl, j=j: e.dma_start(out=vf(ring[sl]), in_=vf(scr[j])),
                          reads=[("scr", j, di)], writes=[("ring", sl)], dma_sem=s_rl[sl])
            return sl

        def w_in_job(l, g, ncols=768):
            c0 = g * 768
            src = w_in[l, :, c0:c0 + ncols].rearrange("(k p) c -> p k c", p=128)
            return wload([(lambda r, n=ncols: r[:, 0:8 * n].rearrange("p (k c) -> p k c", k=8), src)], [("wb_in", l, g)])

        def mm(bank, out_ap, lhsT, rhs, start, stop, reads, extra_w=()):
            pg.op("pe", lambda e: e.matmul(out_ap, lhsT, rhs, start=start, stop=stop),
                  reads=reads, writes=[("ps", bank)] + list(extra_w))

        def proj_fm(l, slot, ncols, c0, m, bank, mp=128):
            rv = ring[slot][:, 0:8 * ncols].rearrange("p (k c) -> p k c", k=8)
            for kc in range(8):
                mm(bank, banks[bank][0:m, :], rv[:, kc, c0:c0 + m], yT[:, kc, :], kc == 0, kc == 7,
                   [("ring", slot), "yT"])

        def proj_tm(slot, ncols, c0, n, blk, bank):
            rv = ring[slot][:, 0:8 * ncols].rearrange("p (k c) -> p k c", k=8)
            for kc in range(8):
                mm(bank, banks[bank][:, 0:n], yT[:, kc, blk * 128:(blk + 1) * 128], rv[:, kc, c0:c0 + n], kc == 0, kc == 7,
                   [("ring", slot), "yT"])

        def rms_stage(gcol, out_fn, out_keys_fn, out_dt_bf=True):
            bank = pspool.get()
            for c in range(8):
                b = bpool.get()
                pg.op("act", lambda e, c=c, b=b: e.activation(out=bunits[b][:], in_=hT[:, c, :], func=AF.Square),
                      reads=[("hT", c)], writes=[("b", b)])
                mm(bank, banks[bank][:], ones, bunits[b][:], c == 0, c == 7, [("b", b), "cb"])
                bpool.put(b)
            r = fpool.get()
            pg.op("act", lambda e: e.activation(out=funits[r][:], in_=banks[bank][:], func=AF.Ln, bias=epsc, scale=1.0 / D),
                  reads=[("ps", bank), "pk"], writes=[("f", r)])
            pg.op("act", lambda e: e.activation(out=funits[r][:], in_=funits[r][:], func=AF.Exp, scale=-0.5), reads=[("f", r)], writes=[("f", r)])
            pspool.put(bank)
            for c in range(8):
                pg.op("dve", lambda e, c=c: e.scalar_tensor_tensor(out=out_fn(c), in0=hT[:, c, :], scalar=gcol[:, c:c + 1],
                                                                  in1=funits[r][:], op0=ALU.mult, op1=ALU.mult),
                      reads=[("hT", c), ("f", r), "pk"], writes=out_keys_fn(c))
            fpool.put(r)

        gam = []
        for h in range(6):
            lg = np.log1p(-np.exp2(np.float32(-5.0 - h))).astype(np.float32)
            gam.append(float(np.exp(np.float32(128.0) * lg)))

        def do_tile(t):
            cur_t[0] = t
            job_i[0] = 0
            tok = slice(t * TT, (t + 1) * TT)
            pg.op("pool", lambda e, tok=tok: e.dma_start(out=hT[:], in_=xT.rearrange("(c p) t -> p c t", p=128)[:, :, tok]),
                  reads=[], writes=[("hT", c) for c in range(8)] + ["poolq"], dma_sem=s_x)
            pg.op("pool", lambda e, tok=tok: e.dma_start(out=posi[:], in_=pos[0:1, tok].broadcast_to([128, TT])),
                  reads=[], writes=["posi", "poolq"], dma_sem=s_pos)
            ang = fpool.get()
            tmp = fpool.get()
            kf = fpool.get()
            pg.op("dve", lambda e: e.tensor_copy(out=funits[ang][:], in_=posi[:]), reads=["posi"], writes=[("f", ang)])
            pg.op("dve", lambda e: e.tensor_scalar(out=funits[ang][:], in0=funits[ang][:], scalar1=pk[:, 180:181], scalar2=None, op0=ALU.mult),
                  reads=[("f", ang), "pk"], writes=[("f", ang)])
            C1 = 6.28125
            C2 = float(2.0 * np.pi - 6.28125)
            for which, off in ((1, 0.0), (0, 0.25)):
                pg.op("dve", lambda e, off=off: e.tensor_scalar(out=funits[tmp][:], in0=funits[ang][:], scalar1=float(1.0 / (2 * np.pi)), scalar2=off,
                                                               op0=ALU.mult, op1=ALU.add), reads=[("f", ang)], writes=[("f", tmp)])
                pg.op("dve", lambda e: e.tensor_copy(out=ki[:], in_=funits[tmp][:]), reads=[("f", tmp)], writes=["ki"])
                pg.op("dve", lambda e: e.tensor_copy(out=funits[kf][:], in_=ki[:]), reads=["ki"], writes=[("f", kf)])
                pg.op("dve", lambda e: e.scalar_tensor_tensor(out=funits[tmp][:], in0=funits[kf][:], scalar=-C1, in1=funits[ang][:], op0=ALU.mult, op1=ALU.add),
                      reads=[("f", kf), ("f", ang)], writes=[("f", tmp)])
                pg.op("dve", lambda e: e.scalar_tensor_tensor(out=funits[tmp][:], in0=funits[kf][:], scalar=-C2, in1=funits[tmp][:], op0=ALU.mult, op1=ALU.add),
                      reads=[("f", kf), ("f", tmp)], writes=[("f", tmp)])
                if off != 0.0:
                    pg.op("dve", lambda e: e.tensor_scalar(out=funits[tmp][:], in0=funits[tmp][:], scalar1=float(np.pi / 2), scalar2=None, op0=ALU.add),
                          reads=[("f", tmp)], writes=[("f", tmp)])
                pg.op("dve", lambda e: e.tensor_scalar(out=funits[tmp][:], in0=funits[tmp][:], scalar1=-3.1415925, scalar2=3.1415925, op0=ALU.max, op1=ALU.min),
                      reads=[("f", tmp)], writes=[("f", tmp)])
                pg.op("act", lambda e, which=which: e.activation(out=ropeT[:, which, :], in_=funits[tmp][:], func=AF.Sin),
                      reads=[("f", tmp)], writes=[("rope", which)])
            RS = float(128.0 ** -0.5)
            pg.op("dve", lambda e: e.tensor_scalar(out=ropeT[:, 2, :], in0=ropeT[:, 0, :], scalar1=RS, scalar2=None, op0=ALU.mult),
                  reads=[("rope", 0)], writes=[("rope", 2)])
            pg.op("dve", lambda e: e.tensor_scalar(out=ropeT[:, 3, :], in0=ropeT[:, 1, :], scalar1=pk[:, 182:183], scalar2=None, op0=ALU.mult),
                  reads=[("rope", 1), "pk"], writes=[("rope", 3)])
            pg.op("dve", lambda e: e.tensor_scalar(out=ropeT[:, 1, :], in0=ropeT[:, 1, :], scalar1=pk[:, 181:182], scalar2=None, op0=ALU.mult),
                  reads=[("rope", 1), ("rope", 3), "pk"], writes=[("rope", 1)])
            for u_ in (ang, tmp, kf):
                fpool.put(u_)

            def do_layer(l):
                if KSTAGE < 2:
                    return
                rms_stage(pk[:, l * 8:(l + 1) * 8], lambda c: yT[:, c, :], lambda c: ["yT"])
                if DEBUG and t == 0 and l == 0:
                    dbg("yT", yT[:], [128, 8, TT], BF16, ["yT"])

                if KSTAGE < 3:
                    return
                sx = w_in_job(l, 0)
                sg = w_in_job(l, 1)
                pend = []

                def lru_A(n):
                    bx = pspool.get()
                    proj_fm(l, sx, 768, n * 96, 96, bx)
                    xl = xls[n % 2]
                    xk = ("xl", n % 2)
                    hi = (l * 8 + n) * 4
                    pg.op("pool", lambda e: e.tensor_copy(out=xl[0:96, 0:3], in_=hist[0:96, hi:hi + 3]),
                          reads=[("hist", l * 8 + n)], writes=[xk])
                    pg.op("act", lambda e: e.activation(out=xl[0:96, 3:TT + 3], in_=banks[bx][0:96, :], func=AF.Copy),
                          reads=[("ps", bx)], writes=[xk])
                    pspool.put(bx)
                    bg = pspool.get()
                    proj_fm(l, sg, 768, n * 96, 96, bg)
                    return bg

                def lru_B1(n, bg):
                    xl = xls[n % 2]
                    xk = ("xl", n % 2)
                    col = l * 8 + n
                    cw = lambda k: pk[0:96, 52 + col * 4 + k:52 + col * 4 + k + 1]
                    xc = fpool.get()
                    XC = funits[xc][0:96, :]
                    pg.op("dve", lambda e: e.tensor_scalar(out=XC, in0=xl[0:96, 0:TT], scalar1=cw(0), scalar2=pk[0:96, 116 + col:117 + col],
                                                           op0=ALU.mult, op1=ALU.add), reads=[xk, "pk"], writes=[("f", xc)])
                    for k in (1, 2, 3):
                        pg.op("dve", lambda e, k=k: e.scalar_tensor_tensor(out=XC, in0=xl[0:96, k:k + TT], scalar=cw(k), in1=XC,
                                                                          op0=ALU.mult, op1=ALU.add), reads=[xk, "pk", ("f", xc)], writes=[("f", xc)])
                    hi = col * 4
                    pg.op("pool", lambda e: e.tensor_copy(out=hist[0:96, hi:hi + 3], in_=xl[0:96, TT:TT + 3]),
                          reads=[xk], writes=[("hist", col)])
                    xb = bpool.get()
                    pg.op("pool", lambda e: e.tensor_copy(out=bunits[xb][0:96, :], in_=XC), reads=[("f", xc)], writes=[("b", xb)])
                    ba = pspool.get()
                    bxx = pspool.get()
                    wo = col * 96
                    mm(ba, banks[ba][0:96, :], wab[0:96, wo:wo + 96], bunits[xb][0:96, :], True, True, [("b", xb), "wab"])
                    mm(bxx, banks[bxx][0:96, :], wxb[0:96, wo:wo + 96], bunits[xb][0:96, :], True, True, [("b", xb), "wxb"])
                    bpool.put(xb)
                    tha = fpool.get()
                    thx = fpool.get()
                    thg = fpool.get()
                    a2 = fpool.get()
                    A, X, G, A2 = (funits[i][0:96, :] for i in (tha, thx, thg, a2))
                    pg.op("act", lambda e: e.activation(out=A, in_=banks[ba][0:96, :], func=AF.Tanh, bias=dv[0:96, col:col + 1], scale=0.5),
                          reads=[("ps", ba)] + DVK, writes=[("f", tha)])
                    pg.op("act", lambda e: e.activation(out=A, in_=A, func=AF.Exp, bias=dv[0:96, 48 + col:49 + col], scale=dv[0:96, 48 + col:49 + col]),
                          reads=[("f", tha)] + DVK, writes=[("f", tha)])
                    pg.op("pool", lambda e: e.tensor_tensor(out=A2, in0=A, in1=A, op=ALU.mult),
                          reads=[("f", tha)], writes=[("f", a2)])
                    pg.op("act", lambda e: e.activation(out=X, in_=banks[bxx][0:96, :], func=AF.Tanh, bias=dv[0:96, 16 + col:17 + col], scale=0.5),
                          reads=[("ps", bxx)] + DVK, writes=[("f", thx)])
                    pspool.put(ba)
                    pspool.put(bxx)
                    pg.op("act", lambda e: e.activation(out=G, in_=banks[bg][0:96, :], func=AF.Tanh, scale=0.5),
                          reads=[("ps", bg)], writes=[("f", thg)])
                    pg.op("act", lambda e: e.activation(out=A2, in_=A2, func=AF.Ln, bias=onec[0:96, :], scale=-1.0),
                          reads=[("f", a2), "pk"], writes=[("f", a2)])
                    pg.op("act", lambda e: e.activation(out=A2, in_=A2, func=AF.Exp, scale=0.5),
                          reads=[("f", a2)], writes=[("f", a2)])
                    return (bg, xc, tha, thx, thg, a2)

                def lru_B2(n, st_):
                    bg, xc, tha, thx, thg, a2 = st_
                    col = l * 8 + n
                    XC = funits[xc][0:96, :]
                    A, X, G, A2 = (funits[i][0:96, :] for i in (tha, thx, thg, a2))
                    pg.op("dve", lambda e: e.scalar_tensor_tensor(out=X, in0=X, scalar=1.0, in1=XC, op0=ALU.add, op1=ALU.mult),
                          reads=[("f", thx), ("f", xc)], writes=[("f", thx)])
                    pg.op("dve", lambda e: e.scalar_tensor_tensor(out=X, in0=X, scalar=0.5, in1=A2, op0=ALU.mult, op1=ALU.mult),
                          reads=[("f", thx), ("f", a2)], writes=[("f", thx)])
                    pg.op("dve", lambda e: e.tensor_tensor_scan(out=XC, data0=A, data1=X, initial=hst[0:96, col:col + 1], op0=ALU.mult, op1=ALU.add),
                          reads=[("f", tha), ("f", thx), "hst"], writes=[("f", xc)])
                    pg.op("pool", lambda e: e.tensor_copy(out=hst[0:96, col:col + 1], in_=funits[xc][0:96, TT - 1:TT]),
                          reads=[("f", xc)], writes=["hst"])
                    pg.op("dve", lambda e: e.scalar_tensor_tensor(out=G, in0=G, scalar=1.0, in1=banks[bg][0:96, :], op0=ALU.add, op1=ALU.mult),
                          reads=[("f", thg), ("ps", bg)], writes=[("f", thg)])
                    pspool.put(bg)
                    pg.op("dve", lambda e: e.scalar_tensor_tensor(out=mixL[0:96, n, :], in0=G, scalar=0.5, in1=XC, op0=ALU.mult, op1=ALU.mult),
                          reads=[("f", thg), ("f", xc)], writes=[("mixL", n)])
                    if DEBUG and t == 0 and l == 0 and n == 0:
                        dbg("lru_h0", funits[xc][:], [128, TT], F32, [("f", xc)])
                    for u_ in (xc, tha, thx, thg, a2):
                        fpool.put(u_)

                bgs = {}
                sts = {}
                bgs[0] = lru_A(0)
                bgs[1] = lru_A(1)
                sts[0] = lru_B1(0, bgs[0])
                for n in range(8):
                    if n + 2 < 8:
                        bgs[n + 2] = lru_A(n + 2)
                    if n + 1 < 8:
                        sts[n + 1] = lru_B1(n + 1, bgs[n + 1])
                    lru_B2(n, sts[n])

                if KSTAGE < 4:
                    return
                sq_ = w_in_job(l, 2)
                sk_ = w_in_job(l, 3)
                sv_ = w_in_job(l, 4)
                sgr = w_in_job(l, 5)
                for hg in range(2):
                    for hh in range(3):
                        h = hg * 3 + hh
                        for (slot, cidx, sidx, dst, dkey) in ((sq_, 0, 1, qt, "qt"), (sk_, 2, 3, kt, "kt")):
                            bq = pspool.get()
                            proj_fm(l, slot, 768, h * 128, 128, bq)
                            qb = bpool.get()
                            pg.op("act", lambda e, bq=bq, qb=qb: e.activation(out=bunits[qb][:], in_=banks[bq][:], func=AF.Copy),
                                  reads=[("ps", bq)], writes=[("b", qb), ("psr", bq)])
                            bs_ = pspool.get()
                            mm(bs_, banks[bs_][:], perm, bunits[qb][:], True, True, [("b", qb), "cb"])
                            bpool.put(qb)
                            if KQ < 1:
                                pspool.put(bq)
                                pspool.put(bs_)
                                continue
                            t1 = fpool.get()
                            t2 = fpool.get()
                            pg.op("dve", lambda e, bq=bq, t1=t1, cidx=cidx: e.scalar_tensor_tensor(out=funits[t1][:], in0=ropeT[:, cidx, :], scalar=1.0, in1=banks[bq][:], op0=ALU.mult, op1=ALU.mult),
                                  reads=[("ps", bq), ("psr", bq), ("rope", cidx)], writes=[("f", t1)])
                            pg.op("dve", lambda e, bs_=bs_, t2=t2, sidx=sidx: e.scalar_tensor_tensor(out=funits[t2][:], in0=ropeT[:, sidx, :], scalar=1.0, in1=banks[bs_][:], op0=ALU.mult, op1=ALU.mult),
                                  reads=[("ps", bs_), ("rope", sidx)], writes=[("f", t2)])
                            pspool.put(bq)
                            pspool.put(bs_)
                            if KQ < 2:
                                fpool.put(t1)
                                fpool.put(t2)
                                continue
                            lo_t = qlo if dkey == "qt" else klo
                            lkey = "qlo" if dkey == "qt" else "klo"
                            pg.op("dve", lambda e, t1=t1, t2=t2: e.tensor_tensor(out=funits[t1][:], in0=funits[t1][:], in1=funits[t2][:], op=ALU.add),
                                  reads=[("f", t1), ("f", t2)], writes=[("f", t1)])
                            pg.op("act", lambda e, t1=t1, dst=dst, hh=hh: e.activation(out=dst[:, hh, :], in_=funits[t1][:], func=AF.Copy),
                                  reads=[("f", t1)], writes=[(dkey, hh)])
                            pg.op("pool", lambda e, t2=t2, dst=dst, hh=hh: e.tensor_copy(out=funits[t2][:], in_=dst[:, hh, :]),
                                  reads=[(dkey, hh), ("f", t2)], writes=[("f", t2)])
                            pg.op("pool", lambda e, t1=t1, t2=t2, lo_t=lo_t, hh=hh: e.tensor_tensor(
                                out=lo_t[:, hh, :], in0=funits[t1][:], in1=funits[t2][:], op=ALU.subtract),
                                reads=[("f", t1), ("f", t2)], writes=[(lkey, hh)])
                            if dkey == "qt":
                                pg.op("pool", lambda e, t1=t1, hh=hh, h=h: e.tensor_tensor(
                                    out=qg[:, hh, :].rearrange("p (n i) -> p n i", n=4),
                                    in0=funits[t1][:].rearrange("p (n i) -> p n i", n=4),
                                    in1=qwc[:, h * 128:(h + 1) * 128].unsqueeze(1).broadcast_to([128, 4, 128]), op=ALU.mult),
                                    reads=[("f", t1), "cst"], writes=[("qg", hh)])
                            fpool.put(t1)
                            fpool.put(t2)
                    if DEBUG and t == 0 and l == 0 and hg == 0:
                        dbg("qt", qt[:], [128, 3, TT], BF16, [("qt", i) for i in range(3)])
                        dbg("kt", kt[:], [128, 3, TT], BF16, [("kt", i) for i in range(3)])
                    if KRET < 2:
                        continue
                    for blk in range(4):
                        bv = pspool.get()
                        proj_tm(sv_, 768, hg * 384, 384, blk, bv)
                        pg.op("act", lambda e, bv=bv, blk=blk: e.activation(out=vbf[:, blk, :], in_=banks[bv][:, 0:384], func=AF.Copy),
                              reads=[("ps", bv)], writes=[("vbf", blk)])
                        pg.op("dve", lambda e, bv=bv, blk=blk, hg=hg: e.scalar_tensor_tensor(out=vw[:, blk, :], in0=kwc[:, hg * 384:(hg + 1) * 384], scalar=1.0, in1=banks[bv][:, 0:384], op0=ALU.mult, op1=ALU.mult),
                              reads=[("ps", bv), "cst"], writes=[("vw", blk)])
                        pspool.put(bv)
                    if KRET < 3:
                        continue
                    for blk in range(4):
                        bt = pspool.get()
                        btb = banks[bt][:].bitcast(BF16)
                        for hh in range(3):
                            pg.op("pe", lambda e, btb=btb, hh=hh, blk=blk: e.transpose(btb[:, hh * 128:(hh + 1) * 128], kt[:, hh, blk * 128:(blk + 1) * 128], ident),
                                  reads=[("kt", hh), "cb"], writes=[("ps", bt)])
                        pg.op("act", lambda e, btb=btb, blk=blk: e.activation(out=ktm[:, blk, :], in_=btb[:, 0:384], func=AF.Copy),
                              reads=[("ps", bt)], writes=[("ktm", blk)])
                        pspool.put(bt)
                    if KRET < 4:
                        continue
                    bo = [pspool.get() for _ in range(3)]
                    for n in range(4):
                        cs = slice(n * 128, (n + 1) * 128)
                        bsc = pspool.get()
                        for hh in range(3):
                            osc = banks[bsc][:, hh * 128:(hh + 1) * 128]
                            mm(bsc, osc, kt[:, hh, cs], qt[:, hh, cs], True, False, [("kt", hh), ("qt", hh)])
                            mm(bsc, osc, kt[:, hh, cs], qlo[:, hh, cs], False, False, [("kt", hh), ("qlo", hh)])
                            mm(bsc, osc, klo[:, hh, cs], qt[:, hh, cs], False, True, [("klo", hh), ("qt", hh)])
                        Pb = bpool.get()
                        pg.op("dve", lambda e, bsc=bsc, Pb=Pb, hg=hg: e.scalar_tensor_tensor(out=bunits[Pb][:, 0:384], in0=decayT[:, hg * 384:(hg + 1) * 384], scalar=1.0, in1=banks[bsc][:, 0:384], op0=ALU.mult, op1=ALU.mult),
                              reads=[("ps", bsc), "cst"], writes=[("b", Pb)])
                        pspool.put(bsc)
                        for hh in range(3):
                            h = hg * 3 + hh
                            mm(bo[hh], banks[bo[hh]][:, cs], vbf[:, n, hh * 128:(hh + 1) * 128], bunits[Pb][:, hh * 128:(hh + 1) * 128], True, False,
                               [("vbf", n), ("b", Pb)])
                            mm(bo[hh], banks[bo[hh]][:, cs], Sb[:, l, h * 128:(h + 1) * 128], qg[:, hh, cs], False, True, [("Sb", l, hg), ("qg", hh)])
                        bpool.put(Pb)
                        bkv = pspool.get()
                        for hh in range(3):
                            mm(bkv, banks[bkv][:, hh * 128:(hh + 1) * 128], ktm[:, n, hh * 128:(hh + 1) * 128], vw[:, n, hh * 128:(hh + 1) * 128], True, True,
                               [("ktm", n), ("vw", n)])
                        for hh in range(3):
                            h = hg * 3 + hh
                            pg.op("dve", lambda e, h=h, hh=hh, bkv=bkv: e.scalar_tensor_tensor(
                                out=Sst[:, l, h * 128:(h + 1) * 128], in0=Sst[:, l, h * 128:(h + 1) * 128], scalar=gam[h],
                                in1=banks[bkv][:, hh * 128:(hh + 1) * 128], op0=ALU.mult, op1=ALU.add),
                                reads=[("ps", bkv), ("Sst", l, hg)], writes=[("Sst", l, hg)])
                        pspool.put(bkv)
                        pg.op("act", lambda e, hg=hg: e.activation(out=Sb[:, l, hg * 384:(hg + 1) * 384], in_=Sst[:, l, hg * 384:(hg + 1) * 384], func=AF.Copy),
                              reads=[("Sst", l, hg)], writes=[("Sb", l, hg)])
                    if KRET < 5:
                        for b_ in bo:
                            pspool.put(b_)
                        continue
                    for hh in range(3):
                        h = hg * 3 + hh
                        b = bpool.get()
                        pg.op("act", lambda e, b=b, boa=banks[bo[hh]][:]: e.activation(out=bunits[b][:], in_=boa, func=AF.Square),
                              reads=[("ps", bo[hh])], writes=[("b", b)])
                        bm = pspool.get()
                        mm(bm, banks[bm][:], ones, bunits[b][:], True, True, [("b", b), "cb"])
                        bpool.put(b)
                        r = fpool.get()
                        pg.op("act", lambda e, r=r, bm=bm: e.activation(out=funits[r][:], in_=banks[bm][:], func=AF.Ln, bias=epsc, scale=1.0 / 128),
                              reads=[("ps", bm), "pk"], writes=[("f", r)])
                        pspool.put(bm)
                        pg.op("act", lambda e, r=r: e.activation(out=funits[r][:], in_=funits[r][:], func=AF.Exp, scale=-0.5), reads=[("f", r)], writes=[("f", r)])
                        on = fpool.get()
                        pg.op("dve", lambda e, r=r, on=on, boa=banks[bo[hh]][:], h=h: e.scalar_tensor_tensor(out=funits[on][:], in0=funits[r][:], scalar=pk[:, 40 + l * 6 + h:41 + l * 6 + h],
                                                                                        in1=boa, op0=ALU.mult, op1=ALU.mult),
                              reads=[("ps", bo[hh]), ("f", r), "pk"], writes=[("f", on)])
                        fpool.put(r)
                        bg = pspool.get()
                        proj_fm(l, sgr, 768, h * 128, 128, bg)
                        tg = fpool.get()
                        pg.op("act", lambda e, tg=tg, bg=bg: e.activation(out=funits[tg][:], in_=banks[bg][:], func=AF.Tanh, scale=0.5),
                              reads=[("ps", bg)], writes=[("f", tg)])
                        pg.op("dve", lambda e, tg=tg, bg=bg: e.scalar_tensor_tensor(out=funits[tg][:], in0=funits[tg][:], scalar=1.0, in1=banks[bg][:], op0=ALU.add, op1=ALU.mult),
                              reads=[("f", tg), ("ps", bg)], writes=[("f", tg)])
                        pspool.put(bg)
                        pg.op("dve", lambda e, tg=tg, on=on, h=h: e.scalar_tensor_tensor(out=mixR[:, h, :], in0=funits[tg][:], scalar=0.5, in1=funits[on][:], op0=ALU.mult, op1=ALU.mult),
                              reads=[("f", tg), ("f", on)], writes=[("mixR", h)])
                        if DEBUG and t == 0 and l == 0 and h == 0:
                            dbg("ret_on0", funits[on][:], [128, TT], F32, [("f", on)])
                        fpool.put(tg)
                        fpool.put(on)
                    for b_ in bo:
                        pspool.put(b_)

                if KSTAGE < 5:
                    return
                su = w_in_job(l, 6, 512)
                def w_cols_job(c0, ncols):
                    src = w_in[l, :, c0:c0 + ncols].rearrange("(k p) c -> p k c", p=128)
                    return wload([(lambda r, n=ncols: r[:, 0:8 * n].rearrange("p (k c) -> p k c", k=8), src)],
                                 [("wb_in", l, c0 // 768), ("wb_in", l, (c0 + ncols - 1) // 768)])
                svs = w_cols_job(5120, 512)
                sgs = w_cols_job(5632, 512)
                for blk in range(4):
                    bv = pspool.get()
                    proj_tm(svs, 512, 0, 512, blk, bv)
                    so = blk * 12
                    tq = fpool.get()
                    pg.op("act", lambda e, bv=bv, tq=tq: e.activation(out=funits[tq][:], in_=banks[bv][:], func=AF.Copy),
                          reads=[("ps", bv)], writes=[("f", tq)])
                    pspool.put(bv)
                    pg.op("dve", lambda e, tq=tq, so=so: e.bn_stats(out=small[:, so:so + 6], in_=funits[tq][:]), reads=[("f", tq)], writes=[("small", blk)])
                    pg.op("dve", lambda e, so=so: e.bn_aggr(out=small[:, so + 6:so + 8], in_=small[:, so:so + 6]), reads=[("small", blk)], writes=[("small", blk)])
                    pg.op("act", lambda e, so=so: e.activation(out=small[:, so + 8:so + 9], in_=small[:, so + 7:so + 8], func=AF.Ln, bias=epsc, scale=1.0),
                          reads=[("small", blk), "pk"], writes=[("small", blk)])
                    pg.op("act", lambda e, so=so: e.activation(out=small[:, so + 8:so + 9], in_=small[:, so + 8:so + 9], func=AF.Exp, scale=-0.5), reads=[("small", blk)], writes=[("small", blk)])
                    pg.op("dve", lambda e, so=so, tq=tq: e.scalar_tensor_tensor(out=funits[tq][:], in0=funits[tq][:], scalar=small[:, so + 6:so + 7], in1=lng[:, l, :],
                                                                                op0=ALU.subtract, op1=ALU.mult),
                          reads=[("f", tq), ("small", blk), ("lng", l)], writes=[("f", tq)])
                    pg.op("dve", lambda e, so=so, tq=tq, blk=blk: e.scalar_tensor_tensor(out=vn[:, blk, :], in0=funits[tq][:], scalar=small[:, so + 8:so + 9], in1=lnb[:, l, :],
                                                                                  op0=ALU.mult, op1=ALU.add),
                          reads=[("f", tq), ("small", blk), ("lnb", l)], writes=[("vn", blk)])
                    fpool.put(tq)
                for g in range(4):
                    gs = slice(g * 128, (g + 1) * 128)
                    bmix = pspool.get()
                    for n in range(4):
                        cs = slice(n * 128, (n + 1) * 128)
                        mm(bmix, banks[bmix][:, cs], vn[:, n, gs], wsT[:, l, gs], True, False, [("vn", n), ("wsT", l)])
                        mm(bmix, banks[bmix][:, cs], ones[0:1, :], bsb[0:1, l * 512 + g * 128:l * 512 + (g + 1) * 128], False, True, ["cb", "bsb"])
                    bu = pspool.get()
                    proj_fm(l, su, 512, g * 128, 128, bu)
                    bgg = pspool.get()
                    proj_fm(l, sgs, 512, g * 128, 128, bgg)
                    m_ = fpool.get()
                    pg.op("act", lambda e, m_=m_, bmix=bmix: e.activation(out=funits[m_][:], in_=banks[bmix][:], func=AF.Copy),
                          reads=[("ps", bmix)], writes=[("f", m_)])
                    pspool.put(bmix)
                    pg.op("dve", lambda e, m_=m_, bu=bu: e.scalar_tensor_tensor(out=funits[m_][:], in0=funits[m_][:], scalar=1.0, in1=banks[bu][:], op0=ALU.mult, op1=ALU.mult),
                          reads=[("f", m_), ("ps", bu)], writes=[("f", m_)])
                    pspool.put(bu)
                    tg = fpool.get()
                    pg.op("act", lambda e, tg=tg, bgg=bgg: e.activation(out=funits[tg][:], in_=banks[bgg][:], func=AF.Tanh, scale=0.5),
                          reads=[("ps", bgg)], writes=[("f", tg)])
                    pg.op("dve", lambda e, tg=tg, bgg=bgg: e.scalar_tensor_tensor(out=funits[tg][:], in0=funits[tg][:], scalar=1.0, in1=banks[bgg][:], op0=ALU.add, op1=ALU.mult),
                          reads=[("f", tg), ("ps", bgg)], writes=[("f", tg)])
                    pspool.put(bgg)
                    pg.op("dve", lambda e, tg=tg, m_=m_, g=g: e.scalar_tensor_tensor(out=mixS[:, g, :], in0=funits[tg][:], scalar=0.5, in1=funits[m_][:], op0=ALU.mult, op1=ALU.mult),
                          reads=[("f", tg), ("f", m_)], writes=[("mixS", g)])
                    fpool.put(tg)
                    fpool.put(m_)
                if DEBUG and t == 0 and l == 0:
                    dbg("mixL", mixL[:], [128, 8, TT], BF16, [("mixL", i) for i in range(8)])
                    dbg("mixR", mixR[:], [128, 6, TT], BF16, [("mixR", i) for i in range(6)])
                    dbg("mixS", mixS[:], [128, 4, TT], BF16, [("mixS", i) for i in range(4)])

                if KSTAGE < 6:
                    return
                for piece in range(4):
                    c0 = piece * 256
                    dm = [
                        (lambda r: r[0:96, 0:2048].rearrange("p (n c) -> p n c", n=8),
                         w_out[l, 0:768, c0:c0 + 256].rearrange("(n p) c -> p n c", p=96)),
                        (lambda r: r[:, 2048:3584].rearrange("p (n c) -> p n c", n=6),
                         w_out[l, 768:1536, c0:c0 + 256].rearrange("(n p) c -> p n c", p=128)),
                        (lambda r: r[:, 3584:4608].rearrange("p (n c) -> p n c", n=4),
                         w_out[l, 1536:2048, c0:c0 + 256].rearrange("(n p) c -> p n c", p=128)),
                    ]
                    so_ = wload(dm, [("wb_out", l)])
                    R = ring[so_]
                    for d2 in range(2):
                        c = piece * 2 + d2
                        bw = pspool.get()
                        steps = []
                        for n in range(8):
                            steps.append((R[0:96, n * 256 + d2 * 128:n * 256 + d2 * 128 + 128], mixL[0:96, n, :], ("mixL", n)))
                        for h in range(6):
                            steps.append((R[:, 2048 + h * 256 + d2 * 128:2048 + h * 256 + d2 * 128 + 128], mixR[:, h, :], ("mixR", h)))
                        for g in range(4):
                            steps.append((R[:, 3584 + g * 256 + d2 * 128:3584 + g * 256 + d2 * 128 + 128], mixS[:, g, :], ("mixS", g)))
                        for i, (lh, rh, key) in enumerate(steps):
                            mm(bw, banks[bw][:], lh, rh, i == 0, i == len(steps) - 1, [("ring", so_), key])
                        pg.op("dve", lambda e, c=c, bw=bw: e.scalar_tensor_tensor(out=hT[:, c, :], in0=hT[:, c, :], scalar=1.0, in1=banks[bw][:], op0=ALU.mult, op1=ALU.add),
                              reads=[("hT", c), ("ps", bw)], writes=[("hT", c)])
                        pspool.put(bw)

                if KSTAGE < 7:
                    return
                rms_stage(pk[:, 16 + l * 8:16 + (l + 1) * 8], lambda c: yT[:, c, :], lambda c: ["yT"])
                pg.op("pool", lambda e, tok=tok: e.dma_start(out=pTt[:], in_=pT[l].rearrange("(k p) t -> p k t", p=128)[:, :, tok]),
                      reads=[], writes=["pTt", "poolq"], dma_sem=s_p)
                spp = wload([(lambda r: r[:, 0:2048].rearrange("p (k c) -> p k c", k=2), w_pp[l].rearrange("(k p) c -> p k c", p=128))], [("wb_pp", l)])
                for half in range(2):
                    sgt = wload([(lambda r: r[:, 0:4096].rearrange("p (k c) -> p k c", k=8),
                                  w_gate[l, :, half * 512:(half + 1) * 512].rearrange("(k p) c -> p k c", p=128))], [("wb_gate", l)])
                    for cc in range(4):
                        c = half * 4 + cc
                        bgt = pspool.get()
                        proj_fm(l, sgt, 512, cc * 128, 128, bgt)
                        bpp = pspool.get()
                        rp = ring[spp][:, 0:2048].rearrange("p (k c) -> p k c", k=2)
                        for k in range(2):
                            mm(bpp, banks[bpp][:], rp[:, k, c * 128:(c + 1) * 128], pTt[:, k, :], k == 0, k == 1, [("ring", spp), "pTt"])
                        th = fpool.get()
                        pg.op("act", lambda e, th=th, bgt=bgt: e.activation(out=funits[th][:], in_=banks[bgt][:], func=AF.Tanh, scale=0.5),
                              reads=[("ps", bgt)], writes=[("f", th)])
                        pspool.put(bgt)
                        pg.op("dve", lambda e, th=th, bpp=bpp: e.scalar_tensor_tensor(out=funits[th][:], in0=funits[th][:], scalar=1.0, in1=banks[bpp][:], op0=ALU.add, op1=ALU.mult),
                              reads=[("f", th), ("ps", bpp)], writes=[("f", th)])
                        pspool.put(bpp)
                        pg.op("dve", lambda e, th=th, c=c: e.scalar_tensor_tensor(out=hT[:, c, :], in0=funits[th][:], scalar=0.5, in1=hT[:, c, :], op0=ALU.mult, op1=ALU.add),
                              reads=[("f", th), ("hT", c)], writes=[("hT", c)])
                        fpool.put(th)
                if DEBUG and t == 0:
                    dbg(f"h_l{l}", hT[:], [128, 8, TT], F32, [("hT", c) for c in range(8)])

            for l_ in range(DEPTH):
                do_layer(l_)

            outs = {}

            def fin_out(c):
                u = fpool.get()
                outs[c] = u
                return funits[u][:]

            def fin_keys(c):
                return [("f", outs[c])]

            bank = pspool.get()
            for c in range(8):
                b = bpool.get()
                pg.op("act", lambda e, c=c, b=b: e.activation(out=bunits[b][:], in_=hT[:, c, :], func=AF.Square), reads=[("hT", c)], writes=[("b", b)])
                mm(bank, banks[bank][:], ones, bunits[b][:], c == 0, c == 7, [("b", b), "cb"])
                bpool.put(b)
            r = fpool.get()
            pg.op("act", lambda e, r=r, bank=bank: e.activation(out=funits[r][:], in_=banks[bank][:], func=AF.Ln, bias=epsc, scale=1.0 / D),
                  reads=[("ps", bank), "pk"], writes=[("f", r)])
            pg.op("act", lambda e, r=r: e.activation(out=funits[r][:], in_=funits[r][:], func=AF.Exp, scale=-0.5), reads=[("f", r)], writes=[("f", r)])
            pspool.put(bank)
            for c in range(8):
                u = fpool.get()
                pg.op("dve", lambda e, c=c, u=u, r=r: e.scalar_tensor_tensor(out=funits[u][:], in0=hT[:, c, :], scalar=pk[:, 32 + c:33 + c], in1=funits[r][:],
                                                                          op0=ALU.mult, op1=ALU.mult),
                      reads=[("hT", c), ("f", r), "pk"], writes=[("f", u)])
                pg.op("pool", lambda e, c=c, u=u, tok=tok: e.dma_start(out=outT[c * 128:(c + 1) * 128, tok], in_=funits[u][:]),
                      reads=[("f", u)], writes=[("outsem", c), "poolq"], dma_sem=s_out[c])
                fpool.put(u)
            fpool.put(r)

        for t_ in range(NT_RUN):
            do_tile(t_)

        def final_waits():
            res = []
            for s_ in s_out:
                if s_ in pg.dma_counts:
                    res.append((s_, pg.dma_counts[s_]))
            if s_dbg in pg.dma_counts:
                res.append((s_dbg, pg.dma_counts[s_dbg]))
            return res

        with nc.Block() as block:
            pg.emit(block, eng_sems, final_waits)
    return nc, list(dbg_outs.keys())


def _consts():
    f32 = np.float32
    H, C = 6, 128
    log_g = np.log1p(-np.exp2(-5.0 - np.arange(H, dtype=f32))).astype(f32)
    idx = np.arange(C, dtype=f32)
    diff = idx[:, None] - idx[None, :]
    causal = diff >= 0
    decay = np.where(causal[None], np.exp(np.where(causal, diff, 0.0)[None] * log_g[:, None, None]), 0.0).astype(f32)
    decayT = np.ascontiguousarray(decay.transpose(2, 0, 1)).reshape(128, 768)
    k_w = np.exp((C - 1.0 - idx)[:, None] * log_g[None, :]).astype(f32)
    kwc = np.repeat(k_w[:, :, None], 128, axis=2).reshape(128, 768)
    q_w = np.exp((idx + 1.0)[:, None] * log_g[None, :]).astype(f32)
    qw = np.broadcast_to(q_w.T.reshape(1, 768), (128, 768))
    ident = np.eye(128, dtype=f32)
    perm = np.zeros((128, 128), f32)
    for m in range(128):
        perm[(m + 64) % 128, m] = 1.0
    ones = np.ones((128, 128), f32)
    pad = np.zeros((128, 384), f32)
    mask = (idx[:, None] <= idx[None, :]).astype(f32)
    cst = np.concatenate([decayT, kwc, qw, ident, perm, ones, pad, mask], axis=1).astype(f32)
    return np.ascontiguousarray(cst)


def _pack(inp):
    f32 = np.float32
    pk = np.zeros((128, NPK), f32)
    pk[:, 0:16] = inp["mix_norm"].reshape(2, 8, 128).transpose(2, 0, 1).reshape(128, 16)
    pk[:, 16:32] = inp["ple_norm"].reshape(2, 8, 128).transpose(2, 0, 1).reshape(128, 16)
    pk[:, 32:40] = inp["final_norm"].reshape(8, 128).T
    pk[:, 40:52] = inp["ret_norm"].reshape(2, 6, 128).transpose(2, 0, 1).reshape(128, 12)
    cw = inp["lru_conv_w"].reshape(2, 4, 8, 96).transpose(3, 0, 2, 1)
    pk[0:96, 52:116] = cw.reshape(96, 64)
    for name, c0 in (("lru_conv_b", 116), ("lru_b_a", 132), ("lru_b_x", 148), ("lru_lambda", 164)):
        pk[0:96, c0:c0 + 16] = inp[name].reshape(2, 8, 96).transpose(2, 0, 1).reshape(96, 16)
    half = 64
    inv = (10000.0 ** (-np.arange(half, dtype=f32) / half)).astype(f32)
    pk[:, 180] = np.concatenate([inv, inv])
    sgn = np.concatenate([-np.ones(64, f32), np.ones(64, f32)])
    pk[:, 181] = sgn
    pk[:, 182] = sgn * f32(128.0 ** -0.5)
    pk[:, 183] = EPS
    pk[:, 184] = 1.0
    return pk


def kernel(**inputs):
    inp = {k: np.asarray(v) for k, v in inputs.items()}
    f32 = np.float32
    nc, dbg_names = build_nc()
    cst = _consts()
    pk = _pack(inp)
    shared = {
        "w_in": np.ascontiguousarray(inp["w_in"], f32),
        "w_out": np.ascontiguousarray(inp["w_out"], f32),
        "w_gate": np.ascontiguousarray(inp["w_ple_gate"], f32),
        "w_pp": np.ascontiguousarray(inp["w_ple_proj"], f32),
        "w_a": np.ascontiguousarray(inp["lru_w_a"], f32),
        "w_x": np.ascontiguousarray(inp["lru_w_x"], f32),
        "w_sT": np.ascontiguousarray(inp["sgu_w_s"].transpose(0, 3, 1, 2).reshape(2, 128, 512), f32),
        "b_s": np.ascontiguousarray(inp["sgu_b_s"].reshape(1, 1024), f32),
        "ln_g": np.ascontiguousarray(inp["sgu_ln_g"], f32),
        "ln_b": np.ascontiguousarray(inp["sgu_ln_b"], f32),
        "pk": pk,
        "cst": cst,
    }
    in_maps = []
    for core in range(8):
        b = core % 4
        m = dict(shared)
        m["xT"] = np.ascontiguousarray(inp["x"][b].T, f32)
        m["pT"] = np.ascontiguousarray(inp["p"][:, b].transpose(0, 2, 1), f32)
        m["pos"] = np.ascontiguousarray(inp["positions"][b].reshape(1, S), np.int32)
        in_maps.append(m)
    res = run_bass_kernel_spmd(nc, in_maps, core_ids=list(range(8)))
    out = np.stack([np.ascontiguousarray(res.results[b]["outT"].T) for b in range(4)], axis=0).astype(f32)
    if DEBUG:
        kernel.debug = {n: res.results[0]["dbg_" + n] for n in dbg_names}
    return out
```

```python
import os
import numpy as np
import concourse.bass as bass
import concourse.mybir as mybir
from concourse.bass_utils import run_bass_kernel_spmd

F32 = mybir.dt.float32
BF16 = mybir.dt.bfloat16
I32 = mybir.dt.int32
ALU = mybir.AluOpType
AF = mybir.ActivationFunctionType

D = 1024
S = 4096
TT = 512
NT = S // TT
DEPTH = 2
D_IN = 6144
EPS = 1e-6
NPK = 192
PI = float(np.pi)

DEBUG = os.environ.get("KDEBUG", "") != ""
NT_RUN = int(os.environ.get("KNT", NT))
KSTAGE = int(os.environ.get("KSTAGE", 99))
KRET = int(os.environ.get("KRET", 99))
KQ = int(os.environ.get("KQ", 99))
KSKIPW = os.environ.get("KSKIPW", "") != ""


class Op:
    __slots__ = ("eng", "fn", "reads", "writes", "dma_sem", "dma_val", "deps", "signal", "val", "idx")


class Prog:
    ENGS = ("pe", "act", "dve", "pool", "sp")

    def __init__(self, nc):
        self.nc = nc
        self.ops = []
        self.last_w = {}
        self.readers = {}
        self.dma_counts = {}
        self.shared = set()

    def op(self, eng, fn, reads=(), writes=(), dma_sem=None):
        o = Op()
        o.eng, o.fn, o.reads, o.writes = eng, fn, tuple(reads), tuple(writes)
        o.dma_sem = dma_sem
        o.signal = False
        o.val = 0
        o.idx = len(self.ops)
        if dma_sem is not None:
            self.dma_counts[dma_sem] = self.dma_counts.get(dma_sem, 0) + 16
            o.dma_val = self.dma_counts[dma_sem]
        else:
            o.dma_val = 0
        deps = set()
        is_dma = dma_sem is not None
        def need(p):
            return p.dma_sem is not None or is_dma or not (p.eng == eng == "pe")
        for k in o.reads:
            w = self.last_w.get(k)
            if w is not None and need(w):
                deps.add(w.idx)
            if isinstance(k, tuple) and k and k[0] == "ps":
                for r in self.readers.get(k, ()):
                    if r.eng != eng:
                        deps.add(r.idx)
        for k in o.writes:
            w = self.last_w.get(k)
            if w is not None and need(w):
                deps.add(w.idx)
            for r in self.readers.get(k, ()):
                if need(r):
                    deps.add(r.idx)
        deps.discard(o.idx)
        o.deps = deps
        wset = set(o.writes)
        for k in o.writes:
            self.last_w[k] = o
            self.readers[k] = []
        for k in o.reads:
            if k not in wset:
                self.readers.setdefault(k, []).append(o)
        self.ops.append(o)
        return o

    def emit(self, block, eng_sems, final_waits):
        ops = self.ops
        waits = {}
        for en in self.ENGS:
            widx = {}
            wdma = {}
            for o in ops:
                if o.eng != en:
                    continue
                best = {}
                dm = {}
                for d in o.deps:
                    p = ops[d]
                    if p.dma_sem is not None:
                        v = self.dma_counts[p.dma_sem] if p.dma_sem in self.shared else p.dma_val
                        if v > dm.get(p.dma_sem, 0):
                            dm[p.dma_sem] = v
                    else:
                        if d > best.get(p.eng, -1):
                            best[p.eng] = d
                lst = []
                for pe_, d in best.items():
                    if d > widx.get(pe_, -1):
                        widx[pe_] = d
                        ops[d].signal = True
                        lst.append(("e", d))
                for sm, v in dm.items():
                    if v > wdma.get(sm, 0):
                        wdma[sm] = v
                        lst.append(("d", sm, v))
                waits[o.idx] = lst
        cnt = {e: 0 for e in self.ENGS}
        for o in ops:
            if o.dma_sem is None and o.signal:
                cnt[o.eng] += 1
                o.val = cnt[o.eng]

        def run(eng_name, eng):
            for o in ops:
                if o.eng != eng_name:
                    continue
                for w in waits[o.idx]:
                    if w[0] == "e":
                        p = ops[w[1]]
                        eng.wait_ge(eng_sems[p.eng], p.val)
                    else:
                        eng.wait_ge(w[1], w[2])
                ins = o.fn(eng)
                if o.dma_sem is not None:
                    ins.then_inc(o.dma_sem, 16)
                elif o.signal:
                    ins.then_inc(eng_sems[o.eng], 1)
            if eng_name == "sp":
                for s, v in final_waits():
                    eng.wait_ge(s, v)

        @block.tensor
        def _(e):
            run("pe", e)

        @block.scalar
        def _(e):
            run("act", e)

        @block.vector
        def _(e):
            run("dve", e)

        @block.gpsimd
        def _(e):
            run("pool", e)

        @block.sync
        def _(e):
            run("sp", e)


class Pool_:
    def __init__(self, items):
        self.free = list(items)

    def get(self):
        return self.free.pop(0)

    def put(self, x):
        self.free.append(x)


def build_nc():
    nc = bass.Bass("TRN2", target_bir_lowering=False)
    dt_in = lambda name, shape, dt=F32: nc.dram_tensor(name, shape, dt, kind="ExternalInput").ap()
    xT = dt_in("xT", [D, S])
    pT = dt_in("pT", [DEPTH, 256, S])
    pos = dt_in("pos", [1, S], I32)
    w_in = dt_in("w_in", [DEPTH, D, D_IN])
    w_out = dt_in("w_out", [DEPTH, 2048, D])
    w_gate = dt_in("w_gate", [DEPTH, D, D])
    w_pp = dt_in("w_pp", [DEPTH, 256, D])
    w_a = dt_in("w_a", [DEPTH, 8, 96, 96])
    w_x = dt_in("w_x", [DEPTH, 8, 96, 96])
    w_sT = dt_in("w_sT", [DEPTH, 128, 512])
    b_s = dt_in("b_s", [1, DEPTH * 512])
    ln_g = dt_in("ln_g", [DEPTH, 512])
    ln_b = dt_in("ln_b", [DEPTH, 512])
    pk_d = dt_in("pk", [128, NPK])
    cst_d = dt_in("cst", [128, 4 * 768 + 128])
    outT = nc.dram_tensor("outT", [D, S], F32, kind="ExternalOutput").ap()
    dbg_outs = {}

    wb_in = wb_out = wb_gate = wb_pp = pTb = None
    scr = nc.dram_tensor("scr", [DEPTH * 16, 128, 6144], BF16, kind="Internal").ap()

    from contextlib import ExitStack
    es = ExitStack()
    sb = lambda name, shape, dt=F32: es.enter_context(nc.sbuf_tensor("sb_" + name, shape, dt))
    sem = lambda name: es.enter_context(nc.semaphore(name))

    with es:
        hT = sb("hT", [128, 8, TT])
        yT = sb("yT", [128, 8, TT], BF16)
        NSLOT = 4
        ring = [sb(f"ring{i}", [128, 6144], BF16) for i in range(NSLOT)]
        mixL = sb("mixL", [128, 8, TT], BF16)
        mixR = sb("mixR", [128, 6, TT], BF16)
        mixS = sb("mixS", [128, 4, TT], BF16)
        NF, NB = 12, 8
        funits = [sb(f"fu{i}", [128, TT]) for i in range(NF)]
        bunits = [sb(f"bu{i}", [128, TT], BF16) for i in range(NB)]
        xls = [sb(f"xl{i}", [128, TT + 4]) for i in range(2)]
        qt = sb("qt", [128, 3, TT], BF16)
        kt = sb("kt", [128, 3, TT], BF16)
        qg = sb("qg", [128, 3, TT], BF16)
        qlo = sb("qlo", [128, 3, TT], BF16)
        klo = sb("klo", [128, 3, TT], BF16)
        vbf = sb("vbf", [128, 4, 384], BF16)
        vw = sb("vw", [128, 4, 384], BF16)
        ktm = sb("ktm", [128, 4, 384], BF16)
        Sst = sb("Sst", [128, DEPTH, 768])
        Sb = sb("Sb", [128, DEPTH, 768], BF16)
        ropeT = sb("ropeT", [128, 4, TT])
        cstT = sb("cstT", [128, 3 * 768])
        maskT = sb("maskT", [128, 128])
        cbT = sb("cbT", [128, 3 * 128], BF16)
        pk = sb("pk", [128, NPK])
        dv = sb("dv", [128, 64])
        vn = sb("vn", [128, 4, TT], BF16)
        wsT = sb("wsT", [128, DEPTH, 512], BF16)
        lng = sb("lng", [128, DEPTH, 512])
        lnb = sb("lnb", [128, DEPTH, 512])
        bsb = sb("bsb", [1, DEPTH * 512], BF16)
        pTt = sb("pTt", [128, 2, TT], BF16)
        wab = sb("wab", [96, DEPTH * 8 * 96], BF16)
        wxb = sb("wxb", [96, DEPTH * 8 * 96], BF16)
        hst = sb("hst", [128, DEPTH * 8])
        hist = sb("hist", [128, DEPTH * 8 * 4])
        posi = sb("posi", [128, TT], I32)
        ki = sb("ki", [128, TT], I32)
        small = sb("small", [128, 64])
        banks = [es.enter_context(nc.psum_tensor(f"ps{i}", [128, TT], F32)) for i in range(8)]

        eng_sems = {e: sem(f"s_{e}") for e in Prog.ENGS}
        s_cast = [sem(f"s_cast{i}") for i in range(4)]
        s_wb = [sem(f"s_wb{i}") for i in range(NSLOT)]
        s_rl = [sem(f"s_rl{i}") for i in range(NSLOT)]
        s_ring = [sem(f"s_ring{i}") for i in range(NSLOT)]
        s_ld = sem("s_ld")
        s_x = sem("s_x")
        s_pos = sem("s_pos")
        s_p = sem("s_p")
        s_out = [sem(f"s_out{i}") for i in range(8)]
        s_dbg = sem("s_dbg")

        pg = Prog(nc)
        pg.shared = set([s_ld] + list(s_cast))
        fpool = Pool_(range(NF))
        bpool = Pool_(range(NB))
        pspool = Pool_(range(8))

        ident = cbT[:, 0:128]
        perm = cbT[:, 128:256]
        ones = cbT[:, 256:384]
        decayT = cstT[:, 0:768]
        kwc = cstT[:, 768:1536]
        qwc = cstT[:, 1536:2304]
        epsc = pk[:, 183:184]
        onec = pk[:, 184:185]

        def dbg(name, ap, shape, dt, reads):
            if not DEBUG:
                return
            o = nc.dram_tensor("dbg_" + name, list(shape), dt, kind="ExternalOutput").ap()
            dbg_outs[name] = o
            pg.op("sp", lambda e, o=o, ap=ap: e.dma_start(out=o, in_=ap), reads=reads, writes=["dbgsem"], dma_sem=s_dbg)

        cast_i = [0]

        def cast_dma(out_ap, in_ap, wkey):
            sm = s_cast[cast_i[0] % len(s_cast)]
            cast_i[0] += 1
            pg.op("pool", lambda e: e.dma_start(out=out_ap, in_=in_ap), reads=[], writes=[wkey], dma_sem=sm)

        def ld(out_ap, in_ap, wkey):
            pg.op("sp", lambda e: e.dma_start(out=out_ap, in_=in_ap), reads=[], writes=[wkey], dma_sem=s_ld)

        ld(pk[:], pk_d, "pk")
        ld(cstT[:], cst_d[:, 0:2304], "cst")
        ld(maskT[:], cst_d[:, 2304 + 768:2304 + 768 + 128], "mask")
        for l in range(DEPTH):
            ld(lng[:, l, :], ln_g[l:l + 1, :].broadcast_to([128, 512]), ("lng", l))
            ld(lnb[:, l, :], ln_b[l:l + 1, :].broadcast_to([128, 512]), ("lnb", l))
        cast_dma(cbT[:], cst_d[:, 2304:2304 + 384], "cb")
        cast_dma(bsb[:], b_s, "bsb")
        cast_dma(wab[:].rearrange("c (l n d) -> c l n d", l=DEPTH, n=8), w_a.rearrange("l n c d -> c l n d"), "wab")
        cast_dma(wxb[:].rearrange("c (l n d) -> c l n d", l=DEPTH, n=8), w_x.rearrange("l n c d -> c l n d"), "wxb")
        for l in range(0):
            for g in range(8):
                cast_dma(wb_in[l, :, g * 768:(g + 1) * 768], w_in[l, :, g * 768:(g + 1) * 768], ("wb_in", l, g))
            for g in range(2):
                cast_dma(wb_out[l, g * 1024:(g + 1) * 1024, :], w_out[l, g * 1024:(g + 1) * 1024, :], ("wb_out", l))
            cast_dma(wb_gate[l], w_gate[l], ("wb_gate", l))
            cast_dma(wb_pp[l], w_pp[l], ("wb_pp", l))
            cast_dma(pTb[l], pT[l], ("pTb", l))

        pg.op("dve", lambda e: e.memset(Sst[:], 0.0), writes=["Sst"])
        pg.op("dve", lambda e: e.memset(Sb[:], 0.0), writes=["Sb"])
        pg.op("dve", lambda e: e.memset(hst[:], 0.0), writes=["hst"])
        pg.op("dve", lambda e: e.memset(hist[:], 0.0), writes=[("hist", i) for i in range(16)])
        for i in range(2):
            pg.op("dve", lambda e, i=i: e.memset(xls[i][:], 0.0), writes=[("xl", i)])
        for l in range(DEPTH):
            u = fpool.get()
            ld(funits[u][:], w_sT[l], ("f", u))
            pg.op("dve", lambda e, l=l, u=u: e.tensor_tensor(
                out=wsT[:, l, :].rearrange("p (g i) -> p g i", g=4),
                in0=funits[u][:].rearrange("p (g i) -> p g i", g=4),
                in1=maskT[:].unsqueeze(1).broadcast_to([128, 4, 128]), op=ALU.mult),
                reads=[("f", u), "mask"], writes=[("wsT", l)])
            fpool.put(u)
        pg.op("dve", lambda e: e.tensor_scalar(out=dv[:, 0:32], in0=pk[:, 132:164], scalar1=0.5, scalar2=None, op0=ALU.mult),
              reads=["pk"], writes=["dv0"])
        pg.op("act", lambda e: e.activation(out=small[:, 0:16], in_=pk[:, 164:180], func=AF.Exp, scale=-1.0),
              reads=["pk"], writes=["sm0"])
        pg.op("act", lambda e: e.activation(out=small[:, 16:32], in_=small[:, 0:16], func=AF.Ln, bias=onec, scale=1.0),
              reads=["sm0", "pk"], writes=["sm1"])
        pg.op("dve", lambda e: e.tensor_scalar(out=dv[:, 32:48], in0=small[:, 16:32], scalar1=-8.0, scalar2=None, op0=ALU.mult),
              reads=["sm1"], writes=["dv1"])
        pg.op("dve", lambda e: e.tensor_scalar(out=dv[:, 48:64], in0=small[:, 16:32], scalar1=-4.0, scalar2=None, op0=ALU.mult),
              reads=["sm1"], writes=["dv2"])
        DVK = ["dv0", "dv1", "dv2"]

        ring_i = [0]
        cur_t = [0]
        job_i = [0]

        def wload(dmas, rkeys):
            sl = ring_i[0] % NSLOT
            ring_i[0] += 1
            j = job_i[0]
            job_i[0] += 1
            for di, (vf, src) in enumerate(dmas):
                if cur_t[0] == 0:
                    pg.op("pool", lambda e, vf=vf, src=src, sl=sl: e.dma_start(out=vf(ring[sl]), in_=src),
                          reads=[], writes=[("ring", sl), "poolq"], dma_sem=s_ring[sl])
                    pg.op("sp", lambda e, vf=vf, sl=sl, j=j: e.dma_start(out=vf(scr[j]), in_=vf(ring[sl])),
                          reads=[("ring", sl)], writes=[("scr", j, di)], dma_sem=s_wb[sl])
                else:
                    pg.op("sp", lambda e, vf=vf, sl=sl, j=j: e.dma_start(out=vf(ring[sl]), in_=vf(scr[j])),
                          reads=[("scr", j, di)], writes=[("ring", sl)], dma_sem=s_rl[sl])
            return sl

        def w_in_job(l, g, ncols=768):
            c0 = g * 768
            src = w_in[l, :, c0:c0 + ncols].rearrange("(k p) c -> p k c", p=128)
            return wload([(lambda r, n=ncols: r[:, 0:8 * n].rearrange("p (k c) -> p k c", k=8), src)], [("wb_in", l, g)])

        def mm(bank, out_ap, lhsT, rhs, start, stop, reads, extra_w=()):
            pg.op("pe", lambda e: e.matmul(out_ap, lhsT, rhs, start=start, stop=stop),
                  reads=reads, writes=[("ps", bank)] + list(extra_w))

        def proj_fm(l, slot, ncols, c0, m, bank, mp=128):
            rv = ring[slot][:, 0:8 * ncols].rearrange("p (k c) -> p k c", k=8)
            for kc in range(8):
                mm(bank, banks[bank][0:m, :], rv[:, kc, c0:c0 + m], yT[:, kc, :], kc == 0, kc == 7,
                   [("ring", slot), "yT"])

        def proj_tm(slot, ncols, c0, n, blk, bank):
            rv = ring[slot][:, 0:8 * ncols].rearrange("p (k c) -> p k c", k=8)
            for kc in range(8):
                mm(bank, banks[bank][:, 0:n], yT[:, kc, blk * 128:(blk + 1) * 128], rv[:, kc, c0:c0 + n], kc == 0, kc == 7,
                   [("ring", slot), "yT"])

        def rms_stage(gcol, out_fn, out_keys_fn, out_dt_bf=True):
            bank = pspool.get()
            for c in range(8):
                b = bpool.get()
                pg.op("act", lambda e, c=c, b=b: e.activation(out=bunits[b][:], in_=hT[:, c, :], func=AF.Square),
                      reads=[("hT", c)], writes=[("b", b)])
                mm(bank, banks[bank][:], ones, bunits[b][:], c == 0, c == 7, [("b", b), "cb"])
                bpool.put(b)
            r = fpool.get()
            pg.op("act", lambda e: e.activation(out=funits[r][:], in_=banks[bank][:], func=AF.Ln, bias=epsc, scale=1.0 / D),
                  reads=[("ps", bank), "pk"], writes=[("f", r)])
            pg.op("act", lambda e: e.activation(out=funits[r][:], in_=funits[r][:], func=AF.Exp, scale=-0.5), reads=[("f", r)], writes=[("f", r)])
            pspool.put(bank)
            for c in range(8):
                pg.op("dve", lambda e, c=c: e.scalar_tensor_tensor(out=out_fn(c), in0=hT[:, c, :], scalar=gcol[:, c:c + 1],
                                                                  in1=funits[r][:], op0=ALU.mult, op1=ALU.mult),
                      reads=[("hT", c), ("f", r), "pk"], writes=out_keys_fn(c))
            fpool.put(r)

        gam = []
        for h in range(6):
            lg = np.log1p(-np.exp2(np.float32(-5.0 - h))).astype(np.float32)
            gam.append(float(np.exp(np.float32(128.0) * lg)))

        def do_tile(t):
            cur_t[0] = t
            job_i[0] = 0
            tok = slice(t * TT, (t + 1) * TT)
            pg.op("pool", lambda e, tok=tok: e.dma_start(out=hT[:], in_=xT.rearrange("(c p) t -> p c t", p=128)[:, :, tok]),
                  reads=[], writes=[("hT", c) for c in range(8)] + ["poolq"], dma_sem=s_x)
            pg.op("pool", lambda e, tok=tok: e.dma_start(out=posi[:], in_=pos[0:1, tok].broadcast_to([128, TT])),
                  reads=[], writes=["posi", "poolq"], dma_sem=s_pos)
            ang = fpool.get()
            tmp = fpool.get()
            kf = fpool.get()
            pg.op("dve", lambda e: e.tensor_copy(out=funits[ang][:], in_=posi[:]), reads=["posi"], writes=[("f", ang)])
            pg.op("dve", lambda e: e.tensor_scalar(out=funits[ang][:], in0=funits[ang][:], scalar1=pk[:, 180:181], scalar2=None, op0=ALU.mult),
                  reads=[("f", ang), "pk"], writes=[("f", ang)])
            C1 = 6.28125
            C2 = float(2.0 * np.pi - 6.28125)
            for which, off in ((1, 0.0), (0, 0.25)):
                pg.op("dve", lambda e, off=off: e.tensor_scalar(out=funits[tmp][:], in0=funits[ang][:], scalar1=float(1.0 / (2 * np.pi)), scalar2=off,
                                                               op0=ALU.mult, op1=ALU.add), reads=[("f", ang)], writes=[("f", tmp)])
                pg.op("dve", lambda e: e.tensor_copy(out=ki[:], in_=funits[tmp][:]), reads=[("f", tmp)], writes=["ki"])
                pg.op("dve", lambda e: e.tensor_copy(out=funits[kf][:], in_=ki[:]), reads=["ki"], writes=[("f", kf)])
                pg.op("dve", lambda e: e.scalar_tensor_tensor(out=funits[tmp][:], in0=funits[kf][:], scalar=-C1, in1=funits[ang][:], op0=ALU.mult, op1=ALU.add),
                      reads=[("f", kf), ("f", ang)], writes=[("f", tmp)])
                pg.op("dve", lambda e: e.scalar_tensor_tensor(out=funits[tmp][:], in0=funits[kf][:], scalar=-C2, in1=funits[tmp][:], op0=ALU.mult, op1=ALU.add),
                      reads=[("f", kf), ("f", tmp)], writes=[("f", tmp)])
                if off != 0.0:
                    pg.op("dve", lambda e: e.tensor_scalar(out=funits[tmp][:], in0=funits[tmp][:], scalar1=float(np.pi / 2), scalar2=None, op0=ALU.add),
                          reads=[("f", tmp)], writes=[("f", tmp)])
                pg.op("dve", lambda e: e.tensor_scalar(out=funits[tmp][:], in0=funits[tmp][:], scalar1=-3.1415925, scalar2=3.1415925, op0=ALU.max, op1=ALU.min),
                      reads=[("f", tmp)], writes=[("f", tmp)])
                pg.op("act", lambda e, which=which: e.activation(out=ropeT[:, which, :], in_=funits[tmp][:], func=AF.Sin),
                      reads=[("f", tmp)], writes=[("rope", which)])
            RS = float(128.0 ** -0.5)
            pg.op("dve", lambda e: e.tensor_scalar(out=ropeT[:, 2, :], in0=ropeT[:, 0, :], scalar1=RS, scalar2=None, op0=ALU.mult),
                  reads=[("rope", 0)], writes=[("rope", 2)])
            pg.op("dve", lambda e: e.tensor_scalar(out=ropeT[:, 3, :], in0=ropeT[:, 1, :], scalar1=pk[:, 182:183], scalar2=None, op0=ALU.mult),
                  reads=[("rope", 1), "pk"], writes=[("rope", 3)])
            pg.op("dve", lambda e: e.tensor_scalar(out=ropeT[:, 1, :], in0=ropeT[:, 1, :], scalar1=pk[:, 181:182], scalar2=None, op0=ALU.mult),
                  reads=[("rope", 1), ("rope", 3), "pk"], writes=[("rope", 1)])
            for u_ in (ang, tmp, kf):
                fpool.put(u_)

            def do_layer(l):
                if KSTAGE < 2:
                    return
                rms_stage(pk[:, l * 8:(l + 1) * 8], lambda c: yT[:, c, :], lambda c: ["yT"])
                if DEBUG and t == 0 and l == 0:
                    dbg("yT", yT[:], [128, 8, TT], BF16, ["yT"])

                if KSTAGE < 3:
                    return
                sx = w_in_job(l, 0)
                sg = w_in_job(l, 1)
                pend = []

                def lru_A(n):
                    bx = pspool.get()
                    proj_fm(l, sx, 768, n * 96, 96, bx)
                    xl = xls[n % 2]
                    xk = ("xl", n % 2)
                    hi = (l * 8 + n) * 4
                    pg.op("pool", lambda e: e.tensor_copy(out=xl[0:96, 0:3], in_=hist[0:96, hi:hi + 3]),
                          reads=[("hist", l * 8 + n)], writes=[xk])
                    pg.op("act", lambda e: e.activation(out=xl[0:96, 3:TT + 3], in_=banks[bx][0:96, :], func=AF.Copy),
                          reads=[("ps", bx)], writes=[xk])
                    pspool.put(bx)
                    bg = pspool.get()
                    proj_fm(l, sg, 768, n * 96, 96, bg)
                    return bg

                def lru_B1(n, bg):
                    xl = xls[n % 2]
                    xk = ("xl", n % 2)
                    col = l * 8 + n
                    cw = lambda k: pk[0:96, 52 + col * 4 + k:52 + col * 4 + k + 1]
                    xc = fpool.get()
                    XC = funits[xc][0:96, :]
                    pg.op("dve", lambda e: e.tensor_scalar(out=XC, in0=xl[0:96, 0:TT], scalar1=cw(0), scalar2=pk[0:96, 116 + col:117 + col],
                                                           op0=ALU.mult, op1=ALU.add), reads=[xk, "pk"], writes=[("f", xc)])
                    for k in (1, 2, 3):
                        pg.op("dve", lambda e, k=k: e.scalar_tensor_tensor(out=XC, in0=xl[0:96, k:k + TT], scalar=cw(k), in1=XC,
                                                                          op0=ALU.mult, op1=ALU.add), reads=[xk, "pk", ("f", xc)], writes=[("f", xc)])
                    hi = col * 4
                    pg.op("pool", lambda e: e.tensor_copy(out=hist[0:96, hi:hi + 3], in_=xl[0:96, TT:TT + 3]),
                          reads=[xk], writes=[("hist", col)])
                    xb = bpool.get()
                    pg.op("pool", lambda e: e.tensor_copy(out=bunits[xb][0:96, :], in_=XC), reads=[("f", xc)], writes=[("b", xb)])
                    ba = pspool.get()
                    bxx = pspool.get()
                    wo = col * 96
                    mm(ba, banks[ba][0:96, :], wab[0:96, wo:wo + 96], bunits[xb][0:96, :], True, True, [("b", xb), "wab"])
                    mm(bxx, banks[bxx][0:96, :], wxb[0:96, wo:wo + 96], bunits[xb][0:96, :], True, True, [("b", xb), "wxb"])
                    bpool.put(xb)
                    tha = fpool.get()
                    thx = fpool.get()
                    thg = fpool.get()
                    a2 = fpool.get()
                    A, X, G, A2 = (funits[i][0:96, :] for i in (tha, thx, thg, a2))
                    pg.op("act", lambda e: e.activation(out=A, in_=banks[ba][0:96, :], func=AF.Tanh, bias=dv[0:96, col:col + 1], scale=0.5),
                          reads=[("ps", ba)] + DVK, writes=[("f", tha)])
                    pg.op("act", lambda e: e.activation(out=X, in_=banks[bxx][0:96, :], func=AF.Tanh, bias=dv[0:96, 16 + col:17 + col], scale=0.5),
                          reads=[("ps", bxx)] + DVK, writes=[("f", thx)])
                    pspool.put(ba)
                    pspool.put(bxx)
                    pg.op("act", lambda e: e.activation(out=G, in_=banks[bg][0:96, :], func=AF.Tanh, scale=0.5),
                          reads=[("ps", bg)], writes=[("f", thg)])
                    pg.op("act", lambda e: e.activation(out=A2, in_=A, func=AF.Exp, bias=dv[0:96, 32 + col:33 + col], scale=dv[0:96, 32 + col:33 + col]),
                          reads=[("f", tha)] + DVK, writes=[("f", a2)])
                    pg.op("act", lambda e: e.activation(out=A, in_=A, func=AF.Exp, bias=dv[0:96, 48 + col:49 + col], scale=dv[0:96, 48 + col:49 + col]),
                          reads=[("f", tha)] + DVK, writes=[("f", tha)])
                    pg.op("act", lambda e: e.activation(out=A2, in_=A2, func=AF.Ln, bias=onec[0:96, :], scale=-1.0),
                          reads=[("f", a2), "pk"], writes=[("f", a2)])
                    pg.op("act", lambda e: e.activation(out=A2, in_=A2, func=AF.Exp, scale=0.5),
                          reads=[("f", a2)], writes=[("f", a2)])
                    return (bg, xc, tha, thx, thg, a2)

                def lru_B2(n, st_):
                    bg, xc, tha, thx, thg, a2 = st_
                    col = l * 8 + n
                    XC = funits[xc][0:96, :]
                    A, X, G, A2 = (funits[i][0:96, :] for i in (tha, thx, thg, a2))
                    pg.op("dve", lambda e: e.scalar_tensor_tensor(out=X, in0=X, scalar=1.0, in1=XC, op0=ALU.add, op1=ALU.mult),
                          reads=[("f", thx), ("f", xc)], writes=[("f", thx)])
                    pg.op("dve", lambda e: e.scalar_tensor_tensor(out=X, in0=X, scalar=0.5, in1=A2, op0=ALU.mult, op1=ALU.mult),
                          reads=[("f", thx), ("f", a2)], writes=[("f", thx)])
                    pg.op("dve", lambda e: e.tensor_tensor_scan(out=XC, data0=A, data1=X, initial=hst[0:96, col:col + 1], op0=ALU.mult, op1=ALU.add),
                          reads=[("f", tha), ("f", thx), "hst"], writes=[("f", xc)])
                    pg.op("pool", lambda e: e.tensor_copy(out=hst[0:96, col:col + 1], in_=funits[xc][0:96, TT - 1:TT]),
                          reads=[("f", xc)], writes=["hst"])
                    pg.op("dve", lambda e: e.scalar_tensor_tensor(out=G, in0=G, scalar=1.0, in1=banks[bg][0:96, :], op0=ALU.add, op1=ALU.mult),
                          reads=[("f", thg), ("ps", bg)], writes=[("f", thg)])
                    pspool.put(bg)
                    pg.op("dve", lambda e: e.scalar_tensor_tensor(out=mixL[0:96, n, :], in0=G, scalar=0.5, in1=XC, op0=ALU.mult, op1=ALU.mult),
                          reads=[("f", thg), ("f", xc)], writes=[("mixL", n)])
                    if DEBUG and t == 0 and l == 0 and n == 0:
                        dbg("lru_h0", funits[xc][:], [128, TT], F32, [("f", xc)])
                    for u_ in (xc, tha, thx, thg, a2):
                        fpool.put(u_)

                bgs = {}
                sts = {}
                bgs[0] = lru_A(0)
                bgs[1] = lru_A(1)
                sts[0] = lru_B1(0, bgs[0])
                for n in range(8):
                    if n + 2 < 8:
                        bgs[n + 2] = lru_A(n + 2)
                    if n + 1 < 8:
                        sts[n + 1] = lru_B1(n + 1, bgs[n + 1])
                    lru_B2(n, sts[n])

                if KSTAGE < 4:
                    return
                sq_ = w_in_job(l, 2)
                sk_ = w_in_job(l, 3)
                sv_ = w_in_job(l, 4)
                sgr = w_in_job(l, 5)
                for hg in range(2):
                    items = []
                    for hh in range(3):
                        h = hg * 3 + hh

                        def rope_item(slot, cidx, sidx, dst, dkey, hh=hh, h=h):
                            bq = pspool.get()
                            proj_fm(l, slot, 768, h * 128, 128, bq)
                            qb = bpool.get()
                            pg.op("act", lambda e, bq=bq, qb=qb: e.activation(out=bunits[qb][:], in_=banks[bq][:], func=AF.Copy),
                                  reads=[("ps", bq)], writes=[("b", qb), ("psr", bq)])
                            bs_ = pspool.get()
                            mm(bs_, banks[bs_][:], perm, bunits[qb][:], True, True, [("b", qb), "cb"])
                            bpool.put(qb)
                            yield
                            if KQ < 1:
                                pspool.put(bq)
                                pspool.put(bs_)
                                return
                            t1 = fpool.get()
                            t2 = fpool.get()
                            pg.op("dve", lambda e, bq=bq, t1=t1, cidx=cidx: e.scalar_tensor_tensor(out=funits[t1][:], in0=ropeT[:, cidx, :], scalar=1.0, in1=banks[bq][:], op0=ALU.mult, op1=ALU.mult),
                                  reads=[("ps", bq), ("psr", bq), ("rope", cidx)], writes=[("f", t1)])
                            pg.op("dve", lambda e, bs_=bs_, t2=t2, sidx=sidx: e.scalar_tensor_tensor(out=funits[t2][:], in0=ropeT[:, sidx, :], scalar=1.0, in1=banks[bs_][:], op0=ALU.mult, op1=ALU.mult),
                                  reads=[("ps", bs_), ("rope", sidx)], writes=[("f", t2)])
                            pspool.put(bq)
                            pspool.put(bs_)
                            if KQ < 2:
                                fpool.put(t1)
                                fpool.put(t2)
                                return
                            lo_t = qlo if dkey == "qt" else klo
                            lkey = "qlo" if dkey == "qt" else "klo"
                            pg.op("dve", lambda e, t1=t1, t2=t2: e.tensor_tensor(out=funits[t1][:], in0=funits[t1][:], in1=funits[t2][:], op=ALU.add),
                                  reads=[("f", t1), ("f", t2)], writes=[("f", t1)])
                            pg.op("act", lambda e, t1=t1, dst=dst, hh=hh: e.activation(out=dst[:, hh, :], in_=funits[t1][:], func=AF.Copy),
                                  reads=[("f", t1)], writes=[(dkey, hh)])
                            pg.op("pool", lambda e, t2=t2, dst=dst, hh=hh: e.tensor_copy(out=funits[t2][:], in_=dst[:, hh, :]),
                                  reads=[(dkey, hh), ("f", t2)], writes=[("f", t2)])
                            pg.op("pool", lambda e, t1=t1, t2=t2, lo_t=lo_t, hh=hh: e.tensor_tensor(
                                out=lo_t[:, hh, :], in0=funits[t1][:], in1=funits[t2][:], op=ALU.subtract),
                                reads=[("f", t1), ("f", t2)], writes=[(lkey, hh)])
                            if dkey == "qt":
                                pg.op("pool", lambda e, t1=t1, hh=hh, h=h: e.tensor_tensor(
                                    out=qg[:, hh, :].rearrange("p (n i) -> p n i", n=4),
                                    in0=funits[t1][:].rearrange("p (n i) -> p n i", n=4),
                                    in1=qwc[:, h * 128:(h + 1) * 128].unsqueeze(1).broadcast_to([128, 4, 128]), op=ALU.mult),
                                    reads=[("f", t1), "cst"], writes=[("qg", hh)])
                            fpool.put(t1)
                            fpool.put(t2)

                        for args_ in ((sq_, 0, 1, qt, "qt"), (sk_, 2, 3, kt, "kt")):
                            items.append(rope_item(*args_))
                    next(items[0])
                    for i_ in range(6):
                        if i_ + 1 < 6:
                            next(items[i_ + 1])
                        for _ in items[i_]:
                            pass
                    if DEBUG and t == 0 and l == 0 and hg == 0:
                        dbg("qt", qt[:], [128, 3, TT], BF16, [("qt", i) for i in range(3)])
                        dbg("kt", kt[:], [128, 3, TT], BF16, [("kt", i) for i in range(3)])
                    if KRET < 2:
                        continue
                    for blk in range(4):
                        bv = pspool.get()
                        proj_tm(sv_, 768, hg * 384, 384, blk, bv)
                        pg.op("act", lambda e, bv=bv, blk=blk: e.activation(out=vbf[:, blk, :], in_=banks[bv][:, 0:384], func=AF.Copy),
                              reads=[("ps", bv)], writes=[("vbf", blk)])
                        pg.op("dve", lambda e, bv=bv, blk=blk, hg=hg: e.scalar_tensor_tensor(out=vw[:, blk, :], in0=kwc[:, hg * 384:(hg + 1) * 384], scalar=1.0, in1=banks[bv][:, 0:384], op0=ALU.mult, op1=ALU.mult),
                              reads=[("ps", bv), "cst"], writes=[("vw", blk)])
                        pspool.put(bv)
                    if KRET < 3:
                        continue
                    for blk in range(4):
                        bt = pspool.get()
                        btb = banks[bt][:].bitcast(BF16)
                        for hh in range(3):
                            pg.op("pe", lambda e, btb=btb, hh=hh, blk=blk: e.transpose(btb[:, hh * 128:(hh + 1) * 128], kt[:, hh, blk * 128:(blk + 1) * 128], ident),
                                  reads=[("kt", hh), "cb"], writes=[("ps", bt)])
                        pg.op("act", lambda e, btb=btb, blk=blk: e.activation(out=ktm[:, blk, :], in_=btb[:, 0:384], func=AF.Copy),
                              reads=[("ps", bt)], writes=[("ktm", blk)])
                        pspool.put(bt)
                    if KRET < 4:
                        continue
                    bo = [pspool.get() for _ in range(3)]
                    for n in range(4):
                        cs = slice(n * 128, (n + 1) * 128)
                        bsc = pspool.get()
                        for hh in range(3):
                            osc = banks[bsc][:, hh * 128:(hh + 1) * 128]
                            mm(bsc, osc, kt[:, hh, cs], qt[:, hh, cs], True, False, [("kt", hh), ("qt", hh)])
                            mm(bsc, osc, kt[:, hh, cs], qlo[:, hh, cs], False, False, [("kt", hh), ("qlo", hh)])
                            mm(bsc, osc, klo[:, hh, cs], qt[:, hh, cs], False, True, [("klo", hh), ("qt", hh)])
                        Pb = bpool.get()
                        pg.op("dve", lambda e, bsc=bsc, Pb=Pb, hg=hg: e.scalar_tensor_tensor(out=bunits[Pb][:, 0:384], in0=decayT[:, hg * 384:(hg + 1) * 384], scalar=1.0, in1=banks[bsc][:, 0:384], op0=ALU.mult, op1=ALU.mult),
                              reads=[("ps", bsc), "cst"], writes=[("b", Pb)])
                        pspool.put(bsc)
                        for hh in range(3):
                            h = hg * 3 + hh
                            mm(bo[hh], banks[bo[hh]][:, cs], vbf[:, n, hh * 128:(hh + 1) * 128], bunits[Pb][:, hh * 128:(hh + 1) * 128], True, False,
                               [("vbf", n), ("b", Pb)])
                            mm(bo[hh], banks[bo[hh]][:, cs], Sb[:, l, h * 128:(h + 1) * 128], qg[:, hh, cs], False, True, [("Sb", l, hg), ("qg", hh)])
                        bpool.put(Pb)
                        bkv = pspool.get()
                        for hh in range(3):
                            mm(bkv, banks[bkv][:, hh * 128:(hh + 1) * 128], ktm[:, n, hh * 128:(hh + 1) * 128], vw[:, n, hh * 128:(hh + 1) * 128], True, True,
                               [("ktm", n), ("vw", n)])
                        for hh in range(3):
                            h = hg * 3 + hh
                            pg.op("dve", lambda e, h=h, hh=hh, bkv=bkv: e.scalar_tensor_tensor(
                                out=Sst[:, l, h * 128:(h + 1) * 128], in0=Sst[:, l, h * 128:(h + 1) * 128], scalar=gam[h],
                                in1=banks[bkv][:, hh * 128:(hh + 1) * 128], op0=ALU.mult, op1=ALU.add),
                                reads=[("ps", bkv), ("Sst", l, hg)], writes=[("Sst", l, hg)])
                        pspool.put(bkv)
                        pg.op("act", lambda e, hg=hg: e.activation(out=Sb[:, l, hg * 384:(hg + 1) * 384], in_=Sst[:, l, hg * 384:(hg + 1) * 384], func=AF.Copy),
                              reads=[("Sst", l, hg)], writes=[("Sb", l, hg)])
                    if KRET < 5:
                        for b_ in bo:
                            pspool.put(b_)
                        continue
                    for hh in range(3):
                        h = hg * 3 + hh
                        b = bpool.get()
                        pg.op("act", lambda e, b=b, boa=banks[bo[hh]][:]: e.activation(out=bunits[b][:], in_=boa, func=AF.Square),
                              reads=[("ps", bo[hh])], writes=[("b", b)])
                        bm = pspool.get()
                        mm(bm, banks[bm][:], ones, bunits[b][:], True, True, [("b", b), "cb"])
                        bpool.put(b)
                        r = fpool.get()
                        pg.op("act", lambda e, r=r, bm=bm: e.activation(out=funits[r][:], in_=banks[bm][:], func=AF.Ln, bias=epsc, scale=1.0 / 128),
                              reads=[("ps", bm), "pk"], writes=[("f", r)])
                        pspool.put(bm)
                        pg.op("act", lambda e, r=r: e.activation(out=funits[r][:], in_=funits[r][:], func=AF.Exp, scale=-0.5), reads=[("f", r)], writes=[("f", r)])
                        on = fpool.get()
                        pg.op("dve", lambda e, r=r, on=on, boa=banks[bo[hh]][:], h=h: e.scalar_tensor_tensor(out=funits[on][:], in0=funits[r][:], scalar=pk[:, 40 + l * 6 + h:41 + l * 6 + h],
                                                                                        in1=boa, op0=ALU.mult, op1=ALU.mult),
                              reads=[("ps", bo[hh]), ("f", r), "pk"], writes=[("f", on)])
                        fpool.put(r)
                        bg = pspool.get()
                        proj_fm(l, sgr, 768, h * 128, 128, bg)
                        tg = fpool.get()
                        pg.op("act", lambda e, tg=tg, bg=bg: e.activation(out=funits[tg][:], in_=banks[bg][:], func=AF.Tanh, scale=0.5),
                              reads=[("ps", bg)], writes=[("f", tg)])
                        pg.op("dve", lambda e, tg=tg, bg=bg: e.scalar_tensor_tensor(out=funits[tg][:], in0=funits[tg][:], scalar=1.0, in1=banks[bg][:], op0=ALU.add, op1=ALU.mult),
                              reads=[("f", tg), ("ps", bg)], writes=[("f", tg)])
                        pspool.put(bg)
                        pg.op("dve", lambda e, tg=tg, on=on, h=h: e.scalar_tensor_tensor(out=mixR[:, h, :], in0=funits[tg][:], scalar=0.5, in1=funits[on][:], op0=ALU.mult, op1=ALU.mult),
                              reads=[("f", tg), ("f", on)], writes=[("mixR", h)])
                        if DEBUG and t == 0 and l == 0 and h == 0:
                            dbg("ret_on0", funits[on][:], [128, TT], F32, [("f", on)])
                        fpool.put(tg)
                        fpool.put(on)
                    for b_ in bo:
                        pspool.put(b_)

                if KSTAGE < 5:
                    return
                su = w_in_job(l, 6, 512)
                def w_cols_job(c0, ncols):
                    src = w_in[l, :, c0:c0 + ncols].rearrange("(k p) c -> p k c", p=128)
                    return wload([(lambda r, n=ncols: r[:, 0:8 * n].rearrange("p (k c) -> p k c", k=8), src)],
                                 [("wb_in", l, c0 // 768), ("wb_in", l, (c0 + ncols - 1) // 768)])
                svs = w_cols_job(5120, 512)
                sgs = w_cols_job(5632, 512)
                for blk in range(4):
                    bv = pspool.get()
                    proj_tm(svs, 512, 0, 512, blk, bv)
                    so = blk * 12
                    tq = fpool.get()
                    pg.op("act", lambda e, bv=bv, tq=tq: e.activation(out=funits[tq][:], in_=banks[bv][:], func=AF.Copy),
                          reads=[("ps", bv)], writes=[("f", tq)])
                    pspool.put(bv)
                    pg.op("dve", lambda e, tq=tq, so=so: e.bn_stats(out=small[:, so:so + 6], in_=funits[tq][:]), reads=[("f", tq)], writes=[("small", blk)])
                    pg.op("dve", lambda e, so=so: e.bn_aggr(out=small[:, so + 6:so + 8], in_=small[:, so:so + 6]), reads=[("small", blk)], writes=[("small", blk)])
                    pg.op("act", lambda e, so=so: e.activation(out=small[:, so + 8:so + 9], in_=small[:, so + 7:so + 8], func=AF.Ln, bias=epsc, scale=1.0),
                          reads=[("small", blk), "pk"], writes=[("small", blk)])
                    pg.op("act", lambda e, so=so: e.activation(out=small[:, so + 8:so + 9], in_=small[:, so + 8:so + 9], func=AF.Exp, scale=-0.5), reads=[("small", blk)], writes=[("small", blk)])
                    pg.op("dve", lambda e, so=so, tq=tq: e.scalar_tensor_tensor(out=funits[tq][:], in0=funits[tq][:], scalar=small[:, so + 6:so + 7], in1=lng[:, l, :],
                                                                                op0=ALU.subtract, op1=ALU.mult),
                          reads=[("f", tq), ("small", blk), ("lng", l)], writes=[("f", tq)])
                    pg.op("dve", lambda e, so=so, tq=tq, blk=blk: e.scalar_tensor_tensor(out=vn[:, blk, :], in0=funits[tq][:], scalar=small[:, so + 8:so + 9], in1=lnb[:, l, :],
                                                                                  op0=ALU.mult, op1=ALU.add),
                          reads=[("f", tq), ("small", blk), ("lnb", l)], writes=[("vn", blk)])
                    fpool.put(tq)
                for g in range(4):
                    gs = slice(g * 128, (g + 1) * 128)
                    bmix = pspool.get()
                    for n in range(4):
                        cs = slice(n * 128, (n + 1) * 128)
                        mm(bmix, banks[bmix][:, cs], vn[:, n, gs], wsT[:, l, gs], True, False, [("vn", n), ("wsT", l)])
                        mm(bmix, banks[bmix][:, cs], ones[0:1, :], bsb[0:1, l * 512 + g * 128:l * 512 + (g + 1) * 128], False, True, ["cb", "bsb"])
                    bu = pspool.get()
                    proj_fm(l, su, 512, g * 128, 128, bu)
                    bgg = pspool.get()
                    proj_fm(l, sgs, 512, g * 128, 128, bgg)
                    m_ = fpool.get()
                    pg.op("act", lambda e, m_=m_, bmix=bmix: e.activation(out=funits[m_][:], in_=banks[bmix][:], func=AF.Copy),
                          reads=[("ps", bmix)], writes=[("f", m_)])
                    pspool.put(bmix)
                    pg.op("dve", lambda e, m_=m_, bu=bu: e.scalar_tensor_tensor(out=funits[m_][:], in0=funits[m_][:], scalar=1.0, in1=banks[bu][:], op0=ALU.mult, op1=ALU.mult),
                          reads=[("f", m_), ("ps", bu)], writes=[("f", m_)])
                    pspool.put(bu)
                    tg = fpool.get()
                    pg.op("act", lambda e, tg=tg, bgg=bgg: e.activation(out=funits[tg][:], in_=banks[bgg][:], func=AF.Tanh, scale=0.5),
                          reads=[("ps", bgg)], writes=[("f", tg)])
                    pg.op("dve", lambda e, tg=tg, bgg=bgg: e.scalar_tensor_tensor(out=funits[tg][:], in0=funits[tg][:], scalar=1.0, in1=banks[bgg][:], op0=ALU.add, op1=ALU.mult),
                          reads=[("f", tg), ("ps", bgg)], writes=[("f", tg)])
                    pspool.put(bgg)
                    pg.op("dve", lambda e, tg=tg, m_=m_, g=g: e.scalar_tensor_tensor(out=mixS[:, g, :], in0=funits[tg][:], scalar=0.5, in1=funits[m_][:], op0=ALU.mult, op1=ALU.mult),
                          reads=[("f", tg), ("f", m_)], writes=[("mixS", g)])
                    fpool.put(tg)
                    fpool.put(m_)
                if DEBUG and t == 0 and l == 0:
                    dbg("mixL", mixL[:], [128, 8, TT], BF16, [("mixL", i) for i in range(8)])
                    dbg("mixR", mixR[:], [128, 6, TT], BF16, [("mixR", i) for i in range(6)])
                    dbg("mixS", mixS[:], [128, 4, TT], BF16, [("mixS", i) for i in range(4)])

                if KSTAGE < 6:
                    return
                for piece in range(4):
                    c0 = piece * 256
                    dm = [
                        (lambda r: r[0:96, 0:2048].rearrange("p (n c) -> p n c", n=8),
                         w_out[l, 0:768, c0:c0 + 256].rearrange("(n p) c -> p n c", p=96)),
                        (lambda r: r[:, 2048:3584].rearrange("p (n c) -> p n c", n=6),
                         w_out[l, 768:1536, c0:c0 + 256].rearrange("(n p) c -> p n c", p=128)),
                        (lambda r: r[:, 3584:4608].rearrange("p (n c) -> p n c", n=4),
                         w_out[l, 1536:2048, c0:c0 + 256].rearrange("(n p) c -> p n c", p=128)),
                    ]
                    so_ = wload(dm, [("wb_out", l)])
                    R = ring[so_]
                    for d2 in range(2):
                        c = piece * 2 + d2
                        bw = pspool.get()
                        steps = []
                        for n in range(8):
                            steps.append((R[0:96, n * 256 + d2 * 128:n * 256 + d2 * 128 + 128], mixL[0:96, n, :], ("mixL", n)))
                        for h in range(6):
                            steps.append((R[:, 2048 + h * 256 + d2 * 128:2048 + h * 256 + d2 * 128 + 128], mixR[:, h, :], ("mixR", h)))
                        for g in range(4):
                            steps.append((R[:, 3584 + g * 256 + d2 * 128:3584 + g * 256 + d2 * 128 + 128], mixS[:, g, :], ("mixS", g)))
                        for i, (lh, rh, key) in enumerate(steps):
                            mm(bw, banks[bw][:], lh, rh, i == 0, i == len(steps) - 1, [("ring", so_), key])
                        pg.op("dve", lambda e, c=c, bw=bw: e.scalar_tensor_tensor(out=hT[:, c, :], in0=hT[:, c, :], scalar=1.0, in1=banks[bw][:], op0=ALU.mult, op1=ALU.add),
                              reads=[("hT", c), ("ps", bw)], writes=[("hT", c)])
                        pspool.put(bw)

                if KSTAGE < 7:
                    return
                rms_stage(pk[:, 16 + l * 8:16 + (l + 1) * 8], lambda c: yT[:, c, :], lambda c: ["yT"])
                pg.op("pool", lambda e, tok=tok: e.dma_start(out=pTt[:], in_=pT[l].rearrange("(k p) t -> p k t", p=128)[:, :, tok]),
                      reads=[], writes=["pTt", "poolq"], dma_sem=s_p)
                spp = wload([(lambda r: r[:, 0:2048].rearrange("p (k c) -> p k c", k=2), w_pp[l].rearrange("(k p) c -> p k c", p=128))], [("wb_pp", l)])
                for half in range(2):
                    sgt = wload([(lambda r: r[:, 0:4096].rearrange("p (k c) -> p k c", k=8),
                                  w_gate[l, :, half * 512:(half + 1) * 512].rearrange("(k p) c -> p k c", p=128))], [("wb_gate", l)])
                    for cc in range(4):
                        c = half * 4 + cc
                        bgt = pspool.get()
                        proj_fm(l, sgt, 512, cc * 128, 128, bgt)
                        bpp = pspool.get()
                        rp = ring[spp][:, 0:2048].rearrange("p (k c) -> p k c", k=2)
                        for k in range(2):
                            mm(bpp, banks[bpp][:], rp[:, k, c * 128:(c + 1) * 128], pTt[:, k, :], k == 0, k == 1, [("ring", spp), "pTt"])
                        th = fpool.get()
                        pg.op("act", lambda e, th=th, bgt=bgt: e.activation(out=funits[th][:], in_=banks[bgt][:], func=AF.Tanh, scale=0.5),
                              reads=[("ps", bgt)], writes=[("f", th)])
                        pspool.put(bgt)
                        pg.op("dve", lambda e, th=th, bpp=bpp: e.scalar_tensor_tensor(out=funits[th][:], in0=funits[th][:], scalar=1.0, in1=banks[bpp][:], op0=ALU.add, op1=ALU.mult),
                              reads=[("f", th), ("ps", bpp)], writes=[("f", th)])
                        pspool.put(bpp)
                        pg.op("dve", lambda e, th=th, c=c: e.scalar_tensor_tensor(out=hT[:, c, :], in0=funits[th][:], scalar=0.5, in1=hT[:, c, :], op0=ALU.mult, op1=ALU.add),
                              reads=[("f", th), ("hT", c)], writes=[("hT", c)])
                        fpool.put(th)
                if DEBUG and t == 0:
                    dbg(f"h_l{l}", hT[:], [128, 8, TT], F32, [("hT", c) for c in range(8)])

            for l_ in range(DEPTH):
                do_layer(l_)

            outs = {}

            def fin_out(c):
                u = fpool.get()
                outs[c] = u
                return funits[u][:]

            def fin_keys(c):
                return [("f", outs[c])]

            bank = pspool.get()
            for c in range(8):
                b = bpool.get()
                pg.op("act", lambda e, c=c, b=b: e.activation(out=bunits[b][:], in_=hT[:, c, :], func=AF.Square), reads=[("hT", c)], writes=[("b", b)])
                mm(bank, banks[bank][:], ones, bunits[b][:], c == 0, c == 7, [("b", b), "cb"])
                bpool.put(b)
            r = fpool.get()
            pg.op("act", lambda e, r=r, bank=bank: e.activation(out=funits[r][:], in_=banks[bank][:], func=AF.Ln, bias=epsc, scale=1.0 / D),
                  reads=[("ps", bank), "pk"], writes=[("f", r)])
            pg.op("act", lambda e, r=r: e.activation(out=funits[r][:], in_=funits[r][:], func=AF.Exp, scale=-0.5), reads=[("f", r)], writes=[("f", r)])
            pspool.put(bank)
            for c in range(8):
                u = fpool.get()
                pg.op("dve", lambda e, c=c, u=u, r=r: e.scalar_tensor_tensor(out=funits[u][:], in0=hT[:, c, :], scalar=pk[:, 32 + c:33 + c], in1=funits[r][:],
                                                                          op0=ALU.mult, op1=ALU.mult),
                      reads=[("hT", c), ("f", r), "pk"], writes=[("f", u)])
                pg.op("pool", lambda e, c=c, u=u, tok=tok: e.dma_start(out=outT[c * 128:(c + 1) * 128, tok], in_=funits[u][:]),
                      reads=[("f", u)], writes=[("outsem", c), "poolq"], dma_sem=s_out[c])
                fpool.put(u)
            fpool.put(r)

        for t_ in range(NT_RUN):
            do_tile(t_)

        def final_waits():
            res = []
            for s_ in s_out:
                if s_ in pg.dma_counts:
                    res.append((s_, pg.dma_counts[s_]))
            if s_dbg in pg.dma_counts:
                res.append((s_dbg, pg.dma_counts[s_dbg]))
            return res

        with nc.Block() as block:
            pg.emit(block, eng_sems, final_waits)
    return nc, list(dbg_outs.keys())


def _consts():
    f32 = np.float32
    H, C = 6, 128
    log_g = np.log1p(-np.exp2(-5.0 - np.arange(H, dtype=f32))).astype(f32)
    idx = np.arange(C, dtype=f32)
    diff = idx[:, None] - idx[None, :]
    causal = diff >= 0
    decay = np.where(causal[None], np.exp(np.where(causal, diff, 0.0)[None] * log_g[:, None, None]), 0.0).astype(f32)
    decayT = np.ascontiguousarray(decay.transpose(2, 0, 1)).reshape(128, 768)
    k_w = np.exp((C - 1.0 - idx)[:, None] * log_g[None, :]).astype(f32)
    kwc = np.repeat(k_w[:, :, None], 128, axis=2).reshape(128, 768)
    q_w = np.exp((idx + 1.0)[:, None] * log_g[None, :]).astype(f32)
    qw = np.broadcast_to(q_w.T.reshape(1, 768), (128, 768))
    ident = np.eye(128, dtype=f32)
    perm = np.zeros((128, 128), f32)
    for m in range(128):
        perm[(m + 64) % 128, m] = 1.0
    ones = np.ones((128, 128), f32)
    pad = np.zeros((128, 384), f32)
    mask = (idx[:, None] <= idx[None, :]).astype(f32)
    cst = np.concatenate([decayT, kwc, qw, ident, perm, ones, pad, mask], axis=1).astype(f32)
    return np.ascontiguousarray(cst)


def _pack(inp):
    f32 = np.float32
    pk = np.zeros((128, NPK), f32)
    pk[:, 0:16] = inp["mix_norm"].reshape(2, 8, 128).transpose(2, 0, 1).reshape(128, 16)
    pk[:, 16:32] = inp["ple_norm"].reshape(2, 8, 128).transpose(2, 0, 1).reshape(128, 16)
    pk[:, 32:40] = inp["final_norm"].reshape(8, 128).T
    pk[:, 40:52] = inp["ret_norm"].reshape(2, 6, 128).transpose(2, 0, 1).reshape(128, 12)
    cw = inp["lru_conv_w"].reshape(2, 4, 8, 96).transpose(3, 0, 2, 1)
    pk[0:96, 52:116] = cw.reshape(96, 64)
    for name, c0 in (("lru_conv_b", 116), ("lru_b_a", 132), ("lru_b_x", 148), ("lru_lambda", 164)):
        pk[0:96, c0:c0 + 16] = inp[name].reshape(2, 8, 96).transpose(2, 0, 1).reshape(96, 16)
    half = 64
    inv = (10000.0 ** (-np.arange(half, dtype=f32) / half)).astype(f32)
    pk[:, 180] = np.concatenate([inv, inv])
    sgn = np.concatenate([-np.ones(64, f32), np.ones(64, f32)])
    pk[:, 181] = sgn
    pk[:, 182] = sgn * f32(128.0 ** -0.5)
    pk[:, 183] = EPS
    pk[:, 184] = 1.0
    return pk


def kernel(**inputs):
    inp = {k: np.asarray(v) for k, v in inputs.items()}
    f32 = np.float32
    nc, dbg_names = build_nc()
    cst = _consts()
    pk = _pack(inp)
    shared = {
        "w_in": np.ascontiguousarray(inp["w_in"], f32),
        "w_out": np.ascontiguousarray(inp["w_out"], f32),
        "w_gate": np.ascontiguousarray(inp["w_ple_gate"], f32),
        "w_pp": np.ascontiguousarray(inp["w_ple_proj"], f32),
        "w_a": np.ascontiguousarray(inp["lru_w_a"], f32),
        "w_x": np.ascontiguousarray(inp["lru_w_x"], f32),
        "w_sT": np.ascontiguousarray(inp["sgu_w_s"].transpose(0, 3, 1, 2).reshape(2, 128, 512), f32),
        "b_s": np.ascontiguousarray(inp["sgu_b_s"].reshape(1, 1024), f32),
        "ln_g": np.ascontiguousarray(inp["sgu_ln_g"], f32),
        "ln_b": np.ascontiguousarray(inp["sgu_ln_b"], f32),
        "pk": pk,
        "cst": cst,
    }
    in_maps = []
    for core in range(8):
        b = core % 4
        m = dict(shared)
        m["xT"] = np.ascontiguousarray(inp["x"][b].T, f32)
        m["pT"] = np.ascontiguousarray(inp["p"][:, b].transpose(0, 2, 1), f32)
        m["pos"] = np.ascontiguousarray(inp["positions"][b].reshape(1, S), np.int32)
        in_maps.append(m)
    res = run_bass_kernel_spmd(nc, in_maps, core_ids=list(range(8)))
    out = np.stack([np.ascontiguousarray(res.results[b]["outT"].T) for b in range(4)], axis=0).astype(f32)
    if DEBUG:
        kernel.debug = {n: res.results[0]["dbg_" + n] for n in dbg_names}
    return out
```

```python
import os
import numpy as np
import concourse.bass as bass
import concourse.mybir as mybir
from concourse.bass_utils import run_bass_kernel_spmd

F32 = mybir.dt.float32
BF16 = mybir.dt.bfloat16
I32 = mybir.dt.int32
ALU = mybir.AluOpType
AF = mybir.ActivationFunctionType

D = 1024
S = 4096
TT = 512
NT = S // TT
DEPTH = 2
D_IN = 6144
EPS = 1e-6
NPK = 192
PI = float(np.pi)

DEBUG = os.environ.get("KDEBUG", "") != ""
NT_RUN = int(os.environ.get("KNT", NT))
KSTAGE = int(os.environ.get("KSTAGE", 99))
KRET = int(os.environ.get("KRET", 99))
KQ = int(os.environ.get("KQ", 99))
KSKIPW = os.environ.get("KSKIPW", "") != ""


class Op:
    __slots__ = ("eng", "fn", "reads", "writes", "dma_sem", "dma_val", "deps", "signal", "val", "idx")


class Prog:
    ENGS = ("pe", "act", "dve", "pool", "sp")

    def __init__(self, nc):
        self.nc = nc
        self.ops = []
        self.last_w = {}
        self.readers = {}
        self.dma_counts = {}
        self.shared = set()

    def op(self, eng, fn, reads=(), writes=(), dma_sem=None):
        o = Op()
        o.eng, o.fn, o.reads, o.writes = eng, fn, tuple(reads), tuple(writes)
        o.dma_sem = dma_sem
        o.signal = False
        o.val = 0
        o.idx = len(self.ops)
        if dma_sem is not None:
            self.dma_counts[dma_sem] = self.dma_counts.get(dma_sem, 0) + 16
            o.dma_val = self.dma_counts[dma_sem]
        else:
            o.dma_val = 0
        deps = set()
        is_dma = dma_sem is not None
        def need(p):
            return p.dma_sem is not None or is_dma or not (p.eng == eng == "pe")
        for k in o.reads:
            w = self.last_w.get(k)
            if w is not None and need(w):
                deps.add(w.idx)
            if isinstance(k, tuple) and k and k[0] == "ps":
                for r in self.readers.get(k, ()):
                    if r.eng != eng:
                        deps.add(r.idx)
        for k in o.writes:
            w = self.last_w.get(k)
            if w is not None and need(w):
                deps.add(w.idx)
            for r in self.readers.get(k, ()):
                if need(r):
                    deps.add(r.idx)
        deps.discard(o.idx)
        o.deps = deps
        wset = set(o.writes)
        for k in o.writes:
            self.last_w[k] = o
            self.readers[k] = []
        for k in o.reads:
            if k not in wset:
                self.readers.setdefault(k, []).append(o)
        self.ops.append(o)
        return o

    def emit(self, block, eng_sems, final_waits):
        ops = self.ops
        waits = {}
        for en in self.ENGS:
            widx = {}
            wdma = {}
            for o in ops:
                if o.eng != en:
                    continue
                best = {}
                dm = {}
                for d in o.deps:
                    p = ops[d]
                    if p.dma_sem is not None:
                        v = self.dma_counts[p.dma_sem] if p.dma_sem in self.shared else p.dma_val
                        if v > dm.get(p.dma_sem, 0):
                            dm[p.dma_sem] = v
                    else:
                        if d > best.get(p.eng, -1):
                            best[p.eng] = d
                lst = []
                for pe_, d in best.items():
                    if d > widx.get(pe_, -1):
                        widx[pe_] = d
                        ops[d].signal = True
                        lst.append(("e", d))
                for sm, v in dm.items():
                    if v > wdma.get(sm, 0):
                        wdma[sm] = v
                        lst.append(("d", sm, v))
                waits[o.idx] = lst
        cnt = {e: 0 for e in self.ENGS}
        for o in ops:
            if o.dma_sem is None and o.signal:
                cnt[o.eng] += 1
                o.val = cnt[o.eng]

        def run(eng_name, eng):
            for o in ops:
                if o.eng != eng_name:
                    continue
                for w in waits[o.idx]:
                    if w[0] == "e":
                        p = ops[w[1]]
                        eng.wait_ge(eng_sems[p.eng], p.val)
                    else:
                        eng.wait_ge(w[1], w[2])
                ins = o.fn(eng)
                if o.dma_sem is not None:
                    ins.then_inc(o.dma_sem, 16)
                elif o.signal:
                    ins.then_inc(eng_sems[o.eng], 1)
            if eng_name == "sp":
                for s, v in final_waits():
                    eng.wait_ge(s, v)

        @block.tensor
        def _(e):
            run("pe", e)

        @block.scalar
        def _(e):
            run("act", e)

        @block.vector
        def _(e):
            run("dve", e)

        @block.gpsimd
        def _(e):
            run("pool", e)

        @block.sync
        def _(e):
            run("sp", e)


class Pool_:
    def __init__(self, items):
        self.free = list(items)

    def get(self):
        return self.free.pop(0)

    def put(self, x):
        self.free.append(x)


def build_nc():
    nc = bass.Bass("TRN2", target_bir_lowering=False)
    dt_in = lambda name, shape, dt=F32: nc.dram_tensor(name, shape, dt, kind="ExternalInput").ap()
    xT = dt_in("xT", [D, S])
    pT = dt_in("pT", [DEPTH, 256, S])
    pos = dt_in("pos", [1, S], I32)
    w_in = dt_in("w_in", [DEPTH, D, D_IN])
    w_out = dt_in("w_out", [DEPTH, 2048, D])
    w_gate = dt_in("w_gate", [DEPTH, D, D])
    w_pp = dt_in("w_pp", [DEPTH, 256, D])
    w_a = dt_in("w_a", [DEPTH, 8, 96, 96])
    w_x = dt_in("w_x", [DEPTH, 8, 96, 96])
    w_sT = dt_in("w_sT", [DEPTH, 128, 512])
    b_s = dt_in("b_s", [1, DEPTH * 512])
    ln_g = dt_in("ln_g", [DEPTH, 512])
    ln_b = dt_in("ln_b", [DEPTH, 512])
    pk_d = dt_in("pk", [128, NPK])
    cst_d = dt_in("cst", [128, 4 * 768 + 128])
    outT = nc.dram_tensor("outT", [D, S], F32, kind="ExternalOutput").ap()
    dbg_outs = {}

    wb_in = wb_out = wb_gate = wb_pp = pTb = None
    scr = nc.dram_tensor("scr", [DEPTH * 16, 128, 6144], BF16, kind="Internal").ap()

    from contextlib import ExitStack
    es = ExitStack()
    sb = lambda name, shape, dt=F32: es.enter_context(nc.sbuf_tensor("sb_" + name, shape, dt))
    sem = lambda name: es.enter_context(nc.semaphore(name))

    with es:
        hT = sb("hT", [128, 8, TT])
        yT = sb("yT", [128, 8, TT], BF16)
        NSLOT = 4
        ring = [sb(f"ring{i}", [128, 6144], BF16) for i in range(NSLOT)]
        mixL = sb("mixL", [128, 8, TT], BF16)
        mixR = sb("mixR", [128, 6, TT], BF16)
        mixS = sb("mixS", [128, 4, TT], BF16)
        NF, NB = 12, 8
        funits = [sb(f"fu{i}", [128, TT]) for i in range(NF)]
        bunits = [sb(f"bu{i}", [128, TT], BF16) for i in range(NB)]
        xls = [sb(f"xl{i}", [128, TT + 4]) for i in range(2)]
        qt = sb("qt", [128, 3, TT], BF16)
        kt = sb("kt", [128, 3, TT], BF16)
        qg = sb("qg", [128, 3, TT], BF16)
        qlo = sb("qlo", [128, 3, TT], BF16)
        klo = sb("klo", [128, 3, TT], BF16)
        vbf = sb("vbf", [128, 4, 384], BF16)
        vw = sb("vw", [128, 4, 384], BF16)
        ktm = sb("ktm", [128, 4, 384], BF16)
        Sst = sb("Sst", [128, DEPTH, 768])
        Sb = sb("Sb", [128, DEPTH, 768], BF16)
        ropeT = sb("ropeT", [128, 4, TT])
        cstT = sb("cstT", [128, 3 * 768])
        maskT = sb("maskT", [128, 128])
        cbT = sb("cbT", [128, 3 * 128], BF16)
        pk = sb("pk", [128, NPK])
        dv = sb("dv", [128, 64])
        vn = sb("vn", [128, 4, TT], BF16)
        wsT = sb("wsT", [128, DEPTH, 512], BF16)
        lng = sb("lng", [128, DEPTH, 512])
        lnb = sb("lnb", [128, DEPTH, 512])
        bsb = sb("bsb", [1, DEPTH * 512], BF16)
        pTt = sb("pTt", [128, 2, TT], BF16)
        wab = sb("wab", [96, DEPTH * 8 * 96], BF16)
        wxb = sb("wxb", [96, DEPTH * 8 * 96], BF16)
        hst = sb("hst", [128, DEPTH * 8])
        hist = sb("hist", [128, DEPTH * 8 * 4])
        posi = sb("posi", [128, TT], I32)
        ki = sb("ki", [128, TT], I32)
        small = sb("small", [128, 64])
        banks = [es.enter_context(nc.psum_tensor(f"ps{i}", [128, TT], F32)) for i in range(8)]

        eng_sems = {e: sem(f"s_{e}") for e in Prog.ENGS}
        s_cast = [sem(f"s_cast{i}") for i in range(4)]
        s_wb = [sem(f"s_wb{i}") for i in range(NSLOT)]
        s_rl = [sem(f"s_rl{i}") for i in range(NSLOT)]
        s_ring = [sem(f"s_ring{i}") for i in range(NSLOT)]
        s_ld = sem("s_ld")
        s_x = sem("s_x")
        s_pos = sem("s_pos")
        s_p = sem("s_p")
        s_out = [sem(f"s_out{i}") for i in range(8)]
        s_dbg = sem("s_dbg")

        pg = Prog(nc)
        pg.shared = set([s_ld] + list(s_cast))
        fpool = Pool_(range(NF))
        bpool = Pool_(range(NB))
        pspool = Pool_(range(8))

        ident = cbT[:, 0:128]
        perm = cbT[:, 128:256]
        ones = cbT[:, 256:384]
        decayT = cstT[:, 0:768]
        kwc = cstT[:, 768:1536]
        qwc = cstT[:, 1536:2304]
        epsc = pk[:, 183:184]
        onec = pk[:, 184:185]

        def dbg(name, ap, shape, dt, reads):
            if not DEBUG:
                return
            o = nc.dram_tensor("dbg_" + name, list(shape), dt, kind="ExternalOutput").ap()
            dbg_outs[name] = o
            pg.op("sp", lambda e, o=o, ap=ap: e.dma_start(out=o, in_=ap), reads=reads, writes=["dbgsem"], dma_sem=s_dbg)

        cast_i = [0]

        def cast_dma(out_ap, in_ap, wkey):
            sm = s_cast[cast_i[0] % len(s_cast)]
            cast_i[0] += 1
            pg.op("pool", lambda e: e.dma_start(out=out_ap, in_=in_ap), reads=[], writes=[wkey], dma_sem=sm)

        def ld(out_ap, in_ap, wkey):
            pg.op("sp", lambda e: e.dma_start(out=out_ap, in_=in_ap), reads=[], writes=[wkey], dma_sem=s_ld)

        ld(pk[:], pk_d, "pk")
        ld(cstT[:], cst_d[:, 0:2304], "cst")
        ld(maskT[:], cst_d[:, 2304 + 768:2304 + 768 + 128], "mask")
        for l in range(DEPTH):
            ld(lng[:, l, :], ln_g[l:l + 1, :].broadcast_to([128, 512]), ("lng", l))
            ld(lnb[:, l, :], ln_b[l:l + 1, :].broadcast_to([128, 512]), ("lnb", l))
        cast_dma(cbT[:], cst_d[:, 2304:2304 + 384], "cb")
        cast_dma(bsb[:], b_s, "bsb")
        cast_dma(wab[:].rearrange("c (l n d) -> c l n d", l=DEPTH, n=8), w_a.rearrange("l n c d -> c l n d"), "wab")
        cast_dma(wxb[:].rearrange("c (l n d) -> c l n d", l=DEPTH, n=8), w_x.rearrange("l n c d -> c l n d"), "wxb")
        for l in range(0):
            for g in range(8):
                cast_dma(wb_in[l, :, g * 768:(g + 1) * 768], w_in[l, :, g * 768:(g + 1) * 768], ("wb_in", l, g))
            for g in range(2):
                cast_dma(wb_out[l, g * 1024:(g + 1) * 1024, :], w_out[l, g * 1024:(g + 1) * 1024, :], ("wb_out", l))
            cast_dma(wb_gate[l], w_gate[l], ("wb_gate", l))
            cast_dma(wb_pp[l], w_pp[l], ("wb_pp", l))
            cast_dma(pTb[l], pT[l], ("pTb", l))

        pg.op("dve", lambda e: e.memset(Sst[:], 0.0), writes=["Sst"])
        pg.op("dve", lambda e: e.memset(Sb[:], 0.0), writes=["Sb"])
        pg.op("dve", lambda e: e.memset(hst[:], 0.0), writes=["hst"])
        pg.op("dve", lambda e: e.memset(hist[:], 0.0), writes=[("hist", i) for i in range(16)])
        for i in range(2):
            pg.op("dve", lambda e, i=i: e.memset(xls[i][:], 0.0), writes=[("xl", i)])
        for l in range(DEPTH):
            u = fpool.get()
            ld(funits[u][:], w_sT[l], ("f", u))
            pg.op("dve", lambda e, l=l, u=u: e.tensor_tensor(
                out=wsT[:, l, :].rearrange("p (g i) -> p g i", g=4),
                in0=funits[u][:].rearrange("p (g i) -> p g i", g=4),
                in1=maskT[:].unsqueeze(1).broadcast_to([128, 4, 128]), op=ALU.mult),
                reads=[("f", u), "mask"], writes=[("wsT", l)])
            fpool.put(u)
        pg.op("dve", lambda e: e.tensor_scalar(out=dv[:, 0:32], in0=pk[:, 132:164], scalar1=0.5, scalar2=None, op0=ALU.mult),
              reads=["pk"], writes=["dv0"])
        pg.op("act", lambda e: e.activation(out=small[:, 0:16], in_=pk[:, 164:180], func=AF.Exp, scale=-1.0),
              reads=["pk"], writes=["sm0"])
        pg.op("act", lambda e: e.activation(out=small[:, 16:32], in_=small[:, 0:16], func=AF.Ln, bias=onec, scale=1.0),
              reads=["sm0", "pk"], writes=["sm1"])
        pg.op("dve", lambda e: e.tensor_scalar(out=dv[:, 32:48], in0=small[:, 16:32], scalar1=-8.0, scalar2=None, op0=ALU.mult),
              reads=["sm1"], writes=["dv1"])
        pg.op("dve", lambda e: e.tensor_scalar(out=dv[:, 48:64], in0=small[:, 16:32], scalar1=-4.0, scalar2=None, op0=ALU.mult),
              reads=["sm1"], writes=["dv2"])
        DVK = ["dv0", "dv1", "dv2"]

        ring_i = [0]
        cur_t = [0]
        job_i = [0]

        def wload(dmas, rkeys):
            sl = ring_i[0] % NSLOT
            ring_i[0] += 1
            j = job_i[0]
            job_i[0] += 1
            for di, (vf, src) in enumerate(dmas):
                if cur_t[0] == 0:
                    pg.op("pool", lambda e, vf=vf, src=src, sl=sl: e.dma_start(out=vf(ring[sl]), in_=src),
                          reads=[], writes=[("ring", sl), "poolq"], dma_sem=s_ring[sl])
                    pg.op("sp", lambda e, vf=vf, sl=sl, j=j: e.dma_start(out=vf(scr[j]), in_=vf(ring[sl])),
                          reads=[("ring", sl)], writes=[("scr", j, di)], dma_sem=s_wb[sl])
                else:
                    pg.op("sp", lambda e, vf=vf, sl=sl, j=j: e.dma_start(out=vf(ring[sl]), in_=vf(scr[j])),
                          reads=[("scr", j, di)], writes=[("ring", sl)], dma_sem=s_rl[sl])
            return sl

        def w_in_job(l, g, ncols=768):
            c0 = g * 768
            src = w_in[l, :, c0:c0 + ncols].rearrange("(k p) c -> p k c", p=128)
            return wload([(lambda r, n=ncols: r[:, 0:8 * n].rearrange("p (k c) -> p k c", k=8), src)], [("wb_in", l, g)])

        def mm(bank, out_ap, lhsT, rhs, start, stop, reads, extra_w=()):
            pg.op("pe", lambda e: e.matmul(out_ap, lhsT, rhs, start=start, stop=stop),
                  reads=reads, writes=[("ps", bank)] + list(extra_w))

        def proj_fm(l, slot, ncols, c0, m, bank, mp=128):
            rv = ring[slot][:, 0:8 * ncols].rearrange("p (k c) -> p k c", k=8)
            for kc in range(8):
                mm(bank, banks[bank][0:m, :], rv[:, kc, c0:c0 + m], yT[:, kc, :], kc == 0, kc == 7,
                   [("ring", slot), "yT"])

        def proj_tm(slot, ncols, c0, n, blk, bank):
            rv = ring[slot][:, 0:8 * ncols].rearrange("p (k c) -> p k c", k=8)
            for kc in range(8):
                mm(bank, banks[bank][:, 0:n], yT[:, kc, blk * 128:(blk + 1) * 128], rv[:, kc, c0:c0 + n], kc == 0, kc == 7,
                   [("ring", slot), "yT"])

        def rms_stage(gcol, out_fn, out_keys_fn, out_dt_bf=True):
            bank = pspool.get()
            for c in range(8):
                b = bpool.get()
                pg.op("act", lambda e, c=c, b=b: e.activation(out=bunits[b][:], in_=hT[:, c, :], func=AF.Square),
                      reads=[("hT", c)], writes=[("b", b)])
                mm(bank, banks[bank][:], ones, bunits[b][:], c == 0, c == 7, [("b", b), "cb"])
                bpool.put(b)
            r = fpool.get()
            pg.op("act", lambda e: e.activation(out=funits[r][:], in_=banks[bank][:], func=AF.Ln, bias=epsc, scale=1.0 / D),
                  reads=[("ps", bank), "pk"], writes=[("f", r)])
            pg.op("act", lambda e: e.activation(out=funits[r][:], in_=funits[r][:], func=AF.Exp, scale=-0.5), reads=[("f", r)], writes=[("f", r)])
            pspool.put(bank)
            for c in range(8):
                pg.op("dve", lambda e, c=c: e.scalar_tensor_tensor(out=out_fn(c), in0=hT[:, c, :], scalar=gcol[:, c:c + 1],
                                                                  in1=funits[r][:], op0=ALU.mult, op1=ALU.mult),
                      reads=[("hT", c), ("f", r), "pk"], writes=out_keys_fn(c))
            fpool.put(r)

        gam = []
        for h in range(6):
            lg = np.log1p(-np.exp2(np.float32(-5.0 - h))).astype(np.float32)
            gam.append(float(np.exp(np.float32(128.0) * lg)))

        def do_tile(t):
            cur_t[0] = t
            job_i[0] = 0
            tok = slice(t * TT, (t + 1) * TT)
            pg.op("pool", lambda e, tok=tok: e.dma_start(out=hT[:], in_=xT.rearrange("(c p) t -> p c t", p=128)[:, :, tok]),
                  reads=[], writes=[("hT", c) for c in range(8)] + ["poolq"], dma_sem=s_x)
            pg.op("pool", lambda e, tok=tok: e.dma_start(out=posi[:], in_=pos[0:1, tok].broadcast_to([128, TT])),
                  reads=[], writes=["posi", "poolq"], dma_sem=s_pos)
            ang = fpool.get()
            tmp = fpool.get()
            kf = fpool.get()
            pg.op("dve", lambda e: e.tensor_copy(out=funits[ang][:], in_=posi[:]), reads=["posi"], writes=[("f", ang)])
            pg.op("dve", lambda e: e.tensor_scalar(out=funits[ang][:], in0=funits[ang][:], scalar1=pk[:, 180:181], scalar2=None, op0=ALU.mult),
                  reads=[("f", ang), "pk"], writes=[("f", ang)])
            C1 = 6.28125
            C2 = float(2.0 * np.pi - 6.28125)
            for which, off in ((1, 0.0), (0, 0.25)):
                pg.op("dve", lambda e, off=off: e.tensor_scalar(out=funits[tmp][:], in0=funits[ang][:], scalar1=float(1.0 / (2 * np.pi)), scalar2=off,
                                                               op0=ALU.mult, op1=ALU.add), reads=[("f", ang)], writes=[("f", tmp)])
                pg.op("dve", lambda e: e.tensor_copy(out=ki[:], in_=funits[tmp][:]), reads=[("f", tmp)], writes=["ki"])
                pg.op("dve", lambda e: e.tensor_copy(out=funits[kf][:], in_=ki[:]), reads=["ki"], writes=[("f", kf)])
                pg.op("dve", lambda e: e.scalar_tensor_tensor(out=funits[tmp][:], in0=funits[kf][:], scalar=-C1, in1=funits[ang][:], op0=ALU.mult, op1=ALU.add),
                      reads=[("f", kf), ("f", ang)], writes=[("f", tmp)])
                pg.op("dve", lambda e: e.scalar_tensor_tensor(out=funits[tmp][:], in0=funits[kf][:], scalar=-C2, in1=funits[tmp][:], op0=ALU.mult, op1=ALU.add),
                      reads=[("f", kf), ("f", tmp)], writes=[("f", tmp)])
                if off != 0.0:
                    pg.op("dve", lambda e: e.tensor_scalar(out=funits[tmp][:], in0=funits[tmp][:], scalar1=float(np.pi / 2), scalar2=None, op0=ALU.add),
                          reads=[("f", tmp)], writes=[("f", tmp)])
                pg.op("dve", lambda e: e.tensor_scalar(out=funits[tmp][:], in0=funits[tmp][:], scalar1=-3.1415925, scalar2=3.1415925, op0=ALU.max, op1=ALU.min),
                      reads=[("f", tmp)], writes=[("f", tmp)])
                pg.op("act", lambda e, which=which: e.activation(out=ropeT[:, which, :], in_=funits[tmp][:], func=AF.Sin),
                      reads=[("f", tmp)], writes=[("rope", which)])
            RS = float(128.0 ** -0.5)
            pg.op("dve", lambda e: e.tensor_scalar(out=ropeT[:, 2, :], in0=ropeT[:, 0, :], scalar1=RS, scalar2=None, op0=ALU.mult),
                  reads=[("rope", 0)], writes=[("rope", 2)])
            pg.op("dve", lambda e: e.tensor_scalar(out=ropeT[:, 3, :], in0=ropeT[:, 1, :], scalar1=pk[:, 182:183], scalar2=None, op0=ALU.mult),
                  reads=[("rope", 1), "pk"], writes=[("rope", 3)])
            pg.op("dve", lambda e: e.tensor_scalar(out=ropeT[:, 1, :], in0=ropeT[:, 1, :], scalar1=pk[:, 181:182], scalar2=None, op0=ALU.mult),
                  reads=[("rope", 1), ("rope", 3), "pk"], writes=[("rope", 1)])
            for u_ in (ang, tmp, kf):
                fpool.put(u_)

            def do_layer(l):
                if KSTAGE < 2:
                    return
                rms_stage(pk[:, l * 8:(l + 1) * 8], lambda c: yT[:, c, :], lambda c: ["yT"])
                if DEBUG and t == 0 and l == 0:
                    dbg("yT", yT[:], [128, 8, TT], BF16, ["yT"])

                if KSTAGE < 3:
                    return
                sx = w_in_job(l, 0)
                sg = w_in_job(l, 1)
                pend = []

                def lru_A(n):
                    bx = pspool.get()
                    proj_fm(l, sx, 768, n * 96, 96, bx)
                    xl = xls[n % 2]
                    xk = ("xl", n % 2)
                    hi = (l * 8 + n) * 4
                    pg.op("pool", lambda e: e.tensor_copy(out=xl[0:96, 0:3], in_=hist[0:96, hi:hi + 3]),
                          reads=[("hist", l * 8 + n)], writes=[xk])
                    pg.op("act", lambda e: e.activation(out=xl[0:96, 3:TT + 3], in_=banks[bx][0:96, :], func=AF.Copy),
                          reads=[("ps", bx)], writes=[xk])
                    pspool.put(bx)
                    bg = pspool.get()
                    proj_fm(l, sg, 768, n * 96, 96, bg)
                    return bg

                def lru_B1(n, bg):
                    xl = xls[n % 2]
                    xk = ("xl", n % 2)
                    col = l * 8 + n
                    cw = lambda k: pk[0:96, 52 + col * 4 + k:52 + col * 4 + k + 1]
                    xc = fpool.get()
                    XC = funits[xc][0:96, :]
                    pg.op("dve", lambda e: e.tensor_scalar(out=XC, in0=xl[0:96, 0:TT], scalar1=cw(0), scalar2=pk[0:96, 116 + col:117 + col],
                                                           op0=ALU.mult, op1=ALU.add), reads=[xk, "pk"], writes=[("f", xc)])
                    for k in (1, 2, 3):
                        pg.op("dve", lambda e, k=k: e.scalar_tensor_tensor(out=XC, in0=xl[0:96, k:k + TT], scalar=cw(k), in1=XC,
                                                                          op0=ALU.mult, op1=ALU.add), reads=[xk, "pk", ("f", xc)], writes=[("f", xc)])
                    hi = col * 4
                    pg.op("pool", lambda e: e.tensor_copy(out=hist[0:96, hi:hi + 3], in_=xl[0:96, TT:TT + 3]),
                          reads=[xk], writes=[("hist", col)])
                    xb = bpool.get()
                    pg.op("pool", lambda e: e.tensor_copy(out=bunits[xb][0:96, :], in_=XC), reads=[("f", xc)], writes=[("b", xb)])
                    ba = pspool.get()
                    bxx = pspool.get()
                    wo = col * 96
                    mm(ba, banks[ba][0:96, :], wab[0:96, wo:wo + 96], bunits[xb][0:96, :], True, True, [("b", xb), "wab"])
                    mm(bxx, banks[bxx][0:96, :], wxb[0:96, wo:wo + 96], bunits[xb][0:96, :], True, True, [("b", xb), "wxb"])
                    bpool.put(xb)
                    tha = fpool.get()
                    thx = fpool.get()
                    thg = fpool.get()
                    a2 = fpool.get()
                    A, X, G, A2 = (funits[i][0:96, :] for i in (tha, thx, thg, a2))
                    pg.op("act", lambda e: e.activation(out=A, in_=banks[ba][0:96, :], func=AF.Tanh, bias=dv[0:96, col:col + 1], scale=0.5),
                          reads=[("ps", ba)] + DVK, writes=[("f", tha)])
                    pg.op("act", lambda e: e.activation(out=X, in_=banks[bxx][0:96, :], func=AF.Tanh, bias=dv[0:96, 16 + col:17 + col], scale=0.5),
                          reads=[("ps", bxx)] + DVK, writes=[("f", thx)])
                    pspool.put(ba)
                    pspool.put(bxx)
                    pg.op("act", lambda e: e.activation(out=G, in_=banks[bg][0:96, :], func=AF.Tanh, scale=0.5),
                          reads=[("ps", bg)], writes=[("f", thg)])
                    pg.op("act", lambda e: e.activation(out=A2, in_=A, func=AF.Exp, bias=dv[0:96, 32 + col:33 + col], scale=dv[0:96, 32 + col:33 + col]),
                          reads=[("f", tha)] + DVK, writes=[("f", a2)])
                    pg.op("act", lambda e: e.activation(out=A, in_=A, func=AF.Exp, bias=dv[0:96, 48 + col:49 + col], scale=dv[0:96, 48 + col:49 + col]),
                          reads=[("f", tha)] + DVK, writes=[("f", tha)])
                    pg.op("act", lambda e: e.activation(out=A2, in_=A2, func=AF.Ln, bias=onec[0:96, :], scale=-1.0),
                          reads=[("f", a2), "pk"], writes=[("f", a2)])
                    pg.op("act", lambda e: e.activation(out=A2, in_=A2, func=AF.Exp, scale=0.5),
                          reads=[("f", a2)], writes=[("f", a2)])
                    return (bg, xc, tha, thx, thg, a2)

                def lru_B2(n, st_):
                    bg, xc, tha, thx, thg, a2 = st_
                    col = l * 8 + n
                    XC = funits[xc][0:96, :]
                    A, X, G, A2 = (funits[i][0:96, :] for i in (tha, thx, thg, a2))
                    pg.op("dve", lambda e: e.scalar_tensor_tensor(out=X, in0=X, scalar=1.0, in1=XC, op0=ALU.add, op1=ALU.mult),
                          reads=[("f", thx), ("f", xc)], writes=[("f", thx)])
                    pg.op("dve", lambda e: e.scalar_tensor_tensor(out=X, in0=X, scalar=0.5, in1=A2, op0=ALU.mult, op1=ALU.mult),
                          reads=[("f", thx), ("f", a2)], writes=[("f", thx)])
                    pg.op("dve", lambda e: e.tensor_tensor_scan(out=XC, data0=A, data1=X, initial=hst[0:96, col:col + 1], op0=ALU.mult, op1=ALU.add),
                          reads=[("f", tha), ("f", thx), "hst"], writes=[("f", xc)])
                    pg.op("pool", lambda e: e.tensor_copy(out=hst[0:96, col:col + 1], in_=funits[xc][0:96, TT - 1:TT]),
                          reads=[("f", xc)], writes=["hst"])
                    pg.op("dve", lambda e: e.scalar_tensor_tensor(out=G, in0=G, scalar=1.0, in1=banks[bg][0:96, :], op0=ALU.add, op1=ALU.mult),
                          reads=[("f", thg), ("ps", bg)], writes=[("f", thg)])
                    pspool.put(bg)
                    pg.op("dve", lambda e: e.scalar_tensor_tensor(out=mixL[0:96, n, :], in0=G, scalar=0.5, in1=XC, op0=ALU.mult, op1=ALU.mult),
                          reads=[("f", thg), ("f", xc)], writes=[("mixL", n)])
                    if DEBUG and t == 0 and l == 0 and n == 0:
                        dbg("lru_h0", funits[xc][:], [128, TT], F32, [("f", xc)])
                    for u_ in (xc, tha, thx, thg, a2):
                        fpool.put(u_)

                bgs = {}
                sts = {}
                bgs[0] = lru_A(0)
                bgs[1] = lru_A(1)
                sts[0] = lru_B1(0, bgs[0])
                for n in range(8):
                    if n + 2 < 8:
                        bgs[n + 2] = lru_A(n + 2)
                    if n + 1 < 8:
                        sts[n + 1] = lru_B1(n + 1, bgs[n + 1])
                    lru_B2(n, sts[n])

                if KSTAGE < 4:
                    return
                sq_ = w_in_job(l, 2)
                sk_ = w_in_job(l, 3)
                sv_ = w_in_job(l, 4)
                sgr = w_in_job(l, 5)
                for hg in range(2):
                    items = []
                    for hh in range(3):
                        h = hg * 3 + hh

                        def rope_item(slot, cidx, sidx, dst, dkey, hh=hh, h=h):
                            bq = pspool.get()
                            proj_fm(l, slot, 768, h * 128, 128, bq)
                            qb = bpool.get()
                            pg.op("act", lambda e, bq=bq, qb=qb: e.activation(out=bunits[qb][:], in_=banks[bq][:], func=AF.Copy),
                                  reads=[("ps", bq)], writes=[("b", qb), ("psr", bq)])
                            bs_ = pspool.get()
                            mm(bs_, banks[bs_][:], perm, bunits[qb][:], True, True, [("b", qb), "cb"])
                            bpool.put(qb)
                            yield
                            if KQ < 1:
                                pspool.put(bq)
                                pspool.put(bs_)
                                return
                            t1 = fpool.get()
                            t2 = fpool.get()
                            pg.op("dve", lambda e, bq=bq, t1=t1, cidx=cidx: e.scalar_tensor_tensor(out=funits[t1][:], in0=ropeT[:, cidx, :], scalar=1.0, in1=banks[bq][:], op0=ALU.mult, op1=ALU.mult),
                                  reads=[("ps", bq), ("psr", bq), ("rope", cidx)], writes=[("f", t1)])
                            pg.op("dve", lambda e, bs_=bs_, t2=t2, sidx=sidx: e.scalar_tensor_tensor(out=funits[t2][:], in0=ropeT[:, sidx, :], scalar=1.0, in1=banks[bs_][:], op0=ALU.mult, op1=ALU.mult),
                                  reads=[("ps", bs_), ("rope", sidx)], writes=[("f", t2)])
                            pspool.put(bq)
                            pspool.put(bs_)
                            if KQ < 2:
                                fpool.put(t1)
                                fpool.put(t2)
                                return
                            lo_t = qlo if dkey == "qt" else klo
                            lkey = "qlo" if dkey == "qt" else "klo"
                            pg.op("dve", lambda e, t1=t1, t2=t2: e.tensor_tensor(out=funits[t1][:], in0=funits[t1][:], in1=funits[t2][:], op=ALU.add),
                                  reads=[("f", t1), ("f", t2)], writes=[("f", t1)])
                            pg.op("act", lambda e, t1=t1, dst=dst, hh=hh: e.activation(out=dst[:, hh, :], in_=funits[t1][:], func=AF.Copy),
                                  reads=[("f", t1)], writes=[(dkey, hh)])
                            pg.op("pool", lambda e, t2=t2, dst=dst, hh=hh: e.tensor_copy(out=funits[t2][:], in_=dst[:, hh, :]),
                                  reads=[(dkey, hh), ("f", t2)], writes=[("f", t2)])
                            pg.op("pool", lambda e, t1=t1, t2=t2, lo_t=lo_t, hh=hh: e.tensor_tensor(
                                out=lo_t[:, hh, :], in0=funits[t1][:], in1=funits[t2][:], op=ALU.subtract),
                                reads=[("f", t1), ("f", t2)], writes=[(lkey, hh)])
                            if dkey == "qt":
                                pg.op("pool", lambda e, t1=t1, hh=hh, h=h: e.tensor_tensor(
                                    out=qg[:, hh, :].rearrange("p (n i) -> p n i", n=4),
                                    in0=funits[t1][:].rearrange("p (n i) -> p n i", n=4),
                                    in1=qwc[:, h * 128:(h + 1) * 128].unsqueeze(1).broadcast_to([128, 4, 128]), op=ALU.mult),
                                    reads=[("f", t1), "cst"], writes=[("qg", hh)])
                            fpool.put(t1)
                            fpool.put(t2)

                        for args_ in ((sq_, 0, 1, qt, "qt"), (sk_, 2, 3, kt, "kt")):
                            items.append(rope_item(*args_))
                    next(items[0])
                    for i_ in range(6):
                        if i_ + 1 < 6:
                            next(items[i_ + 1])
                        for _ in items[i_]:
                            pass
                    if DEBUG and t == 0 and l == 0 and hg == 0:
                        dbg("qt", qt[:], [128, 3, TT], BF16, [("qt", i) for i in range(3)])
                        dbg("kt", kt[:], [128, 3, TT], BF16, [("kt", i) for i in range(3)])
                    if KRET < 2:
                        continue
                    for blk in range(4):
                        bv = pspool.get()
                        proj_tm(sv_, 768, hg * 384, 384, blk, bv)
                        pg.op("act", lambda e, bv=bv, blk=blk: e.activation(out=vbf[:, blk, :], in_=banks[bv][:, 0:384], func=AF.Copy),
                              reads=[("ps", bv)], writes=[("vbf", blk)])
                        pg.op("dve", lambda e, bv=bv, blk=blk, hg=hg: e.scalar_tensor_tensor(out=vw[:, blk, :], in0=kwc[:, hg * 384:(hg + 1) * 384], scalar=1.0, in1=banks[bv][:, 0:384], op0=ALU.mult, op1=ALU.mult),
                              reads=[("ps", bv), "cst"], writes=[("vw", blk)])
                        pspool.put(bv)
                    if KRET < 3:
                        continue
                    for blk in range(4):
                        bt = pspool.get()
                        btb = banks[bt][:].bitcast(BF16)
                        for hh in range(3):
                            pg.op("pe", lambda e, btb=btb, hh=hh, blk=blk: e.transpose(btb[:, hh * 128:(hh + 1) * 128], kt[:, hh, blk * 128:(blk + 1) * 128], ident),
                                  reads=[("kt", hh), "cb"], writes=[("ps", bt)])
                        pg.op("act", lambda e, btb=btb, blk=blk: e.activation(out=ktm[:, blk, :], in_=btb[:, 0:384], func=AF.Copy),
                              reads=[("ps", bt)], writes=[("ktm", blk)])
                        pspool.put(bt)
                    if KRET < 4:
                        continue
                    bo = [pspool.get() for _ in range(3)]
                    for n in range(4):
                        cs = slice(n * 128, (n + 1) * 128)
                        bsc = pspool.get()
                        for hh in range(3):
                            osc = banks[bsc][:, hh * 128:(hh + 1) * 128]
                            mm(bsc, osc, kt[:, hh, cs], qt[:, hh, cs], True, False, [("kt", hh), ("qt", hh)])
                            mm(bsc, osc, kt[:, hh, cs], qlo[:, hh, cs], False, False, [("kt", hh), ("qlo", hh)])
                            mm(bsc, osc, klo[:, hh, cs], qt[:, hh, cs], False, True, [("klo", hh), ("qt", hh)])
                        Pb = bpool.get()
                        pg.op("dve", lambda e, bsc=bsc, Pb=Pb, hg=hg: e.scalar_tensor_tensor(out=bunits[Pb][:, 0:384], in0=decayT[:, hg * 384:(hg + 1) * 384], scalar=1.0, in1=banks[bsc][:, 0:384], op0=ALU.mult, op1=ALU.mult),
                              reads=[("ps", bsc), "cst"], writes=[("b", Pb)])
                        pspool.put(bsc)
                        for hh in range(3):
                            h = hg * 3 + hh
                            mm(bo[hh], banks[bo[hh]][:, cs], vbf[:, n, hh * 128:(hh + 1) * 128], bunits[Pb][:, hh * 128:(hh + 1) * 128], True, False,
                               [("vbf", n), ("b", Pb)])
                            mm(bo[hh], banks[bo[hh]][:, cs], Sb[:, l, h * 128:(h + 1) * 128], qg[:, hh, cs], False, True, [("Sb", l, hg), ("qg", hh)])
                        bpool.put(Pb)
                        bkv = pspool.get()
                        for hh in range(3):
                            mm(bkv, banks[bkv][:, hh * 128:(hh + 1) * 128], ktm[:, n, hh * 128:(hh + 1) * 128], vw[:, n, hh * 128:(hh + 1) * 128], True, True,
                               [("ktm", n), ("vw", n)])
                        for hh in range(3):
                            h = hg * 3 + hh
                            pg.op("dve", lambda e, h=h, hh=hh, bkv=bkv: e.scalar_tensor_tensor(
                                out=Sst[:, l, h * 128:(h + 1) * 128], in0=Sst[:, l, h * 128:(h + 1) * 128], scalar=gam[h],
                                in1=banks[bkv][:, hh * 128:(hh + 1) * 128], op0=ALU.mult, op1=ALU.add),
                                reads=[("ps", bkv), ("Sst", l, hg)], writes=[("Sst", l, hg)])
                        pspool.put(bkv)
                        pg.op("act", lambda e, hg=hg: e.activation(out=Sb[:, l, hg * 384:(hg + 1) * 384], in_=Sst[:, l, hg * 384:(hg + 1) * 384], func=AF.Copy),
                              reads=[("Sst", l, hg)], writes=[("Sb", l, hg)])
                    if KRET < 5:
                        for b_ in bo:
                            pspool.put(b_)
                        continue
                    for hh in range(3):
                        h = hg * 3 + hh
                        bg = pspool.get()
                        proj_fm(l, sgr, 768, h * 128, 128, bg)
                        b = bpool.get()
                        pg.op("act", lambda e, b=b, boa=banks[bo[hh]][:]: e.activation(out=bunits[b][:], in_=boa, func=AF.Square),
                              reads=[("ps", bo[hh])], writes=[("b", b)])
                        bm = pspool.get()
                        mm(bm, banks[bm][:], ones, bunits[b][:], True, True, [("b", b), "cb"])
                        bpool.put(b)
                        r = fpool.get()
                        pg.op("act", lambda e, r=r, bm=bm: e.activation(out=funits[r][:], in_=banks[bm][:], func=AF.Ln, bias=epsc, scale=1.0 / 128),
                              reads=[("ps", bm), "pk"], writes=[("f", r)])
                        pspool.put(bm)
                        pg.op("act", lambda e, r=r: e.activation(out=funits[r][:], in_=funits[r][:], func=AF.Exp, scale=-0.5), reads=[("f", r)], writes=[("f", r)])
                        on = fpool.get()
                        pg.op("dve", lambda e, r=r, on=on, boa=banks[bo[hh]][:], h=h: e.scalar_tensor_tensor(out=funits[on][:], in0=funits[r][:], scalar=pk[:, 40 + l * 6 + h:41 + l * 6 + h],
                                                                                        in1=boa, op0=ALU.mult, op1=ALU.mult),
                              reads=[("ps", bo[hh]), ("f", r), "pk"], writes=[("f", on)])
                        fpool.put(r)
                        tg = fpool.get()
                        pg.op("act", lambda e, tg=tg, bg=bg: e.activation(out=funits[tg][:], in_=banks[bg][:], func=AF.Tanh, scale=0.5),
                              reads=[("ps", bg)], writes=[("f", tg)])
                        pg.op("dve", lambda e, tg=tg, bg=bg: e.scalar_tensor_tensor(out=funits[tg][:], in0=funits[tg][:], scalar=1.0, in1=banks[bg][:], op0=ALU.add, op1=ALU.mult),
                              reads=[("f", tg), ("ps", bg)], writes=[("f", tg)])
                        pspool.put(bg)
                        pg.op("dve", lambda e, tg=tg, on=on, h=h: e.scalar_tensor_tensor(out=mixR[:, h, :], in0=funits[tg][:], scalar=0.5, in1=funits[on][:], op0=ALU.mult, op1=ALU.mult),
                              reads=[("f", tg), ("f", on)], writes=[("mixR", h)])
                        if DEBUG and t == 0 and l == 0 and h == 0:
                            dbg("ret_on0", funits[on][:], [128, TT], F32, [("f", on)])
                        fpool.put(tg)
                        fpool.put(on)
                    for b_ in bo:
                        pspool.put(b_)

                if KSTAGE < 5:
                    return
                su = w_in_job(l, 6, 512)
                def w_cols_job(c0, ncols):
                    src = w_in[l, :, c0:c0 + ncols].rearrange("(k p) c -> p k c", p=128)
                    return wload([(lambda r, n=ncols: r[:, 0:8 * n].rearrange("p (k c) -> p k c", k=8), src)],
                                 [("wb_in", l, c0 // 768), ("wb_in", l, (c0 + ncols - 1) // 768)])
                svs = w_cols_job(5120, 512)
                sgs = w_cols_job(5632, 512)
                for blk in range(4):
                    bv = pspool.get()
                    proj_tm(svs, 512, 0, 512, blk, bv)
                    so = blk * 12
                    tq = fpool.get()
                    pg.op("act", lambda e, bv=bv, tq=tq: e.activation(out=funits[tq][:], in_=banks[bv][:], func=AF.Copy),
                          reads=[("ps", bv)], writes=[("f", tq)])
                    pspool.put(bv)
                    pg.op("dve", lambda e, tq=tq, so=so: e.bn_stats(out=small[:, so:so + 6], in_=funits[tq][:]), reads=[("f", tq)], writes=[("small", blk)])
                    pg.op("dve", lambda e, so=so: e.bn_aggr(out=small[:, so + 6:so + 8], in_=small[:, so:so + 6]), reads=[("small", blk)], writes=[("small", blk)])
                    pg.op("act", lambda e, so=so: e.activation(out=small[:, so + 8:so + 9], in_=small[:, so + 7:so + 8], func=AF.Ln, bias=epsc, scale=1.0),
                          reads=[("small", blk), "pk"], writes=[("small", blk)])
                    pg.op("act", lambda e, so=so: e.activation(out=small[:, so + 8:so + 9], in_=small[:, so + 8:so + 9], func=AF.Exp, scale=-0.5), reads=[("small", blk)], writes=[("small", blk)])
                    pg.op("dve", lambda e, so=so, tq=tq: e.scalar_tensor_tensor(out=funits[tq][:], in0=funits[tq][:], scalar=small[:, so + 6:so + 7], in1=lng[:, l, :],
                                                                                op0=ALU.subtract, op1=ALU.mult),
                          reads=[("f", tq), ("small", blk), ("lng", l)], writes=[("f", tq)])
                    pg.op("dve", lambda e, so=so, tq=tq, blk=blk: e.scalar_tensor_tensor(out=vn[:, blk, :], in0=funits[tq][:], scalar=small[:, so + 8:so + 9], in1=lnb[:, l, :],
                                                                                  op0=ALU.mult, op1=ALU.add),
                          reads=[("f", tq), ("small", blk), ("lnb", l)], writes=[("vn", blk)])
                    fpool.put(tq)
                for g in range(4):
                    gs = slice(g * 128, (g + 1) * 128)
                    bmix = pspool.get()
                    for n in range(4):
                        cs = slice(n * 128, (n + 1) * 128)
                        mm(bmix, banks[bmix][:, cs], vn[:, n, gs], wsT[:, l, gs], True, False, [("vn", n), ("wsT", l)])
                        mm(bmix, banks[bmix][:, cs], ones[0:1, :], bsb[0:1, l * 512 + g * 128:l * 512 + (g + 1) * 128], False, True, ["cb", "bsb"])
                    bu = pspool.get()
                    proj_fm(l, su, 512, g * 128, 128, bu)
                    bgg = pspool.get()
                    proj_fm(l, sgs, 512, g * 128, 128, bgg)
                    m_ = fpool.get()
                    pg.op("act", lambda e, m_=m_, bmix=bmix: e.activation(out=funits[m_][:], in_=banks[bmix][:], func=AF.Copy),
                          reads=[("ps", bmix)], writes=[("f", m_)])
                    pspool.put(bmix)
                    pg.op("dve", lambda e, m_=m_, bu=bu: e.scalar_tensor_tensor(out=funits[m_][:], in0=funits[m_][:], scalar=1.0, in1=banks[bu][:], op0=ALU.mult, op1=ALU.mult),
                          reads=[("f", m_), ("ps", bu)], writes=[("f", m_)])
                    pspool.put(bu)
                    tg = fpool.get()
                    pg.op("act", lambda e, tg=tg, bgg=bgg: e.activation(out=funits[tg][:], in_=banks[bgg][:], func=AF.Tanh, scale=0.5),
                          reads=[("ps", bgg)], writes=[("f", tg)])
                    pg.op("dve", lambda e, tg=tg, bgg=bgg: e.scalar_tensor_tensor(out=funits[tg][:], in0=funits[tg][:], scalar=1.0, in1=banks[bgg][:], op0=ALU.add, op1=ALU.mult),
                          reads=[("f", tg), ("ps", bgg)], writes=[("f", tg)])
                    pspool.put(bgg)
                    pg.op("dve", lambda e, tg=tg, m_=m_, g=g: e.scalar_tensor_tensor(out=mixS[:, g, :], in0=funits[tg][:], scalar=0.5, in1=funits[m_][:], op0=ALU.mult, op1=ALU.mult),
                          reads=[("f", tg), ("f", m_)], writes=[("mixS", g)])
                    fpool.put(tg)
                    fpool.put(m_)
                if DEBUG and t == 0 and l == 0:
                    dbg("mixL", mixL[:], [128, 8, TT], BF16, [("mixL", i) for i in range(8)])
                    dbg("mixR", mixR[:], [128, 6, TT], BF16, [("mixR", i) for i in range(6)])
                    dbg("mixS", mixS[:], [128, 4, TT], BF16, [("mixS", i) for i in range(4)])

                if KSTAGE < 6:
                    return
                for piece in range(4):
                    c0 = piece * 256
                    dm = [
                        (lambda r: r[0:96, 0:2048].rearrange("p (n c) -> p n c", n=8),
                         w_out[l, 0:768, c0:c0 + 256].rearrange("(n p) c -> p n c", p=96)),
                        (lambda r: r[:, 2048:3584].rearrange("p (n c) -> p n c", n=6),
                         w_out[l, 768:1536, c0:c0 + 256].rearrange("(n p) c -> p n c", p=128)),
                        (lambda r: r[:, 3584:4608].rearrange("p (n c) -> p n c", n=4),
                         w_out[l, 1536:2048, c0:c0 + 256].rearrange("(n p) c -> p n c", p=128)),
                    ]
                    so_ = wload(dm, [("wb_out", l)])
                    R = ring[so_]
                    for d2 in range(2):
                        c = piece * 2 + d2
                        bw = pspool.get()
                        steps = []
                        for n in range(8):
                            steps.append((R[0:96, n * 256 + d2 * 128:n * 256 + d2 * 128 + 128], mixL[0:96, n, :], ("mixL", n)))
                        for h in range(6):
                            steps.append((R[:, 2048 + h * 256 + d2 * 128:2048 + h * 256 + d2 * 128 + 128], mixR[:, h, :], ("mixR", h)))
                        for g in range(4):
                            steps.append((R[:, 3584 + g * 256 + d2 * 128:3584 + g * 256 + d2 * 128 + 128], mixS[:, g, :], ("mixS", g)))
                        for i, (lh, rh, key) in enumerate(steps):
                            mm(bw, banks[bw][:], lh, rh, i == 0, i == len(steps) - 1, [("ring", so_), key])
                        pg.op("dve", lambda e, c=c, bw=bw: e.scalar_tensor_tensor(out=hT[:, c, :], in0=hT[:, c, :], scalar=1.0, in1=banks[bw][:], op0=ALU.mult, op1=ALU.add),
                              reads=[("hT", c), ("ps", bw)], writes=[("hT", c)])
                        pspool.put(bw)

                if KSTAGE < 7:
                    return
                rms_stage(pk[:, 16 + l * 8:16 + (l + 1) * 8], lambda c: yT[:, c, :], lambda c: ["yT"])
                pg.op("pool", lambda e, tok=tok: e.dma_start(out=pTt[:], in_=pT[l].rearrange("(k p) t -> p k t", p=128)[:, :, tok]),
                      reads=[], writes=["pTt", "poolq"], dma_sem=s_p)
                spp = wload([(lambda r: r[:, 0:2048].rearrange("p (k c) -> p k c", k=2), w_pp[l].rearrange("(k p) c -> p k c", p=128))], [("wb_pp", l)])
                for half in range(2):
                    sgt = wload([(lambda r: r[:, 0:4096].rearrange("p (k c) -> p k c", k=8),
                                  w_gate[l, :, half * 512:(half + 1) * 512].rearrange("(k p) c -> p k c", p=128))], [("wb_gate", l)])
                    for cc in range(4):
                        c = half * 4 + cc
                        bgt = pspool.get()
                        proj_fm(l, sgt, 512, cc * 128, 128, bgt)
                        bpp = pspool.get()
                        rp = ring[spp][:, 0:2048].rearrange("p (k c) -> p k c", k=2)
                        for k in range(2):
                            mm(bpp, banks[bpp][:], rp[:, k, c * 128:(c + 1) * 128], pTt[:, k, :], k == 0, k == 1, [("ring", spp), "pTt"])
                        th = fpool.get()
                        pg.op("act", lambda e, th=th, bgt=bgt: e.activation(out=funits[th][:], in_=banks[bgt][:], func=AF.Tanh, scale=0.5),
                              reads=[("ps", bgt)], writes=[("f", th)])
                        pspool.put(bgt)
                        pg.op("dve", lambda e, th=th, bpp=bpp: e.scalar_tensor_tensor(out=funits[th][:], in0=funits[th][:], scalar=1.0, in1=banks[bpp][:], op0=ALU.add, op1=ALU.mult),
                              reads=[("f", th), ("ps", bpp)], writes=[("f", th)])
                        pspool.put(bpp)
                        pg.op("dve", lambda e, th=th, c=c: e.scalar_tensor_tensor(out=hT[:, c, :], in0=funits[th][:], scalar=0.5, in1=hT[:, c, :], op0=ALU.mult, op1=ALU.add),
                              reads=[("f", th), ("hT", c)], writes=[("hT", c)])
                        fpool.put(th)
                if DEBUG and t == 0:
                    dbg(f"h_l{l}", hT[:], [128, 8, TT], F32, [("hT", c) for c in range(8)])

            for l_ in range(DEPTH):
                do_layer(l_)

            outs = {}

            def fin_out(c):
                u = fpool.get()
                outs[c] = u
                return funits[u][:]

            def fin_keys(c):
                return [("f", outs[c])]

            bank = pspool.get()
            for c in range(8):
                b = bpool.get()
                pg.op("act", lambda e, c=c, b=b: e.activation(out=bunits[b][:], in_=hT[:, c, :], func=AF.Square), reads=[("hT", c)], writes=[("b", b)])
                mm(bank, banks[bank][:], ones, bunits[b][:], c == 0, c == 7, [("b", b), "cb"])
                bpool.put(b)
            r = fpool.get()
            pg.op("act", lambda e, r=r, bank=bank: e.activation(out=funits[r][:], in_=banks[bank][:], func=AF.Ln, bias=epsc, scale=1.0 / D),
                  reads=[("ps", bank), "pk"], writes=[("f", r)])
            pg.op("act", lambda e, r=r: e.activation(out=funits[r][:], in_=funits[r][:], func=AF.Exp, scale=-0.5), reads=[("f", r)], writes=[("f", r)])
            pspool.put(bank)
            for c in range(8):
                u = fpool.get()
                pg.op("dve", lambda e, c=c, u=u, r=r: e.scalar_tensor_tensor(out=funits[u][:], in0=hT[:, c, :], scalar=pk[:, 32 + c:33 + c], in1=funits[r][:],
                                                                          op0=ALU.mult, op1=ALU.mult),
                      reads=[("hT", c), ("f", r), "pk"], writes=[("f", u)])
                pg.op("pool", lambda e, c=c, u=u, tok=tok: e.dma_start(out=outT[c * 128:(c + 1) * 128, tok], in_=funits[u][:]),
                      reads=[("f", u)], writes=[("outsem", c), "poolq"], dma_sem=s_out[c])
                fpool.put(u)
            fpool.put(r)

        for t_ in range(NT_RUN):
            do_tile(t_)

        def final_waits():
            res = []
            for s_ in s_out:
                if s_ in pg.dma_counts:
                    res.append((s_, pg.dma_counts[s_]))
            if s_dbg in pg.dma_counts:
                res.append((s_dbg, pg.dma_counts[s_dbg]))
            return res

        with nc.Block() as block:
            pg.emit(block, eng_sems, final_waits)
    return nc, list(dbg_outs.keys())


def _consts():
    f32 = np.float32
    H, C = 6, 128
    log_g = np.log1p(-np.exp2(-5.0 - np.arange(H, dtype=f32))).astype(f32)
    idx = np.arange(C, dtype=f32)
    diff = idx[:, None] - idx[None, :]
    causal = diff >= 0
    decay = np.where(causal[None], np.exp(np.where(causal, diff, 0.0)[None] * log_g[:, None, None]), 0.0).astype(f32)
    decayT = np.ascontiguousarray(decay.transpose(2, 0, 1)).reshape(128, 768)
    k_w = np.exp((C - 1.0 - idx)[:, None] * log_g[None, :]).astype(f32)
    kwc = np.repeat(k_w[:, :, None], 128, axis=2).reshape(128, 768)
    q_w = np.exp((idx + 1.0)[:, None] * log_g[None, :]).astype(f32)
    qw = np.broadcast_to(q_w.T.reshape(1, 768), (128, 768))
    ident = np.eye(128, dtype=f32)
    perm = np.zeros((128, 128), f32)
    for m in range(128):
        perm[(m + 64) % 128, m] = 1.0
    ones = np.ones((128, 128), f32)
    pad = np.zeros((128, 384), f32)
    mask = (idx[:, None] <= idx[None, :]).astype(f32)
    cst = np.concatenate([decayT, kwc, qw, ident, perm, ones, pad, mask], axis=1).astype(f32)
    return np.ascontiguousarray(cst)


def _pack(inp):
    f32 = np.float32
    pk = np.zeros((128, NPK), f32)
    pk[:, 0:16] = inp["mix_norm"].reshape(2, 8, 128).transpose(2, 0, 1).reshape(128, 16)
    pk[:, 16:32] = inp["ple_norm"].reshape(2, 8, 128).transpose(2, 0, 1).reshape(128, 16)
    pk[:, 32:40] = inp["final_norm"].reshape(8, 128).T
    pk[:, 40:52] = inp["ret_norm"].reshape(2, 6, 128).transpose(2, 0, 1).reshape(128, 12)
    cw = inp["lru_conv_w"].reshape(2, 4, 8, 96).transpose(3, 0, 2, 1)
    pk[0:96, 52:116] = cw.reshape(96, 64)
    for name, c0 in (("lru_conv_b", 116), ("lru_b_a", 132), ("lru_b_x", 148), ("lru_lambda", 164)):
        pk[0:96, c0:c0 + 16] = inp[name].reshape(2, 8, 96).transpose(2, 0, 1).reshape(96, 16)
    half = 64
    inv = (10000.0 ** (-np.arange(half, dtype=f32) / half)).astype(f32)
    pk[:, 180] = np.concatenate([inv, inv])
    sgn = np.concatenate([-np.ones(64, f32), np.ones(64, f32)])
    pk[:, 181] = sgn
    pk[:, 182] = sgn * f32(128.0 ** -0.5)
    pk[:, 183] = EPS
    pk[:, 184] = 1.0
    return pk


def kernel(**inputs):
    inp = {k: np.asarray(v) for k, v in inputs.items()}
    f32 = np.float32
    nc, dbg_names = build_nc()
    cst = _consts()
    pk = _pack(inp)
    shared = {
        "w_in": np.ascontiguousarray(inp["w_in"], f32),
        "w_out": np.ascontiguousarray(inp["w_out"], f32),
        "w_gate": np.ascontiguousarray(inp["w_ple_gate"], f32),
        "w_pp": np.ascontiguousarray(inp["w_ple_proj"], f32),
        "w_a": np.ascontiguousarray(inp["lru_w_a"], f32),
        "w_x": np.ascontiguousarray(inp["lru_w_x"], f32),
        "w_sT": np.ascontiguousarray(inp["sgu_w_s"].transpose(0, 3, 1, 2).reshape(2, 128, 512), f32),
        "b_s": np.ascontiguousarray(inp["sgu_b_s"].reshape(1, 1024), f32),
        "ln_g": np.ascontiguousarray(inp["sgu_ln_g"], f32),
        "ln_b": np.ascontiguousarray(inp["sgu_ln_b"], f32),
        "pk": pk,
        "cst": cst,
    }
    in_maps = []
    for core in range(8):
        b = core % 4
        m = dict(shared)
        m["xT"] = np.ascontiguousarray(inp["x"][b].T, f32)
        m["pT"] = np.ascontiguousarray(inp["p"][:, b].transpose(0, 2, 1), f32)
        m["pos"] = np.ascontiguousarray(inp["positions"][b].reshape(1, S), np.int32)
        in_maps.append(m)
    res = run_bass_kernel_spmd(nc, in_maps, core_ids=list(range(8)))
    out = np.stack([np.ascontiguousarray(res.results[b]["outT"].T) for b in range(4)], axis=0).astype(f32)
    if DEBUG:
        kernel.debug = {n: res.results[0]["dbg_" + n] for n in dbg_names}
    return out
```

```python
import os
import numpy as np
import concourse.bass as bass
import concourse.mybir as mybir
from concourse.bass_utils import run_bass_kernel_spmd

F32 = mybir.dt.float32
BF16 = mybir.dt.bfloat16
I32 = mybir.dt.int32
ALU = mybir.AluOpType
AF = mybir.ActivationFunctionType

D = 1024
S = 4096
TT = 512
NT = S // TT
DEPTH = 2
D_IN = 6144
EPS = 1e-6
NPK = 192
PI = float(np.pi)

DEBUG = os.environ.get("KDEBUG", "") != ""
NT_RUN = int(os.environ.get("KNT", NT))
KSTAGE = int(os.environ.get("KSTAGE", 99))
KRET = int(os.environ.get("KRET", 99))
KQ = int(os.environ.get("KQ", 99))
KSKIPW = os.environ.get("KSKIPW", "") != ""


class Op:
    __slots__ = ("eng", "fn", "reads", "writes", "dma_sem", "dma_val", "deps", "signal", "val", "idx")


class Prog:
    ENGS = ("pe", "act", "dve", "pool", "sp")

    def __init__(self, nc):
        self.nc = nc
        self.ops = []
        self.last_w = {}
        self.readers = {}
        self.dma_counts = {}
        self.shared = set()

    def op(self, eng, fn, reads=(), writes=(), dma_sem=None):
        o = Op()
        o.eng, o.fn, o.reads, o.writes = eng, fn, tuple(reads), tuple(writes)
        o.dma_sem = dma_sem
        o.signal = False
        o.val = 0
        o.idx = len(self.ops)
        if dma_sem is not None:
            self.dma_counts[dma_sem] = self.dma_counts.get(dma_sem, 0) + 16
            o.dma_val = self.dma_counts[dma_sem]
        else:
            o.dma_val = 0
        deps = set()
        is_dma = dma_sem is not None
        def need(p):
            return p.dma_sem is not None or is_dma or not (p.eng == eng == "pe")
        for k in o.reads:
            w = self.last_w.get(k)
            if w is not None and need(w):
                deps.add(w.idx)
            if isinstance(k, tuple) and k and k[0] == "ps":
                for r in self.readers.get(k, ()):
                    if r.eng != eng:
                        deps.add(r.idx)
        for k in o.writes:
            w = self.last_w.get(k)
            if w is not None and need(w):
                deps.add(w.idx)
            for r in self.readers.get(k, ()):
                if need(r):
                    deps.add(r.idx)
        deps.discard(o.idx)
        o.deps = deps
        wset = set(o.writes)
        for k in o.writes:
            self.last_w[k] = o
            self.readers[k] = []
        for k in o.reads:
            if k not in wset:
                self.readers.setdefault(k, []).append(o)
        self.ops.append(o)
        return o

    def emit(self, block, eng_sems, final_waits):
        ops = self.ops
        waits = {}
        for en in self.ENGS:
            widx = {}
            wdma = {}
            for o in ops:
                if o.eng != en:
                    continue
                best = {}
                dm = {}
                for d in o.deps:
                    p = ops[d]
                    if p.dma_sem is not None:
                        v = self.dma_counts[p.dma_sem] if p.dma_sem in self.shared else p.dma_val
                        if v > dm.get(p.dma_sem, 0):
                            dm[p.dma_sem] = v
                    else:
                        if d > best.get(p.eng, -1):
                            best[p.eng] = d
                lst = []
                for pe_, d in best.items():
                    if d > widx.get(pe_, -1):
                        widx[pe_] = d
                        ops[d].signal = True
                        lst.append(("e", d))
                for sm, v in dm.items():
                    if v > wdma.get(sm, 0):
                        wdma[sm] = v
                        lst.append(("d", sm, v))
                waits[o.idx] = lst
        cnt = {e: 0 for e in self.ENGS}
        for o in ops:
            if o.dma_sem is None and o.signal:
                cnt[o.eng] += 1
                o.val = cnt[o.eng]

        def run(eng_name, eng):
            for o in ops:
                if o.eng != eng_name:
                    continue
                for w in waits[o.idx]:
                    if w[0] == "e":
                        p = ops[w[1]]
                        eng.wait_ge(eng_sems[p.eng], p.val)
                    else:
                        eng.wait_ge(w[1], w[2])
                ins = o.fn(eng)
                if o.dma_sem is not None:
                    ins.then_inc(o.dma_sem, 16)
                elif o.signal:
                    ins.then_inc(eng_sems[o.eng], 1)
            if eng_name == "sp":
                for s, v in final_waits():
                    eng.wait_ge(s, v)

        @block.tensor
        def _(e):
            run("pe", e)

        @block.scalar
        def _(e):
            run("act", e)

        @block.vector
        def _(e):
            run("dve", e)

        @block.gpsimd
        def _(e):
            run("pool", e)

        @block.sync
        def _(e):
            run("sp", e)


class Pool_:
    def __init__(self, items):
        self.free = list(items)

    def get(self):
        return self.free.pop(0)

    def put(self, x):
        self.free.append(x)


def build_nc():
    nc = bass.Bass("TRN2", target_bir_lowering=False)
    dt_in = lambda name, shape, dt=F32: nc.dram_tensor(name, shape, dt, kind="ExternalInput").ap()
    xT = dt_in("xT", [D, S])
    pT = dt_in("pT", [DEPTH, 256, S])
    pos = dt_in("pos", [1, S], I32)
    w_in = dt_in("w_in", [DEPTH, D, D_IN])
    w_out = dt_in("w_out", [DEPTH, 2048, D])
    w_gate = dt_in("w_gate", [DEPTH, D, D])
    w_pp = dt_in("w_pp", [DEPTH, 256, D])
    w_a = dt_in("w_a", [DEPTH, 8, 96, 96])
    w_x = dt_in("w_x", [DEPTH, 8, 96, 96])
    w_sT = dt_in("w_sT", [DEPTH, 128, 512])
    b_s = dt_in("b_s", [1, DEPTH * 512])
    ln_g = dt_in("ln_g", [DEPTH, 512])
    ln_b = dt_in("ln_b", [DEPTH, 512])
    pk_d = dt_in("pk", [128, NPK])
    cst_d = dt_in("cst", [128, 4 * 768 + 128])
    outT = nc.dram_tensor("outT", [D, S], F32, kind="ExternalOutput").ap()
    dbg_outs = {}

    wb_in = wb_out = wb_gate = wb_pp = pTb = None
    scr = nc.dram_tensor("scr", [DEPTH * 16, 128, 6144], BF16, kind="Internal").ap()

    from contextlib import ExitStack
    es = ExitStack()
    sb = lambda name, shape, dt=F32: es.enter_context(nc.sbuf_tensor("sb_" + name, shape, dt))
    sem = lambda name: es.enter_context(nc.semaphore(name))

    with es:
        hT = sb("hT", [128, 8, TT])
        yT = sb("yT", [128, 8, TT], BF16)
        NSLOT = 4
        ring = [sb(f"ring{i}", [128, 6144], BF16) for i in range(NSLOT)]
        mixL = sb("mixL", [128, 8, TT], BF16)
        mixR = sb("mixR", [128, 6, TT], BF16)
        mixS = sb("mixS", [128, 4, TT], BF16)
        NF, NB = 12, 8
        funits = [sb(f"fu{i}", [128, TT]) for i in range(NF)]
        bunits = [sb(f"bu{i}", [128, TT], BF16) for i in range(NB)]
        xls = [sb(f"xl{i}", [128, TT + 4]) for i in range(2)]
        qt = sb("qt", [128, 3, TT], BF16)
        kt = sb("kt", [128, 3, TT], BF16)
        qg = sb("qg", [128, 3, TT], BF16)
        qlo = sb("qlo", [128, 3, TT], BF16)
        klo = sb("klo", [128, 3, TT], BF16)
        vbf = sb("vbf", [128, 4, 384], BF16)
        vw = sb("vw", [128, 4, 384], BF16)
        ktm = sb("ktm", [128, 4, 384], BF16)
        Sst = sb("Sst", [128, DEPTH, 768])
        Sb = sb("Sb", [128, DEPTH, 768], BF16)
        ropeT = sb("ropeT", [128, 4, TT])
        cstT = sb("cstT", [128, 3 * 768])
        maskT = sb("maskT", [128, 128])
        cbT = sb("cbT", [128, 3 * 128], BF16)
        pk = sb("pk", [128, NPK])
        dv = sb("dv", [128, 64])
        vn = sb("vn", [128, 4, TT], BF16)
        wsT = sb("wsT", [128, DEPTH, 512], BF16)
        lng = sb("lng", [128, DEPTH, 512])
        lnb = sb("lnb", [128, DEPTH, 512])
        bsb = sb("bsb", [1, DEPTH * 512], BF16)
        pTt = sb("pTt", [128, 2, TT], BF16)
        wab = sb("wab", [96, DEPTH * 8 * 96], BF16)
        wxb = sb("wxb", [96, DEPTH * 8 * 96], BF16)
        hst = sb("hst", [128, DEPTH * 8])
        hist = sb("hist", [128, DEPTH * 8 * 4])
        posi = sb("posi", [128, TT], I32)
        ki = sb("ki", [128, TT], I32)
        small = sb("small", [128, 64])
        banks = [es.enter_context(nc.psum_tensor(f"ps{i}", [128, TT], F32)) for i in range(8)]

        eng_sems = {e: sem(f"s_{e}") for e in Prog.ENGS}
        s_cast = [sem(f"s_cast{i}") for i in range(4)]
        s_wb = [sem(f"s_wb{i}") for i in range(NSLOT)]
        s_rl = [sem(f"s_rl{i}") for i in range(NSLOT)]
        s_ring = [sem(f"s_ring{i}") for i in range(NSLOT)]
        s_ld = sem("s_ld")
        s_x = sem("s_x")
        s_pos = sem("s_pos")
        s_p = sem("s_p")
        s_out = [sem(f"s_out{i}") for i in range(8)]
        s_dbg = sem("s_dbg")

        pg = Prog(nc)
        pg.shared = set([s_ld] + list(s_cast))
        fpool = Pool_(range(NF))
        bpool = Pool_(range(NB))
        pspool = Pool_(range(8))

        ident = cbT[:, 0:128]
        perm = cbT[:, 128:256]
        ones = cbT[:, 256:384]
        decayT = cstT[:, 0:768]
        kwc = cstT[:, 768:1536]
        qwc = cstT[:, 1536:2304]
        epsc = pk[:, 183:184]
        onec = pk[:, 184:185]

        def dbg(name, ap, shape, dt, reads):
            if not DEBUG:
                return
            o = nc.dram_tensor("dbg_" + name, list(shape), dt, kind="ExternalOutput").ap()
            dbg_outs[name] = o
            pg.op("sp", lambda e, o=o, ap=ap: e.dma_start(out=o, in_=ap), reads=reads, writes=["dbgsem"], dma_sem=s_dbg)

        cast_i = [0]

        def cast_dma(out_ap, in_ap, wkey):
            sm = s_cast[cast_i[0] % len(s_cast)]
            cast_i[0] += 1
            pg.op("pool", lambda e: e.dma_start(out=out_ap, in_=in_ap), reads=[], writes=[wkey], dma_sem=sm)

        def ld(out_ap, in_ap, wkey):
            pg.op("sp", lambda e: e.dma_start(out=out_ap, in_=in_ap), reads=[], writes=[wkey], dma_sem=s_ld)

        ld(pk[:], pk_d, "pk")
        ld(cstT[:], cst_d[:, 0:2304], "cst")
        ld(maskT[:], cst_d[:, 2304 + 768:2304 + 768 + 128], "mask")
        for l in range(DEPTH):
            ld(lng[:, l, :], ln_g[l:l + 1, :].broadcast_to([128, 512]), ("lng", l))
            ld(lnb[:, l, :], ln_b[l:l + 1, :].broadcast_to([128, 512]), ("lnb", l))
        cast_dma(cbT[:], cst_d[:, 2304:2304 + 384], "cb")
        cast_dma(bsb[:], b_s, "bsb")
        cast_dma(wab[:].rearrange("c (l n d) -> c l n d", l=DEPTH, n=8), w_a.rearrange("l n c d -> c l n d"), "wab")
        cast_dma(wxb[:].rearrange("c (l n d) -> c l n d", l=DEPTH, n=8), w_x.rearrange("l n c d -> c l n d"), "wxb")
        for l in range(0):
            for g in range(8):
                cast_dma(wb_in[l, :, g * 768:(g + 1) * 768], w_in[l, :, g * 768:(g + 1) * 768], ("wb_in", l, g))
            for g in range(2):
                cast_dma(wb_out[l, g * 1024:(g + 1) * 1024, :], w_out[l, g * 1024:(g + 1) * 1024, :], ("wb_out", l))
            cast_dma(wb_gate[l], w_gate[l], ("wb_gate", l))
            cast_dma(wb_pp[l], w_pp[l], ("wb_pp", l))
            cast_dma(pTb[l], pT[l], ("pTb", l))

        pg.op("dve", lambda e: e.memset(Sst[:], 0.0), writes=["Sst"])
        pg.op("dve", lambda e: e.memset(Sb[:], 0.0), writes=["Sb"])
        pg.op("dve", lambda e: e.memset(hst[:], 0.0), writes=["hst"])
        pg.op("dve", lambda e: e.memset(hist[:], 0.0), writes=[("hist", i) for i in range(16)])
        for i in range(2):
            pg.op("dve", lambda e, i=i: e.memset(xls[i][:], 0.0), writes=[("xl", i)])
        for l in range(DEPTH):
            u = fpool.get()
            ld(funits[u][:], w_sT[l], ("f", u))
            pg.op("dve", lambda e, l=l, u=u: e.tensor_tensor(
                out=wsT[:, l, :].rearrange("p (g i) -> p g i", g=4),
                in0=funits[u][:].rearrange("p (g i) -> p g i", g=4),
                in1=maskT[:].unsqueeze(1).broadcast_to([128, 4, 128]), op=ALU.mult),
                reads=[("f", u), "mask"], writes=[("wsT", l)])
            fpool.put(u)
        pg.op("dve", lambda e: e.tensor_scalar(out=dv[:, 0:32], in0=pk[:, 132:164], scalar1=0.5, scalar2=None, op0=ALU.mult),
              reads=["pk"], writes=["dv0"])
        pg.op("act", lambda e: e.activation(out=small[:, 0:16], in_=pk[:, 164:180], func=AF.Exp, scale=-1.0),
              reads=["pk"], writes=["sm0"])
        pg.op("act", lambda e: e.activation(out=small[:, 16:32], in_=small[:, 0:16], func=AF.Ln, bias=onec, scale=1.0),
              reads=["sm0", "pk"], writes=["sm1"])
        pg.op("dve", lambda e: e.tensor_scalar(out=dv[:, 32:48], in0=small[:, 16:32], scalar1=-8.0, scalar2=None, op0=ALU.mult),
              reads=["sm1"], writes=["dv1"])
        pg.op("dve", lambda e: e.tensor_scalar(out=dv[:, 48:64], in0=small[:, 16:32], scalar1=-4.0, scalar2=None, op0=ALU.mult),
              reads=["sm1"], writes=["dv2"])
        DVK = ["dv0", "dv1", "dv2"]

        ring_i = [0]
        cur_t = [0]
        job_i = [0]

        def wload(dmas, rkeys):
            sl = ring_i[0] % NSLOT
            ring_i[0] += 1
            j = job_i[0]
            job_i[0] += 1
            for di, (vf, src) in enumerate(dmas):
                if cur_t[0] == 0:
                    pg.op("pool", lambda e, vf=vf, src=src, sl=sl: e.dma_start(out=vf(ring[sl]), in_=src),
                          reads=[], writes=[("ring", sl), "poolq"], dma_sem=s_ring[sl])
                    pg.op("sp", lambda e, vf=vf, sl=sl, j=j: e.dma_start(out=vf(scr[j]), in_=vf(ring[sl])),
                          reads=[("ring", sl)], writes=[("scr", j, di)], dma_sem=s_wb[sl])
                else:
                    pg.op("sp", lambda e, vf=vf, sl=sl, j=j: e.dma_start(out=vf(ring[sl]), in_=vf(scr[j])),
                          reads=[("scr", j, di)], writes=[("ring", sl)], dma_sem=s_rl[sl])
            return sl

        def w_in_job(l, g, ncols=768):
            c0 = g * 768
            src = w_in[l, :, c0:c0 + ncols].rearrange("(k p) c -> p k c", p=128)
            return wload([(lambda r, n=ncols: r[:, 0:8 * n].rearrange("p (k c) -> p k c", k=8), src)], [("wb_in", l, g)])

        def mm(bank, out_ap, lhsT, rhs, start, stop, reads, extra_w=()):
            pg.op("pe", lambda e: e.matmul(out_ap, lhsT, rhs, start=start, stop=stop),
                  reads=reads, writes=[("ps", bank)] + list(extra_w))

        def proj_fm(l, slot, ncols, c0, m, bank, mp=128):
            rv = ring[slot][:, 0:8 * ncols].rearrange("p (k c) -> p k c", k=8)
            for kc in range(8):
                mm(bank, banks[bank][0:m, :], rv[:, kc, c0:c0 + m], yT[:, kc, :], kc == 0, kc == 7,
                   [("ring", slot), "yT"])

        def proj_tm(slot, ncols, c0, n, blk, bank):
            rv = ring[slot][:, 0:8 * ncols].rearrange("p (k c) -> p k c", k=8)
            for kc in range(8):
                mm(bank, banks[bank][:, 0:n], yT[:, kc, blk * 128:(blk + 1) * 128], rv[:, kc, c0:c0 + n], kc == 0, kc == 7,
                   [("ring", slot), "yT"])

        def rms_stage(gcol, out_fn, out_keys_fn, out_dt_bf=True):
            bank = pspool.get()
            for c in range(8):
                b = bpool.get()
                pg.op("act", lambda e, c=c, b=b: e.activation(out=bunits[b][:], in_=hT[:, c, :], func=AF.Square),
                      reads=[("hT", c)], writes=[("b", b)])
                mm(bank, banks[bank][:], ones, bunits[b][:], c == 0, c == 7, [("b", b), "cb"])
                bpool.put(b)
            r = fpool.get()
            pg.op("act", lambda e: e.activation(out=funits[r][:], in_=banks[bank][:], func=AF.Ln, bias=epsc, scale=1.0 / D),
                  reads=[("ps", bank), "pk"], writes=[("f", r)])
            pg.op("act", lambda e: e.activation(out=funits[r][:], in_=funits[r][:], func=AF.Exp, scale=-0.5), reads=[("f", r)], writes=[("f", r)])
            pspool.put(bank)
            for c in range(8):
                pg.op("dve", lambda e, c=c: e.scalar_tensor_tensor(out=out_fn(c), in0=hT[:, c, :], scalar=gcol[:, c:c + 1],
                                                                  in1=funits[r][:], op0=ALU.mult, op1=ALU.mult),
                      reads=[("hT", c), ("f", r), "pk"], writes=out_keys_fn(c))
            fpool.put(r)

        gam = []
        for h in range(6):
            lg = np.log1p(-np.exp2(np.float32(-5.0 - h))).astype(np.float32)
            gam.append(float(np.exp(np.float32(128.0) * lg)))

        def do_tile(t):
            cur_t[0] = t
            job_i[0] = 0
            tok = slice(t * TT, (t + 1) * TT)
            pg.op("pool", lambda e, tok=tok: e.dma_start(out=hT[:], in_=xT.rearrange("(c p) t -> p c t", p=128)[:, :, tok]),
                  reads=[], writes=[("hT", c) for c in range(8)] + ["poolq"], dma_sem=s_x)
            pg.op("pool", lambda e, tok=tok: e.dma_start(out=posi[:], in_=pos[0:1, tok].broadcast_to([128, TT])),
                  reads=[], writes=["posi", "poolq"], dma_sem=s_pos)
            ang = fpool.get()
            tmp = fpool.get()
            kf = fpool.get()
            pg.op("dve", lambda e: e.tensor_copy(out=funits[ang][:], in_=posi[:]), reads=["posi"], writes=[("f", ang)])
            pg.op("dve", lambda e: e.tensor_scalar(out=funits[ang][:], in0=funits[ang][:], scalar1=pk[:, 180:181], scalar2=None, op0=ALU.mult),
                  reads=[("f", ang), "pk"], writes=[("f", ang)])
            C1 = 6.28125
            C2 = float(2.0 * np.pi - 6.28125)
            for which, off in ((1, 0.0), (0, 0.25)):
                pg.op("dve", lambda e, off=off: e.tensor_scalar(out=funits[tmp][:], in0=funits[ang][:], scalar1=float(1.0 / (2 * np.pi)), scalar2=off,
                                                               op0=ALU.mult, op1=ALU.add), reads=[("f", ang)], writes=[("f", tmp)])
                pg.op("dve", lambda e: e.tensor_copy(out=ki[:], in_=funits[tmp][:]), reads=[("f", tmp)], writes=["ki"])
                pg.op("dve", lambda e: e.tensor_copy(out=funits[kf][:], in_=ki[:]), reads=["ki"], writes=[("f", kf)])
                pg.op("dve", lambda e: e.scalar_tensor_tensor(out=funits[tmp][:], in0=funits[kf][:], scalar=-C1, in1=funits[ang][:], op0=ALU.mult, op1=ALU.add),
                      reads=[("f", kf), ("f", ang)], writes=[("f", tmp)])
                pg.op("dve", lambda e: e.scalar_tensor_tensor(out=funits[tmp][:], in0=funits[kf][:], scalar=-C2, in1=funits[tmp][:], op0=ALU.mult, op1=ALU.add),
                      reads=[("f", kf), ("f", tmp)], writes=[("f", tmp)])
                if off != 0.0:
                    pg.op("dve", lambda e: e.tensor_scalar(out=funits[tmp][:], in0=funits[tmp][:], scalar1=float(np.pi / 2), scalar2=None, op0=ALU.add),
                          reads=[("f", tmp)], writes=[("f", tmp)])
                pg.op("dve", lambda e: e.tensor_scalar(out=funits[tmp][:], in0=funits[tmp][:], scalar1=-3.1415925, scalar2=3.1415925, op0=ALU.max, op1=ALU.min),
                      reads=[("f", tmp)], writes=[("f", tmp)])
                pg.op("act", lambda e, which=which: e.activation(out=ropeT[:, which, :], in_=funits[tmp][:], func=AF.Sin),
                      reads=[("f", tmp)], writes=[("rope", which)])
            RS = float(128.0 ** -0.5)
            pg.op("dve", lambda e: e.tensor_scalar(out=ropeT[:, 2, :], in0=ropeT[:, 0, :], scalar1=RS, scalar2=None, op0=ALU.mult),
                  reads=[("rope", 0)], writes=[("rope", 2)])
            pg.op("dve", lambda e: e.tensor_scalar(out=ropeT[:, 3, :], in0=ropeT[:, 1, :], scalar1=pk[:, 182:183], scalar2=None, op0=ALU.mult),
                  reads=[("rope", 1), "pk"], writes=[("rope", 3)])
            pg.op("dve", lambda e: e.tensor_scalar(out=ropeT[:, 1, :], in0=ropeT[:, 1, :], scalar1=pk[:, 181:182], scalar2=None, op0=ALU.mult),
                  reads=[("rope", 1), ("rope", 3), "pk"], writes=[("rope", 1)])
            for u_ in (ang, tmp, kf):
                fpool.put(u_)

            def do_layer(l):
                if KSTAGE < 2:
                    return
                rms_stage(pk[:, l * 8:(l + 1) * 8], lambda c: yT[:, c, :], lambda c: ["yT"])
                if DEBUG and t == 0 and l == 0:
                    dbg("yT", yT[:], [128, 8, TT], BF16, ["yT"])

                if KSTAGE < 3:
                    return
                sx = w_in_job(l, 0)
                sg = w_in_job(l, 1)
                pend = []

                def lru_A(n):
                    bx = pspool.get()
                    proj_fm(l, sx, 768, n * 96, 96, bx)
                    xl = xls[n % 2]
                    xk = ("xl", n % 2)
                    hi = (l * 8 + n) * 4
                    pg.op("pool", lambda e: e.tensor_copy(out=xl[0:96, 0:3], in_=hist[0:96, hi:hi + 3]),
                          reads=[("hist", l * 8 + n)], writes=[xk])
                    pg.op("act", lambda e: e.activation(out=xl[0:96, 3:TT + 3], in_=banks[bx][0:96, :], func=AF.Copy),
                          reads=[("ps", bx)], writes=[xk])
                    pspool.put(bx)
                    bg = pspool.get()
                    proj_fm(l, sg, 768, n * 96, 96, bg)
                    return bg

                def lru_B1(n, bg):
                    xl = xls[n % 2]
                    xk = ("xl", n % 2)
                    col = l * 8 + n
                    cw = lambda k: pk[0:96, 52 + col * 4 + k:52 + col * 4 + k + 1]
                    xc = fpool.get()
                    XC = funits[xc][0:96, :]
                    pg.op("dve", lambda e: e.tensor_scalar(out=XC, in0=xl[0:96, 0:TT], scalar1=cw(0), scalar2=pk[0:96, 116 + col:117 + col],
                                                           op0=ALU.mult, op1=ALU.add), reads=[xk, "pk"], writes=[("f", xc)])
                    for k in (1, 2, 3):
                        pg.op("dve", lambda e, k=k: e.scalar_tensor_tensor(out=XC, in0=xl[0:96, k:k + TT], scalar=cw(k), in1=XC,
                                                                          op0=ALU.mult, op1=ALU.add), reads=[xk, "pk", ("f", xc)], writes=[("f", xc)])
                    hi = col * 4
                    pg.op("pool", lambda e: e.tensor_copy(out=hist[0:96, hi:hi + 3], in_=xl[0:96, TT:TT + 3]),
                          reads=[xk], writes=[("hist", col)])
                    xb = bpool.get()
                    pg.op("pool", lambda e: e.tensor_copy(out=bunits[xb][0:96, :], in_=XC), reads=[("f", xc)], writes=[("b", xb)])
                    ba = pspool.get()
                    bxx = pspool.get()
                    wo = col * 96
                    mm(ba, banks[ba][0:96, :], wab[0:96, wo:wo + 96], bunits[xb][0:96, :], True, True, [("b", xb), "wab"])
                    mm(bxx, banks[bxx][0:96, :], wxb[0:96, wo:wo + 96], bunits[xb][0:96, :], True, True, [("b", xb), "wxb"])
                    bpool.put(xb)
                    tha = fpool.get()
                    thx = fpool.get()
                    thg = fpool.get()
                    a2 = fpool.get()
                    A, X, G, A2 = (funits[i][0:96, :] for i in (tha, thx, thg, a2))
                    pg.op("act", lambda e: e.activation(out=A, in_=banks[ba][0:96, :], func=AF.Tanh, bias=dv[0:96, col:col + 1], scale=0.5),
                          reads=[("ps", ba)] + DVK, writes=[("f", tha)])
                    pg.op("act", lambda e: e.activation(out=X, in_=banks[bxx][0:96, :], func=AF.Tanh, bias=dv[0:96, 16 + col:17 + col], scale=0.5),
                          reads=[("ps", bxx)] + DVK, writes=[("f", thx)])
                    pspool.put(ba)
                    pspool.put(bxx)
                    pg.op("act", lambda e: e.activation(out=G, in_=banks[bg][0:96, :], func=AF.Tanh, scale=0.5),
                          reads=[("ps", bg)], writes=[("f", thg)])
                    pg.op("act", lambda e: e.activation(out=A2, in_=A, func=AF.Exp, bias=dv[0:96, 32 + col:33 + col], scale=dv[0:96, 32 + col:33 + col]),
                          reads=[("f", tha)] + DVK, writes=[("f", a2)])
                    pg.op("act", lambda e: e.activation(out=A, in_=A, func=AF.Exp, bias=dv[0:96, 48 + col:49 + col], scale=dv[0:96, 48 + col:49 + col]),
                          reads=[("f", tha)] + DVK, writes=[("f", tha)])
                    pg.op("act", lambda e: e.activation(out=A2, in_=A2, func=AF.Ln, bias=onec[0:96, :], scale=-1.0),
                          reads=[("f", a2), "pk"], writes=[("f", a2)])
                    pg.op("act", lambda e: e.activation(out=A2, in_=A2, func=AF.Exp, scale=0.5),
                          reads=[("f", a2)], writes=[("f", a2)])
                    return (bg, xc, tha, thx, thg, a2)

                def lru_B2(n, st_):
                    bg, xc, tha, thx, thg, a2 = st_
                    col = l * 8 + n
                    XC = funits[xc][0:96, :]
                    A, X, G, A2 = (funits[i][0:96, :] for i in (tha, thx, thg, a2))
                    pg.op("dve", lambda e: e.scalar_tensor_tensor(out=X, in0=X, scalar=1.0, in1=XC, op0=ALU.add, op1=ALU.mult),
                          reads=[("f", thx), ("f", xc)], writes=[("f", thx)])
                    pg.op("dve", lambda e: e.scalar_tensor_tensor(out=X, in0=X, scalar=0.5, in1=A2, op0=ALU.mult, op1=ALU.mult),
                          reads=[("f", thx), ("f", a2)], writes=[("f", thx)])
                    pg.op("dve", lambda e: e.tensor_tensor_scan(out=XC, data0=A, data1=X, initial=hst[0:96, col:col + 1], op0=ALU.mult, op1=ALU.add),
                          reads=[("f", tha), ("f", thx), "hst"], writes=[("f", xc)])
                    pg.op("pool", lambda e: e.tensor_copy(out=hst[0:96, col:col + 1], in_=funits[xc][0:96, TT - 1:TT]),
                          reads=[("f", xc)], writes=["hst"])
                    pg.op("dve", lambda e: e.scalar_tensor_tensor(out=G, in0=G, scalar=1.0, in1=banks[bg][0:96, :], op0=ALU.add, op1=ALU.mult),
                          reads=[("f", thg), ("ps", bg)], writes=[("f", thg)])
                    pspool.put(bg)
                    pg.op("dve", lambda e: e.scalar_tensor_tensor(out=mixL[0:96, n, :], in0=G, scalar=0.5, in1=XC, op0=ALU.mult, op1=ALU.mult),
                          reads=[("f", thg), ("f", xc)], writes=[("mixL", n)])
                    if DEBUG and t == 0 and l == 0 and n == 0:
                        dbg("lru_h0", funits[xc][:], [128, TT], F32, [("f", xc)])
                    for u_ in (xc, tha, thx, thg, a2):
                        fpool.put(u_)

                bgs = {}
                sts = {}
                bgs[0] = lru_A(0)
                bgs[1] = lru_A(1)
                sts[0] = lru_B1(0, bgs[0])
                for n in range(8):
                    if n + 2 < 8:
                        bgs[n + 2] = lru_A(n + 2)
                    if n + 1 < 8:
                        sts[n + 1] = lru_B1(n + 1, bgs[n + 1])
                    lru_B2(n, sts[n])

                if KSTAGE < 4:
                    return
                sq_ = w_in_job(l, 2)
                sk_ = w_in_job(l, 3)
                sv_ = w_in_job(l, 4)
                sgr = w_in_job(l, 5)
                for hg in range(2):
                    items = []
                    for hh in range(3):
                        h = hg * 3 + hh

                        def rope_item(slot, cidx, sidx, dst, dkey, hh=hh, h=h):
                            bq = pspool.get()
                            proj_fm(l, slot, 768, h * 128, 128, bq)
                            qb = bpool.get()
                            pg.op("act", lambda e, bq=bq, qb=qb: e.activation(out=bunits[qb][:], in_=banks[bq][:], func=AF.Copy),
                                  reads=[("ps", bq)], writes=[("b", qb), ("psr", bq)])
                            bs_ = pspool.get()
                            mm(bs_, banks[bs_][:], perm, bunits[qb][:], True, True, [("b", qb), "cb"])
                            bpool.put(qb)
                            yield
                            if KQ < 1:
                                pspool.put(bq)
                                pspool.put(bs_)
                                return
                            t1 = fpool.get()
                            t2 = fpool.get()
                            pg.op("dve", lambda e, bq=bq, t1=t1, cidx=cidx: e.scalar_tensor_tensor(out=funits[t1][:], in0=ropeT[:, cidx, :], scalar=1.0, in1=banks[bq][:], op0=ALU.mult, op1=ALU.mult),
                                  reads=[("ps", bq), ("psr", bq), ("rope", cidx)], writes=[("f", t1)])
                            pg.op("dve", lambda e, bs_=bs_, t2=t2, sidx=sidx: e.scalar_tensor_tensor(out=funits[t2][:], in0=ropeT[:, sidx, :], scalar=1.0, in1=banks[bs_][:], op0=ALU.mult, op1=ALU.mult),
                                  reads=[("ps", bs_), ("rope", sidx)], writes=[("f", t2)])
                            pspool.put(bq)
                            pspool.put(bs_)
                            if KQ < 2:
                                fpool.put(t1)
                                fpool.put(t2)
                                return
                            lo_t = qlo if dkey == "qt" else klo
                            lkey = "qlo" if dkey == "qt" else "klo"
                            pg.op("dve", lambda e, t1=t1, t2=t2: e.tensor_tensor(out=funits[t1][:], in0=funits[t1][:], in1=funits[t2][:], op=ALU.add),
                                  reads=[("f", t1), ("f", t2)], writes=[("f", t1)])
                            pg.op("act", lambda e, t1=t1, dst=dst, hh=hh: e.activation(out=dst[:, hh, :], in_=funits[t1][:], func=AF.Copy),
                                  reads=[("f", t1)], writes=[(dkey, hh)])
                            pg.op("pool", lambda e, t2=t2, dst=dst, hh=hh: e.tensor_copy(out=funits[t2][:], in_=dst[:, hh, :]),
                                  reads=[(dkey, hh), ("f", t2)], writes=[("f", t2)])
                            pg.op("pool", lambda e, t1=t1, t2=t2, lo_t=lo_t, hh=hh: e.tensor_tensor(
                                out=lo_t[:, hh, :], in0=funits[t1][:], in1=funits[t2][:], op=ALU.subtract),
                                reads=[("f", t1), ("f", t2)], writes=[(lkey, hh)])
                            if dkey == "qt":
                                pg.op("pool", lambda e, t1=t1, hh=hh, h=h: e.tensor_tensor(
                                    out=qg[:, hh, :].rearrange("p (n i) -> p n i", n=4),
                                    in0=funits[t1][:].rearrange("p (n i) -> p n i", n=4),
                                    in1=qwc[:, h * 128:(h + 1) * 128].unsqueeze(1).broadcast_to([128, 4, 128]), op=ALU.mult),
                                    reads=[("f", t1), "cst"], writes=[("qg", hh)])
                            fpool.put(t1)
                            fpool.put(t2)

                        for args_ in ((sq_, 0, 1, qt, "qt"), (sk_, 2, 3, kt, "kt")):
                            items.append(rope_item(*args_))
                    next(items[0])
                    for i_ in range(6):
                        if i_ + 1 < 6:
                            next(items[i_ + 1])
                        for _ in items[i_]:
                            pass
                    if DEBUG and t == 0 and l == 0 and hg == 0:
                        dbg("qt", qt[:], [128, 3, TT], BF16, [("qt", i) for i in range(3)])
                        dbg("kt", kt[:], [128, 3, TT], BF16, [("kt", i) for i in range(3)])
                    if KRET < 2:
                        continue
                    for blk in range(4):
                        bv = pspool.get()
                        proj_tm(sv_, 768, hg * 384, 384, blk, bv)
                        pg.op("act", lambda e, bv=bv, blk=blk: e.activation(out=vbf[:, blk, :], in_=banks[bv][:, 0:384], func=AF.Copy),
                              reads=[("ps", bv)], writes=[("vbf", blk)])
                        pg.op("dve", lambda e, bv=bv, blk=blk, hg=hg: e.scalar_tensor_tensor(out=vw[:, blk, :], in0=kwc[:, hg * 384:(hg + 1) * 384], scalar=1.0, in1=banks[bv][:, 0:384], op0=ALU.mult, op1=ALU.mult),
                              reads=[("ps", bv), "cst"], writes=[("vw", blk)])
                        pspool.put(bv)
                    if KRET < 3:
                        continue
                    for blk in range(4):
                        bt = pspool.get()
                        btb = banks[bt][:].bitcast(BF16)
                        for hh in range(3):
                            pg.op("pe", lambda e, btb=btb, hh=hh, blk=blk: e.transpose(btb[:, hh * 128:(hh + 1) * 128], kt[:, hh, blk * 128:(blk + 1) * 128], ident),
                                  reads=[("kt", hh), "cb"], writes=[("ps", bt)])
                        pg.op("act", lambda e, btb=btb, blk=blk: e.activation(out=ktm[:, blk, :], in_=btb[:, 0:384], func=AF.Copy),
                              reads=[("ps", bt)], writes=[("ktm", blk)])
                        pspool.put(bt)
                    if KRET < 4:
                        continue
                    bo = [pspool.get() for _ in range(3)]
                    for n in range(4):
                        cs = slice(n * 128, (n + 1) * 128)
                        for hh in range(3):
                            h = hg * 3 + hh
                            mm(bo[hh], banks[bo[hh]][:, cs], Sb[:, l, h * 128:(h + 1) * 128], qg[:, hh, cs], True, False, [("Sb", l, hg), ("qg", hh)])
                        bkv = pspool.get()
                        for hh in range(3):
                            mm(bkv, banks[bkv][:, hh * 128:(hh + 1) * 128], ktm[:, n, hh * 128:(hh + 1) * 128], vw[:, n, hh * 128:(hh + 1) * 128], True, True,
                               [("ktm", n), ("vw", n)])
                        for hh in range(3):
                            h = hg * 3 + hh
                            pg.op("dve", lambda e, h=h, hh=hh, bkv=bkv: e.scalar_tensor_tensor(
                                out=Sst[:, l, h * 128:(h + 1) * 128], in0=Sst[:, l, h * 128:(h + 1) * 128], scalar=gam[h],
                                in1=banks[bkv][:, hh * 128:(hh + 1) * 128], op0=ALU.mult, op1=ALU.add),
                                reads=[("ps", bkv), ("Sst", l, hg)], writes=[("Sst", l, hg)])
                        pspool.put(bkv)
                        pg.op("act", lambda e, hg=hg: e.activation(out=Sb[:, l, hg * 384:(hg + 1) * 384], in_=Sst[:, l, hg * 384:(hg + 1) * 384], func=AF.Copy),
                              reads=[("Sst", l, hg)], writes=[("Sb", l, hg)])
                        bsc = pspool.get()
                        for hh in range(3):
                            osc = banks[bsc][:, hh * 128:(hh + 1) * 128]
                            mm(bsc, osc, kt[:, hh, cs], qt[:, hh, cs], True, False, [("kt", hh), ("qt", hh)])
                            mm(bsc, osc, kt[:, hh, cs], qlo[:, hh, cs], False, False, [("kt", hh), ("qlo", hh)])
                            mm(bsc, osc, klo[:, hh, cs], qt[:, hh, cs], False, True, [("klo", hh), ("qt", hh)])
                        Pb = bpool.get()
                        pg.op("dve", lambda e, bsc=bsc, Pb=Pb, hg=hg: e.scalar_tensor_tensor(out=bunits[Pb][:, 0:384], in0=decayT[:, hg * 384:(hg + 1) * 384], scalar=1.0, in1=banks[bsc][:, 0:384], op0=ALU.mult, op1=ALU.mult),
                              reads=[("ps", bsc), "cst"], writes=[("b", Pb)])
                        pspool.put(bsc)
                        for hh in range(3):
                            h = hg * 3 + hh
                            mm(bo[hh], banks[bo[hh]][:, cs], vbf[:, n, hh * 128:(hh + 1) * 128], bunits[Pb][:, hh * 128:(hh + 1) * 128], False, True,
                               [("vbf", n), ("b", Pb)])
                        bpool.put(Pb)
                    if KRET < 5:
                        for b_ in bo:
                            pspool.put(b_)
                        continue
                    for hh in range(3):
                        h = hg * 3 + hh
                        bg = pspool.get()
                        proj_fm(l, sgr, 768, h * 128, 128, bg)
                        b = bpool.get()
                        pg.op("act", lambda e, b=b, boa=banks[bo[hh]][:]: e.activation(out=bunits[b][:], in_=boa, func=AF.Square),
                              reads=[("ps", bo[hh])], writes=[("b", b)])
                        bm = pspool.get()
                        mm(bm, banks[bm][:], ones, bunits[b][:], True, True, [("b", b), "cb"])
                        bpool.put(b)
                        r = fpool.get()
                        pg.op("act", lambda e, r=r, bm=bm: e.activation(out=funits[r][:], in_=banks[bm][:], func=AF.Ln, bias=epsc, scale=1.0 / 128),
                              reads=[("ps", bm), "pk"], writes=[("f", r)])
                        pspool.put(bm)
                        pg.op("act", lambda e, r=r: e.activation(out=funits[r][:], in_=funits[r][:], func=AF.Exp, scale=-0.5), reads=[("f", r)], writes=[("f", r)])
                        on = fpool.get()
                        pg.op("dve", lambda e, r=r, on=on, boa=banks[bo[hh]][:], h=h: e.scalar_tensor_tensor(out=funits[on][:], in0=funits[r][:], scalar=pk[:, 40 + l * 6 + h:41 + l * 6 + h],
                                                                                        in1=boa, op0=ALU.mult, op1=ALU.mult),
                              reads=[("ps", bo[hh]), ("f", r), "pk"], writes=[("f", on)])
                        fpool.put(r)
                        tg = fpool.get()
                        pg.op("act", lambda e, tg=tg, bg=bg: e.activation(out=funits[tg][:], in_=banks[bg][:], func=AF.Tanh, scale=0.5),
                              reads=[("ps", bg)], writes=[("f", tg)])
                        pg.op("dve", lambda e, tg=tg, bg=bg: e.scalar_tensor_tensor(out=funits[tg][:], in0=funits[tg][:], scalar=1.0, in1=banks[bg][:], op0=ALU.add, op1=ALU.mult),
                              reads=[("f", tg), ("ps", bg)], writes=[("f", tg)])
                        pspool.put(bg)
                        pg.op("dve", lambda e, tg=tg, on=on, h=h: e.scalar_tensor_tensor(out=mixR[:, h, :], in0=funits[tg][:], scalar=0.5, in1=funits[on][:], op0=ALU.mult, op1=ALU.mult),
                              reads=[("f", tg), ("f", on)], writes=[("mixR", h)])
                        if DEBUG and t == 0 and l == 0 and h == 0:
                            dbg("ret_on0", funits[on][:], [128, TT], F32, [("f", on)])
                        fpool.put(tg)
                        fpool.put(on)
                    for b_ in bo:
                        pspool.put(b_)

                if KSTAGE < 5:
                    return
                su = w_in_job(l, 6, 512)
                def w_cols_job(c0, ncols):
                    src = w_in[l, :, c0:c0 + ncols].rearrange("(k p) c -> p k c", p=128)
                    return wload([(lambda r, n=ncols: r[:, 0:8 * n].rearrange("p (k c) -> p k c", k=8), src)],
                                 [("wb_in", l, c0 // 768), ("wb_in", l, (c0 + ncols - 1) // 768)])
                svs = w_cols_job(5120, 512)
                sgs = w_cols_job(5632, 512)
                for blk in range(4):
                    bv = pspool.get()
                    proj_tm(svs, 512, 0, 512, blk, bv)
                    so = blk * 12
                    tq = fpool.get()
                    pg.op("act", lambda e, bv=bv, tq=tq: e.activation(out=funits[tq][:], in_=banks[bv][:], func=AF.Copy),
                          reads=[("ps", bv)], writes=[("f", tq)])
                    pspool.put(bv)
                    pg.op("dve", lambda e, tq=tq, so=so: e.bn_stats(out=small[:, so:so + 6], in_=funits[tq][:]), reads=[("f", tq)], writes=[("small", blk)])
                    pg.op("dve", lambda e, so=so: e.bn_aggr(out=small[:, so + 6:so + 8], in_=small[:, so:so + 6]), reads=[("small", blk)], writes=[("small", blk)])
                    pg.op("act", lambda e, so=so: e.activation(out=small[:, so + 8:so + 9], in_=small[:, so + 7:so + 8], func=AF.Ln, bias=epsc, scale=1.0),
                          reads=[("small", blk), "pk"], writes=[("small", blk)])
                    pg.op("act", lambda e, so=so: e.activation(out=small[:, so + 8:so + 9], in_=small[:, so + 8:so + 9], func=AF.Exp, scale=-0.5), reads=[("small", blk)], writes=[("small", blk)])
                    pg.op("dve", lambda e, so=so, tq=tq: e.scalar_tensor_tensor(out=funits[tq][:], in0=funits[tq][:], scalar=small[:, so + 6:so + 7], in1=lng[:, l, :],
                                                                                op0=ALU.subtract, op1=ALU.mult),
                          reads=[("f", tq), ("small", blk), ("lng", l)], writes=[("f", tq)])
                    pg.op("dve", lambda e, so=so, tq=tq, blk=blk: e.scalar_tensor_tensor(out=vn[:, blk, :], in0=funits[tq][:], scalar=small[:, so + 8:so + 9], in1=lnb[:, l, :],
                                                                                  op0=ALU.mult, op1=ALU.add),
                          reads=[("f", tq), ("small", blk), ("lnb", l)], writes=[("vn", blk)])
                    fpool.put(tq)
                for g in range(4):
                    gs = slice(g * 128, (g + 1) * 128)
                    bmix = pspool.get()
                    for n in range(4):
                        cs = slice(n * 128, (n + 1) * 128)
                        mm(bmix, banks[bmix][:, cs], vn[:, n, gs], wsT[:, l, gs], True, False, [("vn", n), ("wsT", l)])
                        mm(bmix, banks[bmix][:, cs], ones[0:1, :], bsb[0:1, l * 512 + g * 128:l * 512 + (g + 1) * 128], False, True, ["cb", "bsb"])
                    bu = pspool.get()
                    proj_fm(l, su, 512, g * 128, 128, bu)
                    bgg = pspool.get()
                    proj_fm(l, sgs, 512, g * 128, 128, bgg)
                    m_ = fpool.get()
                    pg.op("act", lambda e, m_=m_, bmix=bmix: e.activation(out=funits[m_][:], in_=banks[bmix][:], func=AF.Copy),
                          reads=[("ps", bmix)], writes=[("f", m_)])
                    pspool.put(bmix)
                    pg.op("dve", lambda e, m_=m_, bu=bu: e.scalar_tensor_tensor(out=funits[m_][:], in0=funits[m_][:], scalar=1.0, in1=banks[bu][:], op0=ALU.mult, op1=ALU.mult),
                          reads=[("f", m_), ("ps", bu)], writes=[("f", m_)])
                    pspool.put(bu)
                    tg = fpool.get()
                    pg.op("act", lambda e, tg=tg, bgg=bgg: e.activation(out=funits[tg][:], in_=banks[bgg][:], func=AF.Tanh, scale=0.5),
                          reads=[("ps", bgg)], writes=[("f", tg)])
                    pg.op("dve", lambda e, tg=tg, bgg=bgg: e.scalar_tensor_tensor(out=funits[tg][:], in0=funits[tg][:], scalar=1.0, in1=banks[bgg][:], op0=ALU.add, op1=ALU.mult),
                          reads=[("f", tg), ("ps", bgg)], writes=[("f", tg)])
                    pspool.put(bgg)
                    pg.op("dve", lambda e, tg=tg, m_=m_, g=g: e.scalar_tensor_tensor(out=mixS[:, g, :], in0=funits[tg][:], scalar=0.5, in1=funits[m_][:], op0=ALU.mult, op1=ALU.mult),
                          reads=[("f", tg), ("f", m_)], writes=[("mixS", g)])
                    fpool.put(tg)
                    fpool.put(m_)
                if DEBUG and t == 0 and l == 0:
                    dbg("mixL", mixL[:], [128, 8, TT], BF16, [("mixL", i) for i in range(8)])
                    dbg("mixR", mixR[:], [128, 6, TT], BF16, [("mixR", i) for i in range(6)])
                    dbg("mixS", mixS[:], [128, 4, TT], BF16, [("mixS", i) for i in range(4)])

                if KSTAGE < 6:
                    return
                for piece in range(4):
                    c0 = piece * 256
                    dm = [
                        (lambda r: r[0:96, 0:2048].rearrange("p (n c) -> p n c", n=8),
                         w_out[l, 0:768, c0:c0 + 256].rearrange("(n p) c -> p n c", p=96)),
                        (lambda r: r[:, 2048:3584].rearrange("p (n c) -> p n c", n=6),
                         w_out[l, 768:1536, c0:c0 + 256].rearrange("(n p) c -> p n c", p=128)),
                        (lambda r: r[:, 3584:4608].rearrange("p (n c) -> p n c", n=4),
                         w_out[l, 1536:2048, c0:c0 + 256].rearrange("(n p) c -> p n c", p=128)),
                    ]
                    so_ = wload(dm, [("wb_out", l)])
                    R = ring[so_]
                    for d2 in range(2):
                        c = piece * 2 + d2
                        bw = pspool.get()
                        steps = []
                        for n in range(8):
                            steps.append((R[0:96, n * 256 + d2 * 128:n * 256 + d2 * 128 + 128], mixL[0:96, n, :], ("mixL", n)))
                        for h in range(6):
                            steps.append((R[:, 2048 + h * 256 + d2 * 128:2048 + h * 256 + d2 * 128 + 128], mixR[:, h, :], ("mixR", h)))
                        for g in range(4):
                            steps.append((R[:, 3584 + g * 256 + d2 * 128:3584 + g * 256 + d2 * 128 + 128], mixS[:, g, :], ("mixS", g)))
                        for i, (lh, rh, key) in enumerate(steps):
                            mm(bw, banks[bw][:], lh, rh, i == 0, i == len(steps) - 1, [("ring", so_), key])
                        pg.op("dve", lambda e, c=c, bw=bw: e.scalar_tensor_tensor(out=hT[:, c, :], in0=hT[:, c, :], scalar=1.0, in1=banks[bw][:], op0=ALU.mult, op1=ALU.add),
                              reads=[("hT", c), ("ps", bw)], writes=[("hT", c)])
                        pspool.put(bw)

                if KSTAGE < 7:
                    return
                rms_stage(pk[:, 16 + l * 8:16 + (l + 1) * 8], lambda c: yT[:, c, :], lambda c: ["yT"])
                pg.op("pool", lambda e, tok=tok: e.dma_start(out=pTt[:], in_=pT[l].rearrange("(k p) t -> p k t", p=128)[:, :, tok]),
                      reads=[], writes=["pTt", "poolq"], dma_sem=s_p)
                spp = wload([(lambda r: r[:, 0:2048].rearrange("p (k c) -> p k c", k=2), w_pp[l].rearrange("(k p) c -> p k c", p=128))], [("wb_pp", l)])
                for half in range(2):
                    sgt = wload([(lambda r: r[:, 0:4096].rearrange("p (k c) -> p k c", k=8),
                                  w_gate[l, :, half * 512:(half + 1) * 512].rearrange("(k p) c -> p k c", p=128))], [("wb_gate", l)])
                    for cc in range(4):
                        c = half * 4 + cc
                        bgt = pspool.get()
                        proj_fm(l, sgt, 512, cc * 128, 128, bgt)
                        bpp = pspool.get()
                        rp = ring[spp][:, 0:2048].rearrange("p (k c) -> p k c", k=2)
                        for k in range(2):
                            mm(bpp, banks[bpp][:], rp[:, k, c * 128:(c + 1) * 128], pTt[:, k, :], k == 0, k == 1, [("ring", spp), "pTt"])
                        th = fpool.get()
                        pg.op("act", lambda e, th=th, bgt=bgt: e.activation(out=funits[th][:], in_=banks[bgt][:], func=AF.Tanh, scale=0.5),
                              reads=[("ps", bgt)], writes=[("f", th)])
                        pspool.put(bgt)
                        pg.op("dve", lambda e, th=th, bpp=bpp: e.scalar_tensor_tensor(out=funits[th][:], in0=funits[th][:], scalar=1.0, in1=banks[bpp][:], op0=ALU.add, op1=ALU.mult),
                              reads=[("f", th), ("ps", bpp)], writes=[("f", th)])
                        pspool.put(bpp)
                        pg.op("dve", lambda e, th=th, c=c: e.scalar_tensor_tensor(out=hT[:, c, :], in0=funits[th][:], scalar=0.5, in1=hT[:, c, :], op0=ALU.mult, op1=ALU.add),
                              reads=[("f", th), ("hT", c)], writes=[("hT", c)])
                        fpool.put(th)
                if DEBUG and t == 0:
                    dbg(f"h_l{l}", hT[:], [128, 8, TT], F32, [("hT", c) for c in range(8)])

            for l_ in range(DEPTH):
                do_layer(l_)

            outs = {}

            def fin_out(c):
                u = fpool.get()
                outs[c] = u
                return funits[u][:]

            def fin_keys(c):
                return [("f", outs[c])]

            bank = pspool.get()
            for c in range(8):
                b = bpool.get()
                pg.op("act", lambda e, c=c, b=b: e.activation(out=bunits[b][:], in_=hT[:, c, :], func=AF.Square), reads=[("hT", c)], writes=[("b", b)])
                mm(bank, banks[bank][:], ones, bunits[b][:], c == 0, c == 7, [("b", b), "cb"])
                bpool.put(b)
            r = fpool.get()
            pg.op("act", lambda e, r=r, bank=bank: e.activation(out=funits[r][:], in_=banks[bank][:], func=AF.Ln, bias=epsc, scale=1.0 / D),
                  reads=[("ps", bank), "pk"], writes=[("f", r)])
            pg.op("act", lambda e, r=r: e.activation(out=funits[r][:], in_=funits[r][:], func=AF.Exp, scale=-0.5), reads=[("f", r)], writes=[("f", r)])
            pspool.put(bank)
            for c in range(8):
                u = fpool.get()
                pg.op("dve", lambda e, c=c, u=u, r=r: e.scalar_tensor_tensor(out=funits[u][:], in0=hT[:, c, :], scalar=pk[:, 32 + c:33 + c], in1=funits[r][:],
                                                                          op0=ALU.mult, op1=ALU.mult),
                      reads=[("hT", c), ("f", r), "pk"], writes=[("f", u)])
                pg.op("pool", lambda e, c=c, u=u, tok=tok: e.dma_start(out=outT[c * 128:(c + 1) * 128, tok], in_=funits[u][:]),
                      reads=[("f", u)], writes=[("outsem", c), "poolq"], dma_sem=s_out[c])
                fpool.put(u)
            fpool.put(r)

        for t_ in range(NT_RUN):
            do_tile(t_)

        def final_waits():
            res = []
            for s_ in s_out:
                if s_ in pg.dma_counts:
                    res.append((s_, pg.dma_counts[s_]))
            if s_dbg in pg.dma_counts:
                res.append((s_dbg, pg.dma_counts[s_dbg]))
            return res

        with nc.Block() as block:
            pg.emit(block, eng_sems, final_waits)
    return nc, list(dbg_outs.keys())


def _consts():
    f32 = np.float32
    H, C = 6, 128
    log_g = np.log1p(-np.exp2(-5.0 - np.arange(H, dtype=f32))).astype(f32)
    idx = np.arange(C, dtype=f32)
    diff = idx[:, None] - idx[None, :]
    causal = diff >= 0
    decay = np.where(causal[None], np.exp(np.where(causal, diff, 0.0)[None] * log_g[:, None, None]), 0.0).astype(f32)
    decayT = np.ascontiguousarray(decay.transpose(2, 0, 1)).reshape(128, 768)
    k_w = np.exp((C - 1.0 - idx)[:, None] * log_g[None, :]).astype(f32)
    kwc = np.repeat(k_w[:, :, None], 128, axis=2).reshape(128, 768)
    q_w = np.exp((idx + 1.0)[:, None] * log_g[None, :]).astype(f32)
    qw = np.broadcast_to(q_w.T.reshape(1, 768), (128, 768))
    ident = np.eye(128, dtype=f32)
    perm = np.zeros((128, 128), f32)
    for m in range(128):
        perm[(m + 64) % 128, m] = 1.0
    ones = np.ones((128, 128), f32)
    pad = np.zeros((128, 384), f32)
    mask = (idx[:, None] <= idx[None, :]).astype(f32)
    cst = np.concatenate([decayT, kwc, qw, ident, perm, ones, pad, mask], axis=1).astype(f32)
    return np.ascontiguousarray(cst)


def _pack(inp):
    f32 = np.float32
    pk = np.zeros((128, NPK), f32)
    pk[:, 0:16] = inp["mix_norm"].reshape(2, 8, 128).transpose(2, 0, 1).reshape(128, 16)
    pk[:, 16:32] = inp["ple_norm"].reshape(2, 8, 128).transpose(2, 0, 1).reshape(128, 16)
    pk[:, 32:40] = inp["final_norm"].reshape(8, 128).T
    pk[:, 40:52] = inp["ret_norm"].reshape(2, 6, 128).transpose(2, 0, 1).reshape(128, 12)
    cw = inp["lru_conv_w"].reshape(2, 4, 8, 96).transpose(3, 0, 2, 1)
    pk[0:96, 52:116] = cw.reshape(96, 64)
    for name, c0 in (("lru_conv_b", 116), ("lru_b_a", 132), ("lru_b_x", 148), ("lru_lambda", 164)):
        pk[0:96, c0:c0 + 16] = inp[name].reshape(2, 8, 96).transpose(2, 0, 1).reshape(96, 16)
    half = 64
    inv = (10000.0 ** (-np.arange(half, dtype=f32) / half)).astype(f32)
    pk[:, 180] = np.concatenate([inv, inv])
    sgn = np.concatenate([-np.ones(64, f32), np.ones(64, f32)])
    pk[:, 181] = sgn
    pk[:, 182] = sgn * f32(128.0 ** -0.5)
    pk[:, 183] = EPS
    pk[:, 184] = 1.0
    return pk


def kernel(**inputs):
    inp = {k: np.asarray(v) for k, v in inputs.items()}
    f32 = np.float32
    nc, dbg_names = build_nc()
    cst = _consts()
    pk = _pack(inp)
    shared = {
        "w_in": np.ascontiguousarray(inp["w_in"], f32),
        "w_out": np.ascontiguousarray(inp["w_out"], f32),
        "w_gate": np.ascontiguousarray(inp["w_ple_gate"], f32),
        "w_pp": np.ascontiguousarray(inp["w_ple_proj"], f32),
        "w_a": np.ascontiguousarray(inp["lru_w_a"], f32),
        "w_x": np.ascontiguousarray(inp["lru_w_x"], f32),
        "w_sT": np.ascontiguousarray(inp["sgu_w_s"].transpose(0, 3, 1, 2).reshape(2, 128, 512), f32),
        "b_s": np.ascontiguousarray(inp["sgu_b_s"].reshape(1, 1024), f32),
        "ln_g": np.ascontiguousarray(inp["sgu_ln_g"], f32),
        "ln_b": np.ascontiguousarray(inp["sgu_ln_b"], f32),
        "pk": pk,
        "cst": cst,
    }
    in_maps = []
    for core in range(8):
        b = core % 4
        m = dict(shared)
        m["xT"] = np.ascontiguousarray(inp["x"][b].T, f32)
        m["pT"] = np.ascontiguousarray(inp["p"][:, b].transpose(0, 2, 1), f32)
        m["pos"] = np.ascontiguousarray(inp["positions"][b].reshape(1, S), np.int32)
        in_maps.append(m)
    res = run_bass_kernel_spmd(nc, in_maps, core_ids=list(range(8)))
    out = np.stack([np.ascontiguousarray(res.results[b]["outT"].T) for b in range(4)], axis=0).astype(f32)
    if DEBUG:
        kernel.debug = {n: res.results[0]["dbg_" + n] for n in dbg_names}
    return out
```

```python
import os
import numpy as np
import concourse.bass as bass
import concourse.mybir as mybir
from concourse.bass_utils import run_bass_kernel_spmd

F32 = mybir.dt.float32
BF16 = mybir.dt.bfloat16
I32 = mybir.dt.int32
ALU = mybir.AluOpType
AF = mybir.ActivationFunctionType

D = 1024
S = 4096
TT = 512
NT = S // TT
DEPTH = 2
D_IN = 6144
EPS = 1e-6
NPK = 192
PI = float(np.pi)

DEBUG = os.environ.get("KDEBUG", "") != ""
NT_RUN = int(os.environ.get("KNT", NT))
KSTAGE = int(os.environ.get("KSTAGE", 99))
KRET = int(os.environ.get("KRET", 99))
KQ = int(os.environ.get("KQ", 99))
KSKIPW = os.environ.get("KSKIPW", "") != ""


class Op:
    __slots__ = ("eng", "fn", "reads", "writes", "dma_sem", "dma_val", "deps", "signal", "val", "idx")


class Prog:
    ENGS = ("pe", "act", "dve", "pool", "sp")

    def __init__(self, nc):
        self.nc = nc
        self.ops = []
        self.last_w = {}
        self.readers = {}
        self.dma_counts = {}
        self.shared = set()

    def op(self, eng, fn, reads=(), writes=(), dma_sem=None):
        o = Op()
        o.eng, o.fn, o.reads, o.writes = eng, fn, tuple(reads), tuple(writes)
        o.dma_sem = dma_sem
        o.signal = False
        o.val = 0
        o.idx = len(self.ops)
        if dma_sem is not None:
            self.dma_counts[dma_sem] = self.dma_counts.get(dma_sem, 0) + 16
            o.dma_val = self.dma_counts[dma_sem]
        else:
            o.dma_val = 0
        deps = set()
        is_dma = dma_sem is not None
        def need(p):
            return p.dma_sem is not None or is_dma or not (p.eng == eng == "pe")
        for k in o.reads:
            w = self.last_w.get(k)
            if w is not None and need(w):
                deps.add(w.idx)
            if isinstance(k, tuple) and k and k[0] == "ps":
                for r in self.readers.get(k, ()):
                    if r.eng != eng:
                        deps.add(r.idx)
        for k in o.writes:
            w = self.last_w.get(k)
            if w is not None and need(w):
                deps.add(w.idx)
            for r in self.readers.get(k, ()):
                if need(r):
                    deps.add(r.idx)
        deps.discard(o.idx)
        o.deps = deps
        wset = set(o.writes)
        for k in o.writes:
            self.last_w[k] = o
            self.readers[k] = []
        for k in o.reads:
            if k not in wset:
                self.readers.setdefault(k, []).append(o)
        self.ops.append(o)
        return o

    def emit(self, block, eng_sems, final_waits):
        ops = self.ops
        waits = {}
        for en in self.ENGS:
            widx = {}
            wdma = {}
            for o in ops:
                if o.eng != en:
                    continue
                best = {}
                dm = {}
                for d in o.deps:
                    p = ops[d]
                    if p.dma_sem is not None:
                        v = self.dma_counts[p.dma_sem] if p.dma_sem in self.shared else p.dma_val
                        if v > dm.get(p.dma_sem, 0):
                            dm[p.dma_sem] = v
                    else:
                        if d > best.get(p.eng, -1):
                            best[p.eng] = d
                lst = []
                for pe_, d in best.items():
                    if d > widx.get(pe_, -1):
                        widx[pe_] = d
                        ops[d].signal = True
                        lst.append(("e", d))
                for sm, v in dm.items():
                    if v > wdma.get(sm, 0):
                        wdma[sm] = v
                        lst.append(("d", sm, v))
                waits[o.idx] = lst
        cnt = {e: 0 for e in self.ENGS}
        for o in ops:
            if o.dma_sem is None and o.signal:
                cnt[o.eng] += 1
                o.val = cnt[o.eng]

        def run(eng_name, eng):
            for o in ops:
                if o.eng != eng_name:
                    continue
                for w in waits[o.idx]:
                    if w[0] == "e":
                        p = ops[w[1]]
                        eng.wait_ge(eng_sems[p.eng], p.val)
                    else:
                        eng.wait_ge(w[1], w[2])
                ins = o.fn(eng)
                if o.dma_sem is not None:
                    ins.then_inc(o.dma_sem, 16)
                elif o.signal:
                    ins.then_inc(eng_sems[o.eng], 1)
            if eng_name == "sp":
                for s, v in final_waits():
                    eng.wait_ge(s, v)

        @block.tensor
        def _(e):
            run("pe", e)

        @block.scalar
        def _(e):
            run("act", e)

        @block.vector
        def _(e):
            run("dve", e)

        @block.gpsimd
        def _(e):
            run("pool", e)

        @block.sync
        def _(e):
            run("sp", e)


class Pool_:
    def __init__(self, items):
        self.free = list(items)

    def get(self):
        return self.free.pop(0)

    def put(self, x):
        self.free.append(x)


def build_nc():
    nc = bass.Bass("TRN2", target_bir_lowering=False)
    dt_in = lambda name, shape, dt=F32: nc.dram_tensor(name, shape, dt, kind="ExternalInput").ap()
    xT = dt_in("xT", [D, S])
    pT = dt_in("pT", [DEPTH, 256, S])
    pos = dt_in("pos", [1, S], I32)
    w_in = dt_in("w_in", [DEPTH, D, D_IN])
    w_out = dt_in("w_out", [DEPTH, 2048, D])
    w_gate = dt_in("w_gate", [DEPTH, D, D])
    w_pp = dt_in("w_pp", [DEPTH, 256, D])
    w_a = dt_in("w_a", [DEPTH, 8, 96, 96])
    w_x = dt_in("w_x", [DEPTH, 8, 96, 96])
    w_sT = dt_in("w_sT", [DEPTH, 128, 512])
    b_s = dt_in("b_s", [1, DEPTH * 512])
    ln_g = dt_in("ln_g", [DEPTH, 512])
    ln_b = dt_in("ln_b", [DEPTH, 512])
    pk_d = dt_in("pk", [128, NPK])
    cst_d = dt_in("cst", [128, 4 * 768 + 128])
    outT = nc.dram_tensor("outT", [D, S], F32, kind="ExternalOutput").ap()
    dbg_outs = {}

    wb_in = wb_out = wb_gate = wb_pp = pTb = None
    scr = nc.dram_tensor("scr", [DEPTH * 16, 128, 6144], BF16, kind="Internal").ap()

    from contextlib import ExitStack
    es = ExitStack()
    sb = lambda name, shape, dt=F32: es.enter_context(nc.sbuf_tensor("sb_" + name, shape, dt))
    sem = lambda name: es.enter_context(nc.semaphore(name))

    with es:
        hT = sb("hT", [128, 8, TT])
        yT = sb("yT", [128, 8, TT], BF16)
        NSLOT = 4
        ring = [sb(f"ring{i}", [128, 6144], BF16) for i in range(NSLOT)]
        mixL = sb("mixL", [128, 8, TT], BF16)
        mixR = sb("mixR", [128, 6, TT], BF16)
        mixS = sb("mixS", [128, 4, TT], BF16)
        NF, NB = 12, 8
        funits = [sb(f"fu{i}", [128, TT]) for i in range(NF)]
        bunits = [sb(f"bu{i}", [128, TT], BF16) for i in range(NB)]
        xls = [sb(f"xl{i}", [128, TT + 4]) for i in range(2)]
        qt = sb("qt", [128, 3, TT], BF16)
        kt = sb("kt", [128, 3, TT], BF16)
        qg = sb("qg", [128, 3, TT], BF16)
        qlo = sb("qlo", [128, 3, TT], BF16)
        klo = sb("klo", [128, 3, TT], BF16)
        vbf = sb("vbf", [128, 4, 384], BF16)
        vw = sb("vw", [128, 4, 384], BF16)
        ktm = sb("ktm", [128, 4, 384], BF16)
        Sst = sb("Sst", [128, DEPTH, 768])
        Sb = sb("Sb", [128, DEPTH, 768], BF16)
        ropeT = sb("ropeT", [128, 4, TT])
        cstT = sb("cstT", [128, 3 * 768])
        maskT = sb("maskT", [128, 128])
        cbT = sb("cbT", [128, 3 * 128], BF16)
        pk = sb("pk", [128, NPK])
        dv = sb("dv", [128, 64])
        vn = sb("vn", [128, 4, TT], BF16)
        wsT = sb("wsT", [128, DEPTH, 512], BF16)
        lng = sb("lng", [128, DEPTH, 512])
        lnb = sb("lnb", [128, DEPTH, 512])
        bsb = sb("bsb", [1, DEPTH * 512], BF16)
        pTt = sb("pTt", [128, 2, TT], BF16)
        wab = sb("wab", [96, DEPTH * 8 * 96], BF16)
        wxb = sb("wxb", [96, DEPTH * 8 * 96], BF16)
        hst = sb("hst", [128, DEPTH * 8])
        hist = sb("hist", [128, DEPTH * 8 * 4])
        posi = sb("posi", [128, TT], I32)
        ki = sb("ki", [128, TT], I32)
        small = sb("small", [128, 64])
        banks = [es.enter_context(nc.psum_tensor(f"ps{i}", [128, TT], F32)) for i in range(8)]

        eng_sems = {e: sem(f"s_{e}") for e in Prog.ENGS}
        s_cast = [sem(f"s_cast{i}") for i in range(4)]
        s_wb = [sem(f"s_wb{i}") for i in range(NSLOT)]
        s_rl = [sem(f"s_rl{i}") for i in range(NSLOT)]
        s_ring = [sem(f"s_ring{i}") for i in range(NSLOT)]
        s_ld = sem("s_ld")
        s_x = sem("s_x")
        s_pos = sem("s_pos")
        s_p = sem("s_p")
        s_out = [sem(f"s_out{i}") for i in range(8)]
        s_dbg = sem("s_dbg")

        pg = Prog(nc)
        pg.shared = set([s_ld] + list(s_cast))
        fpool = Pool_(range(NF))
        bpool = Pool_(range(NB))
        pspool = Pool_(range(8))

        ident = cbT[:, 0:128]
        perm = cbT[:, 128:256]
        ones = cbT[:, 256:384]
        decayT = cstT[:, 0:768]
        kwc = cstT[:, 768:1536]
        qwc = cstT[:, 1536:2304]
        epsc = pk[:, 183:184]
        onec = pk[:, 184:185]

        def dbg(name, ap, shape, dt, reads):
            if not DEBUG:
                return
            o = nc.dram_tensor("dbg_" + name, list(shape), dt, kind="ExternalOutput").ap()
            dbg_outs[name] = o
            pg.op("sp", lambda e, o=o, ap=ap: e.dma_start(out=o, in_=ap), reads=reads, writes=["dbgsem"], dma_sem=s_dbg)

        cast_i = [0]

        def cast_dma(out_ap, in_ap, wkey):
            sm = s_cast[cast_i[0] % len(s_cast)]
            cast_i[0] += 1
            pg.op("pool", lambda e: e.dma_start(out=out_ap, in_=in_ap), reads=[], writes=[wkey], dma_sem=sm)

        def ld(out_ap, in_ap, wkey):
            pg.op("sp", lambda e: e.dma_start(out=out_ap, in_=in_ap), reads=[], writes=[wkey], dma_sem=s_ld)

        ld(pk[:], pk_d, "pk")
        ld(cstT[:], cst_d[:, 0:2304], "cst")
        ld(maskT[:], cst_d[:, 2304 + 768:2304 + 768 + 128], "mask")
        for l in range(DEPTH):
            ld(lng[:, l, :], ln_g[l:l + 1, :].broadcast_to([128, 512]), ("lng", l))
            ld(lnb[:, l, :], ln_b[l:l + 1, :].broadcast_to([128, 512]), ("lnb", l))
        cast_dma(cbT[:], cst_d[:, 2304:2304 + 384], "cb")
        cast_dma(bsb[:], b_s, "bsb")
        cast_dma(wab[:].rearrange("c (l n d) -> c l n d", l=DEPTH, n=8), w_a.rearrange("l n c d -> c l n d"), "wab")
        cast_dma(wxb[:].rearrange("c (l n d) -> c l n d", l=DEPTH, n=8), w_x.rearrange("l n c d -> c l n d"), "wxb")
        for l in range(0):
            for g in range(8):
                cast_dma(wb_in[l, :, g * 768:(g + 1) * 768], w_in[l, :, g * 768:(g + 1) * 768], ("wb_in", l, g))
            for g in range(2):
                cast_dma(wb_out[l, g * 1024:(g + 1) * 1024, :], w_out[l, g * 1024:(g + 1) * 1024, :], ("wb_out", l))
            cast_dma(wb_gate[l], w_gate[l], ("wb_gate", l))
            cast_dma(wb_pp[l], w_pp[l], ("wb_pp", l))
            cast_dma(pTb[l], pT[l], ("pTb", l))

        pg.op("dve", lambda e: e.memset(Sst[:], 0.0), writes=["Sst"])
        pg.op("dve", lambda e: e.memset(Sb[:], 0.0), writes=["Sb"])
        pg.op("dve", lambda e: e.memset(hst[:], 0.0), writes=["hst"])
        pg.op("dve", lambda e: e.memset(hist[:], 0.0), writes=[("hist", i) for i in range(16)])
        for i in range(2):
            pg.op("dve", lambda e, i=i: e.memset(xls[i][:], 0.0), writes=[("xl", i)])
        for l in range(DEPTH):
            u = fpool.get()
            ld(funits[u][:], w_sT[l], ("f", u))
            pg.op("dve", lambda e, l=l, u=u: e.tensor_tensor(
                out=wsT[:, l, :].rearrange("p (g i) -> p g i", g=4),
                in0=funits[u][:].rearrange("p (g i) -> p g i", g=4),
                in1=maskT[:].unsqueeze(1).broadcast_to([128, 4, 128]), op=ALU.mult),
                reads=[("f", u), "mask"], writes=[("wsT", l)])
            fpool.put(u)
        pg.op("dve", lambda e: e.tensor_scalar(out=dv[:, 0:32], in0=pk[:, 132:164], scalar1=0.5, scalar2=None, op0=ALU.mult),
              reads=["pk"], writes=["dv0"])
        pg.op("act", lambda e: e.activation(out=small[:, 0:16], in_=pk[:, 164:180], func=AF.Exp, scale=-1.0),
              reads=["pk"], writes=["sm0"])
        pg.op("act", lambda e: e.activation(out=small[:, 16:32], in_=small[:, 0:16], func=AF.Ln, bias=onec, scale=1.0),
              reads=["sm0", "pk"], writes=["sm1"])
        pg.op("dve", lambda e: e.tensor_scalar(out=dv[:, 32:48], in0=small[:, 16:32], scalar1=-8.0, scalar2=None, op0=ALU.mult),
              reads=["sm1"], writes=["dv1"])
        pg.op("dve", lambda e: e.tensor_scalar(out=dv[:, 48:64], in0=small[:, 16:32], scalar1=-4.0, scalar2=None, op0=ALU.mult),
              reads=["sm1"], writes=["dv2"])
        DVK = ["dv0", "dv1", "dv2"]

        ring_i = [0]
        cur_t = [0]
        job_i = [0]

        def wload(dmas, rkeys):
            sl = ring_i[0] % NSLOT
            ring_i[0] += 1
            j = job_i[0]
            job_i[0] += 1
            for di, (vf, src) in enumerate(dmas):
                if cur_t[0] == 0:
                    pg.op("pool", lambda e, vf=vf, src=src, sl=sl: e.dma_start(out=vf(ring[sl]), in_=src),
                          reads=[], writes=[("ring", sl), "poolq"], dma_sem=s_ring[sl])
                    pg.op("sp", lambda e, vf=vf, sl=sl, j=j: e.dma_start(out=vf(scr[j]), in_=vf(ring[sl])),
                          reads=[("ring", sl)], writes=[("scr", j, di)], dma_sem=s_wb[sl])
                else:
                    pg.op("sp", lambda e, vf=vf, sl=sl, j=j: e.dma_start(out=vf(ring[sl]), in_=vf(scr[j])),
                          reads=[("scr", j, di)], writes=[("ring", sl)], dma_sem=s_rl[sl])
            return sl

        def w_in_job(l, g, ncols=768):
            c0 = g * 768
            src = w_in[l, :, c0:c0 + ncols].rearrange("(k p) c -> p k c", p=128)
            return wload([(lambda r, n=ncols: r[:, 0:8 * n].rearrange("p (k c) -> p k c", k=8), src)], [("wb_in", l, g)])

        def mm(bank, out_ap, lhsT, rhs, start, stop, reads, extra_w=()):
            pg.op("pe", lambda e: e.matmul(out_ap, lhsT, rhs, start=start, stop=stop),
                  reads=reads, writes=[("ps", bank)] + list(extra_w))

        def proj_fm(l, slot, ncols, c0, m, bank, mp=128):
            rv = ring[slot][:, 0:8 * ncols].rearrange("p (k c) -> p k c", k=8)
            for kc in range(8):
                mm(bank, banks[bank][0:m, :], rv[:, kc, c0:c0 + m], yT[:, kc, :], kc == 0, kc == 7,
                   [("ring", slot), ("yT", kc)])

        def proj_tm(slot, ncols, c0, n, blk, bank):
            rv = ring[slot][:, 0:8 * ncols].rearrange("p (k c) -> p k c", k=8)
            for kc in range(8):
                mm(bank, banks[bank][:, 0:n], yT[:, kc, blk * 128:(blk + 1) * 128], rv[:, kc, c0:c0 + n], kc == 0, kc == 7,
                   [("ring", slot), ("yT", kc)])

        def rms_stage(gcol, out_fn, out_keys_fn, out_dt_bf=True):
            bank = pspool.get()
            for c in range(8):
                b = bpool.get()
                pg.op("act", lambda e, c=c, b=b: e.activation(out=bunits[b][:], in_=hT[:, c, :], func=AF.Square),
                      reads=[("hT", c)], writes=[("b", b)])
                mm(bank, banks[bank][:], ones, bunits[b][:], c == 0, c == 7, [("b", b), "cb"])
                bpool.put(b)
            r = fpool.get()
            pg.op("act", lambda e: e.activation(out=funits[r][:], in_=banks[bank][:], func=AF.Ln, bias=epsc, scale=1.0 / D),
                  reads=[("ps", bank), "pk"], writes=[("f", r)])
            pg.op("act", lambda e: e.activation(out=funits[r][:], in_=funits[r][:], func=AF.Exp, scale=-0.5), reads=[("f", r)], writes=[("f", r)])
            pspool.put(bank)
            for c in range(8):
                pg.op("dve", lambda e, c=c: e.scalar_tensor_tensor(out=out_fn(c), in0=hT[:, c, :], scalar=gcol[:, c:c + 1],
                                                                  in1=funits[r][:], op0=ALU.mult, op1=ALU.mult),
                      reads=[("hT", c), ("f", r), "pk"], writes=out_keys_fn(c))
            fpool.put(r)

        gam = []
        for h in range(6):
            lg = np.log1p(-np.exp2(np.float32(-5.0 - h))).astype(np.float32)
            gam.append(float(np.exp(np.float32(128.0) * lg)))

        def do_tile(t):
            cur_t[0] = t
            job_i[0] = 0
            tok = slice(t * TT, (t + 1) * TT)
            pg.op("pool", lambda e, tok=tok: e.dma_start(out=hT[:], in_=xT.rearrange("(c p) t -> p c t", p=128)[:, :, tok]),
                  reads=[], writes=[("hT", c) for c in range(8)] + ["poolq"], dma_sem=s_x)
            pg.op("pool", lambda e, tok=tok: e.dma_start(out=posi[:], in_=pos[0:1, tok].broadcast_to([128, TT])),
                  reads=[], writes=["posi", "poolq"], dma_sem=s_pos)
            ang = fpool.get()
            tmp = fpool.get()
            kf = fpool.get()
            pg.op("dve", lambda e: e.tensor_copy(out=funits[ang][:], in_=posi[:]), reads=["posi"], writes=[("f", ang)])
            pg.op("dve", lambda e: e.tensor_scalar(out=funits[ang][:], in0=funits[ang][:], scalar1=pk[:, 180:181], scalar2=None, op0=ALU.mult),
                  reads=[("f", ang), "pk"], writes=[("f", ang)])
            C1 = 6.28125
            C2 = float(2.0 * np.pi - 6.28125)
            for which, off in ((1, 0.0), (0, 0.25)):
                pg.op("dve", lambda e, off=off: e.tensor_scalar(out=funits[tmp][:], in0=funits[ang][:], scalar1=float(1.0 / (2 * np.pi)), scalar2=off,
                                                               op0=ALU.mult, op1=ALU.add), reads=[("f", ang)], writes=[("f", tmp)])
                pg.op("dve", lambda e: e.tensor_copy(out=ki[:], in_=funits[tmp][:]), reads=[("f", tmp)], writes=["ki"])
                pg.op("dve", lambda e: e.tensor_copy(out=funits[kf][:], in_=ki[:]), reads=["ki"], writes=[("f", kf)])
                pg.op("dve", lambda e: e.scalar_tensor_tensor(out=funits[tmp][:], in0=funits[kf][:], scalar=-C1, in1=funits[ang][:], op0=ALU.mult, op1=ALU.add),
                      reads=[("f", kf), ("f", ang)], writes=[("f", tmp)])
                pg.op("dve", lambda e: e.scalar_tensor_tensor(out=funits[tmp][:], in0=funits[kf][:], scalar=-C2, in1=funits[tmp][:], op0=ALU.mult, op1=ALU.add),
                      reads=[("f", kf), ("f", tmp)], writes=[("f", tmp)])
                if off != 0.0:
                    pg.op("dve", lambda e: e.tensor_scalar(out=funits[tmp][:], in0=funits[tmp][:], scalar1=float(np.pi / 2), scalar2=None, op0=ALU.add),
                          reads=[("f", tmp)], writes=[("f", tmp)])
                pg.op("dve", lambda e: e.tensor_scalar(out=funits[tmp][:], in0=funits[tmp][:], scalar1=-3.1415925, scalar2=3.1415925, op0=ALU.max, op1=ALU.min),
                      reads=[("f", tmp)], writes=[("f", tmp)])
                pg.op("act", lambda e, which=which: e.activation(out=ropeT[:, which, :], in_=funits[tmp][:], func=AF.Sin),
                      reads=[("f", tmp)], writes=[("rope", which)])
            RS = float(128.0 ** -0.5)
            pg.op("dve", lambda e: e.tensor_scalar(out=ropeT[:, 2, :], in0=ropeT[:, 0, :], scalar1=RS, scalar2=None, op0=ALU.mult),
                  reads=[("rope", 0)], writes=[("rope", 2)])
            pg.op("dve", lambda e: e.tensor_scalar(out=ropeT[:, 3, :], in0=ropeT[:, 1, :], scalar1=pk[:, 182:183], scalar2=None, op0=ALU.mult),
                  reads=[("rope", 1), "pk"], writes=[("rope", 3)])
            pg.op("dve", lambda e: e.tensor_scalar(out=ropeT[:, 1, :], in0=ropeT[:, 1, :], scalar1=pk[:, 181:182], scalar2=None, op0=ALU.mult),
                  reads=[("rope", 1), ("rope", 3), "pk"], writes=[("rope", 1)])
            for u_ in (ang, tmp, kf):
                fpool.put(u_)

            def do_layer(l):
                if KSTAGE < 2:
                    return
                rms_stage(pk[:, l * 8:(l + 1) * 8], lambda c: yT[:, c, :], lambda c: [("yT", c)])
                if DEBUG and t == 0 and l == 0:
                    dbg("yT", yT[:], [128, 8, TT], BF16, [("yT", c_) for c_ in range(8)])

                if KSTAGE < 3:
                    return
                sx = w_in_job(l, 0)
                sg = w_in_job(l, 1)
                pend = []

                def lru_A(n):
                    bx = pspool.get()
                    proj_fm(l, sx, 768, n * 96, 96, bx)
                    xl = xls[n % 2]
                    xk = ("xl", n % 2)
                    hi = (l * 8 + n) * 4
                    pg.op("pool", lambda e: e.tensor_copy(out=xl[0:96, 0:3], in_=hist[0:96, hi:hi + 3]),
                          reads=[("hist", l * 8 + n)], writes=[xk])
                    pg.op("act", lambda e: e.activation(out=xl[0:96, 3:TT + 3], in_=banks[bx][0:96, :], func=AF.Copy),
                          reads=[("ps", bx)], writes=[xk])
                    pspool.put(bx)
                    bg = pspool.get()
                    proj_fm(l, sg, 768, n * 96, 96, bg)
                    return bg

                def lru_B1(n, bg):
                    xl = xls[n % 2]
                    xk = ("xl", n % 2)
                    col = l * 8 + n
                    cw = lambda k: pk[0:96, 52 + col * 4 + k:52 + col * 4 + k + 1]
                    xc = fpool.get()
                    XC = funits[xc][0:96, :]
                    pg.op("dve", lambda e: e.tensor_scalar(out=XC, in0=xl[0:96, 0:TT], scalar1=cw(0), scalar2=pk[0:96, 116 + col:117 + col],
                                                           op0=ALU.mult, op1=ALU.add), reads=[xk, "pk"], writes=[("f", xc)])
                    for k in (1, 2, 3):
                        pg.op("dve", lambda e, k=k: e.scalar_tensor_tensor(out=XC, in0=xl[0:96, k:k + TT], scalar=cw(k), in1=XC,
                                                                          op0=ALU.mult, op1=ALU.add), reads=[xk, "pk", ("f", xc)], writes=[("f", xc)])
                    hi = col * 4
                    pg.op("pool", lambda e: e.tensor_copy(out=hist[0:96, hi:hi + 3], in_=xl[0:96, TT:TT + 3]),
                          reads=[xk], writes=[("hist", col)])
                    xb = bpool.get()
                    pg.op("pool", lambda e: e.tensor_copy(out=bunits[xb][0:96, :], in_=XC), reads=[("f", xc)], writes=[("b", xb)])
                    ba = pspool.get()
                    bxx = pspool.get()
                    wo = col * 96
                    mm(ba, banks[ba][0:96, :], wab[0:96, wo:wo + 96], bunits[xb][0:96, :], True, True, [("b", xb), "wab"])
                    mm(bxx, banks[bxx][0:96, :], wxb[0:96, wo:wo + 96], bunits[xb][0:96, :], True, True, [("b", xb), "wxb"])
                    bpool.put(xb)
                    tha = fpool.get()
                    thx = fpool.get()
                    thg = fpool.get()
                    a2 = fpool.get()
                    A, X, G, A2 = (funits[i][0:96, :] for i in (tha, thx, thg, a2))
                    pg.op("act", lambda e: e.activation(out=A, in_=banks[ba][0:96, :], func=AF.Tanh, bias=dv[0:96, col:col + 1], scale=0.5),
                          reads=[("ps", ba)] + DVK, writes=[("f", tha)])
                    pg.op("act", lambda e: e.activation(out=X, in_=banks[bxx][0:96, :], func=AF.Tanh, bias=dv[0:96, 16 + col:17 + col], scale=0.5),
                          reads=[("ps", bxx)] + DVK, writes=[("f", thx)])
                    pspool.put(ba)
                    pspool.put(bxx)
                    pg.op("act", lambda e: e.activation(out=G, in_=banks[bg][0:96, :], func=AF.Tanh, scale=0.5),
                          reads=[("ps", bg)], writes=[("f", thg)])
                    pg.op("act", lambda e: e.activation(out=A2, in_=A, func=AF.Exp, bias=dv[0:96, 32 + col:33 + col], scale=dv[0:96, 32 + col:33 + col]),
                          reads=[("f", tha)] + DVK, writes=[("f", a2)])
                    pg.op("act", lambda e: e.activation(out=A, in_=A, func=AF.Exp, bias=dv[0:96, 48 + col:49 + col], scale=dv[0:96, 48 + col:49 + col]),
                          reads=[("f", tha)] + DVK, writes=[("f", tha)])
                    pg.op("act", lambda e: e.activation(out=A2, in_=A2, func=AF.Ln, bias=onec[0:96, :], scale=-1.0),
                          reads=[("f", a2), "pk"], writes=[("f", a2)])
                    pg.op("act", lambda e: e.activation(out=A2, in_=A2, func=AF.Exp, scale=0.5),
                          reads=[("f", a2)], writes=[("f", a2)])
                    return (bg, xc, tha, thx, thg, a2)

                def lru_B2(n, st_):
                    bg, xc, tha, thx, thg, a2 = st_
                    col = l * 8 + n
                    XC = funits[xc][0:96, :]
                    A, X, G, A2 = (funits[i][0:96, :] for i in (tha, thx, thg, a2))
                    pg.op("dve", lambda e: e.scalar_tensor_tensor(out=X, in0=X, scalar=1.0, in1=XC, op0=ALU.add, op1=ALU.mult),
                          reads=[("f", thx), ("f", xc)], writes=[("f", thx)])
                    pg.op("dve", lambda e: e.scalar_tensor_tensor(out=X, in0=X, scalar=0.5, in1=A2, op0=ALU.mult, op1=ALU.mult),
                          reads=[("f", thx), ("f", a2)], writes=[("f", thx)])
                    pg.op("dve", lambda e: e.tensor_tensor_scan(out=XC, data0=A, data1=X, initial=hst[0:96, col:col + 1], op0=ALU.mult, op1=ALU.add),
                          reads=[("f", tha), ("f", thx), "hst"], writes=[("f", xc)])
                    pg.op("pool", lambda e: e.tensor_copy(out=hst[0:96, col:col + 1], in_=funits[xc][0:96, TT - 1:TT]),
                          reads=[("f", xc)], writes=["hst"])
                    pg.op("dve", lambda e: e.scalar_tensor_tensor(out=G, in0=G, scalar=1.0, in1=banks[bg][0:96, :], op0=ALU.add, op1=ALU.mult),
                          reads=[("f", thg), ("ps", bg)], writes=[("f", thg)])
                    pspool.put(bg)
                    pg.op("dve", lambda e: e.scalar_tensor_tensor(out=mixL[0:96, n, :], in0=G, scalar=0.5, in1=XC, op0=ALU.mult, op1=ALU.mult),
                          reads=[("f", thg), ("f", xc)], writes=[("mixL", n)])
                    if DEBUG and t == 0 and l == 0 and n == 0:
                        dbg("lru_h0", funits[xc][:], [128, TT], F32, [("f", xc)])
                    for u_ in (xc, tha, thx, thg, a2):
                        fpool.put(u_)

                bgs = {}
                sts = {}
                bgs[0] = lru_A(0)
                bgs[1] = lru_A(1)
                sts[0] = lru_B1(0, bgs[0])
                for n in range(8):
                    if n + 2 < 8:
                        bgs[n + 2] = lru_A(n + 2)
                    if n + 1 < 8:
                        sts[n + 1] = lru_B1(n + 1, bgs[n + 1])
                    lru_B2(n, sts[n])

                if KSTAGE < 4:
                    return
                sq_ = w_in_job(l, 2)
                sk_ = w_in_job(l, 3)
                sv_ = w_in_job(l, 4)
                sgr = w_in_job(l, 5)
                for hg in range(2):
                    items = []
                    for hh in range(3):
                        h = hg * 3 + hh

                        def rope_item(slot, cidx, sidx, dst, dkey, hh=hh, h=h):
                            bq = pspool.get()
                            proj_fm(l, slot, 768, h * 128, 128, bq)
                            qb = bpool.get()
                            pg.op("act", lambda e, bq=bq, qb=qb: e.activation(out=bunits[qb][:], in_=banks[bq][:], func=AF.Copy),
                                  reads=[("ps", bq)], writes=[("b", qb), ("psr", bq)])
                            bs_ = pspool.get()
                            mm(bs_, banks[bs_][:], perm, bunits[qb][:], True, True, [("b", qb), "cb"])
                            bpool.put(qb)
                            yield
                            if KQ < 1:
                                pspool.put(bq)
                                pspool.put(bs_)
                                return
                            t1 = fpool.get()
                            t2 = fpool.get()
                            pg.op("dve", lambda e, bq=bq, t1=t1, cidx=cidx: e.scalar_tensor_tensor(out=funits[t1][:], in0=ropeT[:, cidx, :], scalar=1.0, in1=banks[bq][:], op0=ALU.mult, op1=ALU.mult),
                                  reads=[("ps", bq), ("psr", bq), ("rope", cidx)], writes=[("f", t1)])
                            pg.op("dve", lambda e, bs_=bs_, t2=t2, sidx=sidx: e.scalar_tensor_tensor(out=funits[t2][:], in0=ropeT[:, sidx, :], scalar=1.0, in1=banks[bs_][:], op0=ALU.mult, op1=ALU.mult),
                                  reads=[("ps", bs_), ("rope", sidx)], writes=[("f", t2)])
                            pspool.put(bq)
                            pspool.put(bs_)
                            if KQ < 2:
                                fpool.put(t1)
                                fpool.put(t2)
                                return
                            lo_t = qlo if dkey == "qt" else klo
                            lkey = "qlo" if dkey == "qt" else "klo"
                            pg.op("dve", lambda e, t1=t1, t2=t2: e.tensor_tensor(out=funits[t1][:], in0=funits[t1][:], in1=funits[t2][:], op=ALU.add),
                                  reads=[("f", t1), ("f", t2)], writes=[("f", t1)])
                            pg.op("act", lambda e, t1=t1, dst=dst, hh=hh: e.activation(out=dst[:, hh, :], in_=funits[t1][:], func=AF.Copy),
                                  reads=[("f", t1)], writes=[(dkey, hh)])
                            pg.op("pool", lambda e, t2=t2, dst=dst, hh=hh: e.tensor_copy(out=funits[t2][:], in_=dst[:, hh, :]),
                                  reads=[(dkey, hh), ("f", t2)], writes=[("f", t2)])
                            pg.op("pool", lambda e, t1=t1, t2=t2, lo_t=lo_t, hh=hh: e.tensor_tensor(
                                out=lo_t[:, hh, :], in0=funits[t1][:], in1=funits[t2][:], op=ALU.subtract),
                                reads=[("f", t1), ("f", t2)], writes=[(lkey, hh)])
                            if dkey == "qt":
                                pg.op("pool", lambda e, t1=t1, hh=hh, h=h: e.tensor_tensor(
                                    out=qg[:, hh, :].rearrange("p (n i) -> p n i", n=4),
                                    in0=funits[t1][:].rearrange("p (n i) -> p n i", n=4),
                                    in1=qwc[:, h * 128:(h + 1) * 128].unsqueeze(1).broadcast_to([128, 4, 128]), op=ALU.mult),
                                    reads=[("f", t1), "cst"], writes=[("qg", hh)])
                            fpool.put(t1)
                            fpool.put(t2)

                        for args_ in ((sq_, 0, 1, qt, "qt"), (sk_, 2, 3, kt, "kt")):
                            items.append(rope_item(*args_))
                    next(items[0])
                    for i_ in range(6):
                        if i_ + 1 < 6:
                            next(items[i_ + 1])
                        for _ in items[i_]:
                            pass
                    if DEBUG and t == 0 and l == 0 and hg == 0:
                        dbg("qt", qt[:], [128, 3, TT], BF16, [("qt", i) for i in range(3)])
                        dbg("kt", kt[:], [128, 3, TT], BF16, [("kt", i) for i in range(3)])
                    if KRET < 2:
                        continue
                    for blk in range(4):
                        bv = pspool.get()
                        proj_tm(sv_, 768, hg * 384, 384, blk, bv)
                        pg.op("act", lambda e, bv=bv, blk=blk: e.activation(out=vbf[:, blk, :], in_=banks[bv][:, 0:384], func=AF.Copy),
                              reads=[("ps", bv)], writes=[("vbf", blk)])
                        pg.op("dve", lambda e, bv=bv, blk=blk, hg=hg: e.scalar_tensor_tensor(out=vw[:, blk, :], in0=kwc[:, hg * 384:(hg + 1) * 384], scalar=1.0, in1=banks[bv][:, 0:384], op0=ALU.mult, op1=ALU.mult),
                              reads=[("ps", bv), "cst"], writes=[("vw", blk)])
                        pspool.put(bv)
                    if KRET < 3:
                        continue
                    for blk in range(4):
                        bt = pspool.get()
                        btb = banks[bt][:].bitcast(BF16)
                        for hh in range(3):
                            pg.op("pe", lambda e, btb=btb, hh=hh, blk=blk: e.transpose(btb[:, hh * 128:(hh + 1) * 128], kt[:, hh, blk * 128:(blk + 1) * 128], ident),
                                  reads=[("kt", hh), "cb"], writes=[("ps", bt)])
                        pg.op("act", lambda e, btb=btb, blk=blk: e.activation(out=ktm[:, blk, :], in_=btb[:, 0:384], func=AF.Copy),
                              reads=[("ps", bt)], writes=[("ktm", blk)])
                        pspool.put(bt)
                    if KRET < 4:
                        continue
                    bo = [pspool.get() for _ in range(3)]
                    for n in range(4):
                        cs = slice(n * 128, (n + 1) * 128)
                        for hh in range(3):
                            h = hg * 3 + hh
                            mm(bo[hh], banks[bo[hh]][:, cs], Sb[:, l, h * 128:(h + 1) * 128], qg[:, hh, cs], True, False, [("Sb", l, hg), ("qg", hh)])
                        bkv = pspool.get()
                        for hh in range(3):
                            mm(bkv, banks[bkv][:, hh * 128:(hh + 1) * 128], ktm[:, n, hh * 128:(hh + 1) * 128], vw[:, n, hh * 128:(hh + 1) * 128], True, True,
                               [("ktm", n), ("vw", n)])
                        for hh in range(3):
                            h = hg * 3 + hh
                            pg.op("dve", lambda e, h=h, hh=hh, bkv=bkv: e.scalar_tensor_tensor(
                                out=Sst[:, l, h * 128:(h + 1) * 128], in0=Sst[:, l, h * 128:(h + 1) * 128], scalar=gam[h],
                                in1=banks[bkv][:, hh * 128:(hh + 1) * 128], op0=ALU.mult, op1=ALU.add),
                                reads=[("ps", bkv), ("Sst", l, hg)], writes=[("Sst", l, hg)])
                        pspool.put(bkv)
                        pg.op("act", lambda e, hg=hg: e.activation(out=Sb[:, l, hg * 384:(hg + 1) * 384], in_=Sst[:, l, hg * 384:(hg + 1) * 384], func=AF.Copy),
                              reads=[("Sst", l, hg)], writes=[("Sb", l, hg)])
                        bsc = pspool.get()
                        for hh in range(3):
                            osc = banks[bsc][:, hh * 128:(hh + 1) * 128]
                            mm(bsc, osc, kt[:, hh, cs], qt[:, hh, cs], True, False, [("kt", hh), ("qt", hh)])
                            mm(bsc, osc, kt[:, hh, cs], qlo[:, hh, cs], False, False, [("kt", hh), ("qlo", hh)])
                            mm(bsc, osc, klo[:, hh, cs], qt[:, hh, cs], False, True, [("klo", hh), ("qt", hh)])
                        Pb = bpool.get()
                        pg.op("dve", lambda e, bsc=bsc, Pb=Pb, hg=hg: e.scalar_tensor_tensor(out=bunits[Pb][:, 0:384], in0=decayT[:, hg * 384:(hg + 1) * 384], scalar=1.0, in1=banks[bsc][:, 0:384], op0=ALU.mult, op1=ALU.mult),
                              reads=[("ps", bsc), "cst"], writes=[("b", Pb)])
                        pspool.put(bsc)
                        for hh in range(3):
                            h = hg * 3 + hh
                            mm(bo[hh], banks[bo[hh]][:, cs], vbf[:, n, hh * 128:(hh + 1) * 128], bunits[Pb][:, hh * 128:(hh + 1) * 128], False, True,
                               [("vbf", n), ("b", Pb)])
                        bpool.put(Pb)
                    if KRET < 5:
                        for b_ in bo:
                            pspool.put(b_)
                        continue
                    for hh in range(3):
                        h = hg * 3 + hh
                        bg = pspool.get()
                        proj_fm(l, sgr, 768, h * 128, 128, bg)
                        b = bpool.get()
                        pg.op("act", lambda e, b=b, boa=banks[bo[hh]][:]: e.activation(out=bunits[b][:], in_=boa, func=AF.Square),
                              reads=[("ps", bo[hh])], writes=[("b", b)])
                        bm = pspool.get()
                        mm(bm, banks[bm][:], ones, bunits[b][:], True, True, [("b", b), "cb"])
                        bpool.put(b)
                        r = fpool.get()
                        pg.op("act", lambda e, r=r, bm=bm: e.activation(out=funits[r][:], in_=banks[bm][:], func=AF.Ln, bias=epsc, scale=1.0 / 128),
                              reads=[("ps", bm), "pk"], writes=[("f", r)])
                        pspool.put(bm)
                        pg.op("act", lambda e, r=r: e.activation(out=funits[r][:], in_=funits[r][:], func=AF.Exp, scale=-0.5), reads=[("f", r)], writes=[("f", r)])
                        on = fpool.get()
                        pg.op("dve", lambda e, r=r, on=on, boa=banks[bo[hh]][:], h=h: e.scalar_tensor_tensor(out=funits[on][:], in0=funits[r][:], scalar=pk[:, 40 + l * 6 + h:41 + l * 6 + h],
                                                                                        in1=boa, op0=ALU.mult, op1=ALU.mult),
                              reads=[("ps", bo[hh]), ("f", r), "pk"], writes=[("f", on)])
                        fpool.put(r)
                        tg = fpool.get()
                        pg.op("act", lambda e, tg=tg, bg=bg: e.activation(out=funits[tg][:], in_=banks[bg][:], func=AF.Tanh, scale=0.5),
                              reads=[("ps", bg)], writes=[("f", tg)])
                        pg.op("dve", lambda e, tg=tg, bg=bg: e.scalar_tensor_tensor(out=funits[tg][:], in0=funits[tg][:], scalar=1.0, in1=banks[bg][:], op0=ALU.add, op1=ALU.mult),
                              reads=[("f", tg), ("ps", bg)], writes=[("f", tg)])
                        pspool.put(bg)
                        pg.op("dve", lambda e, tg=tg, on=on, h=h: e.scalar_tensor_tensor(out=mixR[:, h, :], in0=funits[tg][:], scalar=0.5, in1=funits[on][:], op0=ALU.mult, op1=ALU.mult),
                              reads=[("f", tg), ("f", on)], writes=[("mixR", h)])
                        if DEBUG and t == 0 and l == 0 and h == 0:
                            dbg("ret_on0", funits[on][:], [128, TT], F32, [("f", on)])
                        fpool.put(tg)
                        fpool.put(on)
                    for b_ in bo:
                        pspool.put(b_)

                if KSTAGE < 5:
                    return
                su = w_in_job(l, 6, 512)
                def w_cols_job(c0, ncols):
                    src = w_in[l, :, c0:c0 + ncols].rearrange("(k p) c -> p k c", p=128)
                    return wload([(lambda r, n=ncols: r[:, 0:8 * n].rearrange("p (k c) -> p k c", k=8), src)],
                                 [("wb_in", l, c0 // 768), ("wb_in", l, (c0 + ncols - 1) // 768)])
                svs = w_cols_job(5120, 512)
                sgs = w_cols_job(5632, 512)
                for blk in range(4):
                    bv = pspool.get()
                    proj_tm(svs, 512, 0, 512, blk, bv)
                    so = blk * 12
                    tq = fpool.get()
                    pg.op("act", lambda e, bv=bv, tq=tq: e.activation(out=funits[tq][:], in_=banks[bv][:], func=AF.Copy),
                          reads=[("ps", bv)], writes=[("f", tq)])
                    pspool.put(bv)
                    pg.op("dve", lambda e, tq=tq, so=so: e.bn_stats(out=small[:, so:so + 6], in_=funits[tq][:]), reads=[("f", tq)], writes=[("small", blk)])
                    pg.op("dve", lambda e, so=so: e.bn_aggr(out=small[:, so + 6:so + 8], in_=small[:, so:so + 6]), reads=[("small", blk)], writes=[("small", blk)])
                    pg.op("act", lambda e, so=so: e.activation(out=small[:, so + 8:so + 9], in_=small[:, so + 7:so + 8], func=AF.Ln, bias=epsc, scale=1.0),
                          reads=[("small", blk), "pk"], writes=[("small", blk)])
                    pg.op("act", lambda e, so=so: e.activation(out=small[:, so + 8:so + 9], in_=small[:, so + 8:so + 9], func=AF.Exp, scale=-0.5), reads=[("small", blk)], writes=[("small", blk)])
                    pg.op("dve", lambda e, so=so, tq=tq: e.scalar_tensor_tensor(out=funits[tq][:], in0=funits[tq][:], scalar=small[:, so + 6:so + 7], in1=lng[:, l, :],
                                                                                op0=ALU.subtract, op1=ALU.mult),
                          reads=[("f", tq), ("small", blk), ("lng", l)], writes=[("f", tq)])
                    pg.op("dve", lambda e, so=so, tq=tq, blk=blk: e.scalar_tensor_tensor(out=vn[:, blk, :], in0=funits[tq][:], scalar=small[:, so + 8:so + 9], in1=lnb[:, l, :],
                                                                                  op0=ALU.mult, op1=ALU.add),
                          reads=[("f", tq), ("small", blk), ("lnb", l)], writes=[("vn", blk)])
                    fpool.put(tq)
                for g in range(4):
                    gs = slice(g * 128, (g + 1) * 128)
                    bmix = pspool.get()
                    for n in range(4):
                        cs = slice(n * 128, (n + 1) * 128)
                        mm(bmix, banks[bmix][:, cs], vn[:, n, gs], wsT[:, l, gs], True, False, [("vn", n), ("wsT", l)])
                        mm(bmix, banks[bmix][:, cs], ones[0:1, :], bsb[0:1, l * 512 + g * 128:l * 512 + (g + 1) * 128], False, True, ["cb", "bsb"])
                    bu = pspool.get()
                    proj_fm(l, su, 512, g * 128, 128, bu)
                    bgg = pspool.get()
                    proj_fm(l, sgs, 512, g * 128, 128, bgg)
                    m_ = fpool.get()
                    pg.op("act", lambda e, m_=m_, bmix=bmix: e.activation(out=funits[m_][:], in_=banks[bmix][:], func=AF.Copy),
                          reads=[("ps", bmix)], writes=[("f", m_)])
                    pspool.put(bmix)
                    pg.op("dve", lambda e, m_=m_, bu=bu: e.scalar_tensor_tensor(out=funits[m_][:], in0=funits[m_][:], scalar=1.0, in1=banks[bu][:], op0=ALU.mult, op1=ALU.mult),
                          reads=[("f", m_), ("ps", bu)], writes=[("f", m_)])
                    pspool.put(bu)
                    tg = fpool.get()
                    pg.op("act", lambda e, tg=tg, bgg=bgg: e.activation(out=funits[tg][:], in_=banks[bgg][:], func=AF.Tanh, scale=0.5),
                          reads=[("ps", bgg)], writes=[("f", tg)])
                    pg.op("dve", lambda e, tg=tg, bgg=bgg: e.scalar_tensor_tensor(out=funits[tg][:], in0=funits[tg][:], scalar=1.0, in1=banks[bgg][:], op0=ALU.add, op1=ALU.mult),
                          reads=[("f", tg), ("ps", bgg)], writes=[("f", tg)])
                    pspool.put(bgg)
                    pg.op("dve", lambda e, tg=tg, m_=m_, g=g: e.scalar_tensor_tensor(out=mixS[:, g, :], in0=funits[tg][:], scalar=0.5, in1=funits[m_][:], op0=ALU.mult, op1=ALU.mult),
                          reads=[("f", tg), ("f", m_)], writes=[("mixS", g)])
                    fpool.put(tg)
                    fpool.put(m_)
                if DEBUG and t == 0 and l == 0:
                    dbg("mixL", mixL[:], [128, 8, TT], BF16, [("mixL", i) for i in range(8)])
                    dbg("mixR", mixR[:], [128, 6, TT], BF16, [("mixR", i) for i in range(6)])
                    dbg("mixS", mixS[:], [128, 4, TT], BF16, [("mixS", i) for i in range(4)])

                if KSTAGE < 6:
                    return
                for piece in range(4):
                    c0 = piece * 256
                    dm = [
                        (lambda r: r[0:96, 0:2048].rearrange("p (n c) -> p n c", n=8),
                         w_out[l, 0:768, c0:c0 + 256].rearrange("(n p) c -> p n c", p=96)),
                        (lambda r: r[:, 2048:3584].rearrange("p (n c) -> p n c", n=6),
                         w_out[l, 768:1536, c0:c0 + 256].rearrange("(n p) c -> p n c", p=128)),
                        (lambda r: r[:, 3584:4608].rearrange("p (n c) -> p n c", n=4),
                         w_out[l, 1536:2048, c0:c0 + 256].rearrange("(n p) c -> p n c", p=128)),
                    ]
                    so_ = wload(dm, [("wb_out", l)])
                    R = ring[so_]
                    for d2 in range(2):
                        c = piece * 2 + d2
                        bw = pspool.get()
                        steps = []
                        for n in range(8):
                            steps.append((R[0:96, n * 256 + d2 * 128:n * 256 + d2 * 128 + 128], mixL[0:96, n, :], ("mixL", n)))
                        for h in range(6):
                            steps.append((R[:, 2048 + h * 256 + d2 * 128:2048 + h * 256 + d2 * 128 + 128], mixR[:, h, :], ("mixR", h)))
                        for g in range(4):
                            steps.append((R[:, 3584 + g * 256 + d2 * 128:3584 + g * 256 + d2 * 128 + 128], mixS[:, g, :], ("mixS", g)))
                        for i, (lh, rh, key) in enumerate(steps):
                            mm(bw, banks[bw][:], lh, rh, i == 0, i == len(steps) - 1, [("ring", so_), key])
                        pg.op("dve", lambda e, c=c, bw=bw: e.scalar_tensor_tensor(out=hT[:, c, :], in0=hT[:, c, :], scalar=1.0, in1=banks[bw][:], op0=ALU.mult, op1=ALU.add),
                              reads=[("hT", c), ("ps", bw)], writes=[("hT", c)])
                        pspool.put(bw)

                if KSTAGE < 7:
                    return
                rms_stage(pk[:, 16 + l * 8:16 + (l + 1) * 8], lambda c: yT[:, c, :], lambda c: [("yT", c)])
                pg.op("pool", lambda e, tok=tok: e.dma_start(out=pTt[:], in_=pT[l].rearrange("(k p) t -> p k t", p=128)[:, :, tok]),
                      reads=[], writes=["pTt", "poolq"], dma_sem=s_p)
                spp = wload([(lambda r: r[:, 0:2048].rearrange("p (k c) -> p k c", k=2), w_pp[l].rearrange("(k p) c -> p k c", p=128))], [("wb_pp", l)])
                for half in range(2):
                    sgt = wload([(lambda r: r[:, 0:4096].rearrange("p (k c) -> p k c", k=8),
                                  w_gate[l, :, half * 512:(half + 1) * 512].rearrange("(k p) c -> p k c", p=128))], [("wb_gate", l)])
                    for cc in range(4):
                        c = half * 4 + cc
                        bgt = pspool.get()
                        proj_fm(l, sgt, 512, cc * 128, 128, bgt)
                        bpp = pspool.get()
                        rp = ring[spp][:, 0:2048].rearrange("p (k c) -> p k c", k=2)
                        for k in range(2):
                            mm(bpp, banks[bpp][:], rp[:, k, c * 128:(c + 1) * 128], pTt[:, k, :], k == 0, k == 1, [("ring", spp), "pTt"])
                        th = fpool.get()
                        pg.op("act", lambda e, th=th, bgt=bgt: e.activation(out=funits[th][:], in_=banks[bgt][:], func=AF.Tanh, scale=0.5),
                              reads=[("ps", bgt)], writes=[("f", th)])
                        pspool.put(bgt)
                        pg.op("dve", lambda e, th=th, bpp=bpp: e.scalar_tensor_tensor(out=funits[th][:], in0=funits[th][:], scalar=1.0, in1=banks[bpp][:], op0=ALU.add, op1=ALU.mult),
                              reads=[("f", th), ("ps", bpp)], writes=[("f", th)])
                        pspool.put(bpp)
                        pg.op("dve", lambda e, th=th, c=c: e.scalar_tensor_tensor(out=hT[:, c, :], in0=funits[th][:], scalar=0.5, in1=hT[:, c, :], op0=ALU.mult, op1=ALU.add),
                              reads=[("f", th), ("hT", c)], writes=[("hT", c)])
                        fpool.put(th)
                if DEBUG and t == 0:
                    dbg(f"h_l{l}", hT[:], [128, 8, TT], F32, [("hT", c) for c in range(8)])

            for l_ in range(DEPTH):
                do_layer(l_)

            outs = {}

            def fin_out(c):
                u = fpool.get()
                outs[c] = u
                return funits[u][:]

            def fin_keys(c):
                return [("f", outs[c])]

            bank = pspool.get()
            for c in range(8):
                b = bpool.get()
                pg.op("act", lambda e, c=c, b=b: e.activation(out=bunits[b][:], in_=hT[:, c, :], func=AF.Square), reads=[("hT", c)], writes=[("b", b)])
                mm(bank, banks[bank][:], ones, bunits[b][:], c == 0, c == 7, [("b", b), "cb"])
                bpool.put(b)
            r = fpool.get()
            pg.op("act", lambda e, r=r, bank=bank: e.activation(out=funits[r][:], in_=banks[bank][:], func=AF.Ln, bias=epsc, scale=1.0 / D),
                  reads=[("ps", bank), "pk"], writes=[("f", r)])
            pg.op("act", lambda e, r=r: e.activation(out=funits[r][:], in_=funits[r][:], func=AF.Exp, scale=-0.5), reads=[("f", r)], writes=[("f", r)])
            pspool.put(bank)
            for c in range(8):
                u = fpool.get()
                pg.op("dve", lambda e, c=c, u=u, r=r: e.scalar_tensor_tensor(out=funits[u][:], in0=hT[:, c, :], scalar=pk[:, 32 + c:33 + c], in1=funits[r][:],
                                                                          op0=ALU.mult, op1=ALU.mult),
                      reads=[("hT", c), ("f", r), "pk"], writes=[("f", u)])
                pg.op("pool", lambda e, c=c, u=u, tok=tok: e.dma_start(out=outT[c * 128:(c + 1) * 128, tok], in_=funits[u][:]),
                      reads=[("f", u)], writes=[("outsem", c), "poolq"], dma_sem=s_out[c])
                fpool.put(u)
            fpool.put(r)

        for t_ in range(NT_RUN):
            do_tile(t_)

        def final_waits():
            res = []
            for s_ in s_out:
                if s_ in pg.dma_counts:
                    res.append((s_, pg.dma_counts[s_]))
            if s_dbg in pg.dma_counts:
                res.append((s_dbg, pg.dma_counts[s_dbg]))
            return res

        with nc.Block() as block:
            pg.emit(block, eng_sems, final_waits)
    return nc, list(dbg_outs.keys())


def _consts():
    f32 = np.float32
    H, C = 6, 128
    log_g = np.log1p(-np.exp2(-5.0 - np.arange(H, dtype=f32))).astype(f32)
    idx = np.arange(C, dtype=f32)
    diff = idx[:, None] - idx[None, :]
    causal = diff >= 0
    decay = np.where(causal[None], np.exp(np.where(causal, diff, 0.0)[None] * log_g[:, None, None]), 0.0).astype(f32)
    decayT = np.ascontiguousarray(decay.transpose(2, 0, 1)).reshape(128, 768)
    k_w = np.exp((C - 1.0 - idx)[:, None] * log_g[None, :]).astype(f32)
    kwc = np.repeat(k_w[:, :, None], 128, axis=2).reshape(128, 768)
    q_w = np.exp((idx + 1.0)[:, None] * log_g[None, :]).astype(f32)
    qw = np.broadcast_to(q_w.T.reshape(1, 768), (128, 768))
    ident = np.eye(128, dtype=f32)
    perm = np.zeros((128, 128), f32)
    for m in range(128):
        perm[(m + 64) % 128, m] = 1.0
    ones = np.ones((128, 128), f32)
    pad = np.zeros((128, 384), f32)
    mask = (idx[:, None] <= idx[None, :]).astype(f32)
    cst = np.concatenate([decayT, kwc, qw, ident, perm, ones, pad, mask], axis=1).astype(f32)
    return np.ascontiguousarray(cst)


def _pack(inp):
    f32 = np.float32
    pk = np.zeros((128, NPK), f32)
    pk[:, 0:16] = inp["mix_norm"].reshape(2, 8, 128).transpose(2, 0, 1).reshape(128, 16)
    pk[:, 16:32] = inp["ple_norm"].reshape(2, 8, 128).transpose(2, 0, 1).reshape(128, 16)
    pk[:, 32:40] = inp["final_norm"].reshape(8, 128).T
    pk[:, 40:52] = inp["ret_norm"].reshape(2, 6, 128).transpose(2, 0, 1).reshape(128, 12)
    cw = inp["lru_conv_w"].reshape(2, 4, 8, 96).transpose(3, 0, 2, 1)
    pk[0:96, 52:116] = cw.reshape(96, 64)
    for name, c0 in (("lru_conv_b", 116), ("lru_b_a", 132), ("lru_b_x", 148), ("lru_lambda", 164)):
        pk[0:96, c0:c0 + 16] = inp[name].reshape(2, 8, 96).transpose(2, 0, 1).reshape(96, 16)
    half = 64
    inv = (10000.0 ** (-np.arange(half, dtype=f32) / half)).astype(f32)
    pk[:, 180] = np.concatenate([inv, inv])
    sgn = np.concatenate([-np.ones(64, f32), np.ones(64, f32)])
    pk[:, 181] = sgn
    pk[:, 182] = sgn * f32(128.0 ** -0.5)
    pk[:, 183] = EPS
    pk[:, 184] = 1.0
    return pk


def kernel(**inputs):
    inp = {k: np.asarray(v) for k, v in inputs.items()}
    f32 = np.float32
    nc, dbg_names = build_nc()
    cst = _consts()
    pk = _pack(inp)
    shared = {
        "w_in": np.ascontiguousarray(inp["w_in"], f32),
        "w_out": np.ascontiguousarray(inp["w_out"], f32),
        "w_gate": np.ascontiguousarray(inp["w_ple_gate"], f32),
        "w_pp": np.ascontiguousarray(inp["w_ple_proj"], f32),
        "w_a": np.ascontiguousarray(inp["lru_w_a"], f32),
        "w_x": np.ascontiguousarray(inp["lru_w_x"], f32),
        "w_sT": np.ascontiguousarray(inp["sgu_w_s"].transpose(0, 3, 1, 2).reshape(2, 128, 512), f32),
        "b_s": np.ascontiguousarray(inp["sgu_b_s"].reshape(1, 1024), f32),
        "ln_g": np.ascontiguousarray(inp["sgu_ln_g"], f32),
        "ln_b": np.ascontiguousarray(inp["sgu_ln_b"], f32),
        "pk": pk,
        "cst": cst,
    }
    in_maps = []
    for core in range(8):
        b = core % 4
        m = dict(shared)
        m["xT"] = np.ascontiguousarray(inp["x"][b].T, f32)
        m["pT"] = np.ascontiguousarray(inp["p"][:, b].transpose(0, 2, 1), f32)
        m["pos"] = np.ascontiguousarray(inp["positions"][b].reshape(1, S), np.int32)
        in_maps.append(m)
    res = run_bass_kernel_spmd(nc, in_maps, core_ids=list(range(8)))
    out = np.stack([np.ascontiguousarray(res.results[b]["outT"].T) for b in range(4)], axis=0).astype(f32)
    if DEBUG:
        kernel.debug = {n: res.results[0]["dbg_" + n] for n in dbg_names}
    return out
```

```python
import os
import numpy as np
import concourse.bass as bass
import concourse.mybir as mybir
from concourse.bass_utils import run_bass_kernel_spmd

F32 = mybir.dt.float32
BF16 = mybir.dt.bfloat16
I32 = mybir.dt.int32
ALU = mybir.AluOpType
AF = mybir.ActivationFunctionType

D = 1024
S = 4096
TT = 512
NT = S // TT
DEPTH = 2
D_IN = 6144
EPS = 1e-6
NPK = 192
PI = float(np.pi)

DEBUG = os.environ.get("KDEBUG", "") != ""
NT_RUN = int(os.environ.get("KNT", NT))
KSTAGE = int(os.environ.get("KSTAGE", 99))
KRET = int(os.environ.get("KRET", 99))
KQ = int(os.environ.get("KQ", 99))
KSKIPW = os.environ.get("KSKIPW", "") != ""


class Op:
    __slots__ = ("eng", "fn", "reads", "writes", "dma_sem", "dma_val", "deps", "signal", "val", "idx")


class Prog:
    ENGS = ("pe", "act", "dve", "pool", "sp")

    def __init__(self, nc):
        self.nc = nc
        self.ops = []
        self.last_w = {}
        self.readers = {}
        self.dma_counts = {}
        self.shared = set()

    def op(self, eng, fn, reads=(), writes=(), dma_sem=None):
        o = Op()
        o.eng, o.fn, o.reads, o.writes = eng, fn, tuple(reads), tuple(writes)
        o.dma_sem = dma_sem
        o.signal = False
        o.val = 0
        o.idx = len(self.ops)
        if dma_sem is not None:
            self.dma_counts[dma_sem] = self.dma_counts.get(dma_sem, 0) + 16
            o.dma_val = self.dma_counts[dma_sem]
        else:
            o.dma_val = 0
        deps = set()
        is_dma = dma_sem is not None
        def need(p):
            return p.dma_sem is not None or is_dma or not (p.eng == eng == "pe")
        for k in o.reads:
            w = self.last_w.get(k)
            if w is not None and need(w):
                deps.add(w.idx)
            if isinstance(k, tuple) and k and k[0] == "ps":
                for r in self.readers.get(k, ()):
                    if r.eng != eng:
                        deps.add(r.idx)
        for k in o.writes:
            w = self.last_w.get(k)
            if w is not None and need(w):
                deps.add(w.idx)
            for r in self.readers.get(k, ()):
                if need(r):
                    deps.add(r.idx)
        deps.discard(o.idx)
        o.deps = deps
        wset = set(o.writes)
        for k in o.writes:
            self.last_w[k] = o
            self.readers[k] = []
        for k in o.reads:
            if k not in wset:
                self.readers.setdefault(k, []).append(o)
        self.ops.append(o)
        return o

    def emit(self, block, eng_sems, final_waits):
        ops = self.ops
        waits = {}
        for en in self.ENGS:
            widx = {}
            wdma = {}
            for o in ops:
                if o.eng != en:
                    continue
                best = {}
                dm = {}
                for d in o.deps:
                    p = ops[d]
                    if p.dma_sem is not None:
                        v = self.dma_counts[p.dma_sem] if p.dma_sem in self.shared else p.dma_val
                        if v > dm.get(p.dma_sem, 0):
                            dm[p.dma_sem] = v
                    else:
                        if d > best.get(p.eng, -1):
                            best[p.eng] = d
                lst = []
                for pe_, d in best.items():
                    if d > widx.get(pe_, -1):
                        widx[pe_] = d
                        ops[d].signal = True
                        lst.append(("e", d))
                for sm, v in dm.items():
                    if v > wdma.get(sm, 0):
                        wdma[sm] = v
                        lst.append(("d", sm, v))
                waits[o.idx] = lst
        cnt = {e: 0 for e in self.ENGS}
        for o in ops:
            if o.dma_sem is None and o.signal:
                cnt[o.eng] += 1
                o.val = cnt[o.eng]

        def run(eng_name, eng):
            for o in ops:
                if o.eng != eng_name:
                    continue
                for w in waits[o.idx]:
                    if w[0] == "e":
                        p = ops[w[1]]
                        eng.wait_ge(eng_sems[p.eng], p.val)
                    else:
                        eng.wait_ge(w[1], w[2])
                ins = o.fn(eng)
                if o.dma_sem is not None:
                    ins.then_inc(o.dma_sem, 16)
                elif o.signal:
                    ins.then_inc(eng_sems[o.eng], 1)
            if eng_name == "sp":
                for s, v in final_waits():
                    eng.wait_ge(s, v)

        @block.tensor
        def _(e):
            run("pe", e)

        @block.scalar
        def _(e):
            run("act", e)

        @block.vector
        def _(e):
            run("dve", e)

        @block.gpsimd
        def _(e):
            run("pool", e)

        @block.sync
        def _(e):
            run("sp", e)


class Pool_:
    def __init__(self, items):
        self.free = list(items)

    def get(self):
        return self.free.pop(0)

    def put(self, x):
        self.free.append(x)


def build_nc():
    nc = bass.Bass("TRN2", target_bir_lowering=False)
    dt_in = lambda name, shape, dt=F32: nc.dram_tensor(name, shape, dt, kind="ExternalInput").ap()
    xT = dt_in("xT", [D, S])
    pT = dt_in("pT", [DEPTH, 256, S])
    pos = dt_in("pos", [1, S], I32)
    w_in = dt_in("w_in", [DEPTH, D, D_IN])
    w_out = dt_in("w_out", [DEPTH, 2048, D])
    w_gate = dt_in("w_gate", [DEPTH, D, D])
    w_pp = dt_in("w_pp", [DEPTH, 256, D])
    w_a = dt_in("w_a", [DEPTH, 8, 96, 96])
    w_x = dt_in("w_x", [DEPTH, 8, 96, 96])
    w_sT = dt_in("w_sT", [DEPTH, 128, 512])
    b_s = dt_in("b_s", [1, DEPTH * 512])
    ln_g = dt_in("ln_g", [DEPTH, 512])
    ln_b = dt_in("ln_b", [DEPTH, 512])
    pk_d = dt_in("pk", [128, NPK])
    cst_d = dt_in("cst", [128, 4 * 768 + 128])
    outT = nc.dram_tensor("outT", [D, S], F32, kind="ExternalOutput").ap()
    dbg_outs = {}

    wb_in = wb_out = wb_gate = wb_pp = pTb = None
    scr = nc.dram_tensor("scr", [DEPTH * 16, 128, 6144], BF16, kind="Internal").ap()

    from contextlib import ExitStack
    es = ExitStack()
    sb = lambda name, shape, dt=F32: es.enter_context(nc.sbuf_tensor("sb_" + name, shape, dt))
    sem = lambda name: es.enter_context(nc.semaphore(name))

    with es:
        hT = sb("hT", [128, 8, TT])
        yT = sb("yT", [128, 8, TT], BF16)
        NSLOT = 4
        ring = [sb(f"ring{i}", [128, 6144], BF16) for i in range(NSLOT)]
        mixL = sb("mixL", [128, 8, TT], BF16)
        mixR = sb("mixR", [128, 6, TT], BF16)
        mixS = sb("mixS", [128, 4, TT], BF16)
        NF, NB = 12, 8
        funits = [sb(f"fu{i}", [128, TT]) for i in range(NF)]
        bunits = [sb(f"bu{i}", [128, TT], BF16) for i in range(NB)]
        xls = [sb(f"xl{i}", [128, TT + 4]) for i in range(2)]
        qt = sb("qt", [128, 3, TT], BF16)
        kt = sb("kt", [128, 3, TT], BF16)
        qg = sb("qg", [128, 3, TT], BF16)
        qlo = sb("qlo", [128, 3, TT], BF16)
        klo = sb("klo", [128, 3, TT], BF16)
        vbf = sb("vbf", [128, 4, 384], BF16)
        vw = sb("vw", [128, 4, 384], BF16)
        ktm = sb("ktm", [128, 4, 384], BF16)
        Sst = sb("Sst", [128, DEPTH, 768])
        Sb = sb("Sb", [128, DEPTH, 768], BF16)
        ropeT = sb("ropeT", [128, 4, TT])
        cstT = sb("cstT", [128, 3 * 768])
        maskT = sb("maskT", [128, 128])
        cbT = sb("cbT", [128, 3 * 128], BF16)
        pk = sb("pk", [128, NPK])
        dv = sb("dv", [128, 64])
        vn = sb("vn", [128, 4, TT], BF16)
        wsT = sb("wsT", [128, DEPTH, 512], BF16)
        lng = sb("lng", [128, DEPTH, 512])
        lnb = sb("lnb", [128, DEPTH, 512])
        bsb = sb("bsb", [1, DEPTH * 512], BF16)
        pTt = sb("pTt", [128, 2, TT], BF16)
        wab = sb("wab", [96, DEPTH * 8 * 96], BF16)
        wxb = sb("wxb", [96, DEPTH * 8 * 96], BF16)
        hst = sb("hst", [128, DEPTH * 8])
        hist = sb("hist", [128, DEPTH * 8 * 4])
        posi = sb("posi", [128, TT], I32)
        ki = sb("ki", [128, TT], I32)
        small = sb("small", [128, 64])
        banks = [es.enter_context(nc.psum_tensor(f"ps{i}", [128, TT], F32)) for i in range(8)]

        eng_sems = {e: sem(f"s_{e}") for e in Prog.ENGS}
        s_cast = [sem(f"s_cast{i}") for i in range(4)]
        s_wb = [sem(f"s_wb{i}") for i in range(NSLOT)]
        s_rl = [sem(f"s_rl{i}") for i in range(NSLOT)]
        s_ring = [sem(f"s_ring{i}") for i in range(NSLOT)]
        s_ld = sem("s_ld")
        s_x = sem("s_x")
        s_pos = sem("s_pos")
        s_p = sem("s_p")
        s_out = [sem(f"s_out{i}") for i in range(8)]
        s_dbg = sem("s_dbg")

        pg = Prog(nc)
        pg.shared = set([s_ld] + list(s_cast))
        fpool = Pool_(range(NF))
        bpool = Pool_(range(NB))
        pspool = Pool_(range(8))

        ident = cbT[:, 0:128]
        perm = cbT[:, 128:256]
        ones = cbT[:, 256:384]
        decayT = cstT[:, 0:768]
        kwc = cstT[:, 768:1536]
        qwc = cstT[:, 1536:2304]
        epsc = pk[:, 183:184]
        onec = pk[:, 184:185]

        def dbg(name, ap, shape, dt, reads):
            if not DEBUG:
                return
            o = nc.dram_tensor("dbg_" + name, list(shape), dt, kind="ExternalOutput").ap()
            dbg_outs[name] = o
            pg.op("sp", lambda e, o=o, ap=ap: e.dma_start(out=o, in_=ap), reads=reads, writes=["dbgsem"], dma_sem=s_dbg)

        cast_i = [0]

        def cast_dma(out_ap, in_ap, wkey):
            sm = s_cast[cast_i[0] % len(s_cast)]
            cast_i[0] += 1
            pg.op("pool", lambda e: e.dma_start(out=out_ap, in_=in_ap), reads=[], writes=[wkey], dma_sem=sm)

        def ld(out_ap, in_ap, wkey):
            pg.op("sp", lambda e: e.dma_start(out=out_ap, in_=in_ap), reads=[], writes=[wkey], dma_sem=s_ld)

        ld(pk[:], pk_d, "pk")
        ld(cstT[:], cst_d[:, 0:2304], "cst")
        ld(maskT[:], cst_d[:, 2304 + 768:2304 + 768 + 128], "mask")
        for l in range(DEPTH):
            ld(lng[:, l, :], ln_g[l:l + 1, :].broadcast_to([128, 512]), ("lng", l))
            ld(lnb[:, l, :], ln_b[l:l + 1, :].broadcast_to([128, 512]), ("lnb", l))
        cast_dma(cbT[:], cst_d[:, 2304:2304 + 384], "cb")
        cast_dma(bsb[:], b_s, "bsb")
        cast_dma(wab[:].rearrange("c (l n d) -> c l n d", l=DEPTH, n=8), w_a.rearrange("l n c d -> c l n d"), "wab")
        cast_dma(wxb[:].rearrange("c (l n d) -> c l n d", l=DEPTH, n=8), w_x.rearrange("l n c d -> c l n d"), "wxb")
        for l in range(0):
            for g in range(8):
                cast_dma(wb_in[l, :, g * 768:(g + 1) * 768], w_in[l, :, g * 768:(g + 1) * 768], ("wb_in", l, g))
            for g in range(2):
                cast_dma(wb_out[l, g * 1024:(g + 1) * 1024, :], w_out[l, g * 1024:(g + 1) * 1024, :], ("wb_out", l))
            cast_dma(wb_gate[l], w_gate[l], ("wb_gate", l))
            cast_dma(wb_pp[l], w_pp[l], ("wb_pp", l))
            cast_dma(pTb[l], pT[l], ("pTb", l))

        pg.op("dve", lambda e: e.memset(Sst[:], 0.0), writes=["Sst"])
        pg.op("dve", lambda e: e.memset(Sb[:], 0.0), writes=["Sb"])
        pg.op("dve", lambda e: e.memset(hst[:], 0.0), writes=["hst"])
        pg.op("dve", lambda e: e.memset(hist[:], 0.0), writes=[("hist", i) for i in range(16)])
        for i in range(2):
            pg.op("dve", lambda e, i=i: e.memset(xls[i][:], 0.0), writes=[("xl", i)])
        for l in range(DEPTH):
            u = fpool.get()
            ld(funits[u][:], w_sT[l], ("f", u))
            pg.op("dve", lambda e, l=l, u=u: e.tensor_tensor(
                out=wsT[:, l, :].rearrange("p (g i) -> p g i", g=4),
                in0=funits[u][:].rearrange("p (g i) -> p g i", g=4),
                in1=maskT[:].unsqueeze(1).broadcast_to([128, 4, 128]), op=ALU.mult),
                reads=[("f", u), "mask"], writes=[("wsT", l)])
            fpool.put(u)
        pg.op("dve", lambda e: e.tensor_scalar(out=dv[:, 0:32], in0=pk[:, 132:164], scalar1=0.5, scalar2=None, op0=ALU.mult),
              reads=["pk"], writes=["dv0"])
        pg.op("act", lambda e: e.activation(out=small[:, 0:16], in_=pk[:, 164:180], func=AF.Exp, scale=-1.0),
              reads=["pk"], writes=["sm0"])
        pg.op("act", lambda e: e.activation(out=small[:, 16:32], in_=small[:, 0:16], func=AF.Ln, bias=onec, scale=1.0),
              reads=["sm0", "pk"], writes=["sm1"])
        pg.op("dve", lambda e: e.tensor_scalar(out=dv[:, 32:48], in0=small[:, 16:32], scalar1=-8.0, scalar2=None, op0=ALU.mult),
              reads=["sm1"], writes=["dv1"])
        pg.op("dve", lambda e: e.tensor_scalar(out=dv[:, 48:64], in0=small[:, 16:32], scalar1=-4.0, scalar2=None, op0=ALU.mult),
              reads=["sm1"], writes=["dv2"])
        DVK = ["dv0", "dv1", "dv2"]

        ring_i = [0]
        cur_t = [0]
        job_i = [0]

        def wload(dmas, rkeys):
            sl = ring_i[0] % NSLOT
            ring_i[0] += 1
            j = job_i[0]
            job_i[0] += 1
            for di, (vf, src) in enumerate(dmas):
                if cur_t[0] == 0:
                    pg.op("pool", lambda e, vf=vf, src=src, sl=sl: e.dma_start(out=vf(ring[sl]), in_=src),
                          reads=[], writes=[("ring", sl), "poolq"], dma_sem=s_ring[sl])
                    pg.op("sp", lambda e, vf=vf, sl=sl, j=j: e.dma_start(out=vf(scr[j]), in_=vf(ring[sl])),
                          reads=[("ring", sl)], writes=[("scr", j, di)], dma_sem=s_wb[sl])
                else:
                    pg.op("sp", lambda e, vf=vf, sl=sl, j=j: e.dma_start(out=vf(ring[sl]), in_=vf(scr[j])),
                          reads=[("scr", j, di)], writes=[("ring", sl)], dma_sem=s_rl[sl])
            return sl

        def w_in_job(l, g, ncols=768):
            c0 = g * 768
            src = w_in[l, :, c0:c0 + ncols].rearrange("(k p) c -> p k c", p=128)
            return wload([(lambda r, n=ncols: r[:, 0:8 * n].rearrange("p (k c) -> p k c", k=8), src)], [("wb_in", l, g)])

        def mm(bank, out_ap, lhsT, rhs, start, stop, reads, extra_w=()):
            pg.op("pe", lambda e: e.matmul(out_ap, lhsT, rhs, start=start, stop=stop),
                  reads=reads, writes=[("ps", bank)] + list(extra_w))

        def proj_fm(l, slot, ncols, c0, m, bank, mp=128):
            rv = ring[slot][:, 0:8 * ncols].rearrange("p (k c) -> p k c", k=8)
            for kc in range(8):
                mm(bank, banks[bank][0:m, :], rv[:, kc, c0:c0 + m], yT[:, kc, :], kc == 0, kc == 7,
                   [("ring", slot), ("yT", kc)])

        def proj_tm(slot, ncols, c0, n, blk, bank):
            rv = ring[slot][:, 0:8 * ncols].rearrange("p (k c) -> p k c", k=8)
            for kc in range(8):
                mm(bank, banks[bank][:, 0:n], yT[:, kc, blk * 128:(blk + 1) * 128], rv[:, kc, c0:c0 + n], kc == 0, kc == 7,
                   [("ring", slot), ("yT", kc)])

        def rms_stage(gcol, out_fn, out_keys_fn, out_dt_bf=True):
            bank = pspool.get()
            for c in range(8):
                b = bpool.get()
                pg.op("act", lambda e, c=c, b=b: e.activation(out=bunits[b][:], in_=hT[:, c, :], func=AF.Square),
                      reads=[("hT", c)], writes=[("b", b)])
                mm(bank, banks[bank][:], ones, bunits[b][:], c == 0, c == 7, [("b", b), "cb"])
                bpool.put(b)
            r = fpool.get()
            pg.op("act", lambda e: e.activation(out=funits[r][:], in_=banks[bank][:], func=AF.Ln, bias=epsc, scale=1.0 / D),
                  reads=[("ps", bank), "pk"], writes=[("f", r)])
            pg.op("act", lambda e: e.activation(out=funits[r][:], in_=funits[r][:], func=AF.Exp, scale=-0.5), reads=[("f", r)], writes=[("f", r)])
            pspool.put(bank)
            for c in range(8):
                pg.op("dve", lambda e, c=c: e.scalar_tensor_tensor(out=out_fn(c), in0=hT[:, c, :], scalar=gcol[:, c:c + 1],
                                                                  in1=funits[r][:], op0=ALU.mult, op1=ALU.mult),
                      reads=[("hT", c), ("f", r), "pk"], writes=out_keys_fn(c))
            fpool.put(r)

        gam = []
        for h in range(6):
            lg = np.log1p(-np.exp2(np.float32(-5.0 - h))).astype(np.float32)
            gam.append(float(np.exp(np.float32(128.0) * lg)))

        def do_tile(t):
            cur_t[0] = t
            job_i[0] = 0
            tok = slice(t * TT, (t + 1) * TT)
            pg.op("pool", lambda e, tok=tok: e.dma_start(out=hT[:], in_=xT.rearrange("(c p) t -> p c t", p=128)[:, :, tok]),
                  reads=[], writes=[("hT", c) for c in range(8)] + ["poolq"], dma_sem=s_x)
            pg.op("pool", lambda e, tok=tok: e.dma_start(out=posi[:], in_=pos[0:1, tok].broadcast_to([128, TT])),
                  reads=[], writes=["posi", "poolq"], dma_sem=s_pos)
            ang = fpool.get()
            tmp = fpool.get()
            kf = fpool.get()
            pg.op("dve", lambda e: e.tensor_copy(out=funits[ang][:], in_=posi[:]), reads=["posi"], writes=[("f", ang)])
            pg.op("dve", lambda e: e.tensor_scalar(out=funits[ang][:], in0=funits[ang][:], scalar1=pk[:, 180:181], scalar2=None, op0=ALU.mult),
                  reads=[("f", ang), "pk"], writes=[("f", ang)])
            C1 = 6.28125
            C2 = float(2.0 * np.pi - 6.28125)
            for which, off in ((1, 0.0), (0, 0.25)):
                pg.op("dve", lambda e, off=off: e.tensor_scalar(out=funits[tmp][:], in0=funits[ang][:], scalar1=float(1.0 / (2 * np.pi)), scalar2=off,
                                                               op0=ALU.mult, op1=ALU.add), reads=[("f", ang)], writes=[("f", tmp)])
                pg.op("dve", lambda e: e.tensor_copy(out=ki[:], in_=funits[tmp][:]), reads=[("f", tmp)], writes=["ki"])
                pg.op("dve", lambda e: e.tensor_copy(out=funits[kf][:], in_=ki[:]), reads=["ki"], writes=[("f", kf)])
                pg.op("dve", lambda e: e.scalar_tensor_tensor(out=funits[tmp][:], in0=funits[kf][:], scalar=-C1, in1=funits[ang][:], op0=ALU.mult, op1=ALU.add),
                      reads=[("f", kf), ("f", ang)], writes=[("f", tmp)])
                pg.op("dve", lambda e: e.scalar_tensor_tensor(out=funits[tmp][:], in0=funits[kf][:], scalar=-C2, in1=funits[tmp][:], op0=ALU.mult, op1=ALU.add),
                      reads=[("f", kf), ("f", tmp)], writes=[("f", tmp)])
                if off != 0.0:
                    pg.op("dve", lambda e: e.tensor_scalar(out=funits[tmp][:], in0=funits[tmp][:], scalar1=float(np.pi / 2), scalar2=None, op0=ALU.add),
                          reads=[("f", tmp)], writes=[("f", tmp)])
                pg.op("dve", lambda e: e.tensor_scalar(out=funits[tmp][:], in0=funits[tmp][:], scalar1=-3.1415925, scalar2=3.1415925, op0=ALU.max, op1=ALU.min),
                      reads=[("f", tmp)], writes=[("f", tmp)])
                pg.op("act", lambda e, which=which: e.activation(out=ropeT[:, which, :], in_=funits[tmp][:], func=AF.Sin),
                      reads=[("f", tmp)], writes=[("rope", which)])
            RS = float(128.0 ** -0.5)
            pg.op("dve", lambda e: e.tensor_scalar(out=ropeT[:, 2, :], in0=ropeT[:, 0, :], scalar1=RS, scalar2=None, op0=ALU.mult),
                  reads=[("rope", 0)], writes=[("rope", 2)])
            pg.op("dve", lambda e: e.tensor_scalar(out=ropeT[:, 3, :], in0=ropeT[:, 1, :], scalar1=pk[:, 182:183], scalar2=None, op0=ALU.mult),
                  reads=[("rope", 1), "pk"], writes=[("rope", 3)])
            pg.op("dve", lambda e: e.tensor_scalar(out=ropeT[:, 1, :], in0=ropeT[:, 1, :], scalar1=pk[:, 181:182], scalar2=None, op0=ALU.mult),
                  reads=[("rope", 1), ("rope", 3), "pk"], writes=[("rope", 1)])
            for u_ in (ang, tmp, kf):
                fpool.put(u_)

            def do_layer(l):
                if KSTAGE < 2:
                    return
                rms_stage(pk[:, l * 8:(l + 1) * 8], lambda c: yT[:, c, :], lambda c: [("yT", c)])
                if DEBUG and t == 0 and l == 0:
                    dbg("yT", yT[:], [128, 8, TT], BF16, [("yT", c_) for c_ in range(8)])

                if KSTAGE < 3:
                    return
                sx = w_in_job(l, 0)
                sg = w_in_job(l, 1)
                pend = []

                def lru_A(n):
                    bx = pspool.get()
                    proj_fm(l, sx, 768, n * 96, 96, bx)
                    xl = xls[n % 2]
                    xk = ("xl", n % 2)
                    hi = (l * 8 + n) * 4
                    pg.op("pool", lambda e: e.tensor_copy(out=xl[0:96, 0:3], in_=hist[0:96, hi:hi + 3]),
                          reads=[("hist", l * 8 + n)], writes=[xk])
                    pg.op("act", lambda e: e.activation(out=xl[0:96, 3:TT + 3], in_=banks[bx][0:96, :], func=AF.Copy),
                          reads=[("ps", bx)], writes=[xk])
                    pspool.put(bx)
                    bg = pspool.get()
                    proj_fm(l, sg, 768, n * 96, 96, bg)
                    return bg

                def lru_B1(n, bg):
                    xl = xls[n % 2]
                    xk = ("xl", n % 2)
                    col = l * 8 + n
                    cw = lambda k: pk[0:96, 52 + col * 4 + k:52 + col * 4 + k + 1]
                    xc = fpool.get()
                    XC = funits[xc][0:96, :]
                    pg.op("dve", lambda e: e.tensor_scalar(out=XC, in0=xl[0:96, 0:TT], scalar1=cw(0), scalar2=pk[0:96, 116 + col:117 + col],
                                                           op0=ALU.mult, op1=ALU.add), reads=[xk, "pk"], writes=[("f", xc)])
                    for k in (1, 2, 3):
                        pg.op("dve", lambda e, k=k: e.scalar_tensor_tensor(out=XC, in0=xl[0:96, k:k + TT], scalar=cw(k), in1=XC,
                                                                          op0=ALU.mult, op1=ALU.add), reads=[xk, "pk", ("f", xc)], writes=[("f", xc)])
                    hi = col * 4
                    pg.op("pool", lambda e: e.tensor_copy(out=hist[0:96, hi:hi + 3], in_=xl[0:96, TT:TT + 3]),
                          reads=[xk], writes=[("hist", col)])
                    xb = bpool.get()
                    pg.op("pool", lambda e: e.tensor_copy(out=bunits[xb][0:96, :], in_=XC), reads=[("f", xc)], writes=[("b", xb)])
                    ba = pspool.get()
                    bxx = pspool.get()
                    wo = col * 96
                    mm(ba, banks[ba][0:96, :], wab[0:96, wo:wo + 96], bunits[xb][0:96, :], True, True, [("b", xb), "wab"])
                    mm(bxx, banks[bxx][0:96, :], wxb[0:96, wo:wo + 96], bunits[xb][0:96, :], True, True, [("b", xb), "wxb"])
                    bpool.put(xb)
                    tha = fpool.get()
                    thx = fpool.get()
                    thg = fpool.get()
                    a2 = fpool.get()
                    A, X, G, A2 = (funits[i][0:96, :] for i in (tha, thx, thg, a2))
                    pg.op("act", lambda e: e.activation(out=A, in_=banks[ba][0:96, :], func=AF.Tanh, bias=dv[0:96, col:col + 1], scale=0.5),
                          reads=[("ps", ba)] + DVK, writes=[("f", tha)])
                    pg.op("act", lambda e: e.activation(out=X, in_=banks[bxx][0:96, :], func=AF.Tanh, bias=dv[0:96, 16 + col:17 + col], scale=0.5),
                          reads=[("ps", bxx)] + DVK, writes=[("f", thx)])
                    pspool.put(ba)
                    pspool.put(bxx)
                    pg.op("act", lambda e: e.activation(out=G, in_=banks[bg][0:96, :], func=AF.Tanh, scale=0.5),
                          reads=[("ps", bg)], writes=[("f", thg)])
                    pg.op("act", lambda e: e.activation(out=A2, in_=A, func=AF.Exp, bias=dv[0:96, 32 + col:33 + col], scale=dv[0:96, 32 + col:33 + col]),
                          reads=[("f", tha)] + DVK, writes=[("f", a2)])
                    pg.op("act", lambda e: e.activation(out=A, in_=A, func=AF.Exp, bias=dv[0:96, 48 + col:49 + col], scale=dv[0:96, 48 + col:49 + col]),
                          reads=[("f", tha)] + DVK, writes=[("f", tha)])
                    pg.op("act", lambda e: e.activation(out=A2, in_=A2, func=AF.Ln, bias=onec[0:96, :], scale=-1.0),
                          reads=[("f", a2), "pk"], writes=[("f", a2)])
                    pg.op("act", lambda e: e.activation(out=A2, in_=A2, func=AF.Exp, scale=0.5),
                          reads=[("f", a2)], writes=[("f", a2)])
                    return (bg, xc, tha, thx, thg, a2)

                def lru_B2(n, st_):
                    bg, xc, tha, thx, thg, a2 = st_
                    col = l * 8 + n
                    XC = funits[xc][0:96, :]
                    A, X, G, A2 = (funits[i][0:96, :] for i in (tha, thx, thg, a2))
                    pg.op("dve", lambda e: e.scalar_tensor_tensor(out=X, in0=X, scalar=1.0, in1=XC, op0=ALU.add, op1=ALU.mult),
                          reads=[("f", thx), ("f", xc)], writes=[("f", thx)])
                    pg.op("dve", lambda e: e.scalar_tensor_tensor(out=X, in0=X, scalar=0.5, in1=A2, op0=ALU.mult, op1=ALU.mult),
                          reads=[("f", thx), ("f", a2)], writes=[("f", thx)])
                    pg.op("dve", lambda e: e.tensor_tensor_scan(out=XC, data0=A, data1=X, initial=hst[0:96, col:col + 1], op0=ALU.mult, op1=ALU.add),
                          reads=[("f", tha), ("f", thx), "hst"], writes=[("f", xc)])
                    pg.op("pool", lambda e: e.tensor_copy(out=hst[0:96, col:col + 1], in_=funits[xc][0:96, TT - 1:TT]),
                          reads=[("f", xc)], writes=["hst"])
                    pg.op("dve", lambda e: e.scalar_tensor_tensor(out=G, in0=G, scalar=1.0, in1=banks[bg][0:96, :], op0=ALU.add, op1=ALU.mult),
                          reads=[("f", thg), ("ps", bg)], writes=[("f", thg)])
                    pspool.put(bg)
                    pg.op("dve", lambda e: e.scalar_tensor_tensor(out=mixL[0:96, n, :], in0=G, scalar=0.5, in1=XC, op0=ALU.mult, op1=ALU.mult),
                          reads=[("f", thg), ("f", xc)], writes=[("mixL", n)])
                    if DEBUG and t == 0 and l == 0 and n == 0:
                        dbg("lru_h0", funits[xc][:], [128, TT], F32, [("f", xc)])
                    for u_ in (xc, tha, thx, thg, a2):
                        fpool.put(u_)

                bgs = {}
                sts = {}
                bgs[0] = lru_A(0)
                bgs[1] = lru_A(1)
                sts[0] = lru_B1(0, bgs[0])
                for n in range(8):
                    if n + 2 < 8:
                        bgs[n + 2] = lru_A(n + 2)
                    if n + 1 < 8:
                        sts[n + 1] = lru_B1(n + 1, bgs[n + 1])
                    lru_B2(n, sts[n])

                if KSTAGE < 4:
                    return
                sq_ = w_in_job(l, 2)
                sk_ = w_in_job(l, 3)
                sv_ = w_in_job(l, 4)
                sgr = w_in_job(l, 5)
                for hg in range(2):
                    items = []
                    for hh in range(3):
                        h = hg * 3 + hh

                        def rope_item(slot, cidx, sidx, dst, dkey, hh=hh, h=h):
                            bq = pspool.get()
                            proj_fm(l, slot, 768, h * 128, 128, bq)
                            qb = bpool.get()
                            pg.op("act", lambda e, bq=bq, qb=qb: e.activation(out=bunits[qb][:], in_=banks[bq][:], func=AF.Copy),
                                  reads=[("ps", bq)], writes=[("b", qb), ("psr", bq)])
                            bs_ = pspool.get()
                            mm(bs_, banks[bs_][:], perm, bunits[qb][:], True, True, [("b", qb), "cb"])
                            bpool.put(qb)
                            yield
                            if KQ < 1:
                                pspool.put(bq)
                                pspool.put(bs_)
                                return
                            t1 = fpool.get()
                            t2 = fpool.get()
                            pg.op("dve", lambda e, bq=bq, t1=t1, cidx=cidx: e.scalar_tensor_tensor(out=funits[t1][:], in0=ropeT[:, cidx, :], scalar=1.0, in1=banks[bq][:], op0=ALU.mult, op1=ALU.mult),
                                  reads=[("ps", bq), ("psr", bq), ("rope", cidx)], writes=[("f", t1)])
                            pg.op("dve", lambda e, bs_=bs_, t2=t2, sidx=sidx: e.scalar_tensor_tensor(out=funits[t2][:], in0=ropeT[:, sidx, :], scalar=1.0, in1=banks[bs_][:], op0=ALU.mult, op1=ALU.mult),
                                  reads=[("ps", bs_), ("rope", sidx)], writes=[("f", t2)])
                            pspool.put(bq)
                            pspool.put(bs_)
                            if KQ < 2:
                                fpool.put(t1)
                                fpool.put(t2)
                                return
                            lo_t = qlo if dkey == "qt" else klo
                            lkey = "qlo" if dkey == "qt" else "klo"
                            pg.op("dve", lambda e, t1=t1, t2=t2: e.tensor_tensor(out=funits[t1][:], in0=funits[t1][:], in1=funits[t2][:], op=ALU.add),
                                  reads=[("f", t1), ("f", t2)], writes=[("f", t1)])
                            pg.op("act", lambda e, t1=t1, dst=dst, hh=hh: e.activation(out=dst[:, hh, :], in_=funits[t1][:], func=AF.Copy),
                                  reads=[("f", t1)], writes=[(dkey, hh)])
                            pg.op("pool", lambda e, t2=t2, dst=dst, hh=hh: e.tensor_copy(out=funits[t2][:], in_=dst[:, hh, :]),
                                  reads=[(dkey, hh), ("f", t2)], writes=[("f", t2)])
                            pg.op("pool", lambda e, t1=t1, t2=t2, lo_t=lo_t, hh=hh: e.tensor_tensor(
                                out=lo_t[:, hh, :], in0=funits[t1][:], in1=funits[t2][:], op=ALU.subtract),
                                reads=[("f", t1), ("f", t2)], writes=[(lkey, hh)])
                            if dkey == "qt":
                                pg.op("pool", lambda e, t1=t1, hh=hh, h=h: e.tensor_tensor(
                                    out=qg[:, hh, :].rearrange("p (n i) -> p n i", n=4),
                                    in0=funits[t1][:].rearrange("p (n i) -> p n i", n=4),
                                    in1=qwc[:, h * 128:(h + 1) * 128].unsqueeze(1).broadcast_to([128, 4, 128]), op=ALU.mult),
                                    reads=[("f", t1), "cst"], writes=[("qg", hh)])
                            fpool.put(t1)
                            fpool.put(t2)

                        for args_ in ((sq_, 0, 1, qt, "qt"), (sk_, 2, 3, kt, "kt")):
                            items.append(rope_item(*args_))
                    next(items[0])
                    for i_ in range(6):
                        if i_ + 1 < 6:
                            next(items[i_ + 1])
                        for _ in items[i_]:
                            pass
                    if DEBUG and t == 0 and l == 0 and hg == 0:
                        dbg("qt", qt[:], [128, 3, TT], BF16, [("qt", i) for i in range(3)])
                        dbg("kt", kt[:], [128, 3, TT], BF16, [("kt", i) for i in range(3)])
                    if KRET < 2:
                        continue
                    for blk in range(4):
                        bv = pspool.get()
                        proj_tm(sv_, 768, hg * 384, 384, blk, bv)
                        pg.op("act", lambda e, bv=bv, blk=blk: e.activation(out=vbf[:, blk, :], in_=banks[bv][:, 0:384], func=AF.Copy),
                              reads=[("ps", bv)], writes=[("vbf", blk)])
                        pg.op("dve", lambda e, bv=bv, blk=blk, hg=hg: e.scalar_tensor_tensor(out=vw[:, blk, :], in0=kwc[:, hg * 384:(hg + 1) * 384], scalar=1.0, in1=banks[bv][:, 0:384], op0=ALU.mult, op1=ALU.mult),
                              reads=[("ps", bv), "cst"], writes=[("vw", blk)])
                        pspool.put(bv)
                    if KRET < 3:
                        continue
                    for blk in range(4):
                        bt = pspool.get()
                        btb = banks[bt][:].bitcast(BF16)
                        for hh in range(3):
                            pg.op("pe", lambda e, btb=btb, hh=hh, blk=blk: e.transpose(btb[:, hh * 128:(hh + 1) * 128], kt[:, hh, blk * 128:(blk + 1) * 128], ident),
                                  reads=[("kt", hh), "cb"], writes=[("ps", bt)])
                        pg.op("act", lambda e, btb=btb, blk=blk: e.activation(out=ktm[:, blk, :], in_=btb[:, 0:384], func=AF.Copy),
                              reads=[("ps", bt)], writes=[("ktm", blk)])
                        pspool.put(bt)
                    if KRET < 4:
                        continue
                    bo = [pspool.get() for _ in range(3)]
                    for n in range(4):
                        cs = slice(n * 128, (n + 1) * 128)
                        for hh in range(3):
                            h = hg * 3 + hh
                            mm(bo[hh], banks[bo[hh]][:, cs], Sb[:, l, h * 128:(h + 1) * 128], qg[:, hh, cs], True, False, [("Sb", l, hg), ("qg", hh)])
                        bkv = pspool.get()
                        for hh in range(3):
                            mm(bkv, banks[bkv][:, hh * 128:(hh + 1) * 128], ktm[:, n, hh * 128:(hh + 1) * 128], vw[:, n, hh * 128:(hh + 1) * 128], True, True,
                               [("ktm", n), ("vw", n)])
                        for hh in range(3):
                            h = hg * 3 + hh
                            pg.op("dve", lambda e, h=h, hh=hh, bkv=bkv: e.scalar_tensor_tensor(
                                out=Sst[:, l, h * 128:(h + 1) * 128], in0=Sst[:, l, h * 128:(h + 1) * 128], scalar=gam[h],
                                in1=banks[bkv][:, hh * 128:(hh + 1) * 128], op0=ALU.mult, op1=ALU.add),
                                reads=[("ps", bkv), ("Sst", l, hg)], writes=[("Sst", l, hg)])
                        pspool.put(bkv)
                        pg.op("act", lambda e, hg=hg: e.activation(out=Sb[:, l, hg * 384:(hg + 1) * 384], in_=Sst[:, l, hg * 384:(hg + 1) * 384], func=AF.Copy),
                              reads=[("Sst", l, hg)], writes=[("Sb", l, hg)])
                        bsc = pspool.get()
                        for hh in range(3):
                            osc = banks[bsc][:, hh * 128:(hh + 1) * 128]
                            mm(bsc, osc, kt[:, hh, cs], qt[:, hh, cs], True, False, [("kt", hh), ("qt", hh)])
                            mm(bsc, osc, kt[:, hh, cs], qlo[:, hh, cs], False, False, [("kt", hh), ("qlo", hh)])
                            mm(bsc, osc, klo[:, hh, cs], qt[:, hh, cs], False, True, [("klo", hh), ("qt", hh)])
                        Pb = bpool.get()
                        pg.op("dve", lambda e, bsc=bsc, Pb=Pb, hg=hg: e.scalar_tensor_tensor(out=bunits[Pb][:, 0:384], in0=decayT[:, hg * 384:(hg + 1) * 384], scalar=1.0, in1=banks[bsc][:, 0:384], op0=ALU.mult, op1=ALU.mult),
                              reads=[("ps", bsc), "cst"], writes=[("b", Pb)])
                        pspool.put(bsc)
                        for hh in range(3):
                            h = hg * 3 + hh
                            mm(bo[hh], banks[bo[hh]][:, cs], vbf[:, n, hh * 128:(hh + 1) * 128], bunits[Pb][:, hh * 128:(hh + 1) * 128], False, True,
                               [("vbf", n), ("b", Pb)])
                        bpool.put(Pb)
                    if KRET < 5:
                        for b_ in bo:
                            pspool.put(b_)
                        continue
                    for hh in range(3):
                        h = hg * 3 + hh
                        bg = pspool.get()
                        proj_fm(l, sgr, 768, h * 128, 128, bg)
                        b = bpool.get()
                        pg.op("act", lambda e, b=b, boa=banks[bo[hh]][:]: e.activation(out=bunits[b][:], in_=boa, func=AF.Square),
                              reads=[("ps", bo[hh])], writes=[("b", b)])
                        bm = pspool.get()
                        mm(bm, banks[bm][:], ones, bunits[b][:], True, True, [("b", b), "cb"])
                        bpool.put(b)
                        r = fpool.get()
                        pg.op("act", lambda e, r=r, bm=bm: e.activation(out=funits[r][:], in_=banks[bm][:], func=AF.Ln, bias=epsc, scale=1.0 / 128),
                              reads=[("ps", bm), "pk"], writes=[("f", r)])
                        pspool.put(bm)
                        pg.op("act", lambda e, r=r: e.activation(out=funits[r][:], in_=funits[r][:], func=AF.Exp, scale=-0.5), reads=[("f", r)], writes=[("f", r)])
                        on = fpool.get()
                        pg.op("dve", lambda e, r=r, on=on, boa=banks[bo[hh]][:], h=h: e.scalar_tensor_tensor(out=funits[on][:], in0=funits[r][:], scalar=pk[:, 40 + l * 6 + h:41 + l * 6 + h],
                                                                                        in1=boa, op0=ALU.mult, op1=ALU.mult),
                              reads=[("ps", bo[hh]), ("f", r), "pk"], writes=[("f", on)])
                        fpool.put(r)
                        tg = fpool.get()
                        pg.op("act", lambda e, tg=tg, bg=bg: e.activation(out=funits[tg][:], in_=banks[bg][:], func=AF.Tanh, scale=0.5),
                              reads=[("ps", bg)], writes=[("f", tg)])
                        pg.op("dve", lambda e, tg=tg, bg=bg: e.scalar_tensor_tensor(out=funits[tg][:], in0=funits[tg][:], scalar=1.0, in1=banks[bg][:], op0=ALU.add, op1=ALU.mult),
                              reads=[("f", tg), ("ps", bg)], writes=[("f", tg)])
                        pspool.put(bg)
                        pg.op("dve", lambda e, tg=tg, on=on, h=h: e.scalar_tensor_tensor(out=mixR[:, h, :], in0=funits[tg][:], scalar=0.5, in1=funits[on][:], op0=ALU.mult, op1=ALU.mult),
                              reads=[("f", tg), ("f", on)], writes=[("mixR", h)])
                        if DEBUG and t == 0 and l == 0 and h == 0:
                            dbg("ret_on0", funits[on][:], [128, TT], F32, [("f", on)])
                        fpool.put(tg)
                        fpool.put(on)
                    for b_ in bo:
                        pspool.put(b_)

                if KSTAGE < 5:
                    return
                su = w_in_job(l, 6, 512)
                def w_cols_job(c0, ncols):
                    src = w_in[l, :, c0:c0 + ncols].rearrange("(k p) c -> p k c", p=128)
                    return wload([(lambda r, n=ncols: r[:, 0:8 * n].rearrange("p (k c) -> p k c", k=8), src)],
                                 [("wb_in", l, c0 // 768), ("wb_in", l, (c0 + ncols - 1) // 768)])
                svs = w_cols_job(5120, 512)
                sgs = w_cols_job(5632, 512)
                for blk in range(4):
                    bv = pspool.get()
                    proj_tm(svs, 512, 0, 512, blk, bv)
                    so = blk * 12
                    tq = fpool.get()
                    pg.op("act", lambda e, bv=bv, tq=tq: e.activation(out=funits[tq][:], in_=banks[bv][:], func=AF.Copy),
                          reads=[("ps", bv)], writes=[("f", tq)])
                    pspool.put(bv)
                    pg.op("dve", lambda e, tq=tq, so=so: e.bn_stats(out=small[:, so:so + 6], in_=funits[tq][:]), reads=[("f", tq)], writes=[("small", blk)])
                    pg.op("dve", lambda e, so=so: e.bn_aggr(out=small[:, so + 6:so + 8], in_=small[:, so:so + 6]), reads=[("small", blk)], writes=[("small", blk)])
                    pg.op("act", lambda e, so=so: e.activation(out=small[:, so + 8:so + 9], in_=small[:, so + 7:so + 8], func=AF.Ln, bias=epsc, scale=1.0),
                          reads=[("small", blk), "pk"], writes=[("small", blk)])
                    pg.op("act", lambda e, so=so: e.activation(out=small[:, so + 8:so + 9], in_=small[:, so + 8:so + 9], func=AF.Exp, scale=-0.5), reads=[("small", blk)], writes=[("small", blk)])
                    pg.op("dve", lambda e, so=so, tq=tq: e.scalar_tensor_tensor(out=funits[tq][:], in0=funits[tq][:], scalar=small[:, so + 6:so + 7], in1=lng[:, l, :],
                                                                                op0=ALU.subtract, op1=ALU.mult),
                          reads=[("f", tq), ("small", blk), ("lng", l)], writes=[("f", tq)])
                    pg.op("dve", lambda e, so=so, tq=tq, blk=blk: e.scalar_tensor_tensor(out=vn[:, blk, :], in0=funits[tq][:], scalar=small[:, so + 8:so + 9], in1=lnb[:, l, :],
                                                                                  op0=ALU.mult, op1=ALU.add),
                          reads=[("f", tq), ("small", blk), ("lnb", l)], writes=[("vn", blk)])
                    fpool.put(tq)
                for g in range(4):
                    gs = slice(g * 128, (g + 1) * 128)
                    bmix = pspool.get()
                    for n in range(4):
                        cs = slice(n * 128, (n + 1) * 128)
                        mm(bmix, banks[bmix][:, cs], vn[:, n, gs], wsT[:, l, gs], True, False, [("vn", n), ("wsT", l)])
                        mm(bmix, banks[bmix][:, cs], ones[0:1, :], bsb[0:1, l * 512 + g * 128:l * 512 + (g + 1) * 128], False, True, ["cb", "bsb"])
                    bu = pspool.get()
                    proj_fm(l, su, 512, g * 128, 128, bu)
                    bgg = pspool.get()
                    proj_fm(l, sgs, 512, g * 128, 128, bgg)
                    m_ = fpool.get()
                    pg.op("act", lambda e, m_=m_, bmix=bmix: e.activation(out=funits[m_][:], in_=banks[bmix][:], func=AF.Copy),
                          reads=[("ps", bmix)], writes=[("f", m_)])
                    pspool.put(bmix)
                    pg.op("dve", lambda e, m_=m_, bu=bu: e.scalar_tensor_tensor(out=funits[m_][:], in0=funits[m_][:], scalar=1.0, in1=banks[bu][:], op0=ALU.mult, op1=ALU.mult),
                          reads=[("f", m_), ("ps", bu)], writes=[("f", m_)])
                    pspool.put(bu)
                    tg = fpool.get()
                    pg.op("act", lambda e, tg=tg, bgg=bgg: e.activation(out=funits[tg][:], in_=banks[bgg][:], func=AF.Tanh, scale=0.5),
                          reads=[("ps", bgg)], writes=[("f", tg)])
                    pg.op("dve", lambda e, tg=tg, bgg=bgg: e.scalar_tensor_tensor(out=funits[tg][:], in0=funits[tg][:], scalar=1.0, in1=banks[bgg][:], op0=ALU.add, op1=ALU.mult),
                          reads=[("f", tg), ("ps", bgg)], writes=[("f", tg)])
                    pspool.put(bgg)
                    pg.op("dve", lambda e, tg=tg, m_=m_, g=g: e.scalar_tensor_tensor(out=mixS[:, g, :], in0=funits[tg][:], scalar=0.5, in1=funits[m_][:], op0=ALU.mult, op1=ALU.mult),
                          reads=[("f", tg), ("f", m_)], writes=[("mixS", g)])
                    fpool.put(tg)
                    fpool.put(m_)
                if DEBUG and t == 0 and l == 0:
                    dbg("mixL", mixL[:], [128, 8, TT], BF16, [("mixL", i) for i in range(8)])
                    dbg("mixR", mixR[:], [128, 6, TT], BF16, [("mixR", i) for i in range(6)])
                    dbg("mixS", mixS[:], [128, 4, TT], BF16, [("mixS", i) for i in range(4)])

                if KSTAGE < 6:
                    return
                for piece in range(4):
                    c0 = piece * 256
                    dm = [
                        (lambda r: r[0:96, 0:2048].rearrange("p (n c) -> p n c", n=8),
                         w_out[l, 0:768, c0:c0 + 256].rearrange("(n p) c -> p n c", p=96)),
                        (lambda r: r[:, 2048:3584].rearrange("p (n c) -> p n c", n=6),
                         w_out[l, 768:1536, c0:c0 + 256].rearrange("(n p) c -> p n c", p=128)),
                        (lambda r: r[:, 3584:4608].rearrange("p (n c) -> p n c", n=4),
                         w_out[l, 1536:2048, c0:c0 + 256].rearrange("(n p) c -> p n c", p=128)),
                    ]
                    so_ = wload(dm, [("wb_out", l)])
                    R = ring[so_]
                    for d2 in range(2):
                        c = piece * 2 + d2
                        bw = pspool.get()
                        steps = []
                        for n in range(8):
                            steps.append((R[0:96, n * 256 + d2 * 128:n * 256 + d2 * 128 + 128], mixL[0:96, n, :], ("mixL", n)))
                        for h in range(6):
                            steps.append((R[:, 2048 + h * 256 + d2 * 128:2048 + h * 256 + d2 * 128 + 128], mixR[:, h, :], ("mixR", h)))
                        for g in range(4):
                            steps.append((R[:, 3584 + g * 256 + d2 * 128:3584 + g * 256 + d2 * 128 + 128], mixS[:, g, :], ("mixS", g)))
                        for i, (lh, rh, key) in enumerate(steps):
                            mm(bw, banks[bw][:], lh, rh, i == 0, i == len(steps) - 1, [("ring", so_), key])
                        pg.op("dve", lambda e, c=c, bw=bw: e.scalar_tensor_tensor(out=hT[:, c, :], in0=hT[:, c, :], scalar=1.0, in1=banks[bw][:], op0=ALU.mult, op1=ALU.add),
                              reads=[("hT", c), ("ps", bw)], writes=[("hT", c)])
                        pspool.put(bw)

                if KSTAGE < 7:
                    return
                rms_stage(pk[:, 16 + l * 8:16 + (l + 1) * 8], lambda c: yT[:, c, :], lambda c: [("yT", c)])
                pg.op("pool", lambda e, tok=tok: e.dma_start(out=pTt[:], in_=pT[l].rearrange("(k p) t -> p k t", p=128)[:, :, tok]),
                      reads=[], writes=["pTt", "poolq"], dma_sem=s_p)
                spp = wload([(lambda r: r[:, 0:2048].rearrange("p (k c) -> p k c", k=2), w_pp[l].rearrange("(k p) c -> p k c", p=128))], [("wb_pp", l)])
                for half in range(2):
                    sgt = wload([(lambda r: r[:, 0:4096].rearrange("p (k c) -> p k c", k=8),
                                  w_gate[l, :, half * 512:(half + 1) * 512].rearrange("(k p) c -> p k c", p=128))], [("wb_gate", l)])
                    for cc in range(4):
                        c = half * 4 + cc
                        bgt = pspool.get()
                        proj_fm(l, sgt, 512, cc * 128, 128, bgt)
                        bpp = pspool.get()
                        rp = ring[spp][:, 0:2048].rearrange("p (k c) -> p k c", k=2)
                        for k in range(2):
                            mm(bpp, banks[bpp][:], rp[:, k, c * 128:(c + 1) * 128], pTt[:, k, :], k == 0, k == 1, [("ring", spp), "pTt"])
                        th = fpool.get()
                        pg.op("act", lambda e, th=th, bgt=bgt: e.activation(out=funits[th][:], in_=banks[bgt][:], func=AF.Tanh, scale=0.5),
                              reads=[("ps", bgt)], writes=[("f", th)])
                        pspool.put(bgt)
                        pg.op("dve", lambda e, th=th, bpp=bpp: e.scalar_tensor_tensor(out=funits[th][:], in0=funits[th][:], scalar=1.0, in1=banks[bpp][:], op0=ALU.add, op1=ALU.mult),
                              reads=[("f", th), ("ps", bpp)], writes=[("f", th)])
                        pspool.put(bpp)
                        pg.op("dve", lambda e, th=th, c=c: e.scalar_tensor_tensor(out=hT[:, c, :], in0=funits[th][:], scalar=0.5, in1=hT[:, c, :], op0=ALU.mult, op1=ALU.add),
                              reads=[("f", th), ("hT", c)], writes=[("hT", c)])
                        fpool.put(th)
                if DEBUG and t == 0:
                    dbg(f"h_l{l}", hT[:], [128, 8, TT], F32, [("hT", c) for c in range(8)])

            for l_ in range(DEPTH):
                do_layer(l_)

            outs = {}

            def fin_out(c):
                u = fpool.get()
                outs[c] = u
                return funits[u][:]

            def fin_keys(c):
                return [("f", outs[c])]

            bank = pspool.get()
            for c in range(8):
                b = bpool.get()
                pg.op("act", lambda e, c=c, b=b: e.activation(out=bunits[b][:], in_=hT[:, c, :], func=AF.Square), reads=[("hT", c)], writes=[("b", b)])
                mm(bank, banks[bank][:], ones, bunits[b][:], c == 0, c == 7, [("b", b), "cb"])
                bpool.put(b)
            r = fpool.get()
            pg.op("act", lambda e, r=r, bank=bank: e.activation(out=funits[r][:], in_=banks[bank][:], func=AF.Ln, bias=epsc, scale=1.0 / D),
                  reads=[("ps", bank), "pk"], writes=[("f", r)])
            pg.op("act", lambda e, r=r: e.activation(out=funits[r][:], in_=funits[r][:], func=AF.Exp, scale=-0.5), reads=[("f", r)], writes=[("f", r)])
            pspool.put(bank)
            for c in range(8):
                u = fpool.get()
                pg.op("dve", lambda e, c=c, u=u, r=r: e.scalar_tensor_tensor(out=funits[u][:], in0=hT[:, c, :], scalar=pk[:, 32 + c:33 + c], in1=funits[r][:],
                                                                          op0=ALU.mult, op1=ALU.mult),
                      reads=[("hT", c), ("f", r), "pk"], writes=[("f", u)])
                pg.op("sp", lambda e, c=c, u=u, tok=tok: e.dma_start(out=outT[c * 128:(c + 1) * 128, tok], in_=funits[u][:]),
                      reads=[("f", u)], writes=[("outsem", c)], dma_sem=s_out[c])
                fpool.put(u)
            fpool.put(r)

        for t_ in range(NT_RUN):
            do_tile(t_)

        def final_waits():
            res = []
            for s_ in s_out:
                if s_ in pg.dma_counts:
                    res.append((s_, pg.dma_counts[s_]))
            if s_dbg in pg.dma_counts:
                res.append((s_dbg, pg.dma_counts[s_dbg]))
            return res

        with nc.Block() as block:
            pg.emit(block, eng_sems, final_waits)
    return nc, list(dbg_outs.keys())


def _consts():
    f32 = np.float32
    H, C = 6, 128
    log_g = np.log1p(-np.exp2(-5.0 - np.arange(H, dtype=f32))).astype(f32)
    idx = np.arange(C, dtype=f32)
    diff = idx[:, None] - idx[None, :]
    causal = diff >= 0
    decay = np.where(causal[None], np.exp(np.where(causal, diff, 0.0)[None] * log_g[:, None, None]), 0.0).astype(f32)
    decayT = np.ascontiguousarray(decay.transpose(2, 0, 1)).reshape(128, 768)
    k_w = np.exp((C - 1.0 - idx)[:, None] * log_g[None, :]).astype(f32)
    kwc = np.repeat(k_w[:, :, None], 128, axis=2).reshape(128, 768)
    q_w = np.exp((idx + 1.0)[:, None] * log_g[None, :]).astype(f32)
    qw = np.broadcast_to(q_w.T.reshape(1, 768), (128, 768))
    ident = np.eye(128, dtype=f32)
    perm = np.zeros((128, 128), f32)
    for m in range(128):
        perm[(m + 64) % 128, m] = 1.0
    ones = np.ones((128, 128), f32)
    pad = np.zeros((128, 384), f32)
    mask = (idx[:, None] <= idx[None, :]).astype(f32)
    cst = np.concatenate([decayT, kwc, qw, ident, perm, ones, pad, mask], axis=1).astype(f32)
    return np.ascontiguousarray(cst)


def _pack(inp):
    f32 = np.float32
    pk = np.zeros((128, NPK), f32)
    pk[:, 0:16] = inp["mix_norm"].reshape(2, 8, 128).transpose(2, 0, 1).reshape(128, 16)
    pk[:, 16:32] = inp["ple_norm"].reshape(2, 8, 128).transpose(2, 0, 1).reshape(128, 16)
    pk[:, 32:40] = inp["final_norm"].reshape(8, 128).T
    pk[:, 40:52] = inp["ret_norm"].reshape(2, 6, 128).transpose(2, 0, 1).reshape(128, 12)
    cw = inp["lru_conv_w"].reshape(2, 4, 8, 96).transpose(3, 0, 2, 1)
    pk[0:96, 52:116] = cw.reshape(96, 64)
    for name, c0 in (("lru_conv_b", 116), ("lru_b_a", 132), ("lru_b_x", 148), ("lru_lambda", 164)):
        pk[0:96, c0:c0 + 16] = inp[name].reshape(2, 8, 96).transpose(2, 0, 1).reshape(96, 16)
    half = 64
    inv = (10000.0 ** (-np.arange(half, dtype=f32) / half)).astype(f32)
    pk[:, 180] = np.concatenate([inv, inv])
    sgn = np.concatenate([-np.ones(64, f32), np.ones(64, f32)])
    pk[:, 181] = sgn
    pk[:, 182] = sgn * f32(128.0 ** -0.5)
    pk[:, 183] = EPS
    pk[:, 184] = 1.0
    return pk


def kernel(**inputs):
    inp = {k: np.asarray(v) for k, v in inputs.items()}
    f32 = np.float32
    nc, dbg_names = build_nc()
    cst = _consts()
    pk = _pack(inp)
    shared = {
        "w_in": np.ascontiguousarray(inp["w_in"], f32),
        "w_out": np.ascontiguousarray(inp["w_out"], f32),
        "w_gate": np.ascontiguousarray(inp["w_ple_gate"], f32),
        "w_pp": np.ascontiguousarray(inp["w_ple_proj"], f32),
        "w_a": np.ascontiguousarray(inp["lru_w_a"], f32),
        "w_x": np.ascontiguousarray(inp["lru_w_x"], f32),
        "w_sT": np.ascontiguousarray(inp["sgu_w_s"].transpose(0, 3, 1, 2).reshape(2, 128, 512), f32),
        "b_s": np.ascontiguousarray(inp["sgu_b_s"].reshape(1, 1024), f32),
        "ln_g": np.ascontiguousarray(inp["sgu_ln_g"], f32),
        "ln_b": np.ascontiguousarray(inp["sgu_ln_b"], f32),
        "pk": pk,
        "cst": cst,
    }
    in_maps = []
    for core in range(8):
        b = core % 4
        m = dict(shared)
        m["xT"] = np.ascontiguousarray(inp["x"][b].T, f32)
        m["pT"] = np.ascontiguousarray(inp["p"][:, b].transpose(0, 2, 1), f32)
        m["pos"] = np.ascontiguousarray(inp["positions"][b].reshape(1, S), np.int32)
        in_maps.append(m)
    res = run_bass_kernel_spmd(nc, in_maps, core_ids=list(range(8)))
    out = np.stack([np.ascontiguousarray(res.results[b]["outT"].T) for b in range(4)], axis=0).astype(f32)
    if DEBUG:
        kernel.debug = {n: res.results[0]["dbg_" + n] for n in dbg_names}
    return out
```

```python
import os
import numpy as np
import concourse.bass as bass
import concourse.mybir as mybir
from concourse.bass_utils import run_bass_kernel_spmd

F32 = mybir.dt.float32
BF16 = mybir.dt.bfloat16
I32 = mybir.dt.int32
ALU = mybir.AluOpType
AF = mybir.ActivationFunctionType

D = 1024
S = 4096
TT = 512
NT = S // TT
DEPTH = 2
D_IN = 6144
EPS = 1e-6
NPK = 192
PI = float(np.pi)

DEBUG = os.environ.get("KDEBUG", "") != ""
NT_RUN = int(os.environ.get("KNT", NT))
KSTAGE = int(os.environ.get("KSTAGE", 99))
KRET = int(os.environ.get("KRET", 99))
KQ = int(os.environ.get("KQ", 99))
KSKIPW = os.environ.get("KSKIPW", "") != ""


class Op:
    __slots__ = ("eng", "fn", "reads", "writes", "dma_sem", "dma_val", "deps", "signal", "val", "idx")


class Prog:
    ENGS = ("pe", "act", "dve", "pool", "sp")

    def __init__(self, nc):
        self.nc = nc
        self.ops = []
        self.last_w = {}
        self.readers = {}
        self.dma_counts = {}
        self.shared = set()

    def op(self, eng, fn, reads=(), writes=(), dma_sem=None):
        o = Op()
        o.eng, o.fn, o.reads, o.writes = eng, fn, tuple(reads), tuple(writes)
        o.dma_sem = dma_sem
        o.signal = False
        o.val = 0
        o.idx = len(self.ops)
        if dma_sem is not None:
            self.dma_counts[dma_sem] = self.dma_counts.get(dma_sem, 0) + 16
            o.dma_val = self.dma_counts[dma_sem]
        else:
            o.dma_val = 0
        deps = set()
        is_dma = dma_sem is not None
        def need(p):
            return p.dma_sem is not None or is_dma or not (p.eng == eng == "pe")
        for k in o.reads:
            w = self.last_w.get(k)
            if w is not None and need(w):
                deps.add(w.idx)
            if isinstance(k, tuple) and k and k[0] == "ps":
                for r in self.readers.get(k, ()):
                    if r.eng != eng:
                        deps.add(r.idx)
        for k in o.writes:
            w = self.last_w.get(k)
            if w is not None and need(w):
                deps.add(w.idx)
            for r in self.readers.get(k, ()):
                if need(r):
                    deps.add(r.idx)
        deps.discard(o.idx)
        o.deps = deps
        wset = set(o.writes)
        for k in o.writes:
            self.last_w[k] = o
            self.readers[k] = []
        for k in o.reads:
            if k not in wset:
                self.readers.setdefault(k, []).append(o)
        self.ops.append(o)
        return o

    def emit(self, block, eng_sems, final_waits):
        ops = self.ops
        waits = {}
        for en in self.ENGS:
            widx = {}
            wdma = {}
            for o in ops:
                if o.eng != en:
                    continue
                best = {}
                dm = {}
                for d in o.deps:
                    p = ops[d]
                    if p.dma_sem is not None:
                        v = self.dma_counts[p.dma_sem] if p.dma_sem in self.shared else p.dma_val
                        if v > dm.get(p.dma_sem, 0):
                            dm[p.dma_sem] = v
                    else:
                        if d > best.get(p.eng, -1):
                            best[p.eng] = d
                lst = []
                for pe_, d in best.items():
                    if d > widx.get(pe_, -1):
                        widx[pe_] = d
                        ops[d].signal = True
                        lst.append(("e", d))
                for sm, v in dm.items():
                    if v > wdma.get(sm, 0):
                        wdma[sm] = v
                        lst.append(("d", sm, v))
                waits[o.idx] = lst
        cnt = {e: 0 for e in self.ENGS}
        for o in ops:
            if o.dma_sem is None and o.signal:
                cnt[o.eng] += 1
                o.val = cnt[o.eng]

        def run(eng_name, eng):
            for o in ops:
                if o.eng != eng_name:
                    continue
                for w in waits[o.idx]:
                    if w[0] == "e":
                        p = ops[w[1]]
                        eng.wait_ge(eng_sems[p.eng], p.val)
                    else:
                        eng.wait_ge(w[1], w[2])
                ins = o.fn(eng)
                if o.dma_sem is not None:
                    ins.then_inc(o.dma_sem, 16)
                elif o.signal:
                    ins.then_inc(eng_sems[o.eng], 1)
            if eng_name == "sp":
                for s, v in final_waits():
                    eng.wait_ge(s, v)

        @block.tensor
        def _(e):
            run("pe", e)

        @block.scalar
        def _(e):
            run("act", e)

        @block.vector
        def _(e):
            run("dve", e)

        @block.gpsimd
        def _(e):
            run("pool", e)

        @block.sync
        def _(e):
            run("sp", e)


class Pool_:
    def __init__(self, items):
        self.free = list(items)

    def get(self):
        return self.free.pop(0)

    def put(self, x):
        self.free.append(x)


def build_nc():
    nc = bass.Bass("TRN2", target_bir_lowering=False)
    dt_in = lambda name, shape, dt=F32: nc.dram_tensor(name, shape, dt, kind="ExternalInput").ap()
    xT = dt_in("xT", [D, S])
    pT = dt_in("pT", [DEPTH, 256, S])
    pos = dt_in("pos", [1, S], I32)
    w_in = dt_in("w_in", [DEPTH, D, D_IN])
    w_out = dt_in("w_out", [DEPTH, 2048, D])
    w_gate = dt_in("w_gate", [DEPTH, D, D])
    w_pp = dt_in("w_pp", [DEPTH, 256, D])
    w_a = dt_in("w_a", [DEPTH, 8, 96, 96])
    w_x = dt_in("w_x", [DEPTH, 8, 96, 96])
    w_sT = dt_in("w_sT", [DEPTH, 128, 512])
    b_s = dt_in("b_s", [1, DEPTH * 512])
    ln_g = dt_in("ln_g", [DEPTH, 512])
    ln_b = dt_in("ln_b", [DEPTH, 512])
    pk_d = dt_in("pk", [128, NPK])
    cst_d = dt_in("cst", [128, 4 * 768 + 128])
    outT = nc.dram_tensor("outT", [D, S], F32, kind="ExternalOutput").ap()
    dbg_outs = {}

    wb_in = wb_out = wb_gate = wb_pp = pTb = None
    scr = nc.dram_tensor("scr", [DEPTH * 16, 128, 6144], BF16, kind="Internal").ap()

    from contextlib import ExitStack
    es = ExitStack()
    sb = lambda name, shape, dt=F32: es.enter_context(nc.sbuf_tensor("sb_" + name, shape, dt))
    sem = lambda name: es.enter_context(nc.semaphore(name))

    with es:
        hT = sb("hT", [128, 8, TT])
        yT = sb("yT", [128, 8, TT], BF16)
        NSLOT = 4
        ring = [sb(f"ring{i}", [128, 6144], BF16) for i in range(NSLOT)]
        mixL = sb("mixL", [128, 8, TT], BF16)
        mixR = sb("mixR", [128, 6, TT], BF16)
        mixS = sb("mixS", [128, 4, TT], BF16)
        NF, NB = 12, 8
        funits = [sb(f"fu{i}", [128, TT]) for i in range(NF)]
        bunits = [sb(f"bu{i}", [128, TT], BF16) for i in range(NB)]
        xls = [sb(f"xl{i}", [128, TT + 4]) for i in range(2)]
        qt = sb("qt", [128, 3, TT], BF16)
        kt = sb("kt", [128, 3, TT], BF16)
        qg = sb("qg", [128, 3, TT], BF16)
        qlo = sb("qlo", [128, 3, TT], BF16)
        klo = sb("klo", [128, 3, TT], BF16)
        vbf = sb("vbf", [128, 4, 384], BF16)
        vw = sb("vw", [128, 4, 384], BF16)
        ktm = sb("ktm", [128, 4, 384], BF16)
        Sst = sb("Sst", [128, DEPTH, 768])
        Sb = sb("Sb", [128, DEPTH, 768], BF16)
        ropeT = sb("ropeT", [128, 4, TT])
        cstT = sb("cstT", [128, 3 * 768])
        maskT = sb("maskT", [128, 128])
        cbT = sb("cbT", [128, 3 * 128], BF16)
        pk = sb("pk", [128, NPK])
        dv = sb("dv", [128, 64])
        vn = sb("vn", [128, 4, TT], BF16)
        wsT = sb("wsT", [128, DEPTH, 512], BF16)
        lng = sb("lng", [128, DEPTH, 512])
        lnb = sb("lnb", [128, DEPTH, 512])
        bsb = sb("bsb", [1, DEPTH * 512], BF16)
        pTt = sb("pTt", [128, 2, TT], BF16)
        wab = sb("wab", [96, DEPTH * 8 * 96], BF16)
        wxb = sb("wxb", [96, DEPTH * 8 * 96], BF16)
        hst = sb("hst", [128, DEPTH * 8])
        hist = sb("hist", [128, DEPTH * 8 * 4])
        posi = sb("posi", [128, TT], I32)
        ki = sb("ki", [128, TT], I32)
        small = sb("small", [128, 64])
        banks = [es.enter_context(nc.psum_tensor(f"ps{i}", [128, TT], F32)) for i in range(8)]

        eng_sems = {e: sem(f"s_{e}") for e in Prog.ENGS}
        s_cast = [sem(f"s_cast{i}") for i in range(4)]
        s_wb = [sem(f"s_wb{i}") for i in range(NSLOT)]
        s_rl = [sem(f"s_rl{i}") for i in range(NSLOT)]
        s_ring = [sem(f"s_ring{i}") for i in range(NSLOT)]
        s_ld = sem("s_ld")
        s_x = sem("s_x")
        s_pos = sem("s_pos")
        s_p = sem("s_p")
        s_out = [sem(f"s_out{i}") for i in range(8)]
        s_dbg = sem("s_dbg")

        pg = Prog(nc)
        pg.shared = set([s_ld] + list(s_cast))
        fpool = Pool_(range(NF))
        bpool = Pool_(range(NB))
        pspool = Pool_(range(8))

        ident = cbT[:, 0:128]
        perm = cbT[:, 128:256]
        ones = cbT[:, 256:384]
        decayT = cstT[:, 0:768]
        kwc = cstT[:, 768:1536]
        qwc = cstT[:, 1536:2304]
        epsc = pk[:, 183:184]
        onec = pk[:, 184:185]

        def dbg(name, ap, shape, dt, reads):
            if not DEBUG:
                return
            o = nc.dram_tensor("dbg_" + name, list(shape), dt, kind="ExternalOutput").ap()
            dbg_outs[name] = o
            pg.op("sp", lambda e, o=o, ap=ap: e.dma_start(out=o, in_=ap), reads=reads, writes=["dbgsem"], dma_sem=s_dbg)

        cast_i = [0]

        def cast_dma(out_ap, in_ap, wkey):
            sm = s_cast[cast_i[0] % len(s_cast)]
            cast_i[0] += 1
            pg.op("pool", lambda e: e.dma_start(out=out_ap, in_=in_ap), reads=[], writes=[wkey], dma_sem=sm)

        def ld(out_ap, in_ap, wkey):
            pg.op("sp", lambda e: e.dma_start(out=out_ap, in_=in_ap), reads=[], writes=[wkey], dma_sem=s_ld)

        ld(pk[:], pk_d, "pk")
        ld(cstT[:], cst_d[:, 0:2304], "cst")
        ld(maskT[:], cst_d[:, 2304 + 768:2304 + 768 + 128], "mask")
        for l in range(DEPTH):
            ld(lng[:, l, :], ln_g[l:l + 1, :].broadcast_to([128, 512]), ("lng", l))
            ld(lnb[:, l, :], ln_b[l:l + 1, :].broadcast_to([128, 512]), ("lnb", l))
        cast_dma(cbT[:], cst_d[:, 2304:2304 + 384], "cb")
        cast_dma(bsb[:], b_s, "bsb")
        cast_dma(wab[:].rearrange("c (l n d) -> c l n d", l=DEPTH, n=8), w_a.rearrange("l n c d -> c l n d"), "wab")
        cast_dma(wxb[:].rearrange("c (l n d) -> c l n d", l=DEPTH, n=8), w_x.rearrange("l n c d -> c l n d"), "wxb")
        for l in range(0):
            for g in range(8):
                cast_dma(wb_in[l, :, g * 768:(g + 1) * 768], w_in[l, :, g * 768:(g + 1) * 768], ("wb_in", l, g))
            for g in range(2):
                cast_dma(wb_out[l, g * 1024:(g + 1) * 1024, :], w_out[l, g * 1024:(g + 1) * 1024, :], ("wb_out", l))
            cast_dma(wb_gate[l], w_gate[l], ("wb_gate", l))
            cast_dma(wb_pp[l], w_pp[l], ("wb_pp", l))
            cast_dma(pTb[l], pT[l], ("pTb", l))

        pg.op("dve", lambda e: e.memset(Sst[:], 0.0), writes=["Sst"])
        pg.op("dve", lambda e: e.memset(Sb[:], 0.0), writes=["Sb"])
        pg.op("dve", lambda e: e.memset(hst[:], 0.0), writes=["hst"])
        pg.op("dve", lambda e: e.memset(hist[:], 0.0), writes=[("hist", i) for i in range(16)])
        for i in range(2):
            pg.op("dve", lambda e, i=i: e.memset(xls[i][:], 0.0), writes=[("xl", i)])
        for l in range(DEPTH):
            u = fpool.get()
            ld(funits[u][:], w_sT[l], ("f", u))
            pg.op("dve", lambda e, l=l, u=u: e.tensor_tensor(
                out=wsT[:, l, :].rearrange("p (g i) -> p g i", g=4),
                in0=funits[u][:].rearrange("p (g i) -> p g i", g=4),
                in1=maskT[:].unsqueeze(1).broadcast_to([128, 4, 128]), op=ALU.mult),
                reads=[("f", u), "mask"], writes=[("wsT", l)])
            fpool.put(u)
        pg.op("dve", lambda e: e.tensor_scalar(out=dv[:, 0:32], in0=pk[:, 132:164], scalar1=0.5, scalar2=None, op0=ALU.mult),
              reads=["pk"], writes=["dv0"])
        pg.op("act", lambda e: e.activation(out=small[:, 0:16], in_=pk[:, 164:180], func=AF.Exp, scale=-1.0),
              reads=["pk"], writes=["sm0"])
        pg.op("act", lambda e: e.activation(out=small[:, 16:32], in_=small[:, 0:16], func=AF.Ln, bias=onec, scale=1.0),
              reads=["sm0", "pk"], writes=["sm1"])
        pg.op("dve", lambda e: e.tensor_scalar(out=dv[:, 32:48], in0=small[:, 16:32], scalar1=-8.0, scalar2=None, op0=ALU.mult),
              reads=["sm1"], writes=["dv1"])
        pg.op("dve", lambda e: e.tensor_scalar(out=dv[:, 48:64], in0=small[:, 16:32], scalar1=-4.0, scalar2=None, op0=ALU.mult),
              reads=["sm1"], writes=["dv2"])
        DVK = ["dv0", "dv1", "dv2"]

        ring_i = [0]
        cur_t = [0]
        job_i = [0]

        def wload(dmas, rkeys):
            sl = ring_i[0] % NSLOT
            ring_i[0] += 1
            j = job_i[0]
            job_i[0] += 1
            for di, (vf, src) in enumerate(dmas):
                if cur_t[0] == 0:
                    pg.op("pool", lambda e, vf=vf, src=src, sl=sl: e.dma_start(out=vf(ring[sl]), in_=src),
                          reads=[], writes=[("ring", sl), "poolq"], dma_sem=s_ring[sl])
                    pg.op("sp", lambda e, vf=vf, sl=sl, j=j: e.dma_start(out=vf(scr[j]), in_=vf(ring[sl])),
                          reads=[("ring", sl)], writes=[("scr", j, di)], dma_sem=s_wb[sl])
                else:
                    pg.op("sp", lambda e, vf=vf, sl=sl, j=j: e.dma_start(out=vf(ring[sl]), in_=vf(scr[j])),
                          reads=[("scr", j, di)], writes=[("ring", sl)], dma_sem=s_rl[sl])
            return sl

        def w_in_job(l, g, ncols=768):
            c0 = g * 768
            src = w_in[l, :, c0:c0 + ncols].rearrange("(k p) c -> p k c", p=128)
            return wload([(lambda r, n=ncols: r[:, 0:8 * n].rearrange("p (k c) -> p k c", k=8), src)], [("wb_in", l, g)])

        def mm(bank, out_ap, lhsT, rhs, start, stop, reads, extra_w=()):
            pg.op("pe", lambda e: e.matmul(out_ap, lhsT, rhs, start=start, stop=stop),
                  reads=reads, writes=[("ps", bank)] + list(extra_w))

        def proj_fm(l, slot, ncols, c0, m, bank, mp=128):
            rv = ring[slot][:, 0:8 * ncols].rearrange("p (k c) -> p k c", k=8)
            for kc in range(8):
                mm(bank, banks[bank][0:m, :], rv[:, kc, c0:c0 + m], yT[:, kc, :], kc == 0, kc == 7,
                   [("ring", slot), ("yT", kc)])

        def proj_tm(slot, ncols, c0, n, blk, bank):
            rv = ring[slot][:, 0:8 * ncols].rearrange("p (k c) -> p k c", k=8)
            for kc in range(8):
                mm(bank, banks[bank][:, 0:n], yT[:, kc, blk * 128:(blk + 1) * 128], rv[:, kc, c0:c0 + n], kc == 0, kc == 7,
                   [("ring", slot), ("yT", kc)])

        def rms_stage(gcol, out_fn, out_keys_fn, out_dt_bf=True):
            bank = pspool.get()
            for c in range(8):
                b = bpool.get()
                pg.op("act", lambda e, c=c, b=b: e.activation(out=bunits[b][:], in_=hT[:, c, :], func=AF.Square),
                      reads=[("hT", c)], writes=[("b", b)])
                mm(bank, banks[bank][:], ones, bunits[b][:], c == 0, c == 7, [("b", b), "cb"])
                bpool.put(b)
            r = fpool.get()
            pg.op("act", lambda e: e.activation(out=funits[r][:], in_=banks[bank][:], func=AF.Ln, bias=epsc, scale=1.0 / D),
                  reads=[("ps", bank), "pk"], writes=[("f", r)])
            pg.op("act", lambda e: e.activation(out=funits[r][:], in_=funits[r][:], func=AF.Exp, scale=-0.5), reads=[("f", r)], writes=[("f", r)])
            pspool.put(bank)
            for c in range(8):
                pg.op("dve", lambda e, c=c: e.scalar_tensor_tensor(out=out_fn(c), in0=hT[:, c, :], scalar=gcol[:, c:c + 1],
                                                                  in1=funits[r][:], op0=ALU.mult, op1=ALU.mult),
                      reads=[("hT", c), ("f", r), "pk"], writes=out_keys_fn(c))
            fpool.put(r)

        gam = []
        for h in range(6):
            lg = np.log1p(-np.exp2(np.float32(-5.0 - h))).astype(np.float32)
            gam.append(float(np.exp(np.float32(128.0) * lg)))

        def do_tile(t):
            cur_t[0] = t
            job_i[0] = 0
            tok = slice(t * TT, (t + 1) * TT)
            pg.op("sp", lambda e, tok=tok: e.dma_start(out=hT[:], in_=xT.rearrange("(c p) t -> p c t", p=128)[:, :, tok]),
                  reads=[], writes=[("hT", c) for c in range(8)], dma_sem=s_x)
            pg.op("pool", lambda e, tok=tok: e.dma_start(out=posi[:], in_=pos[0:1, tok].broadcast_to([128, TT])),
                  reads=[], writes=["posi", "poolq"], dma_sem=s_pos)
            ang = fpool.get()
            tmp = fpool.get()
            kf = fpool.get()
            pg.op("dve", lambda e: e.tensor_copy(out=funits[ang][:], in_=posi[:]), reads=["posi"], writes=[("f", ang)])
            pg.op("dve", lambda e: e.tensor_scalar(out=funits[ang][:], in0=funits[ang][:], scalar1=pk[:, 180:181], scalar2=None, op0=ALU.mult),
                  reads=[("f", ang), "pk"], writes=[("f", ang)])
            C1 = 6.28125
            C2 = float(2.0 * np.pi - 6.28125)
            for which, off in ((1, 0.0), (0, 0.25)):
                pg.op("dve", lambda e, off=off: e.tensor_scalar(out=funits[tmp][:], in0=funits[ang][:], scalar1=float(1.0 / (2 * np.pi)), scalar2=off,
                                                               op0=ALU.mult, op1=ALU.add), reads=[("f", ang)], writes=[("f", tmp)])
                pg.op("dve", lambda e: e.tensor_copy(out=ki[:], in_=funits[tmp][:]), reads=[("f", tmp)], writes=["ki"])
                pg.op("dve", lambda e: e.tensor_copy(out=funits[kf][:], in_=ki[:]), reads=["ki"], writes=[("f", kf)])
                pg.op("dve", lambda e: e.scalar_tensor_tensor(out=funits[tmp][:], in0=funits[kf][:], scalar=-C1, in1=funits[ang][:], op0=ALU.mult, op1=ALU.add),
                      reads=[("f", kf), ("f", ang)], writes=[("f", tmp)])
                pg.op("dve", lambda e: e.scalar_tensor_tensor(out=funits[tmp][:], in0=funits[kf][:], scalar=-C2, in1=funits[tmp][:], op0=ALU.mult, op1=ALU.add),
                      reads=[("f", kf), ("f", tmp)], writes=[("f", tmp)])
                if off != 0.0:
                    pg.op("dve", lambda e: e.tensor_scalar(out=funits[tmp][:], in0=funits[tmp][:], scalar1=float(np.pi / 2), scalar2=None, op0=ALU.add),
                          reads=[("f", tmp)], writes=[("f", tmp)])
                pg.op("dve", lambda e: e.tensor_scalar(out=funits[tmp][:], in0=funits[tmp][:], scalar1=-3.1415925, scalar2=3.1415925, op0=ALU.max, op1=ALU.min),
                      reads=[("f", tmp)], writes=[("f", tmp)])
                pg.op("act", lambda e, which=which: e.activation(out=ropeT[:, which, :], in_=funits[tmp][:], func=AF.Sin),
                      reads=[("f", tmp)], writes=[("rope", which)])
            RS = float(128.0 ** -0.5)
            pg.op("dve", lambda e: e.tensor_scalar(out=ropeT[:, 2, :], in0=ropeT[:, 0, :], scalar1=RS, scalar2=None, op0=ALU.mult),
                  reads=[("rope", 0)], writes=[("rope", 2)])
            pg.op("dve", lambda e: e.tensor_scalar(out=ropeT[:, 3, :], in0=ropeT[:, 1, :], scalar1=pk[:, 182:183], scalar2=None, op0=ALU.mult),
                  reads=[("rope", 1), "pk"], writes=[("rope", 3)])
            pg.op("dve", lambda e: e.tensor_scalar(out=ropeT[:, 1, :], in0=ropeT[:, 1, :], scalar1=pk[:, 181:182], scalar2=None, op0=ALU.mult),
                  reads=[("rope", 1), ("rope", 3), "pk"], writes=[("rope", 1)])
            for u_ in (ang, tmp, kf):
                fpool.put(u_)

            def do_layer(l):
                if KSTAGE < 2:
                    return
                rms_stage(pk[:, l * 8:(l + 1) * 8], lambda c: yT[:, c, :], lambda c: [("yT", c)])
                if DEBUG and t == 0 and l == 0:
                    dbg("yT", yT[:], [128, 8, TT], BF16, [("yT", c_) for c_ in range(8)])

                if KSTAGE < 3:
                    return
                sx = w_in_job(l, 0)
                sg = w_in_job(l, 1)
                pend = []

                def lru_A(n):
                    bx = pspool.get()
                    proj_fm(l, sx, 768, n * 96, 96, bx)
                    xl = xls[n % 2]
                    xk = ("xl", n % 2)
                    hi = (l * 8 + n) * 4
                    pg.op("pool", lambda e: e.tensor_copy(out=xl[0:96, 0:3], in_=hist[0:96, hi:hi + 3]),
                          reads=[("hist", l * 8 + n)], writes=[xk])
                    pg.op("act", lambda e: e.activation(out=xl[0:96, 3:TT + 3], in_=banks[bx][0:96, :], func=AF.Copy),
                          reads=[("ps", bx)], writes=[xk])
                    pspool.put(bx)
                    bg = pspool.get()
                    proj_fm(l, sg, 768, n * 96, 96, bg)
                    return bg

                def lru_B1(n, bg):
                    xl = xls[n % 2]
                    xk = ("xl", n % 2)
                    col = l * 8 + n
                    cw = lambda k: pk[0:96, 52 + col * 4 + k:52 + col * 4 + k + 1]
                    xc = fpool.get()
                    XC = funits[xc][0:96, :]
                    pg.op("dve", lambda e: e.tensor_scalar(out=XC, in0=xl[0:96, 0:TT], scalar1=cw(0), scalar2=pk[0:96, 116 + col:117 + col],
                                                           op0=ALU.mult, op1=ALU.add), reads=[xk, "pk"], writes=[("f", xc)])
                    for k in (1, 2, 3):
                        pg.op("dve", lambda e, k=k: e.scalar_tensor_tensor(out=XC, in0=xl[0:96, k:k + TT], scalar=cw(k), in1=XC,
                                                                          op0=ALU.mult, op1=ALU.add), reads=[xk, "pk", ("f", xc)], writes=[("f", xc)])
                    hi = col * 4
                    pg.op("pool", lambda e: e.tensor_copy(out=hist[0:96, hi:hi + 3], in_=xl[0:96, TT:TT + 3]),
                          reads=[xk], writes=[("hist", col)])
                    xb = bpool.get()
                    pg.op("pool", lambda e: e.tensor_copy(out=bunits[xb][0:96, :], in_=XC), reads=[("f", xc)], writes=[("b", xb)])
                    ba = pspool.get()
                    bxx = pspool.get()
                    wo = col * 96
                    mm(ba, banks[ba][0:96, :], wab[0:96, wo:wo + 96], bunits[xb][0:96, :], True, True, [("b", xb), "wab"])
                    mm(bxx, banks[bxx][0:96, :], wxb[0:96, wo:wo + 96], bunits[xb][0:96, :], True, True, [("b", xb), "wxb"])
                    bpool.put(xb)
                    tha = fpool.get()
                    thx = fpool.get()
                    thg = fpool.get()
                    a2 = fpool.get()
                    A, X, G, A2 = (funits[i][0:96, :] for i in (tha, thx, thg, a2))
                    pg.op("act", lambda e: e.activation(out=A, in_=banks[ba][0:96, :], func=AF.Tanh, bias=dv[0:96, col:col + 1], scale=0.5),
                          reads=[("ps", ba)] + DVK, writes=[("f", tha)])
                    pg.op("act", lambda e: e.activation(out=X, in_=banks[bxx][0:96, :], func=AF.Tanh, bias=dv[0:96, 16 + col:17 + col], scale=0.5),
                          reads=[("ps", bxx)] + DVK, writes=[("f", thx)])
                    pspool.put(ba)
                    pspool.put(bxx)
                    pg.op("act", lambda e: e.activation(out=G, in_=banks[bg][0:96, :], func=AF.Tanh, scale=0.5),
                          reads=[("ps", bg)], writes=[("f", thg)])
                    pg.op("act", lambda e: e.activation(out=A2, in_=A, func=AF.Exp, bias=dv[0:96, 32 + col:33 + col], scale=dv[0:96, 32 + col:33 + col]),
                          reads=[("f", tha)] + DVK, writes=[("f", a2)])
                    pg.op("act", lambda e: e.activation(out=A, in_=A, func=AF.Exp, bias=dv[0:96, 48 + col:49 + col], scale=dv[0:96, 48 + col:49 + col]),
                          reads=[("f", tha)] + DVK, writes=[("f", tha)])
                    pg.op("act", lambda e: e.activation(out=A2, in_=A2, func=AF.Ln, bias=onec[0:96, :], scale=-1.0),
                          reads=[("f", a2), "pk"], writes=[("f", a2)])
                    pg.op("act", lambda e: e.activation(out=A2, in_=A2, func=AF.Exp, scale=0.5),
                          reads=[("f", a2)], writes=[("f", a2)])
                    return (bg, xc, tha, thx, thg, a2)

                def lru_B2(n, st_):
                    bg, xc, tha, thx, thg, a2 = st_
                    col = l * 8 + n
                    XC = funits[xc][0:96, :]
                    A, X, G, A2 = (funits[i][0:96, :] for i in (tha, thx, thg, a2))
                    pg.op("dve", lambda e: e.scalar_tensor_tensor(out=X, in0=X, scalar=1.0, in1=XC, op0=ALU.add, op1=ALU.mult),
                          reads=[("f", thx), ("f", xc)], writes=[("f", thx)])
                    pg.op("dve", lambda e: e.scalar_tensor_tensor(out=X, in0=X, scalar=0.5, in1=A2, op0=ALU.mult, op1=ALU.mult),
                          reads=[("f", thx), ("f", a2)], writes=[("f", thx)])
                    pg.op("dve", lambda e: e.tensor_tensor_scan(out=XC, data0=A, data1=X, initial=hst[0:96, col:col + 1], op0=ALU.mult, op1=ALU.add),
                          reads=[("f", tha), ("f", thx), "hst"], writes=[("f", xc)])
                    pg.op("pool", lambda e: e.tensor_copy(out=hst[0:96, col:col + 1], in_=funits[xc][0:96, TT - 1:TT]),
                          reads=[("f", xc)], writes=["hst"])
                    pg.op("dve", lambda e: e.scalar_tensor_tensor(out=G, in0=G, scalar=1.0, in1=banks[bg][0:96, :], op0=ALU.add, op1=ALU.mult),
                          reads=[("f", thg), ("ps", bg)], writes=[("f", thg)])
                    pspool.put(bg)
                    pg.op("dve", lambda e: e.scalar_tensor_tensor(out=mixL[0:96, n, :], in0=G, scalar=0.5, in1=XC, op0=ALU.mult, op1=ALU.mult),
                          reads=[("f", thg), ("f", xc)], writes=[("mixL", n)])
                    if DEBUG and t == 0 and l == 0 and n == 0:
                        dbg("lru_h0", funits[xc][:], [128, TT], F32, [("f", xc)])
                    for u_ in (xc, tha, thx, thg, a2):
                        fpool.put(u_)

                bgs = {}
                sts = {}
                bgs[0] = lru_A(0)
                bgs[1] = lru_A(1)
                sts[0] = lru_B1(0, bgs[0])
                for n in range(8):
                    if n + 2 < 8:
                        bgs[n + 2] = lru_A(n + 2)
                    if n + 1 < 8:
                        sts[n + 1] = lru_B1(n + 1, bgs[n + 1])
                    lru_B2(n, sts[n])

                if KSTAGE < 4:
                    return
                sq_ = w_in_job(l, 2)
                sk_ = w_in_job(l, 3)
                sv_ = w_in_job(l, 4)
                sgr = w_in_job(l, 5)
                for hg in range(2):
                    items = []
                    for hh in range(3):
                        h = hg * 3 + hh

                        def rope_item(slot, cidx, sidx, dst, dkey, hh=hh, h=h):
                            bq = pspool.get()
                            proj_fm(l, slot, 768, h * 128, 128, bq)
                            qb = bpool.get()
                            pg.op("act", lambda e, bq=bq, qb=qb: e.activation(out=bunits[qb][:], in_=banks[bq][:], func=AF.Copy),
                                  reads=[("ps", bq)], writes=[("b", qb), ("psr", bq)])
                            bs_ = pspool.get()
                            mm(bs_, banks[bs_][:], perm, bunits[qb][:], True, True, [("b", qb), "cb"])
                            bpool.put(qb)
                            yield
                            if KQ < 1:
                                pspool.put(bq)
                                pspool.put(bs_)
                                return
                            t1 = fpool.get()
                            t2 = fpool.get()
                            pg.op("dve", lambda e, bq=bq, t1=t1, cidx=cidx: e.scalar_tensor_tensor(out=funits[t1][:], in0=ropeT[:, cidx, :], scalar=1.0, in1=banks[bq][:], op0=ALU.mult, op1=ALU.mult),
                                  reads=[("ps", bq), ("psr", bq), ("rope", cidx)], writes=[("f", t1)])
                            pg.op("dve", lambda e, bs_=bs_, t2=t2, sidx=sidx: e.scalar_tensor_tensor(out=funits[t2][:], in0=ropeT[:, sidx, :], scalar=1.0, in1=banks[bs_][:], op0=ALU.mult, op1=ALU.mult),
                                  reads=[("ps", bs_), ("rope", sidx)], writes=[("f", t2)])
                            pspool.put(bq)
                            pspool.put(bs_)
                            if KQ < 2:
                                fpool.put(t1)
                                fpool.put(t2)
                                return
                            lo_t = qlo if dkey == "qt" else klo
                            lkey = "qlo" if dkey == "qt" else "klo"
                            pg.op("dve", lambda e, t1=t1, t2=t2: e.tensor_tensor(out=funits[t1][:], in0=funits[t1][:], in1=funits[t2][:], op=ALU.add),
                                  reads=[("f", t1), ("f", t2)], writes=[("f", t1)])
                            pg.op("act", lambda e, t1=t1, dst=dst, hh=hh: e.activation(out=dst[:, hh, :], in_=funits[t1][:], func=AF.Copy),
                                  reads=[("f", t1)], writes=[(dkey, hh)])
                            pg.op("pool", lambda e, t2=t2, dst=dst, hh=hh: e.tensor_copy(out=funits[t2][:], in_=dst[:, hh, :]),
                                  reads=[(dkey, hh), ("f", t2)], writes=[("f", t2)])
                            pg.op("pool", lambda e, t1=t1, t2=t2, lo_t=lo_t, hh=hh: e.tensor_tensor(
                                out=lo_t[:, hh, :], in0=funits[t1][:], in1=funits[t2][:], op=ALU.subtract),
                                reads=[("f", t1), ("f", t2)], writes=[(lkey, hh)])
                            if dkey == "qt":
                                pg.op("pool", lambda e, t1=t1, hh=hh, h=h: e.tensor_tensor(
                                    out=qg[:, hh, :].rearrange("p (n i) -> p n i", n=4),
                                    in0=funits[t1][:].rearrange("p (n i) -> p n i", n=4),
                                    in1=qwc[:, h * 128:(h + 1) * 128].unsqueeze(1).broadcast_to([128, 4, 128]), op=ALU.mult),
                                    reads=[("f", t1), "cst"], writes=[("qg", hh)])
                            fpool.put(t1)
                            fpool.put(t2)

                        for args_ in ((sq_, 0, 1, qt, "qt"), (sk_, 2, 3, kt, "kt")):
                            items.append(rope_item(*args_))
                    next(items[0])
                    for i_ in range(6):
                        if i_ + 1 < 6:
                            next(items[i_ + 1])
                        for _ in items[i_]:
                            pass
                    if DEBUG and t == 0 and l == 0 and hg == 0:
                        dbg("qt", qt[:], [128, 3, TT], BF16, [("qt", i) for i in range(3)])
                        dbg("kt", kt[:], [128, 3, TT], BF16, [("kt", i) for i in range(3)])
                    if KRET < 2:
                        continue
                    for blk in range(4):
                        bv = pspool.get()
                        proj_tm(sv_, 768, hg * 384, 384, blk, bv)
                        pg.op("act", lambda e, bv=bv, blk=blk: e.activation(out=vbf[:, blk, :], in_=banks[bv][:, 0:384], func=AF.Copy),
                              reads=[("ps", bv)], writes=[("vbf", blk)])
                        pg.op("dve", lambda e, bv=bv, blk=blk, hg=hg: e.scalar_tensor_tensor(out=vw[:, blk, :], in0=kwc[:, hg * 384:(hg + 1) * 384], scalar=1.0, in1=banks[bv][:, 0:384], op0=ALU.mult, op1=ALU.mult),
                              reads=[("ps", bv), "cst"], writes=[("vw", blk)])
                        pspool.put(bv)
                    if KRET < 3:
                        continue
                    for blk in range(4):
                        bt = pspool.get()
                        btb = banks[bt][:].bitcast(BF16)
                        for hh in range(3):
                            pg.op("pe", lambda e, btb=btb, hh=hh, blk=blk: e.transpose(btb[:, hh * 128:(hh + 1) * 128], kt[:, hh, blk * 128:(blk + 1) * 128], ident),
                                  reads=[("kt", hh), "cb"], writes=[("ps", bt)])
                        pg.op("act", lambda e, btb=btb, blk=blk: e.activation(out=ktm[:, blk, :], in_=btb[:, 0:384], func=AF.Copy),
                              reads=[("ps", bt)], writes=[("ktm", blk)])
                        pspool.put(bt)
                    if KRET < 4:
                        continue
                    bo = [pspool.get() for _ in range(3)]
                    for n in range(4):
                        cs = slice(n * 128, (n + 1) * 128)
                        for hh in range(3):
                            h = hg * 3 + hh
                            mm(bo[hh], banks[bo[hh]][:, cs], Sb[:, l, h * 128:(h + 1) * 128], qg[:, hh, cs], True, False, [("Sb", l, hg), ("qg", hh)])
                        bkv = pspool.get()
                        for hh in range(3):
                            mm(bkv, banks[bkv][:, hh * 128:(hh + 1) * 128], ktm[:, n, hh * 128:(hh + 1) * 128], vw[:, n, hh * 128:(hh + 1) * 128], True, True,
                               [("ktm", n), ("vw", n)])
                        for hh in range(3):
                            h = hg * 3 + hh
                            pg.op("dve", lambda e, h=h, hh=hh, bkv=bkv: e.scalar_tensor_tensor(
                                out=Sst[:, l, h * 128:(h + 1) * 128], in0=Sst[:, l, h * 128:(h + 1) * 128], scalar=gam[h],
                                in1=banks[bkv][:, hh * 128:(hh + 1) * 128], op0=ALU.mult, op1=ALU.add),
                                reads=[("ps", bkv), ("Sst", l, hg)], writes=[("Sst", l, hg)])
                        pspool.put(bkv)
                        pg.op("act", lambda e, hg=hg: e.activation(out=Sb[:, l, hg * 384:(hg + 1) * 384], in_=Sst[:, l, hg * 384:(hg + 1) * 384], func=AF.Copy),
                              reads=[("Sst", l, hg)], writes=[("Sb", l, hg)])
                        bsc = pspool.get()
                        for hh in range(3):
                            osc = banks[bsc][:, hh * 128:(hh + 1) * 128]
                            mm(bsc, osc, kt[:, hh, cs], qt[:, hh, cs], True, False, [("kt", hh), ("qt", hh)])
                            mm(bsc, osc, kt[:, hh, cs], qlo[:, hh, cs], False, False, [("kt", hh), ("qlo", hh)])
                            mm(bsc, osc, klo[:, hh, cs], qt[:, hh, cs], False, True, [("klo", hh), ("qt", hh)])
                        Pb = bpool.get()
                        pg.op("dve", lambda e, bsc=bsc, Pb=Pb, hg=hg: e.scalar_tensor_tensor(out=bunits[Pb][:, 0:384], in0=decayT[:, hg * 384:(hg + 1) * 384], scalar=1.0, in1=banks[bsc][:, 0:384], op0=ALU.mult, op1=ALU.mult),
                              reads=[("ps", bsc), "cst"], writes=[("b", Pb)])
                        pspool.put(bsc)
                        for hh in range(3):
                            h = hg * 3 + hh
                            mm(bo[hh], banks[bo[hh]][:, cs], vbf[:, n, hh * 128:(hh + 1) * 128], bunits[Pb][:, hh * 128:(hh + 1) * 128], False, True,
                               [("vbf", n), ("b", Pb)])
                        bpool.put(Pb)
                    if KRET < 5:
                        for b_ in bo:
                            pspool.put(b_)
                        continue
                    for hh in range(3):
                        h = hg * 3 + hh
                        bg = pspool.get()
                        proj_fm(l, sgr, 768, h * 128, 128, bg)
                        b = bpool.get()
                        pg.op("act", lambda e, b=b, boa=banks[bo[hh]][:]: e.activation(out=bunits[b][:], in_=boa, func=AF.Square),
                              reads=[("ps", bo[hh])], writes=[("b", b)])
                        bm = pspool.get()
                        mm(bm, banks[bm][:], ones, bunits[b][:], True, True, [("b", b), "cb"])
                        bpool.put(b)
                        r = fpool.get()
                        pg.op("act", lambda e, r=r, bm=bm: e.activation(out=funits[r][:], in_=banks[bm][:], func=AF.Ln, bias=epsc, scale=1.0 / 128),
                              reads=[("ps", bm), "pk"], writes=[("f", r)])
                        pspool.put(bm)
                        pg.op("act", lambda e, r=r: e.activation(out=funits[r][:], in_=funits[r][:], func=AF.Exp, scale=-0.5), reads=[("f", r)], writes=[("f", r)])
                        on = fpool.get()
                        pg.op("dve", lambda e, r=r, on=on, boa=banks[bo[hh]][:], h=h: e.scalar_tensor_tensor(out=funits[on][:], in0=funits[r][:], scalar=pk[:, 40 + l * 6 + h:41 + l * 6 + h],
                                                                                        in1=boa, op0=ALU.mult, op1=ALU.mult),
                              reads=[("ps", bo[hh]), ("f", r), "pk"], writes=[("f", on)])
                        fpool.put(r)
                        tg = fpool.get()
                        pg.op("act", lambda e, tg=tg, bg=bg: e.activation(out=funits[tg][:], in_=banks[bg][:], func=AF.Tanh, scale=0.5),
                              reads=[("ps", bg)], writes=[("f", tg)])
                        pg.op("dve", lambda e, tg=tg, bg=bg: e.scalar_tensor_tensor(out=funits[tg][:], in0=funits[tg][:], scalar=1.0, in1=banks[bg][:], op0=ALU.add, op1=ALU.mult),
                              reads=[("f", tg), ("ps", bg)], writes=[("f", tg)])
                        pspool.put(bg)
                        pg.op("dve", lambda e, tg=tg, on=on, h=h: e.scalar_tensor_tensor(out=mixR[:, h, :], in0=funits[tg][:], scalar=0.5, in1=funits[on][:], op0=ALU.mult, op1=ALU.mult),
                              reads=[("f", tg), ("f", on)], writes=[("mixR", h)])
                        if DEBUG and t == 0 and l == 0 and h == 0:
                            dbg("ret_on0", funits[on][:], [128, TT], F32, [("f", on)])
                        fpool.put(tg)
                        fpool.put(on)
                    for b_ in bo:
                        pspool.put(b_)

                if KSTAGE < 5:
                    return
                su = w_in_job(l, 6, 512)
                def w_cols_job(c0, ncols):
                    src = w_in[l, :, c0:c0 + ncols].rearrange("(k p) c -> p k c", p=128)
                    return wload([(lambda r, n=ncols: r[:, 0:8 * n].rearrange("p (k c) -> p k c", k=8), src)],
                                 [("wb_in", l, c0 // 768), ("wb_in", l, (c0 + ncols - 1) // 768)])
                svs = w_cols_job(5120, 512)
                sgs = w_cols_job(5632, 512)
                for blk in range(4):
                    bv = pspool.get()
                    proj_tm(svs, 512, 0, 512, blk, bv)
                    so = blk * 12
                    tq = fpool.get()
                    pg.op("act", lambda e, bv=bv, tq=tq: e.activation(out=funits[tq][:], in_=banks[bv][:], func=AF.Copy),
                          reads=[("ps", bv)], writes=[("f", tq)])
                    pspool.put(bv)
                    pg.op("dve", lambda e, tq=tq, so=so: e.bn_stats(out=small[:, so:so + 6], in_=funits[tq][:]), reads=[("f", tq)], writes=[("small", blk)])
                    pg.op("dve", lambda e, so=so: e.bn_aggr(out=small[:, so + 6:so + 8], in_=small[:, so:so + 6]), reads=[("small", blk)], writes=[("small", blk)])
                    pg.op("act", lambda e, so=so: e.activation(out=small[:, so + 8:so + 9], in_=small[:, so + 7:so + 8], func=AF.Ln, bias=epsc, scale=1.0),
                          reads=[("small", blk), "pk"], writes=[("small", blk)])
                    pg.op("act", lambda e, so=so: e.activation(out=small[:, so + 8:so + 9], in_=small[:, so + 8:so + 9], func=AF.Exp, scale=-0.5), reads=[("small", blk)], writes=[("small", blk)])
                    pg.op("dve", lambda e, so=so, tq=tq: e.scalar_tensor_tensor(out=funits[tq][:], in0=funits[tq][:], scalar=small[:, so + 6:so + 7], in1=lng[:, l, :],
                                                                                op0=ALU.subtract, op1=ALU.mult),
                          reads=[("f", tq), ("small", blk), ("lng", l)], writes=[("f", tq)])
                    pg.op("dve", lambda e, so=so, tq=tq, blk=blk: e.scalar_tensor_tensor(out=vn[:, blk, :], in0=funits[tq][:], scalar=small[:, so + 8:so + 9], in1=lnb[:, l, :],
                                                                                  op0=ALU.mult, op1=ALU.add),
                          reads=[("f", tq), ("small", blk), ("lnb", l)], writes=[("vn", blk)])
                    fpool.put(tq)
                for g in range(4):
                    gs = slice(g * 128, (g + 1) * 128)
                    bmix = pspool.get()
                    for n in range(4):
                        cs = slice(n * 128, (n + 1) * 128)
                        mm(bmix, banks[bmix][:, cs], vn[:, n, gs], wsT[:, l, gs], True, False, [("vn", n), ("wsT", l)])
                        mm(bmix, banks[bmix][:, cs], ones[0:1, :], bsb[0:1, l * 512 + g * 128:l * 512 + (g + 1) * 128], False, True, ["cb", "bsb"])
                    bu = pspool.get()
                    proj_fm(l, su, 512, g * 128, 128, bu)
                    bgg = pspool.get()
                    proj_fm(l, sgs, 512, g * 128, 128, bgg)
                    m_ = fpool.get()
                    pg.op("act", lambda e, m_=m_, bmix=bmix: e.activation(out=funits[m_][:], in_=banks[bmix][:], func=AF.Copy),
                          reads=[("ps", bmix)], writes=[("f", m_)])
                    pspool.put(bmix)
                    pg.op("dve", lambda e, m_=m_, bu=bu: e.scalar_tensor_tensor(out=funits[m_][:], in0=funits[m_][:], scalar=1.0, in1=banks[bu][:], op0=ALU.mult, op1=ALU.mult),
                          reads=[("f", m_), ("ps", bu)], writes=[("f", m_)])
                    pspool.put(bu)
                    tg = fpool.get()
                    pg.op("act", lambda e, tg=tg, bgg=bgg: e.activation(out=funits[tg][:], in_=banks[bgg][:], func=AF.Tanh, scale=0.5),
                          reads=[("ps", bgg)], writes=[("f", tg)])
                    pg.op("dve", lambda e, tg=tg, bgg=bgg: e.scalar_tensor_tensor(out=funits[tg][:], in0=funits[tg][:], scalar=1.0, in1=banks[bgg][:], op0=ALU.add, op1=ALU.mult),
                          reads=[("f", tg), ("ps", bgg)], writes=[("f", tg)])
                    pspool.put(bgg)
                    pg.op("dve", lambda e, tg=tg, m_=m_, g=g: e.scalar_tensor_tensor(out=mixS[:, g, :], in0=funits[tg][:], scalar=0.5, in1=funits[m_][:], op0=ALU.mult, op1=ALU.mult),
                          reads=[("f", tg), ("f", m_)], writes=[("mixS", g)])
                    fpool.put(tg)
                    fpool.put(m_)
                if DEBUG and t == 0 and l == 0:
                    dbg("mixL", mixL[:], [128, 8, TT], BF16, [("mixL", i) for i in range(8)])
                    dbg("mixR", mixR[:], [128, 6, TT], BF16, [("mixR", i) for i in range(6)])
                    dbg("mixS", mixS[:], [128, 4, TT], BF16, [("mixS", i) for i in range(4)])

                if KSTAGE < 6:
                    return
                for piece in range(4):
                    c0 = piece * 256
                    dm = [
                        (lambda r: r[0:96, 0:2048].rearrange("p (n c) -> p n c", n=8),
                         w_out[l, 0:768, c0:c0 + 256].rearrange("(n p) c -> p n c", p=96)),
                        (lambda r: r[:, 2048:3584].rearrange("p (n c) -> p n c", n=6),
                         w_out[l, 768:1536, c0:c0 + 256].rearrange("(n p) c -> p n c", p=128)),
                        (lambda r: r[:, 3584:4608].rearrange("p (n c) -> p n c", n=4),
                         w_out[l, 1536:2048, c0:c0 + 256].rearrange("(n p) c -> p n c", p=128)),
                    ]
                    so_ = wload(dm, [("wb_out", l)])
                    R = ring[so_]
                    for d2 in range(2):
                        c = piece * 2 + d2
                        bw = pspool.get()
                        steps = []
                        for n in range(8):
                            steps.append((R[0:96, n * 256 + d2 * 128:n * 256 + d2 * 128 + 128], mixL[0:96, n, :], ("mixL", n)))
                        for h in range(6):
                            steps.append((R[:, 2048 + h * 256 + d2 * 128:2048 + h * 256 + d2 * 128 + 128], mixR[:, h, :], ("mixR", h)))
                        for g in range(4):
                            steps.append((R[:, 3584 + g * 256 + d2 * 128:3584 + g * 256 + d2 * 128 + 128], mixS[:, g, :], ("mixS", g)))
                        for i, (lh, rh, key) in enumerate(steps):
                            mm(bw, banks[bw][:], lh, rh, i == 0, i == len(steps) - 1, [("ring", so_), key])
                        pg.op("dve", lambda e, c=c, bw=bw: e.scalar_tensor_tensor(out=hT[:, c, :], in0=hT[:, c, :], scalar=1.0, in1=banks[bw][:], op0=ALU.mult, op1=ALU.add),
                              reads=[("hT", c), ("ps", bw)], writes=[("hT", c)])
                        pspool.put(bw)

                if KSTAGE < 7:
                    return
                rms_stage(pk[:, 16 + l * 8:16 + (l + 1) * 8], lambda c: yT[:, c, :], lambda c: [("yT", c)])
                pg.op("pool", lambda e, tok=tok: e.dma_start(out=pTt[:], in_=pT[l].rearrange("(k p) t -> p k t", p=128)[:, :, tok]),
                      reads=[], writes=["pTt", "poolq"], dma_sem=s_p)
                spp = wload([(lambda r: r[:, 0:2048].rearrange("p (k c) -> p k c", k=2), w_pp[l].rearrange("(k p) c -> p k c", p=128))], [("wb_pp", l)])
                for half in range(2):
                    sgt = wload([(lambda r: r[:, 0:4096].rearrange("p (k c) -> p k c", k=8),
                                  w_gate[l, :, half * 512:(half + 1) * 512].rearrange("(k p) c -> p k c", p=128))], [("wb_gate", l)])
                    for cc in range(4):
                        c = half * 4 + cc
                        bgt = pspool.get()
                        proj_fm(l, sgt, 512, cc * 128, 128, bgt)
                        bpp = pspool.get()
                        rp = ring[spp][:, 0:2048].rearrange("p (k c) -> p k c", k=2)
                        for k in range(2):
                            mm(bpp, banks[bpp][:], rp[:, k, c * 128:(c + 1) * 128], pTt[:, k, :], k == 0, k == 1, [("ring", spp), "pTt"])
                        th = fpool.get()
                        pg.op("act", lambda e, th=th, bgt=bgt: e.activation(out=funits[th][:], in_=banks[bgt][:], func=AF.Tanh, scale=0.5),
                              reads=[("ps", bgt)], writes=[("f", th)])
                        pspool.put(bgt)
                        pg.op("dve", lambda e, th=th, bpp=bpp: e.scalar_tensor_tensor(out=funits[th][:], in0=funits[th][:], scalar=1.0, in1=banks[bpp][:], op0=ALU.add, op1=ALU.mult),
                              reads=[("f", th), ("ps", bpp)], writes=[("f", th)])
                        pspool.put(bpp)
                        pg.op("dve", lambda e, th=th, c=c: e.scalar_tensor_tensor(out=hT[:, c, :], in0=funits[th][:], scalar=0.5, in1=hT[:, c, :], op0=ALU.mult, op1=ALU.add),
                              reads=[("f", th), ("hT", c)], writes=[("hT", c)])
                        fpool.put(th)
                if DEBUG and t == 0:
                    dbg(f"h_l{l}", hT[:], [128, 8, TT], F32, [("hT", c) for c in range(8)])

            for l_ in range(DEPTH):
                do_layer(l_)

            outs = {}

            def fin_out(c):
                u = fpool.get()
                outs[c] = u
                return funits[u][:]

            def fin_keys(c):
                return [("f", outs[c])]

            bank = pspool.get()
            for c in range(8):
                b = bpool.get()
                pg.op("act", lambda e, c=c, b=b: e.activation(out=bunits[b][:], in_=hT[:, c, :], func=AF.Square), reads=[("hT", c)], writes=[("b", b)])
                mm(bank, banks[bank][:], ones, bunits[b][:], c == 0, c == 7, [("b", b), "cb"])
                bpool.put(b)
            r = fpool.get()
            pg.op("act", lambda e, r=r, bank=bank: e.activation(out=funits[r][:], in_=banks[bank][:], func=AF.Ln, bias=epsc, scale=1.0 / D),
                  reads=[("ps", bank), "pk"], writes=[("f", r)])
            pg.op("act", lambda e, r=r: e.activation(out=funits[r][:], in_=funits[r][:], func=AF.Exp, scale=-0.5), reads=[("f", r)], writes=[("f", r)])
            pspool.put(bank)
            for c in range(8):
                u = fpool.get()
                pg.op("dve", lambda e, c=c, u=u, r=r: e.scalar_tensor_tensor(out=funits[u][:], in0=hT[:, c, :], scalar=pk[:, 32 + c:33 + c], in1=funits[r][:],
                                                                          op0=ALU.mult, op1=ALU.mult),
                      reads=[("hT", c), ("f", r), "pk"], writes=[("f", u)])
                pg.op("sp", lambda e, c=c, u=u, tok=tok: e.dma_start(out=outT[c * 128:(c + 1) * 128, tok], in_=funits[u][:]),
                      reads=[("f", u)], writes=[("outsem", c)], dma_sem=s_out[c])
                fpool.put(u)
            fpool.put(r)

        for t_ in range(NT_RUN):
            do_tile(t_)

        def final_waits():
            res = []
            for s_ in s_out:
                if s_ in pg.dma_counts:
                    res.append((s_, pg.dma_counts[s_]))
            if s_dbg in pg.dma_counts:
                res.append((s_dbg, pg.dma_counts[s_dbg]))
            return res

        with nc.Block() as block:
            pg.emit(block, eng_sems, final_waits)
    return nc, list(dbg_outs.keys())


def _consts():
    f32 = np.float32
    H, C = 6, 128
    log_g = np.log1p(-np.exp2(-5.0 - np.arange(H, dtype=f32))).astype(f32)
    idx = np.arange(C, dtype=f32)
    diff = idx[:, None] - idx[None, :]
    causal = diff >= 0
    decay = np.where(causal[None], np.exp(np.where(causal, diff, 0.0)[None] * log_g[:, None, None]), 0.0).astype(f32)
    decayT = np.ascontiguousarray(decay.transpose(2, 0, 1)).reshape(128, 768)
    k_w = np.exp((C - 1.0 - idx)[:, None] * log_g[None, :]).astype(f32)
    kwc = np.repeat(k_w[:, :, None], 128, axis=2).reshape(128, 768)
    q_w = np.exp((idx + 1.0)[:, None] * log_g[None, :]).astype(f32)
    qw = np.broadcast_to(q_w.T.reshape(1, 768), (128, 768))
    ident = np.eye(128, dtype=f32)
    perm = np.zeros((128, 128), f32)
    for m in range(128):
        perm[(m + 64) % 128, m] = 1.0
    ones = np.ones((128, 128), f32)
    pad = np.zeros((128, 384), f32)
    mask = (idx[:, None] <= idx[None, :]).astype(f32)
    cst = np.concatenate([decayT, kwc, qw, ident, perm, ones, pad, mask], axis=1).astype(f32)
    return np.ascontiguousarray(cst)


def _pack(inp):
    f32 = np.float32
    pk = np.zeros((128, NPK), f32)
    pk[:, 0:16] = inp["mix_norm"].reshape(2, 8, 128).transpose(2, 0, 1).reshape(128, 16)
    pk[:, 16:32] = inp["ple_norm"].reshape(2, 8, 128).transpose(2, 0, 1).reshape(128, 16)
    pk[:, 32:40] = inp["final_norm"].reshape(8, 128).T
    pk[:, 40:52] = inp["ret_norm"].reshape(2, 6, 128).transpose(2, 0, 1).reshape(128, 12)
    cw = inp["lru_conv_w"].reshape(2, 4, 8, 96).transpose(3, 0, 2, 1)
    pk[0:96, 52:116] = cw.reshape(96, 64)
    for name, c0 in (("lru_conv_b", 116), ("lru_b_a", 132), ("lru_b_x", 148), ("lru_lambda", 164)):
        pk[0:96, c0:c0 + 16] = inp[name].reshape(2, 8, 96).transpose(2, 0, 1).reshape(96, 16)
    half = 64
    inv = (10000.0 ** (-np.arange(half, dtype=f32) / half)).astype(f32)
    pk[:, 180] = np.concatenate([inv, inv])
    sgn = np.concatenate([-np.ones(64, f32), np.ones(64, f32)])
    pk[:, 181] = sgn
    pk[:, 182] = sgn * f32(128.0 ** -0.5)
    pk[:, 183] = EPS
    pk[:, 184] = 1.0
    return pk


def kernel(**inputs):
    inp = {k: np.asarray(v) for k, v in inputs.items()}
    f32 = np.float32
    nc, dbg_names = build_nc()
    cst = _consts()
    pk = _pack(inp)
    shared = {
        "w_in": np.ascontiguousarray(inp["w_in"], f32),
        "w_out": np.ascontiguousarray(inp["w_out"], f32),
        "w_gate": np.ascontiguousarray(inp["w_ple_gate"], f32),
        "w_pp": np.ascontiguousarray(inp["w_ple_proj"], f32),
        "w_a": np.ascontiguousarray(inp["lru_w_a"], f32),
        "w_x": np.ascontiguousarray(inp["lru_w_x"], f32),
        "w_sT": np.ascontiguousarray(inp["sgu_w_s"].transpose(0, 3, 1, 2).reshape(2, 128, 512), f32),
        "b_s": np.ascontiguousarray(inp["sgu_b_s"].reshape(1, 1024), f32),
        "ln_g": np.ascontiguousarray(inp["sgu_ln_g"], f32),
        "ln_b": np.ascontiguousarray(inp["sgu_ln_b"], f32),
        "pk": pk,
        "cst": cst,
    }
    in_maps = []
    for core in range(8):
        b = core % 4
        m = dict(shared)
        m["xT"] = np.ascontiguousarray(inp["x"][b].T, f32)
        m["pT"] = np.ascontiguousarray(inp["p"][:, b].transpose(0, 2, 1), f32)
        m["pos"] = np.ascontiguousarray(inp["positions"][b].reshape(1, S), np.int32)
        in_maps.append(m)
    res = run_bass_kernel_spmd(nc, in_maps, core_ids=list(range(8)))
    out = np.stack([np.ascontiguousarray(res.results[b]["outT"].T) for b in range(4)], axis=0).astype(f32)
    if DEBUG:
        kernel.debug = {n: res.results[0]["dbg_" + n] for n in dbg_names}
    return out
```
